# Optimizing a Trainium2 kernel written in Bass

```python
import math
import jax
import jax.numpy as jnp
from jax import lax
import numpy as np

D_MODEL = 2048
BATCH = 4
SEQ = 2048
DEPTH = 4

A_W = D_MODEL // 2
A_GROUPS = 8
A_GW = A_W // A_GROUPS
CHUNK = 128
B_W = D_MODEL // 2
HY_ORDER = 2
HY_EMB = 33
HY_FW = 64
HY_MIN_DECAY = -math.log(1e-2) / 1.5
HY_MAX_DECAY = -math.log(1e-2) / 0.3
C_N = 64
C_W = D_MODEL // 2
C_H = C_W // C_N
W_LORA = 48
A_LORA = 48
V_LORA = 32
G_LORA = 128
N_DIR = 2
N_BRANCH = 3
C_IN = 3 * C_W + G_LORA + N_DIR * W_LORA + N_DIR * A_LORA
N_IN = 2 * A_W + 3 * B_W + C_IN + N_BRANCH * D_MODEL
D_FF = -(-8 * D_MODEL // (3 * 256)) * 256
RMS_EPS = 1e-6
LN_EPS = 1e-5
GN_EPS = 64e-5

kernel_name = 'hybrid_gmlp_hyena_rwkv7_bidir_encoder'


def rms_norm(x, g):
    xf = x.astype(jnp.float32)
    y = xf * lax.rsqrt(jnp.mean(xf * xf, axis=-1, keepdims=True) + RMS_EPS)
    return (y * g.astype(jnp.float32)).astype(x.dtype)


def norm_last(x, eps):
    xf = x.astype(jnp.float32)
    mu = jnp.mean(xf, axis=-1, keepdims=True)
    var = jnp.mean(jnp.square(xf - mu), axis=-1, keepdims=True)
    return (xf - mu) * lax.rsqrt(var + eps)


def shift_prev(z):
    return jnp.pad(z[:, :-1], ((0, 0), (1, 0), (0, 0)))


def shift_next(z):
    return jnp.pad(z[:, 1:], ((0, 0), (0, 1), (0, 0)))


def spatial_gating(u_raw, v_raw, ln_g, ln_b, ws, bs):
    bsz, seq, _ = u_raw.shape
    u = jax.nn.gelu(u_raw, approximate=False)
    v = jax.nn.gelu(v_raw, approximate=False)
    v = (norm_last(v, LN_EPS) * ln_g.astype(jnp.float32) + ln_b.astype(jnp.float32)).astype(u.dtype)
    vc = v.reshape(bsz, seq // CHUNK, CHUNK, A_GROUPS, A_GW)
    s = jnp.einsum('gpq,bcqgd->bcpgd', ws, vc) + bs.T[:, :, None]
    return u * s.reshape(bsz, seq, A_W)


def hyena_filter_spectrum(seq, w1, b1, w2, b2, w3, b3, w4, freq, log_decay):
    w1, b1, w2, b2, w3, b3, w4, freq, log_decay = (
        p.astype(jnp.float32) for p in (w1, b1, w2, b2, w3, b3, w4, freq, log_decay))
    t = jnp.linspace(0.0, 1.0, seq, dtype=jnp.float32)[:, None]
    bands = (HY_EMB - 1) // 2
    fr = jnp.linspace(1e-4, bands - 1, bands, dtype=jnp.float32)
    ang = (2.0 * math.pi / seq) * jnp.arange(seq, dtype=jnp.float32)[:, None] * fr[None, :]
    feats = jnp.concatenate([t, jnp.cos(ang), -jnp.sin(ang)], axis=-1)
    z = jnp.sin(freq * (feats @ w1 + b1))
    z = jnp.sin(freq * (z @ w2 + b2))
    z = jnp.sin(freq * (z @ w3 + b3))
    h = (z @ w4).reshape(seq, N_DIR, HY_ORDER, B_W)
    h = h * jnp.exp(-t[:, :, None, None] * jnp.exp(log_decay)[None])
    h = h / jnp.sum(jnp.abs(h), axis=(0, 1), keepdims=True)
    taps = jnp.concatenate(
        [h[:, 0], jnp.zeros((1, HY_ORDER, B_W), jnp.float32), h[:0:-1, 1]], axis=0)
    return jnp.fft.rfft(taps, axis=0)


def long_conv(z, spec, skip):
    seq = z.shape[1]
    zf = jnp.fft.rfft(z.astype(jnp.float32), n=2 * seq, axis=1)
    y = jnp.fft.irfft(zf * spec[None], n=2 * seq, axis=1)[:, :seq]
    return (y + skip.astype(jnp.float32) * z.astype(jnp.float32)).astype(z.dtype)


def hyena_mixer(cols, conv_w, conv_b, w1, b1, w2, b2, w3, b3, w4, freq, log_decay, bias_d):
    zc = conv_w[0] * shift_prev(cols) + conv_w[1] * cols + conv_w[2] * shift_next(cols) + conv_b
    x1, x2, v = jnp.split(zc, 3, axis=-1)
    spec = hyena_filter_spectrum(cols.shape[1], w1, b1, w2, b2, w3, b3, w4, freq, log_decay)
    z = x1 * long_conv(v, spec[:, 0], bias_d[0])
    return x2 * long_conv(z, spec[:, 1], bias_d[1])


def to_scan(z_fwd, z_bwd):
    return jnp.stack([z_fwd, z_bwd[:, ::-1]], axis=0).transpose(2, 0, 1, 3, 4).astype(jnp.float32)


def wkv7_scan(r, w, k, v, a, b):
    s0 = jnp.zeros(r.shape[1:] + (C_N,), jnp.float32)

    def step(s, inp):
        r_t, w_t, k_t, v_t, a_t, b_t = inp
        sa = jnp.einsum('dbhij,dbhj->dbhi', s, a_t)
        s = s * w_t[..., None, :] + sa[..., :, None] * b_t[..., None, :] + v_t[..., :, None] * k_t[..., None, :]
        return s, jnp.einsum('dbhij,dbhj->dbhi', s, r_t)

    _, y = lax.scan(step, s0, (r, w, k, v, a, b))
    return y


def rwkv7_mixer(cols, v_first, v_mix, mu_prev, mu_next, w0, w2, a0, a2, g2, k_k, k_a, r_k, ln_g, ln_b):
    bsz, seq, _ = cols.shape
    c = cols + mu_prev * (shift_prev(cols) - cols) + mu_next * (shift_next(cols) - cols)
    r, k, v, gd, wd, ad = jnp.split(
        c, [C_W, 2 * C_W, 3 * C_W, 3 * C_W + G_LORA, 3 * C_W + G_LORA + N_DIR * W_LORA], axis=-1)
    wd = wd.reshape(bsz, seq, N_DIR, W_LORA)
    ad = ad.reshape(bsz, seq, N_DIR, A_LORA)
    w_log = -jax.nn.softplus(-(w0 + jnp.einsum('btdr,drc->btdc', jnp.tanh(wd), w2))) - 0.5
    decay = jnp.exp(-jnp.exp(w_log))
    a = jax.nn.sigmoid(a0 + jnp.einsum('btdr,drc->btdc', ad, a2))
    if v_mix is None:
        v_first = v
    else:
        v0, v1, v2 = v_mix
        v = v + (v_first - v) * jax.nn.sigmoid(v0 + (v @ v1) @ v2)
    g = jax.nn.sigmoid(gd) @ g2
    heads = (bsz, seq, C_H, C_N)
    dheads = (bsz, seq, N_DIR, C_H, C_N)
    kk = (k * k_k).reshape(heads).astype(jnp.float32)
    kk = kk / jnp.maximum(jnp.sqrt(jnp.sum(kk * kk, axis=-1, keepdims=True)), 1e-12)
    k_d = (k[:, :, None] * (1.0 + (a - 1.0) * k_a)).reshape(dheads)
    a_h = a.reshape(dheads)
    decay_h = decay.reshape(dheads)
    r_h = r.reshape(heads)
    v_h = v.reshape(heads)
    y = wkv7_scan(
        to_scan(r_h, r_h),
        to_scan(decay_h[:, :, 0], decay_h[:, :, 1]),
        to_scan(k_d[:, :, 0], k_d[:, :, 1]),
        to_scan(v_h, v_h),
        to_scan(-kk, -kk),
        to_scan(kk * a_h[:, :, 0], kk * a_h[:, :, 1]))
    y = (y[:, 0] + y[::-1, 1]).transpose(1, 0, 2, 3)
    y = norm_last(y, GN_EPS).reshape(bsz, seq, C_W) * ln_g.astype(jnp.float32) + ln_b.astype(jnp.float32)
    bonus = jnp.sum(r_h[:, :, None] * k_d * r_k, axis=(2, 4))[..., None] * v_h
    out = (y + bonus.reshape(bsz, seq, C_W).astype(jnp.float32)) * g.astype(jnp.float32)
    return out.astype(cols.dtype), v_first


def setup_inputs(seed: int = 0) -> dict:
    key = jax.random.key(seed)
    keys = iter(jax.random.split(key, 64))
    L = DEPTH

    def nrm(shape, scale):
        return scale * jax.random.normal(next(keys), shape, jnp.float32)

    def unif(shape, lo, hi):
        return jax.random.uniform(next(keys), shape, jnp.float32, lo, hi)

    return {
        'x': nrm((BATCH, SEQ, D_MODEL), 1.0),
        'norm_mix_g': 1.0 + nrm((L, D_MODEL), 0.02),
        'w_in': nrm((L, D_MODEL, N_IN), D_MODEL ** -0.5),
        'gm_ln_g': 1.0 + nrm((L, A_W), 0.02),
        'gm_ln_b': nrm((L, A_W), 0.02),
        'gm_ws': nrm((L, A_GROUPS, CHUNK, CHUNK), CHUNK ** -0.5),
        'gm_bs': 1.0 + nrm((L, A_GROUPS, CHUNK), 0.1),
        'hy_conv_w': nrm((L, 3, 3 * B_W), 0.5),
        'hy_conv_b': nrm((L, 3 * B_W), 0.02),
        'hy_w1': nrm((L, HY_EMB, HY_FW), HY_EMB ** -0.5),
        'hy_b1': nrm((L, HY_FW), 0.02),
        'hy_w2': nrm((L, HY_FW, HY_FW), HY_FW ** -0.5),
        'hy_b2': nrm((L, HY_FW), 0.02),
        'hy_w3': nrm((L, HY_FW, HY_FW), HY_FW ** -0.5),
        'hy_b3': nrm((L, HY_FW), 0.02),
        'hy_w4': nrm((L, HY_FW, N_DIR * HY_ORDER * B_W), HY_FW ** -0.5),
        'hy_freq': 1.0 + nrm((L, HY_FW), 0.1),
        'hy_log_decay': unif((L, N_DIR, HY_ORDER, B_W), math.log(HY_MIN_DECAY), math.log(HY_MAX_DECAY)),
        'hy_bias_d': nrm((L, HY_ORDER, B_W), 0.5),
        'rw_mu_prev': unif((L, C_IN), 0.0, 0.5),
        'rw_mu_next': unif((L, C_IN), 0.0, 0.5),
        'rw_w0': unif((L, N_DIR, C_W), -6.0, -1.0),
        'rw_w2': nrm((L, N_DIR, W_LORA, C_W), 0.5 * W_LORA ** -0.5),
        'rw_a0': nrm((L, N_DIR, C_W), 0.1),
        'rw_a2': nrm((L, N_DIR, A_LORA, C_W), A_LORA ** -0.5),
        'rw_v0': 1.0 + nrm((L - 1, C_W), 0.1),
        'rw_v1': nrm((L - 1, C_W, V_LORA), C_W ** -0.5),
        'rw_v2': nrm((L - 1, V_LORA, C_W), V_LORA ** -0.5),
        'rw_g2': nrm((L, G_LORA, C_W), G_LORA ** -0.5),
        'rw_k_k': 0.85 + nrm((L, C_W), 0.02),
        'rw_k_a': 1.0 + nrm((L, C_W), 0.02),
        'rw_r_k': nrm((L, C_H, C_N), 0.1),
        'rw_ln_g': 1.0 + nrm((L, C_W), 0.02),
        'rw_ln_b': nrm((L, C_W), 0.02),
        'w_branch_a': nrm((L, A_W, D_MODEL), A_W ** -0.5),
        'w_branch_b': nrm((L, B_W, D_MODEL), B_W ** -0.5),
        'w_branch_c': nrm((L, C_W, D_MODEL), C_W ** -0.5),
        'w_out': nrm((L, D_MODEL, D_MODEL), D_MODEL ** -0.5),
        'norm_ffn_g': 1.0 + nrm((L, D_MODEL), 0.02),
        'w_ffn_gate': nrm((L, D_MODEL, D_FF), D_MODEL ** -0.5),
        'w_ffn_up': nrm((L, D_MODEL, D_FF), D_MODEL ** -0.5),
        'w_ffn_down': nrm((L, D_FF, D_MODEL), D_FF ** -0.5),
        'norm_final_g': 1.0 + nrm((D_MODEL,), 0.02),
    }


def reference(x, norm_mix_g, w_in, gm_ln_g, gm_ln_b, gm_ws, gm_bs, hy_conv_w, hy_conv_b,
              hy_w1, hy_b1, hy_w2, hy_b2, hy_w3, hy_b3, hy_w4, hy_freq, hy_log_decay, hy_bias_d,
              rw_mu_prev, rw_mu_next, rw_w0, rw_w2, rw_a0, rw_a2, rw_v0, rw_v1, rw_v2, rw_g2,
              rw_k_k, rw_k_a, rw_r_k, rw_ln_g, rw_ln_b, w_branch_a, w_branch_b, w_branch_c, w_out,
              norm_ffn_g, w_ffn_gate, w_ffn_up, w_ffn_down, norm_final_g):
    bsz, seq, _ = x.shape
    v_first = None
    for l in range(DEPTH):
        h = rms_norm(x, norm_mix_g[l])
        proj = h @ w_in[l]
        a_cols, b_cols, c_cols, gate_cols = jnp.split(
            proj, [2 * A_W, 2 * A_W + 3 * B_W, 2 * A_W + 3 * B_W + C_IN], axis=-1)
        u, v = jnp.split(a_cols, 2, axis=-1)
        y_a = spatial_gating(u, v, gm_ln_g[l], gm_ln_b[l], gm_ws[l], gm_bs[l])
        y_b = hyena_mixer(b_cols, hy_conv_w[l], hy_conv_b[l], hy_w1[l], hy_b1[l], hy_w2[l], hy_b2[l],
                          hy_w3[l], hy_b3[l], hy_w4[l], hy_freq[l], hy_log_decay[l], hy_bias_d[l])
        v_mix = None if l == 0 else (rw_v0[l - 1], rw_v1[l - 1], rw_v2[l - 1])
        y_c, v_first = rwkv7_mixer(c_cols, v_first, v_mix, rw_mu_prev[l], rw_mu_next[l], rw_w0[l], rw_w2[l],
                                   rw_a0[l], rw_a2[l], rw_g2[l], rw_k_k[l], rw_k_a[l], rw_r_k[l],
                                   rw_ln_g[l], rw_ln_b[l])
        gates = jax.nn.sigmoid(gate_cols).reshape(bsz, seq, N_BRANCH, D_MODEL)
        merged = (gates[:, :, 0] * (y_a @ w_branch_a[l])
                  + gates[:, :, 1] * (y_b @ w_branch_b[l])
                  + gates[:, :, 2] * (y_c @ w_branch_c[l]))
        x = x + merged @ w_out[l]
        h = rms_norm(x, norm_ffn_g[l])
        x = x + (jax.nn.silu(h @ w_ffn_gate[l]) * (h @ w_ffn_up[l])) @ w_ffn_down[l]
    return rms_norm(x, norm_final_g)
```

```python
import math
import numpy as np
import ml_dtypes
from contextlib import ExitStack
import concourse.bass as bass
import concourse.mybir as mybir
from concourse.bass_utils import run_bass_kernel_spmd

F32 = mybir.dt.float32
BF16 = mybir.dt.bfloat16
AF = mybir.ActivationFunctionType
ALU = mybir.AluOpType
AX = mybir.AxisListType

NCORES = 8
NCORES_USED = 4
T = 2048
D = 2048
DEPTH = 4
A_W = 1024
B_W = 1024
C_W = 1024
C_IN = 3392
N_IN = 14656
D_FF = 5632
NT = T // 128
NTG = T // 512
KD = D // 128
RMS_EPS = 1e-6
LN_EPS = 1e-5
GN_EPS = 64e-5
HY_MIN_DECAY = -math.log(1e-2) / 1.5
HY_MAX_DECAY = -math.log(1e-2) / 0.3
OFF_B = 2 * A_W
OFF_C = OFF_B + 3 * B_W
OFF_G = OFF_C + C_IN

WEIGHT_SHAPES = {
    'w_in': (DEPTH, D, N_IN), 'gm_ln_g': (DEPTH, A_W), 'gm_ln_b': (DEPTH, A_W),
    'gm_ws': (DEPTH, 8, 128, 128), 'gm_bs': (DEPTH, 8, 128),
    'hy_w1': (DEPTH, 33, 64), 'hy_w2': (DEPTH, 64, 64), 'hy_w3': (DEPTH, 64, 64), 'hy_w4': (DEPTH, 64, 4096),
    'hy_log_decay': (DEPTH, 2, 2, 1024), 'hy_bias_d': (DEPTH, 2, 1024),
    'rw_w2': (DEPTH, 2, 48, 1024), 'rw_a2': (DEPTH, 2, 48, 1024),
    'rw_v1': (DEPTH - 1, 1024, 32), 'rw_v2': (DEPTH - 1, 32, 1024), 'rw_g2': (DEPTH, 128, 1024),
    'w_branch_a': (DEPTH, A_W, D), 'w_branch_b': (DEPTH, B_W, D), 'w_branch_c': (DEPTH, C_W, D),
    'w_out': (DEPTH, D, D), 'w_ffn_gate': (DEPTH, D, D_FF), 'w_ffn_up': (DEPTH, D, D_FF),
    'w_ffn_down': (DEPTH, D_FF, D),
}

PC = {}
_o = 0
for _n, _c in [('nmg', 16), ('nfg', 16), ('cw0', 24), ('cw1', 24), ('cw2', 24), ('cb', 24), ('mup', 29), ('mun', 29),
               ('w0', 16), ('a0', 16), ('v0', 8), ('kk', 8), ('ka', 8), ('rk', 8), ('lng', 8), ('lnb', 8),
               ('hyb', 4), ('nfin', 16)]:
    PC[_n] = (_o, _c)
    _o += _c
NPC = _o
C_TILES = [(i * 128, 128) for i in range(25)] + [(3200, 48), (3248, 48), (3296, 48), (3344, 48)]


def _cols(v):
    return np.ascontiguousarray(np.asarray(v, np.float32).reshape(-1, 128).T)


def make_pcol(inp):
    pc = np.zeros((DEPTH, 128, NPC), np.float32)

    def put(l, name, arr):
        o, c = PC[name]
        assert arr.shape == (128, c), (name, arr.shape)
        pc[l, :, o:o + c] = arr
    for l in range(DEPTH):
        put(l, 'nmg', _cols(inp['norm_mix_g'][l]))
        put(l, 'nfg', _cols(inp['norm_ffn_g'][l]))
        for j in range(3):
            put(l, 'cw%d' % j, _cols(inp['hy_conv_w'][l, j]))
        put(l, 'cb', _cols(inp['hy_conv_b'][l]))
        for nm, src in (('mup', 'rw_mu_prev'), ('mun', 'rw_mu_next')):
            a = np.zeros((128, 29), np.float32)
            for i, (c0, m) in enumerate(C_TILES):
                a[:m, i] = inp[src][l, c0:c0 + m]
            put(l, nm, a)
        put(l, 'w0', _cols(inp['rw_w0'][l]))
        put(l, 'a0', _cols(inp['rw_a0'][l]))
        if l > 0:
            put(l, 'v0', _cols(inp['rw_v0'][l - 1]))
        put(l, 'kk', _cols(inp['rw_k_k'][l]))
        put(l, 'ka', _cols(inp['rw_k_a'][l]))
        put(l, 'rk', _cols(inp['rw_r_k'][l]))
        put(l, 'lng', _cols(inp['rw_ln_g'][l]))
        put(l, 'lnb', _cols(inp['rw_ln_b'][l]))
        hb = np.zeros((128, 4), np.float32)
        hb[:64, 0] = inp['hy_b1'][l]
        hb[:64, 1] = inp['hy_b2'][l]
        hb[:64, 2] = inp['hy_b3'][l]
        hb[:64, 3] = inp['hy_freq'][l]
        put(l, 'hyb', hb)
        put(l, 'nfin', _cols(inp['norm_final_g']))
    return pc


def make_consts():
    c = {}
    c['identF'] = np.eye(128, dtype=np.float32)
    s = np.arange(128)[:, None]
    t = np.arange(128)[None, :]
    su = (s < t).astype(np.float32)
    iu = (s <= t).astype(np.float32)
    c['maskU4'] = np.concatenate([su, iu, su, iu], axis=1)
    c['maskL'] = (s > t).astype(np.float32)
    c['blockones'] = ((s // 64) == (t // 64)).astype(np.float32)
    c['onesF'] = np.ones((128, 128), np.float32)
    c['identB'] = np.eye(128).astype(ml_dtypes.bfloat16)
    c['maskL4'] = np.tile(c['maskL'], (1, 4))
    c['ident4'] = np.tile(c['identF'], (1, 4))
    rm = np.ones((128, T), np.float32)
    rm[:, 0::128] = 0.0
    c['rmask'] = rm
    n = np.arange(T, dtype=np.float64)
    f = np.arange(T, dtype=np.float64) + 0.5
    ang = 2.0 * np.pi * np.outer(n, f) / (2 * T)
    c['CT'] = np.cos(ang).astype(ml_dtypes.bfloat16)
    c['ST'] = np.sin(ang).astype(ml_dtypes.bfloat16)
    c['CF'] = np.ascontiguousarray(np.cos(ang).T).astype(ml_dtypes.bfloat16)
    c['SF'] = np.ascontiguousarray(np.sin(ang).T).astype(ml_dtypes.bfloat16)
    tt = np.linspace(0.0, 1.0, T, dtype=np.float32)[:, None]
    bands = 16
    fr = np.linspace(1e-4, bands - 1, bands, dtype=np.float32)
    a2 = (np.float32(2.0 * math.pi / T) * np.arange(T, dtype=np.float32)[:, None]) * fr[None, :]
    feats = np.concatenate([tt, np.cos(a2), -np.sin(a2)], axis=-1).astype(np.float32)
    c['featsT'] = np.ascontiguousarray(feats.T)
    c['tneg'] = np.ascontiguousarray((-tt[:, 0]).reshape(NT, 128).T)
    return c


CONST_SHAPES = {'identF': ((128, 128), F32), 'maskU4': ((128, 512), F32), 'maskL': ((128, 128), F32),
                'blockones': ((128, 128), F32), 'onesF': ((128, 128), F32), 'maskL4': ((128, 512), F32),
                'ident4': ((128, 512), F32), 'rmask': ((128, T), F32), 'identB': ((128, 128), BF16),
                'CT': ((T, T), BF16), 'ST': ((T, T), BF16), 'CF': ((T, T), BF16), 'SF': ((T, T), BF16),
                'featsT': ((33, T), F32), 'tneg': ((128, NT), F32)}


class Sched:
    NDMA = 12

    def __init__(self, nc, es):
        self.nc = nc
        self.engs = {'pe': nc.tensor, 'act': nc.scalar, 'dve': nc.vector, 'pool': nc.gpsimd, 'sp': nc.sync}
        self.sem = {k: es.enter_context(nc.semaphore("s_" + k)) for k in self.engs}
        self.cnt = {k: 0 for k in self.engs}
        self.waited = {}
        self.dsem = {}
        self.dcnt = {}
        self.dnext = {}
        for q in ('sp', 'act', 'pool'):
            self.dsem[q] = [es.enter_context(nc.semaphore("d_%s%d" % (q, i))) for i in range(self.NDMA)]
            self.dcnt[q] = [0] * self.NDMA
            self.dnext[q] = 0
        self.res = {}
        self.ninst = 0

    def _wait(self, e, tok):
        if tok[0] == 'e':
            _, f, v = tok
            if f == e and e == 'pe':
                return
            key = (e, f)
        else:
            _, q, slot, v = tok
            key = (e, 'd', q, slot)
        if self.waited.get(key, 0) >= v:
            return
        self.waited[key] = v
        sem = self.sem[tok[1]] if tok[0] == 'e' else self.dsem[tok[1]][tok[2]]
        self.engs[e].wait_ge(sem, v)

    def _deps(self, e, reads, writes):
        for r in reads:
            st = self.res.get(r)
            if st and st[0] is not None:
                self._wait(e, st[0])
        for w in writes:
            st = self.res.get(w)
            if st:
                if st[0] is not None:
                    self._wait(e, st[0])
                for t in st[1].values():
                    self._wait(e, t)

    def _record(self, tok, reads, writes):
        for r in reads:
            st = self.res.get(r)
            if st is None:
                st = self.res[r] = [None, {}]
            k = tok[1] if tok[0] == 'e' else (tok[1], tok[2])
            st[1][k] = tok
        for w in writes:
            self.res[w] = [tok, {}]

    def op(self, e, fn, reads=(), writes=()):
        self._deps(e, reads, writes)
        ins = fn()
        self.cnt[e] += 1
        ins.then_inc(self.sem[e], 1)
        self._record(('e', e, self.cnt[e]), reads, writes)
        self.ninst += 1
        return ins

    def dma(self, q, out, in_, reads=(), writes=(), **kw):
        slot = self.dnext[q]
        self.dnext[q] = (slot + 1) % self.NDMA
        if self.dcnt[q][slot] > 0:
            self._wait(q, ('d', q, slot, self.dcnt[q][slot]))
        self._deps(q, reads, writes)
        ins = self.engs[q].dma_start(out=out, in_=in_, **kw)
        self.dcnt[q][slot] += 16
        ins.then_inc(self.dsem[q][slot], 16)
        self._record(('d', q, slot, self.dcnt[q][slot]), reads, writes)
        self.ninst += 1
        return ins

    def barrier(self):
        for e in self.engs:
            for f in self.engs:
                if self.cnt[f] > 0:
                    key = (e, f)
                    if self.waited.get(key, 0) < self.cnt[f]:
                        self.waited[key] = self.cnt[f]
                        self.engs[e].wait_ge(self.sem[f], self.cnt[f])
            for q in self.dsem:
                for slot in range(self.NDMA):
                    if self.dcnt[q][slot] > 0:
                        self._wait(e, ('d', q, slot, self.dcnt[q][slot]))
        self.res = {}


def build(NL=DEPTH, dbg=(), stop=None):
    nc = bass.Bass("TRN2", target_bir_lowering=False)

    def din(name, shape, dt=F32):
        return nc.dram_tensor(name, list(shape), dt, kind="ExternalInput").ap()

    def dscr(name, shape, dt=F32):
        return nc.dram_tensor(name, list(shape), dt, kind="Internal").ap()

    def dout(name, shape, dt=F32):
        return nc.dram_tensor(name, list(shape), dt, kind="ExternalOutput").ap()

    x_in = din("x", [T, D])
    W = {n: din(n, s) for n, s in WEIGHT_SHAPES.items()}
    pcol_d = din("pcol", [DEPTH, 128, NPC])
    CD = {n: din(n, s, dt) for n, (s, dt) in CONST_SHAPES.items()}
    out_d = dout("out", [T, D])
    DBG = {}
    dbg_shapes = {'xT': ([D, T], F32), 'xT0': ([D, T], F32), 'uT': ([A_W, T], F32), 'bT': ([3 * B_W, T], F32), 'cT': ([C_IN, T], F32),
                  'gT': ([3 * D, T], BF16), 'yaT': ([A_W, T], BF16), 'ybT': ([B_W, T], BF16),
                  'ycT': ([C_W, T], BF16), 'hTd': ([D, T], BF16), 'spec': ([2, 2, T, B_W], F32),
                  'mergedT': ([D, T], BF16)}
    for n in dbg:
        DBG[n] = dout("dbg_" + n, *dbg_shapes[n])

    xT = dscr("xT_s", [D, T])
    uT = dscr("uT_s", [A_W, T])
    bT = dscr("bT_s", [3 * B_W, T])
    cT = dscr("cT_s", [C_IN, T])
    gT = dscr("gT_s", [3 * D, T], BF16)
    yaT = dscr("yaT_s", [A_W, T], BF16)
    ybT = dscr("ybT_s", [B_W, T], BF16)
    ycT = dscr("ycT_s", [C_W, T], BF16)
    mgT = dscr("mgT_s", [D, T], BF16)
    actT = dscr("actT_s", [D_FF, T], BF16)
    vfT = dscr("vfT_s", [C_W, T])
    x1tok = dscr("x1tok_s", [T, B_W])
    hfil = dscr("hfil_s", [T, 4096])
    spec = dscr("spec_s", [2, 2, T, B_W])
    xTv = xT.rearrange("(dk p) t -> p dk t", p=128)

    with ExitStack() as es:
        S = Sched(nc, es)

        uid = [0]

        def sb(name, shape, dt=F32, st=es):
            uid[0] += 1
            return st.enter_context(nc.sbuf_tensor("sb%d_%s" % (uid[0], name), list(shape), dt))

        PSA = es.enter_context(nc.psum_tensor("psA", [128, 2048], F32))
        PSB = es.enter_context(nc.psum_tensor("psB", [128, 2048], F32))

        def PSb(i):
            return (PSA if i < 4 else PSB)[:, (i % 4) * 512:(i % 4 + 1) * 512]

        def PSfull(a):
            return PSA if a == 0 else PSB

        def ACT(out, in_, func, r, w, **kw):
            return S.op('act', lambda: nc.scalar.activation(out=out, in_=in_, func=func, **kw), r, w)

        def TT(out, a, b, op, r, w, eng='dve'):
            e = nc.vector if eng == 'dve' else nc.gpsimd
            return S.op(eng, lambda: e.tensor_tensor(out=out, in0=a, in1=b, op=op), r, w)

        def TS(out, a, s1, s2, op0, op1, r, w, eng='dve'):
            e = nc.vector if eng == 'dve' else nc.gpsimd
            if s2 is None:
                return S.op(eng, lambda: e.tensor_scalar(out=out, in0=a, scalar1=s1, scalar2=None, op0=op0), r, w)
            return S.op(eng, lambda: e.tensor_scalar(out=out, in0=a, scalar1=s1, scalar2=s2, op0=op0, op1=op1), r, w)

        def STT(out, a, s, b, op0, op1, r, w, eng='dve'):
            e = nc.vector if eng == 'dve' else nc.gpsimd
            return S.op(eng, lambda: e.scalar_tensor_tensor(out=out, in0=a, scalar=s, in1=b, op0=op0, op1=op1), r, w)

        def RECIP(out, in_, r, w):
            return S.op('dve', lambda: nc.vector.reciprocal(out=out, in_=in_), r, w)

        def CP(out, in_, r, w, eng='dve'):
            if eng == 'act':
                return S.op('act', lambda: nc.scalar.copy(out=out, in_=in_), r, w)
            e = nc.vector if eng == 'dve' else nc.gpsimd
            return S.op(eng, lambda: e.tensor_copy(out=out, in_=in_), r, w)

        def MSET(ap, val, w, eng='pool'):
            e = nc.vector if eng == 'dve' else nc.gpsimd
            return S.op(eng, lambda: e.memset(ap, val), (), w)

        def MM(out, lhsT, rhs, start, stop, r, w):
            return S.op('pe', lambda: nc.tensor.matmul(out, lhsT, rhs, start=start, stop=stop), r, w)

        def TR(out, in_, ident, r, w):
            return S.op('pe', lambda: nc.tensor.transpose(out, in_, ident), r, w)

        def DMA(q, out, in_, r, w, **kw):
            return S.dma(q, out, in_, r, w, **kw)

        cp_flip = [0]

        def EVAC(out, in_, r, w):
            cp_flip[0] ^= 1
            return CP(out, in_, r, w, eng='act' if cp_flip[0] else 'dve')

        identF = sb("identF", [128, 128])
        maskU4 = sb("maskU4", [128, 512])
        maskL = sb("maskL", [128, 128])
        blockones = sb("blockones", [128, 128])
        onesF = sb("onesF", [128, 128])
        tneg = sb("tneg", [128, NT])
        maskL4 = sb("maskL4", [128, 512])
        ident4 = sb("ident4", [128, 512])
        identB = sb("identB", [128, 128], BF16)
        pcol = sb("pcol", [128, NPC])
        for n, t_ in (('identF', identF), ('maskU4', maskU4), ('maskL', maskL), ('blockones', blockones),
                      ('onesF', onesF), ('tneg', tneg), ('maskL4', maskL4), ('ident4', ident4), ('identB', identB)):
            DMA('sp', t_[:], CD[n][:, :], (), ['const'])

        def pc(name, i=0, m=128):
            o, c = PC[name]
            return pcol[0:m, o + i:o + i + 1]

        def dump(name, src_ap):
            if name in DBG:
                DMA('sp', DBG[name], src_ap, ['ALLDRAM'], [('dbg', name)])

        with ExitStack() as ph:
            xt = [sb("p0x%d" % i, [128, D], st=ph) for i in range(2)]
            stg = [sb("p0s%d" % i, [128, 4, 128], st=ph) for i in range(4)]
            n = 0
            for i in range(NT):
                b = i % 2
                DMA('sp', xt[b][:], x_in[i * 128:(i + 1) * 128, :], (), [('p0x', b)])
                for q in range(4):
                    bank = n % 8
                    s_ = n % 4
                    for j in range(4):
                        dk = q * 4 + j
                        TR(PSb(bank)[:, j * 128:(j + 1) * 128], xt[b][:, dk * 128:(dk + 1) * 128], identF[:],
                           [('p0x', b), 'const'], [('ps', bank)])
                    EVAC(stg[s_][:].rearrange("p a b -> p (a b)"), PSb(bank), [('ps', bank)], [('p0s', s_)])
                    DMA('act', xTv[:, q * 4:(q + 1) * 4, i * 128:(i + 1) * 128], stg[s_][:], [('p0s', s_)], ['xT'])
                    n += 1
        S.barrier()
        if 'xT0' in DBG:
            DMA('sp', DBG['xT0'], xT, (), [('dbg', 'xT0')])

        def norm_phase(hT, gname):
            with ExitStack() as ph:
                ph.enter_context(nc.named_scope("norm"))
                xin = [sb("nx%d" % i, [128, KD, 512], st=ph) for i in range(2)]
                sq = [sb("nsq%d" % i, [128, 512], st=ph) for i in range(2)]
                rs = [sb("nrs%d" % i, [128, 512], st=ph) for i in range(2)]
                for tg in range(NTG):
                    b = tg % 2
                    tsl = slice(tg * 512, (tg + 1) * 512)
                    DMA('sp', xin[b][:], xTv[:, :, tsl], ['xT'], [('nx', b)])
                    for dk in range(KD):
                        ACT(sq[dk % 2][:], xin[b][:, dk, :], AF.Square, [('nx', b)], [('nsq', dk % 2)])
                        MM(PSb(b), onesF[:], sq[dk % 2][:], dk == 0, dk == KD - 1, [('nsq', dk % 2), 'const'], [('ps', b)])
                    ACT(rs[b][:], PSb(b), AF.Sqrt, [('ps', b)], [('nrs', b)], scale=1.0 / D, bias=RMS_EPS)
                    RECIP(rs[b][:], rs[b][:], [('nrs', b)], [('nrs', b)])
                    for dk in range(KD):
                        STT(hT[:, dk, tsl], xin[b][:, dk, :], pc(gname, dk), rs[b][:], ALU.mult, ALU.mult,
                            [('nx', b), 'pcol', ('nrs', b)], [('hT', dk, tg)])
                S.barrier()

        def linear_fm(Wv, KT, ctiles, rhs_fn, rkey_fn, epi, wp, wpname, full=False, banks=(0, 1, 2, 3, 4, 5, 6, 7)):
            panels = []
            cur = None
            for ci, (c0, m) in enumerate(ctiles):
                if cur is None or (c0 + m - cur[0]) > 512 or c0 != cur[1]:
                    cur = [c0, c0, []]
                    panels.append(cur)
                cur[2].append((ci, c0, m))
                cur[1] = c0 + m
            nb = 0
            for pi, (p0, p1, tl) in enumerate(panels):
                b = pi % 2
                DMA('pool', wp[b][:, 0:KT, 0:p1 - p0], Wv[:, :, p0:p1], (), [(wpname, b)])
                for (ci, c0, m) in tl:
                    lo = c0 - p0
                    if full:
                        a = nb % 2
                        nb += 1
                        for tg in range(NTG):
                            bank = a * 4 + tg
                            for kt in range(KT):
                                MM(PSb(bank)[0:m, :], wp[b][:, kt, lo:lo + m], rhs_fn(kt, tg), kt == 0, kt == KT - 1,
                                   [(wpname, b), rkey_fn(kt, tg)], [('ps', bank)])
                        epi(ci, c0, m, None, PSfull(a)[0:m, :], [('ps', a * 4 + j) for j in range(4)])
                    else:
                        for tg in range(NTG):
                            bank = banks[nb % len(banks)]
                            nb += 1
                            for kt in range(KT):
                                MM(PSb(bank)[0:m, :], wp[b][:, kt, lo:lo + m], rhs_fn(kt, tg), kt == 0, kt == KT - 1,
                                   [(wpname, b), rkey_fn(kt, tg)], [('ps', bank)])
                            epi(ci, c0, m, tg, PSb(bank)[0:m, :], [('ps', bank)])


        SW = 256
        NSG = T // SW
        INV_SCALE = 2.0 / (2 * T)

        def load_slabs(slC, slS, b, Cm, Sm, g):
            DMA('sp', slC[b][:], Cm.rearrange("(kt p) c -> p kt c", p=128)[:, :, g * SW:(g + 1) * SW], (), [('slC', b)])
            DMA('act', slS[b][:], Sm.rearrange("(kt p) c -> p kt c", p=128)[:, :, g * SW:(g + 1) * SW], (), [('slS', b)])

        def dft_fwd(slC, slS, srcC, srcS, keyC, keyS, epi):
            nb = 0
            for g in range(NSG):
                b = g % 2
                load_slabs(slC, slS, b, CD['CT'], CD['ST'], g)
                for fi in range(SW // 128):
                    ft = g * (SW // 128) + fi
                    for chh in range(2):
                        bC = (nb % 4) * 2
                        bS = bC + 1
                        nb += 1
                        for nt in range(NT):
                            MM(PSb(bC), slC[b][:, nt, fi * 128:(fi + 1) * 128], srcC[:, nt, chh * 512:(chh + 1) * 512],
                               nt == 0, nt == NT - 1, [('slC', b), (keyC, nt)], [('ps', bC)])
                        for nt in range(NT):
                            MM(PSb(bS), slS[b][:, nt, fi * 128:(fi + 1) * 128], srcS[:, nt, chh * 512:(chh + 1) * 512],
                               nt == 0, nt == NT - 1, [('slS', b), (keyS, nt)], [('ps', bS)])
                        epi(ft, chh, bC, bS)

        def hyena_phase(l):
            hy0 = ExitStack()
            asum = sb("hy_asum", [128, 4096], st=hy0)
            with ExitStack() as ph:
                ph.enter_context(nc.named_scope("hy_filt"))
                w1 = sb("hy_w1", [33, 64], st=ph)
                w2 = sb("hy_w2", [64, 64], st=ph)
                w3 = sb("hy_w3", [64, 64], st=ph)
                w4 = sb("hy_w4", [64, 4096], st=ph)
                fT = sb("hy_fT", [33, T], st=ph)
                zT = [sb("hy_zT%d" % i, [64, T], st=ph) for i in range(2)]
                tmpz = sb("hy_tmpz", [64, 512], st=ph)
                edb = sb("hy_edb", [128, 4096], st=ph)
                dec = [sb("hy_dec%d" % i, [128, 512], st=ph) for i in range(3)]
                hdt = [sb("hy_hd%d" % i, [128, 512], st=ph) for i in range(3)]
                hab2 = [sb("hy_hab2%d" % i, [128, 512], st=ph) for i in range(3)]
                habs = [sb("hy_ha%d" % i, [128, 512], st=ph) for i in range(2)]
                DMA('sp', w1[:], W['hy_w1'][l], (), ['hy_w'])
                DMA('sp', w2[:], W['hy_w2'][l], (), ['hy_w'])
                DMA('sp', w3[:], W['hy_w3'][l], (), ['hy_w'])
                DMA('sp', w4[:], W['hy_w4'][l], (), ['hy_w'])
                DMA('sp', fT[:], CD['featsT'][:, :], (), ['hy_fT'])
                DMA('sp', edb[:], W['hy_log_decay'][l:l + 1].rearrange("o a b c -> o (a b c)").broadcast_to([128, 4096]), (), ['hy_edb'])
                ACT(edb[:], edb[:], AF.Exp, ['hy_edb'], ['hy_edb'])
                ob_, _ = PC['hyb']
                fq = pcol[0:64, ob_ + 3:ob_ + 4]
                srcs = [(w1, fT, 33), (w2, zT[0], 64), (w3, zT[1], 64)]
                for li, (wl, src, kk_) in enumerate(srcs):
                    dst = zT[li % 2]
                    bcol = pcol[0:64, ob_ + li:ob_ + li + 1]
                    for tg in range(NTG):
                        tsl = slice(tg * 512, (tg + 1) * 512)
                        bank = tg % 2
                        MM(PSb(bank)[0:64, :], wl[0:kk_, :], src[0:kk_, tsl], True, True, ['hy_w', 'hy_fT', ('hy_zT', 0), ('hy_zT', 1)], [('ps', bank)])
                        TS(dst[:, tsl], PSb(bank)[0:64, :], bcol, fq, ALU.add, ALU.mult, [('ps', bank), 'pcol'], [('hy_zT', li % 2)])
                        for rnd in range(3):
                            TS(tmpz[:], dst[:, tsl], math.pi, -2 * math.pi, ALU.is_gt, ALU.mult, [('hy_zT', li % 2)], ['hy_tmpz'])
                            TT(dst[:, tsl], dst[:, tsl], tmpz[:], ALU.add, [('hy_zT', li % 2), 'hy_tmpz'], [('hy_zT', li % 2)])
                            TS(tmpz[:], dst[:, tsl], -math.pi, 2 * math.pi, ALU.is_lt, ALU.mult, [('hy_zT', li % 2)], ['hy_tmpz'])
                            TT(dst[:, tsl], dst[:, tsl], tmpz[:], ALU.add, [('hy_zT', li % 2), 'hy_tmpz'], [('hy_zT', li % 2)])
                        ACT(dst[:, tsl], dst[:, tsl], AF.Sin, [('hy_zT', li % 2)], [('hy_zT', li % 2)])
                z3 = zT[0]
                n = 0
                for cg in range(8):
                    csl = slice(cg * 512, (cg + 1) * 512)
                    a_ = cg % 2
                    MSET(habs[a_][:], 0.0, [('hy_ha', a_)], eng='pool')
                    for i in range(NT):
                        k = n % 3
                        n += 1
                        bank = 2 + k
                        MM(PSb(bank), z3[:, i * 128:(i + 1) * 128], w4[:, csl], True, True, [('hy_zT', 0), 'hy_w'], [('ps', bank)])
                        ACT(dec[k][:], edb[:, csl], AF.Exp, ['hy_edb', 'const'], [('hy_dec', k)], scale=tneg[:, i:i + 1])
                        TT(hdt[k][:], PSb(bank), dec[k][:], ALU.mult, [('ps', bank), ('hy_dec', k)], [('hy_hd', k)])
                        DMA('sp', hfil[i * 128:(i + 1) * 128, csl], hdt[k][:], [('hy_hd', k)], ['hfil'])
                        ACT(hab2[k][:], hdt[k][:], AF.Abs, [('hy_hd', k)], [('hy_hab2', k)])
                        TT(habs[a_][:], habs[a_][:], hab2[k][:], ALU.add, [('hy_hab2', k), ('hy_ha', a_)], [('hy_ha', a_)], eng='pool')
                    MM(PSb(6 + a_), onesF[:], habs[a_][:], True, True, [('hy_ha', a_), 'const'], [('ps', 6 + a_)])
                    EVAC(asum[:, csl], PSb(6 + a_), [('ps', 6 + a_)], ['hy_asum'])
                S.barrier()
            with ExitStack() as ph:
                ph.enter_context(nc.named_scope("hy_spec"))
                rinv = sb("hy_rinv", [128, 2048], st=ph)
                bdb = sb("hy_bdb", [128, 2048], st=ph)
                hs = sb("hy_hs", [128, NT, 1024], BF16, st=ph)
                hdf = sb("hy_hdf", [128, NT, 1024], BF16, st=ph)
                hf = [sb("hy_hf%d" % i, [128, 1024], st=ph) for i in range(2)]
                hb = [sb("hy_hb%d" % i, [128, 1024], st=ph) for i in range(2)]
                t1 = [sb("hy_t1%d" % i, [128, 1024], st=ph) for i in range(2)]
                slC = [sb("hy_slC%d" % i, [128, NT, SW], BF16, st=ph) for i in range(2)]
                slS = [sb("hy_slS%d" % i, [128, NT, SW], BF16, st=ph) for i in range(2)]
                ko = [sb("hy_ko%d" % i, [128, 512], st=ph) for i in range(4)]
                TT(rinv[:], asum[:, 0:2048], asum[:, 2048:4096], ALU.add, (), ['hy_rinv'])
                RECIP(rinv[:], rinv[:], ['hy_rinv'], ['hy_rinv'])
                DMA('sp', bdb[:], W['hy_bias_d'][l:l + 1].rearrange("o a c -> o (a c)").broadcast_to([128, 2048]), (), ['hy_bdb'])
                for o in range(2):
                    osl = slice(o * 1024, (o + 1) * 1024)
                    for i in range(NT):
                        k = i % 2
                        rows = slice(i * 128, (i + 1) * 128)
                        DMA('sp', hf[k][:], hfil[rows, o * 1024:(o + 1) * 1024], ['hfil'], [('hy_hf', k)])
                        DMA('act', hb[k][:], hfil[rows, 2048 + o * 1024:2048 + (o + 1) * 1024], ['hfil'], [('hy_hb', k)])
                        if i == 0:
                            MSET(hb[k][0:1, :], 0.0, [('hy_hb', k)], eng='dve')
                        TT(t1[k][:], hf[k][:], hb[k][:], ALU.add, [('hy_hf', k), ('hy_hb', k)], [('hy_t1', k)])
                        TT(hs[:, i, :], t1[k][:], rinv[:, osl], ALU.mult, [('hy_t1', k), 'hy_rinv'], [('hy_hs', i)])
                        TT(t1[k][:], hf[k][:], hb[k][:], ALU.subtract, [('hy_hf', k), ('hy_hb', k), ('hy_hs', i)], [('hy_t1', k)], eng='pool')
                        TT(hdf[:, i, :], t1[k][:], rinv[:, osl], ALU.mult, [('hy_t1', k), 'hy_rinv'], [('hy_hdf', i)], eng='pool')
                    kc = [0]

                    def epi_spec(ft, chh, bC, bS):
                        k0 = kc[0] % 2
                        kc[0] += 1
                        csl = slice(o * 1024 + chh * 512, o * 1024 + (chh + 1) * 512)
                        TT(ko[k0][:], PSb(bC), bdb[:, csl], ALU.add, [('ps', bC), 'hy_bdb'], [('hy_ko', k0)])
                        DMA('sp', spec[o, 0, ft * 128:(ft + 1) * 128, chh * 512:(chh + 1) * 512], ko[k0][:], [('hy_ko', k0)], ['spec'])
                        CP(ko[2 + k0][:], PSb(bS), [('ps', bS)], [('hy_ko', 2 + k0)], eng='act')
                        DMA('sp', spec[o, 1, ft * 128:(ft + 1) * 128, chh * 512:(chh + 1) * 512], ko[2 + k0][:], [('hy_ko', 2 + k0)], ['spec'])
                    dft_fwd(slC, slS, hs, hdf, 'hy_hs', 'hy_hdf', epi_spec)
                S.barrier()
            hy0.close()
            with ExitStack() as ph:
                z = sb("hy_z", [128, NT, 1024], BF16, st=ph)
                Yr = sb("hy_Yr", [128, NT, 1024], BF16, st=ph)
                Ys = sb("hy_Ys", [128, NT, 1024], BF16, st=ph)
                slC = [sb("hy_slC%d" % i, [128, NT, SW], BF16, st=ph) for i in range(2)]
                slS = [sb("hy_slS%d" % i, [128, NT, SW], BF16, st=ph) for i in range(2)]
                Kr = [sb("hy_Kr%d" % i, [128, 512], st=ph) for i in range(2)]
                Ks = [sb("hy_Ks%d" % i, [128, 512], st=ph) for i in range(2)]
                Zc = [sb("hy_Zc%d" % i, [128, 512], st=ph) for i in range(2)]
                Zs = [sb("hy_Zs%d" % i, [128, 512], st=ph) for i in range(2)]
                ta = [sb("hy_ta%d" % i, [128, 512], st=ph) for i in range(2)]
                tb = [sb("hy_tb%d" % i, [128, 512], st=ph) for i in range(2)]
                ld = [sb("hy_ld%d" % i, [128, T], st=ph) for i in range(2)]
                stg = [sb("hy_stg%d" % i, [128, 4, 128], st=ph) for i in range(2)]
                xm = [sb("hy_xm%d" % i, [128, 512], st=ph) for i in range(2)]
                yo = [sb("hy_yo%d" % i, [128, 512], BF16, st=ph) for i in range(2)]
                x1v = x1tok.rearrange("(nt p) c -> p nt c", p=128)
                sc_ = nc.named_scope("hy_tr")
                sc_.__enter__()
                n = 0
                for which in range(2):
                    for ct in range(8):
                        k = n % 2
                        r0 = (2048 if which == 0 else 0) + ct * 128
                        DMA('sp', ld[k][:], bT[r0:r0 + 128, :], ['bcT'], [('hy_ld', k)])
                        for q in range(4):
                            bank = (n * 4 + q) % 8
                            for j in range(4):
                                nt = q * 4 + j
                                TR(PSb(bank)[:, j * 128:(j + 1) * 128], ld[k][:, nt * 128:(nt + 1) * 128], identF[:],
                                   [('hy_ld', k), 'const'], [('ps', bank)])
                            if which == 0:
                                EVAC(z[:, q * 4:(q + 1) * 4, ct * 128:(ct + 1) * 128], PSb(bank).rearrange("p (a b) -> p a b", a=4),
                                     [('ps', bank)], [('hy_z', q * 4 + j) for j in range(4)])
                            else:
                                s_ = q % 2
                                EVAC(stg[s_][:], PSb(bank).rearrange("p (a b) -> p a b", a=4), [('ps', bank)], [('hy_stg', s_)])
                                DMA('act', x1v[:, q * 4:(q + 1) * 4, ct * 128:(ct + 1) * 128], stg[s_][:], [('hy_stg', s_)], ['x1tok'])
                        n += 1
                sc_.__exit__(None, None, None)
                for o in range(2):
                    kc = [0]
                    sc_ = nc.named_scope("hy_fwd%d" % o)
                    sc_.__enter__()

                    def epi_mul(ft, chh, bC, bS):
                        k0 = kc[0] % 2
                        kc[0] += 1
                        rows = slice(ft * 128, (ft + 1) * 128)
                        cs = slice(chh * 512, (chh + 1) * 512)
                        DMA('sp', Kr[k0][:], spec[o, 0, rows, cs], ['spec'], [('hy_Kr', k0)])
                        DMA('sp', Ks[k0][:], spec[o, 1, rows, cs], ['spec'], [('hy_Ks', k0)])
                        CP(Zc[k0][:], PSb(bC), [('ps', bC)], [('hy_Zc', k0)], eng='act')
                        CP(Zs[k0][:], PSb(bS), [('ps', bS)], [('hy_Zs', k0)], eng='act')
                        TT(ta[k0][:], Zc[k0][:], Kr[k0][:], ALU.mult, [('hy_Zc', k0), ('hy_Kr', k0)], [('hy_ta', k0)])
                        TT(tb[k0][:], Zs[k0][:], Ks[k0][:], ALU.mult, [('hy_Zs', k0), ('hy_Ks', k0)], [('hy_tb', k0)], eng='pool')
                        TT(Yr[:, ft, cs], ta[k0][:], tb[k0][:], ALU.subtract, [('hy_ta', k0), ('hy_tb', k0)], [('hy_Y', ft)])
                        TT(ta[k0][:], Zc[k0][:], Ks[k0][:], ALU.mult, [('hy_Zc', k0), ('hy_Ks', k0)], [('hy_ta', k0)])
                        TT(tb[k0][:], Zs[k0][:], Kr[k0][:], ALU.mult, [('hy_Zs', k0), ('hy_Kr', k0)], [('hy_tb', k0)], eng='pool')
                        TT(Ys[:, ft, cs], ta[k0][:], tb[k0][:], ALU.add, [('hy_ta', k0), ('hy_tb', k0)], [('hy_Y', ft)], eng='pool')
                    dft_fwd(slC, slS, z, z, 'hy_z', 'hy_z', epi_mul)
                    sc_.__exit__(None, None, None)
                    sc_ = nc.named_scope("hy_inv%d" % o)
                    sc_.__enter__()
                    nb = 0
                    for g in range(NSG):
                        b = g % 2
                        load_slabs(slC, slS, b, CD['CF'], CD['SF'], g)
                        nsl = slice(g * SW, (g + 1) * SW)
                        if o == 0:
                            for ni in range(SW // 128):
                                nt = g * (SW // 128) + ni
                                for chh in range(2):
                                    bank = nb % 8
                                    k0 = nb % 2
                                    nb += 1
                                    cs = slice(chh * 512, (chh + 1) * 512)
                                    DMA('sp', xm[k0][:], x1tok[nt * 128:(nt + 1) * 128, cs], ['x1tok'], [('hy_xm', k0)])
                                    for ft in range(NT):
                                        MM(PSb(bank), slC[b][:, ft, ni * 128:(ni + 1) * 128], Yr[:, ft, cs], ft == 0, False,
                                           [('slC', b), ('hy_Y', ft)], [('ps', bank)])
                                    for ft in range(NT):
                                        MM(PSb(bank), slS[b][:, ft, ni * 128:(ni + 1) * 128], Ys[:, ft, cs], False, ft == NT - 1,
                                           [('slS', b), ('hy_Y', ft)], [('ps', bank)])
                                    STT(z[:, nt, cs], PSb(bank), INV_SCALE, xm[k0][:], ALU.mult, ALU.mult,
                                        [('ps', bank), ('hy_xm', k0)], [('hy_z', nt)])
                        else:
                            for ct in range(8):
                                bank = nb % 8
                                k0 = nb % 2
                                nb += 1
                                csl = slice(ct * 128, (ct + 1) * 128)
                                DMA('sp', xm[k0][:, 0:SW], bT[1024 + ct * 128:1024 + (ct + 1) * 128, nsl], ['bcT'], [('hy_xm', k0)])
                                for ft in range(NT):
                                    MM(PSb(bank)[:, 0:SW], Yr[:, ft, csl], slC[b][:, ft, :], ft == 0, False,
                                       [('slC', b), ('hy_Y', ft)], [('ps', bank)])
                                for ft in range(NT):
                                    MM(PSb(bank)[:, 0:SW], Ys[:, ft, csl], slS[b][:, ft, :], False, ft == NT - 1,
                                       [('slS', b), ('hy_Y', ft)], [('ps', bank)])
                                STT(yo[k0][:, 0:SW], PSb(bank)[:, 0:SW], INV_SCALE, xm[k0][:, 0:SW], ALU.mult, ALU.mult,
                                    [('ps', bank), ('hy_xm', k0)], [('hy_yo', k0)])
                                DMA('act', ybT[ct * 128:(ct + 1) * 128, nsl], yo[k0][:, 0:SW], [('hy_yo', k0)], ['ybT'])
                    sc_.__exit__(None, None, None)
                S.barrier()


        SDT = BF16
        NINV = 1
        NSET = NINV + 1
        import os as _os
        INTERLEAVE = _os.environ.get('RW_IL', '1') == '1'
        WSC = -math.exp(-0.5)

        def rwkv_phase(l):
            with ExitStack() as ph:
                ph.enter_context(nc.named_scope("rwkv"))
                twp = sb("rw_twp", [128, T], BF16, st=ph)
                adp = sb("rw_adp", [128, T], BF16, st=ph)
                sgd = sb("rw_sgd", [128, T], BF16, st=ph)
                w2p = sb("rw_w2p", [128, 1024], BF16, st=ph)
                a2p = sb("rw_a2p", [128, 1024], BF16, st=ph)
                g2 = sb("rw_g2", [128, 1024], BF16, st=ph)
                omka = sb("rw_omka", [128, 8], st=ph)
                rT = sb("rw_r", [128, T], st=ph)
                kT = sb("rw_k", [128, T], st=ph)
                vT = sb("rw_v", [128, T], st=ph)
                kkn = sb("rw_kkn", [128, T], st=ph)
                rkacc = sb("rw_rkacc", [128, T], st=ph)
                ysum = sb("rw_ysum", [128, T], st=ph)
                vR = sb("rw_vR", [128, T], st=ph)
                lw = sb("rw_lw", [128, T], st=ph)
                kd = sb("rw_kd", [128, T], st=ph)
                bd = sb("rw_bd", [128, T], st=ph)
                Lc = sb("rw_L", [128, T], st=ph)
                ex = sb("rw_ex", [128, T], st=ph)
                tmp2 = sb("rw_tmp2", [128, T], st=ph)
                yT_ = sb("rw_y", [128, T], st=ph)
                tot = sb("rw_tot", [128, 16], st=ph)
                Pc = sb("rw_Pc", [128, 16], st=ph)
                ARh = [sb("rw_AR%d" % i, [128, 16, 2, 128], SDT, st=ph) for i in range(2)]
                BK = sb("rw_BK", [128, 16, 2, 128], SDT, st=ph)
                BhT = sb("rw_BhT", [128, 16, 128], SDT, st=ph)
                KhT = sb("rw_KhT", [128, 16, 128], SDT, st=ph)
                VT = sb("rw_VT", [128, 16, 128], SDT, st=ph)
                NB = [sb("rw_NB%d" % i, [128, 4, 2, 128], SDT, st=ph) for i in range(NSET)]
                AK = [sb("rw_AK%d" % i, [128, 4, 2, 128], SDT, st=ph) for i in range(NSET)]
                Mm2 = [[sb("rw_M%d_%d" % (j, i), [128, 4, 128], SDT, st=ph) for i in range(2)] for j in range(NINV)]
                MT2 = [[sb("rw_MT%d_%d" % (j, i), [128, 4, 128], SDT, st=ph) for i in range(2)] for j in range(NINV)]
                Xx2 = [[sb("rw_X%d_%d" % (j, i), [128, 4, 128], SDT, st=ph) for i in range(2)] for j in range(NINV)]
                Xfin = [sb("rw_Xf%d" % i, [128, 4, 128], SDT, st=ph) for i in range(NSET)]
                tb16 = sb("rw_tb16", [128, T], BF16, st=ph)
                St = sb("rw_St", [128, 64], st=ph)
                Sb_ = sb("rw_Sb", [128, 64], SDT, st=ph)
                Wt = sb("rw_Wt", [128, 128], SDT, st=ph)
                Ut = sb("rw_Ut", [128, 128], SDT, st=ph)
                ob = [sb("rw_ob%d" % i, [128, 512], BF16, st=ph) for i in range(2)]

                DMA('pool', g2[:], W['rw_g2'][l], (), ['rw_c'])
                MSET(ARh[0][64:128].rearrange("p a b c -> p (a b c)"), 0.0, ['rw_AR'])
                MSET(ARh[1][0:64].rearrange("p a b c -> p (a b c)"), 0.0, ['rw_AR'])
                for d in range(2):
                    ps_ = slice(d * 64, d * 64 + 48)
                    DMA('pool', w2p[ps_, :], W['rw_w2'][l, d], (), ['rw_c'])
                    DMA('pool', a2p[ps_, :], W['rw_a2'][l, d], (), ['rw_c'])
                    DMA('sp', tmp2[ps_, :], cT[3200 + d * 48:3248 + d * 48, :], ['bcT'], ['rw_tmp2'])
                    DMA('sp', ex[ps_, :], cT[3296 + d * 48:3344 + d * 48, :], ['bcT'], ['rw_ex'])
                    if d == 0:
                        ACT(twp[ps_, :], tmp2[ps_, :], AF.Tanh, ['rw_tmp2'], ['rw_twp'])
                        CP(adp[ps_, :], ex[ps_, :], ['rw_ex'], ['rw_adp'], eng='act')
                    else:
                        ACT(twp[ps_, :], tmp2[ps_, ::-1], AF.Tanh, ['rw_tmp2'], ['rw_twp'])
                        CP(adp[ps_, :], ex[ps_, ::-1], ['rw_ex'], ['rw_adp'], eng='act')
                DMA('sp', Lc[:], cT[3072:3200, :], ['bcT'], ['rw_L'])
                ACT(sgd[:], Lc[:], AF.Sigmoid, ['rw_L'], ['rw_sgd'])
                oka, _ = PC['ka']
                TS(omka[:], pcol[:, oka:oka + 8], -1.0, 1.0, ALU.mult, ALU.add, ['pcol'], ['rw_omka'])
                if l > 0:
                    v1 = sb("rw_v1", [128, 8, 32], st=ph)
                    v2 = sb("rw_v2", [32, 1024], BF16, st=ph)
                    t1v = sb("rw_t1v", [32, T], BF16, st=ph)
                    DMA('sp', v1[:], W['rw_v1'][l - 1].rearrange("(ct p) r -> p ct r", p=128), (), ['rw_c'])
                    DMA('pool', v2[:], W['rw_v2'][l - 1], (), ['rw_c'])
                    for ct in range(8):
                        DMA('sp', kd[:], cT[2048 + ct * 128:2048 + (ct + 1) * 128, :], ['bcT'], ['rw_kd'])
                        for tg in range(NTG):
                            MM(PSb(tg)[0:32, :], v1[:, ct, :], kd[:, tg * 512:(tg + 1) * 512], ct == 0, ct == 7,
                               ['rw_c', 'rw_kd'], [('ps', tg)])
                    for tg in range(NTG):
                        EVAC(t1v[:, tg * 512:(tg + 1) * 512], PSb(tg)[0:32, :], [('ps', tg)], ['rw_t1v'])

                def f2(ap):
                    return ap.rearrange("p a b -> p (a b)")

                def c3(ap):
                    return ap.rearrange("p (c t) -> p c t", t=128)

                def scan_unit(r_ap, lw_, kd_, v_, kk_ap, bd_, yout, ykey):
                    for c in range(16):
                        csl = slice(c * 128, (c + 1) * 128)
                        S.op('dve', lambda: nc.vector.tensor_tensor_scan(out=Lc[:, csl], data0=onesF[:], data1=lw_[:, csl], initial=0.0,
                                                                         op0=ALU.mult, op1=ALU.add), ['const', 'rw_lw'], ['rw_L'])
                    L3 = c3(Lc[:])
                    CP(tot[:], L3[:, :, 127], ['rw_L'], ['rw_tot'])
                    ACT(Pc[:], tot[:], AF.Exp, ['rw_tot'], ['rw_Pc'])
                    ACT(ex[:], Lc[:], AF.Exp, ['rw_L'], ['rw_ex'])
                    for h in range(2):
                        hp_ = slice(h * 64, (h + 1) * 64)
                        TT(ARh[h][hp_, :, 1, :], c3(r_ap[hp_]), c3(ex[hp_, :]), ALU.mult, ['rw_ex', 'rw_in'], ['rw_AR'])
                    ACT(ex[:], Lc[:], AF.Exp, ['rw_L', 'rw_AR'], ['rw_ex'], scale=-1.0)
                    TT(BK[:, :, 0, :], c3(bd_[:]), c3(ex[:]), ALU.mult, ['rw_ex', 'rw_bd'], ['rw_BK'])
                    TT(BK[:, :, 1, :], c3(kd_[:]), c3(ex[:]), ALU.mult, ['rw_ex', 'rw_kd'], ['rw_BK'], eng='pool')
                    TT(tmp2[:], Lc[:], lw_[:], ALU.subtract, ['rw_L', 'rw_lw', 'rw_tmp2'], ['rw_tmp2'])
                    ACT(ex[:], tmp2[:], AF.Exp, ['rw_tmp2', 'rw_BK'], ['rw_ex'])
                    for h in range(2):
                        hp_ = slice(h * 64, (h + 1) * 64)
                        STT(ARh[h][hp_, :, 0, :], c3(kk_ap[hp_]), -1.0, c3(ex[hp_, :]), ALU.mult, ALU.mult, ['rw_ex', 'rw_kkn'], ['rw_AR'])
                    TT(c3(tmp2[:]), tot[:].unsqueeze(2).to_broadcast([128, 16, 128]), L3, ALU.subtract,
                       ['rw_tot', 'rw_L', 'rw_tmp2'], ['rw_tmp2'])
                    ACT(ex[:], tmp2[:], AF.Exp, ['rw_tmp2', 'rw_AR'], ['rw_ex'])
                    for wi, (src, skey, dstT) in enumerate(((bd_, 'rw_bd', BhT), (kd_, 'rw_kd', KhT), (v_, 'rw_in', VT))):
                        if wi < 2:
                            TT(tb16[:], src[:], ex[:], ALU.mult, ['rw_ex', skey, 'rw_tb16'], ['rw_tb16'], eng='pool' if wi else 'dve')
                        else:
                            CP(tb16[:], src[:], ['rw_in', 'rw_vR', 'rw_tb16'], ['rw_tb16'], eng='act')
                        for q in range(4):
                            bank = q
                            pb16 = PSb(bank).bitcast(BF16)
                            for j in range(4):
                                c = q * 4 + j
                                TR(pb16[:, j * 128:(j + 1) * 128], tb16[:, c * 128:(c + 1) * 128], identB[:],
                                   ['rw_tb16', 'const'], [('ps', bank)])
                            EVAC(f2(dstT[:, q * 4:(q + 1) * 4, :]), pb16[:, 0:512], [('ps', bank)], ['rw_T%d' % wi])
                    MSET(St[:], 0.0, ['rw_St'], eng='dve')
                    MSET(Sb_[:], 0.0, ['rw_Sb'], eng='dve')

                    def inv_chain(qd):
                        q3 = qd % NSET
                        st_ = qd % NINV
                        b0, b1, b2 = (2, 3, 4) if st_ == 0 else (5, 6, 7)
                        MTs, Mms, Xxs = MT2[st_], Mm2[st_], Xx2[st_]
                        kM, kMT, kX = 'rw_M%d' % st_, 'rw_MT%d' % st_, 'rw_X%d' % st_
                        gb_ = (b0, b1)
                        for cc in range(2):
                            c = qd * 2 + cc
                            for h in range(2):
                                u = cc * 2 + h
                                MM(PSb(gb_[u // 2])[:, (u % 2) * 256:(u % 2 + 1) * 256], BK[:, c, 0, :], f2(ARh[h][:, c, :, :]), True, True,
                                   ['rw_BK', 'rw_AR'], [('ps', gb_[u // 2])])
                                MM(PSb(b2)[:, u * 128:(u + 1) * 128], ARh[h][:, c, 0, :], BK[:, c, 0, :], True, True,
                                   ['rw_BK', 'rw_AR'], [('ps', b2)])
                        for hb in range(2):
                            TT(NB[q3][:, hb * 2:(hb + 1) * 2, :, :].rearrange("p a b c -> p (a b c)"), PSb(gb_[hb]), maskU4[:], ALU.mult,
                               [('ps', gb_[hb]), 'const'], [('rw_NB', q3)])
                        TT(f2(MTs[0][:]), PSb(b2), maskL4[:], ALU.mult, [('ps', b2), 'const'], [(kMT, 0)])
                        TT(Xxs[0][:], NB[q3][:, :, 0, :], ident4[:].rearrange("p (a b) -> p a b", a=4), ALU.add,
                           [('rw_NB', q3), 'const'], [(kX, 0)], eng='pool')
                        yield
                        cur = 0
                        for lev in range(1, 8):
                            nxt = 1 - cur
                            pA, pB, pC = b2, b1, b0
                            if lev == 1:
                                for cc in range(2):
                                    c = qd * 2 + cc
                                    for h in range(2):
                                        u = cc * 2 + h
                                        MM(PSb(gb_[u // 2])[:, (u % 2) * 256:(u % 2 + 1) * 256], BK[:, c, 1, :], f2(ARh[h][:, c, :, :]), True, True,
                                           ['rw_BK', 'rw_AR'], [('ps', gb_[u // 2])])
                            if lev >= 2:
                                xs, xk = Xxs[(lev - 2) % 2], (kX, (lev - 2) % 2)
                                for u in range(4):
                                    MM(PSb(pC)[:, u * 128:(u + 1) * 128], MTs[cur][:, u, :], xs[:, u, :], True, True,
                                       [(kMT, cur), xk], [('ps', pC)])
                            if lev == 1:
                                for hb in range(2):
                                    TT(AK[q3][:, hb * 2:(hb + 1) * 2, :, :].rearrange("p a b c -> p (a b c)"), PSb(gb_[hb]), maskU4[:], ALU.mult,
                                       [('ps', gb_[hb]), 'const'], [('rw_AK', q3)])
                            if lev <= 6:
                                for u in range(4):
                                    m_prev = NB[q3][:, u, 0, :] if lev == 1 else Mms[cur][:, u, :]
                                    mk = ('rw_NB', q3) if lev == 1 else (kM, cur)
                                    MM(PSb(pA)[:, u * 128:(u + 1) * 128], m_prev, MTs[cur][:, u, :], True, True, [mk, (kMT, cur)], [('ps', pA)])
                                    if lev < 6:
                                        MM(PSb(pB)[:, u * 128:(u + 1) * 128], MTs[cur][:, u, :], m_prev, True, True, [mk, (kMT, cur)], [('ps', pB)])
                            if lev <= 6:
                                CP(f2(MTs[nxt][:]), PSb(pA), [('ps', pA)], [(kMT, nxt)], eng='act')
                            if lev >= 2:
                                if lev == 7:
                                    xd, xdk = Xfin[q3], ('rw_Xfin', q3)
                                else:
                                    xd, xdk = Xxs[(lev - 1) % 2], (kX, (lev - 1) % 2)
                                TT(f2(xd[:]), PSb(pC), f2(xs[:]), ALU.add, [('ps', pC), xk], [xdk])
                            if lev < 6:
                                CP(f2(Mms[nxt][:]), PSb(pB), [('ps', pB)], [(kM, nxt)], eng='act')
                            cur = nxt
                            yield

                    def state_chain(qd):
                        q3 = qd % NSET
                        Xf = Xfin[q3]
                        xkey = ('rw_Xfin', q3)
                        for cc in range(2):
                            c = qd * 2 + cc
                            for h in range(2):
                                u = cc * 2 + h
                                hs_ = slice(h * 64, (h + 1) * 64)
                                MM(PSb(0)[:, hs_], ARh[h][:, c, 0, :], Sb_[:, :], True, False, ['rw_AR', 'rw_Sb'], [('ps', 0)])
                                MM(PSb(0)[:, hs_], AK[q3][:, u, 0, :], VT[:, c, hs_], False, True, [('rw_AK', q3), 'rw_T2'], [('ps', 0)])
                            CP(Wt[:], PSb(0)[:, 0:128], [('ps', 0)], ['rw_Wt'], eng='act')
                            yield
                            for h in range(2):
                                u = cc * 2 + h
                                hs_ = slice(h * 64, (h + 1) * 64)
                                MM(PSb(0)[:, 128 + h * 64:128 + (h + 1) * 64], Xf[:, u, :], Wt[:, hs_], True, True, [xkey, 'rw_Wt'], [('ps', 0)])
                            CP(Ut[:], PSb(0)[:, 128:256], [('ps', 0)], ['rw_Ut'], eng='dve')
                            yield
                            for h in range(2):
                                u = cc * 2 + h
                                hs_ = slice(h * 64, (h + 1) * 64)
                                so_ = PSb(0)[hs_, 256:320]
                                MM(so_, BhT[:, c, hs_], Ut[:, hs_], True, False, ['rw_T0', 'rw_Ut'], [('ps', 0)])
                                MM(so_, KhT[:, c, hs_], VT[:, c, hs_], False, True, ['rw_T1', 'rw_T2'], [('ps', 0)])
                                yo_ = PSb(1)[hs_, cc * 128:(cc + 1) * 128]
                                MM(yo_, Sb_[:, :], ARh[h][:, c, 1, :], True, False, ['rw_Sb', 'rw_AR'], [('ps', 1)])
                                MM(yo_, Ut[:, hs_], NB[q3][:, u, 1, :], False, False, ['rw_Ut', ('rw_NB', q3)], [('ps', 1)])
                                MM(yo_, VT[:, c, hs_], AK[q3][:, u, 1, :], False, True, ['rw_T2', ('rw_AK', q3)], [('ps', 1)])
                            STT(St[:], St[:], Pc[:, c:c + 1], PSb(0)[:, 256:320], ALU.mult, ALU.add, ['rw_St', 'rw_Pc', ('ps', 0)], ['rw_St'])
                            CP(Sb_[:], St[:], ['rw_St'], ['rw_Sb'], eng='act')
                            if cc == 1:
                                CP(yout[:, qd * 256:(qd + 1) * 256], PSb(1)[:, 0:256], [('ps', 1)], [ykey], eng='act')
                            yield

                    next_inv = 0
                    inv_done = [False] * 8
                    active = []
                    state_q = 0
                    state_gen = None
                    states_done = 0
                    while states_done < 8:
                        while len(active) < NINV and next_inv < 8 and next_inv < states_done + NSET:
                            active.append((next_inv, inv_chain(next_inv)))
                            next_inv += 1
                        if state_gen is None and state_q < 8 and inv_done[state_q]:
                            state_gen = state_chain(state_q)
                        if state_gen is not None:
                            try:
                                next(state_gen)
                            except StopIteration:
                                state_gen = None
                                states_done += 1
                                state_q += 1
                        for item in list(active):
                            try:
                                next(item[1])
                            except StopIteration:
                                inv_done[item[0]] = True
                                active.remove(item)

                for hp in range(8):
                    hsl = slice(hp * 128, (hp + 1) * 128)
                    DMA('sp', rT[:], cT[hp * 128:(hp + 1) * 128, :], ['bcT'], ['rw_in'])
                    DMA('sp', kT[:], cT[1024 + hp * 128:1024 + (hp + 1) * 128, :], ['bcT'], ['rw_in'])
                    DMA('sp', vT[:], cT[2048 + hp * 128:2048 + (hp + 1) * 128, :], ['bcT'], ['rw_in'])
                    if l > 0:
                        DMA('sp', tmp2[:], vfT[hsl, :], ['vfT'], ['rw_tmp2'])
                        for tg in range(NTG):
                            tsl = slice(tg * 512, (tg + 1) * 512)
                            MM(PSb(tg), v2[:, hsl], t1v[:, tsl], True, True, ['rw_c', 'rw_t1v'], [('ps', tg)])
                            ACT(ex[:, tsl], PSb(tg), AF.Sigmoid, [('ps', tg), 'pcol'], ['rw_ex'], bias=pc('v0', hp))
                        TT(tmp2[:], tmp2[:], vT[:], ALU.subtract, ['rw_tmp2', 'rw_in'], ['rw_tmp2'])
                        TT(tmp2[:], tmp2[:], ex[:], ALU.mult, ['rw_tmp2', 'rw_ex'], ['rw_tmp2'])
                        TT(vT[:], vT[:], tmp2[:], ALU.add, ['rw_tmp2', 'rw_in'], ['rw_in'])
                    else:
                        DMA('act', vfT[hsl, :], vT[:], ['rw_in'], ['vfT'])
                    TS(kkn[:], kT[:], pc('kk', hp), None, ALU.mult, None, ['rw_in', 'pcol'], ['rw_kkn'])
                    ACT(tmp2[:], kkn[:], AF.Square, ['rw_kkn', 'rw_tmp2'], ['rw_tmp2'])
                    for tg in range(NTG):
                        tsl = slice(tg * 512, (tg + 1) * 512)
                        MM(PSb(tg), blockones[:], tmp2[:, tsl], True, True, ['const', 'rw_tmp2'], [('ps', tg)])
                        ACT(ex[:, tsl], PSb(tg), AF.Ln, [('ps', tg)], ['rw_ex'], bias=1e-30)
                    ACT(ex[:], ex[:], AF.Exp, ['rw_ex'], ['rw_ex'], scale=-0.5)
                    TS(ex[:], ex[:], 1e12, None, ALU.min, None, ['rw_ex'], ['rw_ex'])
                    TT(kkn[:], kkn[:], ex[:], ALU.mult, ['rw_kkn', 'rw_ex'], ['rw_kkn'])
                    CP(vR[:], vT[:, ::-1], ['rw_in'], ['rw_vR'], eng='act')
                    for d in range(2):
                        ps_ = slice(d * 64, d * 64 + 48)
                        if d == 0:
                            r_ap, k_ap, kk_ap, v_ = rT[:], kT[:], kkn[:], vT
                        else:
                            r_ap, k_ap, kk_ap, v_ = rT[:, ::-1], kT[:, ::-1], kkn[:, ::-1], vR
                        for tg in range(NTG):
                            tsl = slice(tg * 512, (tg + 1) * 512)
                            MM(PSb(tg), w2p[ps_, hsl], twp[ps_, tsl], True, True, ['rw_c', 'rw_twp'], [('ps', tg)])
                            ACT(lw[:, tsl], PSb(tg), AF.Sigmoid, [('ps', tg), 'pcol'], ['rw_lw'], bias=pc('w0', d * 8 + hp))
                            MM(PSb(4 + tg), a2p[ps_, hsl], adp[ps_, tsl], True, True, ['rw_c', 'rw_adp'], [('ps', 4 + tg)])
                            ACT(yT_[:, tsl], PSb(4 + tg), AF.Sigmoid, [('ps', 4 + tg), 'pcol'], ['rw_y'], bias=pc('a0', d * 8 + hp))
                        TS(lw[:], lw[:], WSC, None, ALU.mult, None, ['rw_lw'], ['rw_lw'])
                        TS(tmp2[:], yT_[:], pc('ka', hp), omka[:, hp:hp + 1], ALU.mult, ALU.add, ['rw_y', 'pcol', 'rw_omka', 'rw_tmp2'], ['rw_tmp2'])
                        TT(kd[:], tmp2[:], k_ap, ALU.mult, ['rw_tmp2', 'rw_in'], ['rw_kd'])
                        TT(bd[:], yT_[:], kk_ap, ALU.mult, ['rw_kkn', 'rw_y'], ['rw_bd'])
                        if d == 0:
                            STT(rkacc[:], r_ap, pc('rk', hp), kd[:], ALU.mult, ALU.mult, ['rw_in', 'pcol', 'rw_kd'], ['rw_rkacc'])
                        else:
                            STT(tmp2[:], r_ap, pc('rk', hp), kd[:], ALU.mult, ALU.mult, ['rw_in', 'pcol', 'rw_kd', 'rw_tmp2'], ['rw_tmp2'])
                            TT(rkacc[:], rkacc[:], tmp2[:, ::-1], ALU.add, ['rw_rkacc', 'rw_tmp2'], ['rw_rkacc'])
                        if d == 0:
                            scan_unit(r_ap, lw, kd, v_, kk_ap, bd, ysum, 'rw_ysum')
                        else:
                            scan_unit(r_ap, lw, kd, v_, kk_ap, bd, yT_, 'rw_y')
                            TT(ysum[:], ysum[:], yT_[:, ::-1], ALU.add, ['rw_y', 'rw_ysum'], ['rw_ysum'])
                    for tg in range(NTG):
                        tsl = slice(tg * 512, (tg + 1) * 512)
                        k0 = tg % 2
                        MM(PSb(k0), blockones[:], ysum[:, tsl], True, True, ['const', 'rw_ysum'], [('ps', k0)])
                        STT(tmp2[:, tsl], PSb(k0), -1.0 / 64, ysum[:, tsl], ALU.mult, ALU.add, [('ps', k0), 'rw_ysum', 'rw_tmp2'], ['rw_tmp2'])
                        ACT(ex[:, tsl], tmp2[:, tsl], AF.Square, ['rw_tmp2', 'rw_ex'], ['rw_ex'])
                        MM(PSb(2 + k0), blockones[:], ex[:, tsl], True, True, ['const', 'rw_ex'], [('ps', 2 + k0)])
                        ACT(ex[:, tsl], PSb(2 + k0), AF.Ln, [('ps', 2 + k0)], ['rw_ex'], scale=1.0 / 64, bias=GN_EPS)
                        ACT(ex[:, tsl], ex[:, tsl], AF.Exp, ['rw_ex'], ['rw_ex'], scale=-0.5)
                        TT(tmp2[:, tsl], tmp2[:, tsl], ex[:, tsl], ALU.mult, ['rw_tmp2', 'rw_ex'], ['rw_tmp2'])
                        TS(tmp2[:, tsl], tmp2[:, tsl], pc('lng', hp), pc('lnb', hp), ALU.mult, ALU.add, ['rw_tmp2', 'pcol'], ['rw_tmp2'])
                        MM(PSb(4 + k0), blockones[:], rkacc[:, tsl], True, True, ['const', 'rw_rkacc'], [('ps', 4 + k0)])
                        TT(ex[:, tsl], PSb(4 + k0), vT[:, tsl], ALU.mult, [('ps', 4 + k0), 'rw_in', 'rw_ex'], ['rw_ex'])
                        TT(tmp2[:, tsl], tmp2[:, tsl], ex[:, tsl], ALU.add, ['rw_tmp2', 'rw_ex'], ['rw_tmp2'])
                        MM(PSb(6 + k0), g2[:, hsl], sgd[:, tsl], True, True, ['rw_c', 'rw_sgd'], [('ps', 6 + k0)])
                        TT(ob[k0][:], tmp2[:, tsl], PSb(6 + k0), ALU.mult, ['rw_tmp2', ('ps', 6 + k0)], [('rw_ob', k0)])
                        DMA('act', ycT[hsl, tsl], ob[k0][:], [('rw_ob', k0)], ['ycT'])
                S.barrier()


        def merge_phase(l):
            with ExitStack() as ph:
                ph.enter_context(nc.named_scope("merge"))
                PW = 256
                yT3 = [sb("mg_y%d" % i, [128, 8, T], BF16, st=ph) for i in range(3)]
                wpb = [[sb("mg_w%d_%d" % (i, j), [128, 8, PW], BF16, st=ph) for j in range(2)] for i in range(3)]
                gt = [[sb("mg_g%d_%d" % (i, j), [128, T], BF16, st=ph) for j in range(2)] for i in range(3)]
                ta = [sb("mg_ta%d" % i, [128, 512], st=ph) for i in range(2)]
                tb = [sb("mg_tb%d" % i, [128, 512], st=ph) for i in range(2)]
                mo = [sb("mg_mo%d" % i, [128, 512], BF16, st=ph) for i in range(2)]
                for i, src in enumerate((yaT, ybT, ycT)):
                    DMA('sp', yT3[i][:], src.rearrange("(kt p) t -> p kt t", p=128), ['yaT', 'ybT', 'ycT'], [('mg_y', i)])
                Wb = [W[n][l].rearrange("(kt p) c -> p kt c", p=128) for n in ('w_branch_a', 'w_branch_b', 'w_branch_c')]
                n = 0
                for pi in range(D // PW):
                    b = pi % 2
                    for i in range(3):
                        DMA('pool', wpb[i][b][:], Wb[i][:, :, pi * PW:(pi + 1) * PW], (), [('mg_w', i, b)])
                    for ci in range(PW // 128):
                        ct = pi * (PW // 128) + ci
                        gb_ = ct % 2
                        for i in range(3):
                            DMA('sp', gt[i][gb_][:], gT[i * D + ct * 128:i * D + (ct + 1) * 128, :], ['gT'], [('mg_g', i, gb_)])
                        for tg in range(NTG):
                            tsl = slice(tg * 512, (tg + 1) * 512)
                            k0 = n % 2
                            base = (n % 2) * 3
                            n += 1
                            for i in range(3):
                                for kt in range(8):
                                    MM(PSb(base + i), wpb[i][b][:, kt, ci * 128:(ci + 1) * 128], yT3[i][:, kt, tsl], kt == 0, kt == 7,
                                       [('mg_w', i, b), ('mg_y', i)], [('ps', base + i)])
                            TT(ta[k0][:], PSb(base + 0), gt[0][gb_][:, tsl], ALU.mult, [('ps', base + 0), ('mg_g', 0, gb_)], [('mg_ta', k0)])
                            TT(tb[k0][:], PSb(base + 1), gt[1][gb_][:, tsl], ALU.mult, [('ps', base + 1), ('mg_g', 1, gb_)], [('mg_tb', k0)])
                            TT(ta[k0][:], ta[k0][:], tb[k0][:], ALU.add, [('mg_ta', k0), ('mg_tb', k0)], [('mg_ta', k0)], eng='pool')
                            TT(tb[k0][:], PSb(base + 2), gt[2][gb_][:, tsl], ALU.mult, [('ps', base + 2), ('mg_g', 2, gb_), ('mg_ta', k0)], [('mg_tb', k0)])
                            TT(mo[k0][:], ta[k0][:], tb[k0][:], ALU.add, [('mg_ta', k0), ('mg_tb', k0)], [('mg_mo', k0)], eng='pool')
                            DMA('act', mgT[ct * 128:(ct + 1) * 128, tsl], mo[k0][:], [('mg_mo', k0)], ['mgT'])
                S.barrier()

        def resid_epi(xl, xo):
            cnt = [0]

            def epi(ci, c0, m, tg, ps, pk):
                k = cnt[0] % 3
                cnt[0] += 1
                tsl = slice(tg * 512, (tg + 1) * 512)
                DMA('sp', xl[k][:], xT[c0:c0 + 128, tsl], [('xT', ci, tg)], [('rs_xl', k)])
                TT(xo[k][:], ps, xl[k][:], ALU.add, pk + [('rs_xl', k)], [('rs_xo', k)])
                DMA('act', xT[c0:c0 + 128, tsl], xo[k][:], [('rs_xo', k)], [('xT', ci, tg)])
            return epi

        def outproj_phase(l):
            with ExitStack() as ph:
                ph.enter_context(nc.named_scope("outp"))
                mT = sb("op_mT", [128, KD, T], BF16, st=ph)
                wp = [sb("op_wp%d" % i, [128, KD, 512], BF16, st=ph) for i in range(2)]
                xl = [sb("op_xl%d" % i, [128, 512], st=ph) for i in range(3)]
                xo = [sb("op_xo%d" % i, [128, 512], st=ph) for i in range(3)]
                DMA('sp', mT[:], mgT.rearrange("(kt p) t -> p kt t", p=128), ['mgT'], ['op_mT'])
                linear_fm(W['w_out'][l].rearrange("(kt p) c -> p kt c", p=128), KD, [(i * 128, 128) for i in range(16)],
                          lambda kt, tg: mT[:, kt, tg * 512:(tg + 1) * 512], lambda kt, tg: 'op_mT', resid_epi(xl, xo), wp, 'op_wp')
                S.barrier()

        def ffn_phase(l):
            with ExitStack() as hs:
                hT = sb("hT2", [128, KD, T], BF16, st=hs)
                norm_phase(hT, 'nfg')
                with ExitStack() as ph:
                    ph.enter_context(nc.named_scope("ffn_up"))
                    PW = 256
                    wg = [sb("ff_wg%d" % i, [128, KD, PW], BF16, st=ph) for i in range(2)]
                    wu = [sb("ff_wu%d" % i, [128, KD, PW], BF16, st=ph) for i in range(2)]
                    sg = [sb("ff_sg%d" % i, [128, 512], st=ph) for i in range(2)]
                    ao = [sb("ff_ao%d" % i, [128, 512], BF16, st=ph) for i in range(2)]
                    Wg = W['w_ffn_gate'][l].rearrange("(kt p) c -> p kt c", p=128)
                    Wu = W['w_ffn_up'][l].rearrange("(kt p) c -> p kt c", p=128)
                    n = 0
                    for pi in range(D_FF // PW):
                        b = pi % 2
                        DMA('pool', wg[b][:], Wg[:, :, pi * PW:(pi + 1) * PW], (), [('ff_wg', b)])
                        DMA('pool', wu[b][:], Wu[:, :, pi * PW:(pi + 1) * PW], (), [('ff_wu', b)])
                        for ci in range(PW // 128):
                            ft = pi * (PW // 128) + ci
                            for tg in range(NTG):
                                tsl = slice(tg * 512, (tg + 1) * 512)
                                k0 = n % 2
                                bG = (n % 4) * 2
                                bU = bG + 1
                                n += 1
                                for kt in range(KD):
                                    MM(PSb(bG), wg[b][:, kt, ci * 128:(ci + 1) * 128], hT[:, kt, tsl], kt == 0, kt == KD - 1,
                                       [('ff_wg', b), ('hT', kt, tg)], [('ps', bG)])
                                for kt in range(KD):
                                    MM(PSb(bU), wu[b][:, kt, ci * 128:(ci + 1) * 128], hT[:, kt, tsl], kt == 0, kt == KD - 1,
                                       [('ff_wu', b), ('hT', kt, tg)], [('ps', bU)])
                                ACT(sg[k0][:], PSb(bG), AF.Silu, [('ps', bG)], [('ff_sg', k0)])
                                TT(ao[k0][:], sg[k0][:], PSb(bU), ALU.mult, [('ff_sg', k0), ('ps', bU)], [('ff_ao', k0)])
                                DMA('sp', actT[ft * 128:(ft + 1) * 128, tsl], ao[k0][:], [('ff_ao', k0)], ['actT'])
                    S.barrier()
            with ExitStack() as ph:
                ph.enter_context(nc.named_scope("ffn_down"))
                KF = D_FF // 128
                PW = 256
                TH = 1024
                asb = sb("ff_act", [128, KF, TH], BF16, st=ph)
                wd = [sb("ff_wd%d" % i, [128, KF, PW], BF16, st=ph) for i in range(2)]
                xl = [sb("ff_xl%d" % i, [128, 512], st=ph) for i in range(3)]
                xo = [sb("ff_xo%d" % i, [128, 512], st=ph) for i in range(3)]
                Wd = W['w_ffn_down'][l].rearrange("(kt p) c -> p kt c", p=128)
                aTv = actT.rearrange("(kt p) t -> p kt t", p=128)
                n = 0
                pn = 0
                for th in range(T // TH):
                    for kq in range(4):
                        ks = slice(kq * 11, (kq + 1) * 11)
                        DMA('sp', asb[:, ks, :], aTv[:, ks, th * TH:(th + 1) * TH], ['actT'], [('ff_act', kq)])
                    epi = resid_epi(xl, xo)
                    for pi in range(D // PW):
                        b = pn % 2
                        pn += 1
                        DMA('pool', wd[b][:], Wd[:, :, pi * PW:(pi + 1) * PW], (), [('ff_wd', b)])
                        for ci in range(PW // 128):
                            ct = pi * (PW // 128) + ci
                            for tgi in range(TH // 512):
                                tg = th * (TH // 512) + tgi
                                bank = n % 8
                                n += 1
                                for kt in range(KF):
                                    MM(PSb(bank), wd[b][:, kt, ci * 128:(ci + 1) * 128], asb[:, kt, tgi * 512:(tgi + 1) * 512], kt == 0, kt == KF - 1,
                                       [('ff_wd', b), ('ff_act', kt // 11)], [('ps', bank)])
                                epi(ct, ct * 128, 128, tg, PSb(bank), [('ps', bank)])
                S.barrier()

        def final_phase():
            with ExitStack() as ph:
                xin = [sb("fx%d" % i, [128, KD, 512], st=ph) for i in range(2)]
                sq = [sb("fsq%d" % i, [128, 512], st=ph) for i in range(2)]
                rs = [sb("frs%d" % i, [128, 512], st=ph) for i in range(2)]
                ot = [sb("fot%d" % i, [128, D], st=ph) for i in range(2)]
                n = 0
                no = 0
                for tg in range(NTG):
                    b = tg % 2
                    tsl = slice(tg * 512, (tg + 1) * 512)
                    DMA('sp', xin[b][:], xTv[:, :, tsl], ['xT'], [('fx', b)])
                    for dk in range(KD):
                        ACT(sq[dk % 2][:], xin[b][:, dk, :], AF.Square, [('fx', b)], [('fsq', dk % 2)])
                        MM(PSb(b), onesF[:], sq[dk % 2][:], dk == 0, dk == KD - 1, [('fsq', dk % 2), 'const'], [('ps', b)])
                    ACT(rs[b][:], PSb(b), AF.Sqrt, [('ps', b)], [('frs', b)], scale=1.0 / D, bias=RMS_EPS)
                    RECIP(rs[b][:], rs[b][:], [('frs', b)], [('frs', b)])
                    for dk in range(KD):
                        STT(xin[b][:, dk, :], xin[b][:, dk, :], pc('nfin', dk), rs[b][:], ALU.mult, ALU.mult,
                            [('fx', b), 'pcol', ('frs', b)], [('fx', b)])
                    for tt in range(4):
                        o_ = no % 2
                        no += 1
                        for q in range(4):
                            bank = 2 + n % 6
                            n += 1
                            for j in range(4):
                                dk = q * 4 + j
                                TR(PSb(bank)[:, j * 128:(j + 1) * 128], xin[b][:, dk, tt * 128:(tt + 1) * 128], identF[:],
                                   [('fx', b), 'const'], [('ps', bank)])
                            EVAC(ot[o_][:, q * 512:(q + 1) * 512], PSb(bank), [('ps', bank)], [('fot', o_)])
                        row = (tg * 4 + tt) * 128
                        DMA('act', out_d[row:row + 128, :], ot[o_][:], [('fot', o_)], [('out', row)])
                S.barrier()

        for l in range(NL):
            DMA('sp', pcol[:], pcol_d[l], (), ['pcol'])
            Winv = W['w_in'][l].rearrange("(kt p) c -> p kt c", p=128)
            with ExitStack() as hs:
                hT = sb("hT", [128, KD, T], BF16, st=hs)
                norm_phase(hT, 'nmg')
                if l == 0 and 'hTd' in DBG:
                    DMA('sp', DBG['hTd'].rearrange("(dk p) t -> p dk t", p=128), hT[:], [('hT', dk, tg) for dk in range(KD) for tg in range(NTG)], [('dbg', 'hTd')])

                def h_rhs(kt, tg):
                    return hT[:, kt, tg * 512:(tg + 1) * 512]

                def h_key(kt, tg):
                    return ('hT', kt, tg)

                with ExitStack() as ph:
                    ph.enter_context(nc.named_scope("proj"))
                    wp = [sb("wp%d" % i, [128, KD, 512], BF16, st=ph) for i in range(2)]
                    ob = [sb("pob%d" % i, [128, 512], F32, st=ph) for i in range(3)]
                    obh = [sb("pobh%d" % i, [128, 512], BF16, st=ph) for i in range(3)]
                    zc = [sb("pzc%d" % i, [128, T], F32, st=ph) for i in range(2)]
                    ccol = sb("ccol", [128, 29], F32, st=ph)
                    cnt = [0]
                    o_p, _ = PC['mup']
                    o_n, _ = PC['mun']
                    TT(ccol[:], pcol[:, o_p:o_p + 29], pcol[:, o_n:o_n + 29], ALU.add, ['pcol'], ['ccol'])
                    TS(ccol[:], ccol[:], -1.0, 1.0, ALU.mult, ALU.add, ['ccol'], ['ccol'])

                    def epi_u(ci, c0, m, tg, ps, pk):
                        k = cnt[0] % 3
                        cnt[0] += 1
                        ACT(ob[k][:], ps, AF.Gelu, pk, [('pob', k)])
                        DMA('sp', uT[c0:c0 + 128, tg * 512:(tg + 1) * 512], ob[k][:], [('pob', k)], ['uT'])

                    def epi_g(ci, c0, m, tg, ps, pk):
                        k = cnt[0] % 3
                        cnt[0] += 1
                        ACT(obh[k][:], ps, AF.Sigmoid, pk, [('pobh', k)])
                        r0 = c0 - OFF_G
                        DMA('sp', gT[r0:r0 + 128, tg * 512:(tg + 1) * 512], obh[k][:], [('pobh', k)], ['gT'])

                    def tap3(dst, r0, m, ps, pk, a_ap, b_ap, p_ap, n_ap, keys):
                        k = cnt[0] % 2
                        cnt[0] += 1
                        z = zc[k]
                        ACT(z[0:m, :], ps, AF.Identity, pk + keys, [('pzc', k)], scale=a_ap, bias=b_ap)
                        STT(z[0:m, 1:T], ps[:, 0:T - 1], p_ap, z[0:m, 1:T], ALU.mult, ALU.add, pk + keys + [('pzc', k)], [('pzc', k)])
                        STT(z[0:m, 0:T - 1], ps[:, 1:T], n_ap, z[0:m, 0:T - 1], ALU.mult, ALU.add, pk + keys + [('pzc', k)], [('pzc', k)])
                        DMA('sp', dst[r0:r0 + m, :], z[0:m, :], [('pzc', k)], ['bcT'])

                    def epi_b(ci, c0, m, tg, ps, pk):
                        tap3(bT, c0 - OFF_B, m, ps, pk, pc('cw1', ci), pc('cb', ci), pc('cw0', ci), pc('cw2', ci), ['pcol'])

                    def epi_c(ci, c0, m, tg, ps, pk):
                        tap3(cT, c0 - OFF_C, m, ps, pk, ccol[0:m, ci:ci + 1], 0.0, pc('mup', ci, m), pc('mun', ci, m), ['pcol', 'ccol'])

                    linear_fm(Winv, KD, [(i * 128, 128) for i in range(8)], h_rhs, h_key, epi_u, wp, 'wp')
                    linear_fm(Winv, KD, [(OFF_B + i * 128, 128) for i in range(24)], h_rhs, h_key, epi_b, wp, 'wp', full=True)
                    linear_fm(Winv, KD, [(OFF_C + c0, m) for (c0, m) in C_TILES], h_rhs, h_key, epi_c, wp, 'wp', full=True)
                    linear_fm(Winv, KD, [(OFF_G + i * 128, 128) for i in range(48)], h_rhs, h_key, epi_g, wp, 'wp')
                S.barrier()
                if l == 0:
                    dump('uT', uT)
                    dump('bT', bT)
                    dump('cT', cT)
                    dump('gT', gT)

                with ExitStack() as ph:
                    ph.enter_context(nc.named_scope("mixa"))
                    wv = sb("ma_wv", [128, KD, 1024], BF16, st=ph)
                    lng = sb("ma_lng", [128, 1024], F32, st=ph)
                    lnb = sb("ma_lnb", [128, 1024], F32, st=ph)
                    bsb = sb("ma_bsb", [128, 8, 128], F32, st=ph)
                    wsn = sb("ma_wsn", [128, 8, 128], F32, st=ph)
                    wsT = sb("ma_wsT", [128, 8, 128], BF16, st=ph)
                    vg = [sb("ma_vg%d" % i, [128, 1024], F32, st=ph) for i in range(2)]
                    vc = [sb("ma_vc%d" % i, [128, 1024], F32, st=ph) for i in range(2)]
                    vln = [sb("ma_vln%d" % i, [128, 1024], BF16, st=ph) for i in range(2)]
                    ut = [sb("ma_ut%d" % i, [128, 8, 128], F32, st=ph) for i in range(2)]
                    ya = [sb("ma_ya%d" % i, [128, 8, 128], BF16, st=ph) for i in range(2)]
                    tm = [sb("ma_tm%d" % i, [128, 512], F32, st=ph) for i in range(2)]
                    stt = [sb("ma_st%d" % i, [128, 8], F32, st=ph) for i in range(2)]
                    DMA('pool', wv[:], Winv[:, :, A_W:2 * A_W], (), ['ma_wv'])
                    DMA('sp', lng[:], W['gm_ln_g'][l:l + 1, :].broadcast_to([128, 1024]), (), ['ma_c'])
                    DMA('sp', lnb[:], W['gm_ln_b'][l:l + 1, :].broadcast_to([128, 1024]), (), ['ma_c'])
                    DMA('sp', bsb[:].rearrange("p g q -> p (g q)"),
                        W['gm_bs'][l:l + 1].rearrange("o g q -> o (g q)").broadcast_to([128, 1024]), (), ['ma_c'])
                    DMA('sp', wsn[:], W['gm_ws'][l].rearrange("g p q -> p g q"), (), ['ma_wsn'])
                    for hb in range(2):
                        for j in range(4):
                            g = hb * 4 + j
                            TR(PSb(hb)[:, j * 128:(j + 1) * 128], wsn[:, g, :], identF[:], ['ma_wsn', 'const'], [('ps', hb)])
                        EVAC(wsT[:, hb * 4:(hb + 1) * 4, :].rearrange("p g q -> p (g q)"), PSb(hb), [('ps', hb)], ['ma_wsT'])
                    uTv = uT.rearrange("(g d) t -> d g t", d=128)
                    yaTv = yaT.rearrange("(g d) t -> d g t", d=128)
                    for i in range(NT):
                        b = i % 2
                        tsl = slice(i * 128, (i + 1) * 128)
                        tgi = i // 4
                        DMA('sp', ut[b][:], uTv[:, :, tsl], ['uT'], [('ma_ut', b)])
                        for half in range(2):
                            bank = 2 + b * 2 + half
                            for kt in range(KD):
                                MM(PSb(bank), hT[:, kt, tsl], wv[:, kt, half * 512:(half + 1) * 512], kt == 0, kt == KD - 1,
                                   [('hT', kt, tgi), 'ma_wv'], [('ps', bank)])
                            ACT(vg[b][:, half * 512:(half + 1) * 512], PSb(bank), AF.Gelu, [('ps', bank)], [('ma_vg', b, half), ('ma_st', b)],
                                accum_out=stt[b][:, half:half + 1])
                        TT(stt[b][:, 2:3], stt[b][:, 0:1], stt[b][:, 1:2], ALU.add, [('ma_st', b)], [('ma_st', b)])
                        TS(stt[b][:, 3:4], stt[b][:, 2:3], -1.0 / A_W, None, ALU.mult, None, [('ma_st', b)], [('ma_st', b)])
                        TS(vc[b][:], vg[b][:], stt[b][:, 3:4], None, ALU.add, None,
                           [('ma_vg', b, 0), ('ma_vg', b, 1), ('ma_st', b)], [('ma_vc', b)])
                        ACT(vg[b][:], vc[b][:], AF.Square, [('ma_vc', b)], [('ma_vg', b, 0), ('ma_vg', b, 1), ('ma_st', b)],
                            accum_out=stt[b][:, 4:5])
                        ACT(stt[b][:, 5:6], stt[b][:, 4:5], AF.Sqrt, [('ma_st', b)], [('ma_st', b)], scale=1.0 / A_W, bias=LN_EPS)
                        RECIP(stt[b][:, 6:7], stt[b][:, 5:6], [('ma_st', b)], [('ma_st', b)])
                        STT(vc[b][:], vc[b][:], stt[b][:, 6:7], lng[:], ALU.mult, ALU.mult, [('ma_vc', b), ('ma_st', b), 'ma_c'], [('ma_vc', b)])
                        TT(vln[b][:], vc[b][:], lnb[:], ALU.add, [('ma_vc', b), 'ma_c'], [('ma_vln', b)])
                        for hb in range(2):
                            bank = 6 + hb
                            for j in range(4):
                                g = hb * 4 + j
                                MM(PSb(bank)[:, j * 128:(j + 1) * 128], vln[b][:, g * 128:(g + 1) * 128], wsT[:, g, :], True, True,
                                   [('ma_vln', b), 'ma_wsT'], [('ps', bank)])
                            TT(tm[hb][:], PSb(bank), bsb[:, hb * 4:(hb + 1) * 4, :].rearrange("p g q -> p (g q)"), ALU.add,
                               [('ps', bank), 'ma_c'], [('ma_tm', hb)])
                            TT(ya[b][:, hb * 4:(hb + 1) * 4, :].rearrange("p g q -> p (g q)"), tm[hb][:],
                               ut[b][:, hb * 4:(hb + 1) * 4, :].rearrange("p g q -> p (g q)"), ALU.mult,
                               [('ma_tm', hb), ('ma_ut', b)], [('ma_ya', b)])
                        DMA('act', yaTv[:, :, tsl], ya[b][:], [('ma_ya', b)], ['yaT'])
                S.barrier()
            if l == 0:
                dump('yaT', yaT)
            if stop == 'mixa':
                break
            hyena_phase(l)
            if stop == 'hy':
                break
            if l == 0:
                dump('spec', spec)
                dump('ybT', ybT)
            rwkv_phase(l)
            if l == 0:
                dump('ycT', ycT)
            if stop == 'rw':
                break
            merge_phase(l)
            if l == 0:
                dump('mergedT', mgT)
            outproj_phase(l)
            if stop == 'outp':
                break
            ffn_phase(l)
        dump('xT', xT)
        if stop is None:
            final_phase()

        S.barrier()
    return nc


_NC_CACHE = {}


def kernel(**inputs):
    if 'nc' not in _NC_CACHE:
        _NC_CACHE['nc'] = build()
        _NC_CACHE['consts'] = make_consts()
    nc = _NC_CACHE['nc']
    consts = _NC_CACHE['consts']
    inp = {k_: np.ascontiguousarray(np.asarray(v)) for k_, v in inputs.items()}
    pcol = make_pcol(inp)
    base = {n: inp[n] for n in WEIGHT_SHAPES}
    base.update(consts)
    base['pcol'] = pcol
    NB_ = inp['x'].shape[0]
    in_maps = []
    for c in range(NCORES_USED):
        m = dict(base)
        m['x'] = np.ascontiguousarray(inp['x'][c % NB_])
        in_maps.append(m)
    res = run_bass_kernel_spmd(nc, in_maps, core_ids=list(range(NCORES_USED)))
    out = np.stack([np.asarray(res.results[b]['out'], dtype=np.float32) for b in range(NB_)], axis=0)
    return out
```

```python
import math
import numpy as np
import ml_dtypes
from contextlib import ExitStack
import concourse.bass as bass
import concourse.mybir as mybir
from concourse.bass_utils import run_bass_kernel_spmd

F32 = mybir.dt.float32
BF16 = mybir.dt.bfloat16
AF = mybir.ActivationFunctionType
ALU = mybir.AluOpType
AX = mybir.AxisListType

NCORES = 8
NCORES_USED = 4
T = 2048
D = 2048
DEPTH = 4
A_W = 1024
B_W = 1024
C_W = 1024
C_IN = 3392
N_IN = 14656
D_FF = 5632
NT = T // 128
NTG = T // 512
KD = D // 128
RMS_EPS = 1e-6
LN_EPS = 1e-5
GN_EPS = 64e-5
HY_MIN_DECAY = -math.log(1e-2) / 1.5
HY_MAX_DECAY = -math.log(1e-2) / 0.3
OFF_B = 2 * A_W
OFF_C = OFF_B + 3 * B_W
OFF_G = OFF_C + C_IN

WEIGHT_SHAPES = {
    'w_in': (DEPTH, D, N_IN), 'gm_ln_g': (DEPTH, A_W), 'gm_ln_b': (DEPTH, A_W),
    'gm_ws': (DEPTH, 8, 128, 128), 'gm_bs': (DEPTH, 8, 128),
    'hy_w1': (DEPTH, 33, 64), 'hy_w2': (DEPTH, 64, 64), 'hy_w3': (DEPTH, 64, 64), 'hy_w4': (DEPTH, 64, 4096),
    'hy_log_decay': (DEPTH, 2, 2, 1024), 'hy_bias_d': (DEPTH, 2, 1024),
    'rw_w2': (DEPTH, 2, 48, 1024), 'rw_a2': (DEPTH, 2, 48, 1024),
    'rw_v1': (DEPTH - 1, 1024, 32), 'rw_v2': (DEPTH - 1, 32, 1024), 'rw_g2': (DEPTH, 128, 1024),
    'w_branch_a': (DEPTH, A_W, D), 'w_branch_b': (DEPTH, B_W, D), 'w_branch_c': (DEPTH, C_W, D),
    'w_out': (DEPTH, D, D), 'w_ffn_gate': (DEPTH, D, D_FF), 'w_ffn_up': (DEPTH, D, D_FF),
    'w_ffn_down': (DEPTH, D_FF, D),
}

PC = {}
_o = 0
for _n, _c in [('nmg', 16), ('nfg', 16), ('cw0', 24), ('cw1', 24), ('cw2', 24), ('cb', 24), ('mup', 29), ('mun', 29),
               ('w0', 16), ('a0', 16), ('v0', 8), ('kk', 8), ('ka', 8), ('rk', 8), ('lng', 8), ('lnb', 8),
               ('hyb', 4), ('nfin', 16)]:
    PC[_n] = (_o, _c)
    _o += _c
NPC = _o
C_TILES = [(i * 128, 128) for i in range(25)] + [(3200, 48), (3248, 48), (3296, 48), (3344, 48)]


def _cols(v):
    return np.ascontiguousarray(np.asarray(v, np.float32).reshape(-1, 128).T)


def make_pcol(inp):
    pc = np.zeros((DEPTH, 128, NPC), np.float32)

    def put(l, name, arr):
        o, c = PC[name]
        assert arr.shape == (128, c), (name, arr.shape)
        pc[l, :, o:o + c] = arr
    for l in range(DEPTH):
        put(l, 'nmg', _cols(inp['norm_mix_g'][l]))
        put(l, 'nfg', _cols(inp['norm_ffn_g'][l]))
        for j in range(3):
            put(l, 'cw%d' % j, _cols(inp['hy_conv_w'][l, j]))
        put(l, 'cb', _cols(inp['hy_conv_b'][l]))
        for nm, src in (('mup', 'rw_mu_prev'), ('mun', 'rw_mu_next')):
            a = np.zeros((128, 29), np.float32)
            for i, (c0, m) in enumerate(C_TILES):
                a[:m, i] = inp[src][l, c0:c0 + m]
            put(l, nm, a)
        put(l, 'w0', _cols(inp['rw_w0'][l]))
        put(l, 'a0', _cols(inp['rw_a0'][l]))
        if l > 0:
            put(l, 'v0', _cols(inp['rw_v0'][l - 1]))
        put(l, 'kk', _cols(inp['rw_k_k'][l]))
        put(l, 'ka', _cols(inp['rw_k_a'][l]))
        put(l, 'rk', _cols(inp['rw_r_k'][l]))
        put(l, 'lng', _cols(inp['rw_ln_g'][l]))
        put(l, 'lnb', _cols(inp['rw_ln_b'][l]))
        hb = np.zeros((128, 4), np.float32)
        hb[:64, 0] = inp['hy_b1'][l]
        hb[:64, 1] = inp['hy_b2'][l]
        hb[:64, 2] = inp['hy_b3'][l]
        hb[:64, 3] = inp['hy_freq'][l]
        put(l, 'hyb', hb)
        put(l, 'nfin', _cols(inp['norm_final_g']))
    return pc


def make_consts():
    c = {}
    c['identF'] = np.eye(128, dtype=np.float32)
    s = np.arange(128)[:, None]
    t = np.arange(128)[None, :]
    su = (s < t).astype(np.float32)
    iu = (s <= t).astype(np.float32)
    c['maskU4'] = np.concatenate([su, iu, su, iu], axis=1)
    c['maskL'] = (s > t).astype(np.float32)
    c['blockones'] = ((s // 64) == (t // 64)).astype(np.float32)
    c['onesF'] = np.ones((128, 128), np.float32)
    c['identB'] = np.eye(128).astype(ml_dtypes.bfloat16)
    c['maskL4'] = np.tile(c['maskL'], (1, 4))
    c['ident4'] = np.tile(c['identF'], (1, 4))
    rm = np.ones((128, T), np.float32)
    rm[:, 0::128] = 0.0
    c['rmask'] = rm
    n = np.arange(T, dtype=np.float64)
    f = np.arange(T, dtype=np.float64) + 0.5
    ang = 2.0 * np.pi * np.outer(n, f) / (2 * T)
    c['CT'] = np.cos(ang).astype(ml_dtypes.bfloat16)
    c['ST'] = np.sin(ang).astype(ml_dtypes.bfloat16)
    c['CF'] = np.ascontiguousarray(np.cos(ang).T).astype(ml_dtypes.bfloat16)
    c['SF'] = np.ascontiguousarray(np.sin(ang).T).astype(ml_dtypes.bfloat16)
    tt = np.linspace(0.0, 1.0, T, dtype=np.float32)[:, None]
    bands = 16
    fr = np.linspace(1e-4, bands - 1, bands, dtype=np.float32)
    a2 = (np.float32(2.0 * math.pi / T) * np.arange(T, dtype=np.float32)[:, None]) * fr[None, :]
    feats = np.concatenate([tt, np.cos(a2), -np.sin(a2)], axis=-1).astype(np.float32)
    c['featsT'] = np.ascontiguousarray(feats.T)
    c['tneg'] = np.ascontiguousarray((-tt[:, 0]).reshape(NT, 128).T)
    return c


CONST_SHAPES = {'identF': ((128, 128), F32), 'maskU4': ((128, 512), F32), 'maskL': ((128, 128), F32),
                'blockones': ((128, 128), F32), 'onesF': ((128, 128), F32), 'maskL4': ((128, 512), F32),
                'ident4': ((128, 512), F32), 'rmask': ((128, T), F32), 'identB': ((128, 128), BF16),
                'CT': ((T, T), BF16), 'ST': ((T, T), BF16), 'CF': ((T, T), BF16), 'SF': ((T, T), BF16),
                'featsT': ((33, T), F32), 'tneg': ((128, NT), F32)}


class Sched:
    NDMA = 12

    def __init__(self, nc, es):
        self.nc = nc
        self.engs = {'pe': nc.tensor, 'act': nc.scalar, 'dve': nc.vector, 'pool': nc.gpsimd, 'sp': nc.sync}
        self.sem = {k: es.enter_context(nc.semaphore("s_" + k)) for k in self.engs}
        self.cnt = {k: 0 for k in self.engs}
        self.waited = {}
        self.dsem = {}
        self.dcnt = {}
        self.dnext = {}
        for q in ('sp', 'act', 'pool'):
            self.dsem[q] = [es.enter_context(nc.semaphore("d_%s%d" % (q, i))) for i in range(self.NDMA)]
            self.dcnt[q] = [0] * self.NDMA
            self.dnext[q] = 0
        self.res = {}
        self.ninst = 0

    def _wait(self, e, tok):
        if tok[0] == 'e':
            _, f, v = tok
            if f == e and e == 'pe':
                return
            key = (e, f)
        else:
            _, q, slot, v = tok
            key = (e, 'd', q, slot)
        if self.waited.get(key, 0) >= v:
            return
        self.waited[key] = v
        sem = self.sem[tok[1]] if tok[0] == 'e' else self.dsem[tok[1]][tok[2]]
        self.engs[e].wait_ge(sem, v)

    def _deps(self, e, reads, writes):
        for r in reads:
            st = self.res.get(r)
            if st and st[0] is not None:
                self._wait(e, st[0])
        for w in writes:
            st = self.res.get(w)
            if st:
                if st[0] is not None:
                    self._wait(e, st[0])
                for t in st[1].values():
                    self._wait(e, t)

    def _record(self, tok, reads, writes):
        for r in reads:
            st = self.res.get(r)
            if st is None:
                st = self.res[r] = [None, {}]
            k = tok[1] if tok[0] == 'e' else (tok[1], tok[2])
            st[1][k] = tok
        for w in writes:
            self.res[w] = [tok, {}]

    def op(self, e, fn, reads=(), writes=()):
        self._deps(e, reads, writes)
        ins = fn()
        self.cnt[e] += 1
        ins.then_inc(self.sem[e], 1)
        self._record(('e', e, self.cnt[e]), reads, writes)
        self.ninst += 1
        return ins

    def dma(self, q, out, in_, reads=(), writes=(), **kw):
        slot = self.dnext[q]
        self.dnext[q] = (slot + 1) % self.NDMA
        if self.dcnt[q][slot] > 0:
            self._wait(q, ('d', q, slot, self.dcnt[q][slot]))
        self._deps(q, reads, writes)
        ins = self.engs[q].dma_start(out=out, in_=in_, **kw)
        self.dcnt[q][slot] += 16
        ins.then_inc(self.dsem[q][slot], 16)
        self._record(('d', q, slot, self.dcnt[q][slot]), reads, writes)
        self.ninst += 1
        return ins

    def barrier(self):
        for e in self.engs:
            for f in self.engs:
                if self.cnt[f] > 0:
                    key = (e, f)
                    if self.waited.get(key, 0) < self.cnt[f]:
                        self.waited[key] = self.cnt[f]
                        self.engs[e].wait_ge(self.sem[f], self.cnt[f])
            for q in self.dsem:
                for slot in range(self.NDMA):
                    if self.dcnt[q][slot] > 0:
                        self._wait(e, ('d', q, slot, self.dcnt[q][slot]))
        self.res = {}


def build(NL=DEPTH, dbg=(), stop=None):
    nc = bass.Bass("TRN2", target_bir_lowering=False)

    def din(name, shape, dt=F32):
        return nc.dram_tensor(name, list(shape), dt, kind="ExternalInput").ap()

    def dscr(name, shape, dt=F32):
        return nc.dram_tensor(name, list(shape), dt, kind="Internal").ap()

    def dout(name, shape, dt=F32):
        return nc.dram_tensor(name, list(shape), dt, kind="ExternalOutput").ap()

    x_in = din("x", [T, D])
    W = {n: din(n, s) for n, s in WEIGHT_SHAPES.items()}
    pcol_d = din("pcol", [DEPTH, 128, NPC])
    CD = {n: din(n, s, dt) for n, (s, dt) in CONST_SHAPES.items()}
    out_d = dout("out", [T, D])
    DBG = {}
    dbg_shapes = {'xT': ([D, T], F32), 'xT0': ([D, T], F32), 'uT': ([A_W, T], F32), 'bT': ([3 * B_W, T], F32), 'cT': ([C_IN, T], F32),
                  'gT': ([3 * D, T], BF16), 'yaT': ([A_W, T], BF16), 'ybT': ([B_W, T], BF16),
                  'ycT': ([C_W, T], BF16), 'hTd': ([D, T], BF16), 'spec': ([2, 2, T, B_W], F32),
                  'mergedT': ([D, T], BF16)}
    for n in dbg:
        DBG[n] = dout("dbg_" + n, *dbg_shapes[n])

    xT = dscr("xT_s", [D, T])
    uT = dscr("uT_s", [A_W, T])
    bT = dscr("bT_s", [3 * B_W, T])
    cT = dscr("cT_s", [C_IN, T])
    gT = dscr("gT_s", [3 * D, T], BF16)
    yaT = dscr("yaT_s", [A_W, T], BF16)
    ybT = dscr("ybT_s", [B_W, T], BF16)
    ycT = dscr("ycT_s", [C_W, T], BF16)
    mgT = dscr("mgT_s", [D, T], BF16)
    actT = dscr("actT_s", [D_FF, T], BF16)
    vfT = dscr("vfT_s", [C_W, T])
    x1tok = dscr("x1tok_s", [T, B_W])
    hfil = dscr("hfil_s", [T, 4096])
    spec = dscr("spec_s", [2, 2, T, B_W])
    xTv = xT.rearrange("(dk p) t -> p dk t", p=128)

    with ExitStack() as es:
        S = Sched(nc, es)

        uid = [0]

        def sb(name, shape, dt=F32, st=es):
            uid[0] += 1
            return st.enter_context(nc.sbuf_tensor("sb%d_%s" % (uid[0], name), list(shape), dt))

        PSA = es.enter_context(nc.psum_tensor("psA", [128, 2048], F32))
        PSB = es.enter_context(nc.psum_tensor("psB", [128, 2048], F32))

        def PSb(i):
            return (PSA if i < 4 else PSB)[:, (i % 4) * 512:(i % 4 + 1) * 512]

        def PSfull(a):
            return PSA if a == 0 else PSB

        def ACT(out, in_, func, r, w, **kw):
            return S.op('act', lambda: nc.scalar.activation(out=out, in_=in_, func=func, **kw), r, w)

        def TT(out, a, b, op, r, w, eng='dve'):
            e = nc.vector if eng == 'dve' else nc.gpsimd
            return S.op(eng, lambda: e.tensor_tensor(out=out, in0=a, in1=b, op=op), r, w)

        def TS(out, a, s1, s2, op0, op1, r, w, eng='dve'):
            e = nc.vector if eng == 'dve' else nc.gpsimd
            if s2 is None:
                return S.op(eng, lambda: e.tensor_scalar(out=out, in0=a, scalar1=s1, scalar2=None, op0=op0), r, w)
            return S.op(eng, lambda: e.tensor_scalar(out=out, in0=a, scalar1=s1, scalar2=s2, op0=op0, op1=op1), r, w)

        def STT(out, a, s, b, op0, op1, r, w, eng='dve'):
            e = nc.vector if eng == 'dve' else nc.gpsimd
            return S.op(eng, lambda: e.scalar_tensor_tensor(out=out, in0=a, scalar=s, in1=b, op0=op0, op1=op1), r, w)

        def RECIP(out, in_, r, w):
            return S.op('dve', lambda: nc.vector.reciprocal(out=out, in_=in_), r, w)

        def CP(out, in_, r, w, eng='dve'):
            if eng == 'act':
                return S.op('act', lambda: nc.scalar.copy(out=out, in_=in_), r, w)
            e = nc.vector if eng == 'dve' else nc.gpsimd
            return S.op(eng, lambda: e.tensor_copy(out=out, in_=in_), r, w)

        def MSET(ap, val, w, eng='pool'):
            e = nc.vector if eng == 'dve' else nc.gpsimd
            return S.op(eng, lambda: e.memset(ap, val), (), w)

        def MM(out, lhsT, rhs, start, stop, r, w):
            return S.op('pe', lambda: nc.tensor.matmul(out, lhsT, rhs, start=start, stop=stop), r, w)

        def TR(out, in_, ident, r, w):
            return S.op('pe', lambda: nc.tensor.transpose(out, in_, ident), r, w)

        def DMA(q, out, in_, r, w, **kw):
            return S.dma(q, out, in_, r, w, **kw)

        cp_flip = [0]

        def EVAC(out, in_, r, w):
            cp_flip[0] ^= 1
            return CP(out, in_, r, w, eng='act' if cp_flip[0] else 'dve')

        identF = sb("identF", [128, 128])
        maskU4 = sb("maskU4", [128, 512])
        maskL = sb("maskL", [128, 128])
        blockones = sb("blockones", [128, 128])
        onesF = sb("onesF", [128, 128])
        tneg = sb("tneg", [128, NT])
        maskL4 = sb("maskL4", [128, 512])
        ident4 = sb("ident4", [128, 512])
        identB = sb("identB", [128, 128], BF16)
        pcol = sb("pcol", [128, NPC])
        for n, t_ in (('identF', identF), ('maskU4', maskU4), ('maskL', maskL), ('blockones', blockones),
                      ('onesF', onesF), ('tneg', tneg), ('maskL4', maskL4), ('ident4', ident4), ('identB', identB)):
            DMA('sp', t_[:], CD[n][:, :], (), ['const'])

        def pc(name, i=0, m=128):
            o, c = PC[name]
            return pcol[0:m, o + i:o + i + 1]

        def dump(name, src_ap):
            if name in DBG:
                DMA('sp', DBG[name], src_ap, ['ALLDRAM'], [('dbg', name)])

        with ExitStack() as ph:
            xt = [sb("p0x%d" % i, [128, D], st=ph) for i in range(2)]
            stg = [sb("p0s%d" % i, [128, 4, 128], st=ph) for i in range(4)]
            n = 0
            for i in range(NT):
                b = i % 2
                DMA('sp', xt[b][:], x_in[i * 128:(i + 1) * 128, :], (), [('p0x', b)])
                for q in range(4):
                    bank = n % 8
                    s_ = n % 4
                    for j in range(4):
                        dk = q * 4 + j
                        TR(PSb(bank)[:, j * 128:(j + 1) * 128], xt[b][:, dk * 128:(dk + 1) * 128], identF[:],
                           [('p0x', b), 'const'], [('ps', bank)])
                    EVAC(stg[s_][:].rearrange("p a b -> p (a b)"), PSb(bank), [('ps', bank)], [('p0s', s_)])
                    DMA('act', xTv[:, q * 4:(q + 1) * 4, i * 128:(i + 1) * 128], stg[s_][:], [('p0s', s_)], ['xT'])
                    n += 1
        S.barrier()
        if 'xT0' in DBG:
            DMA('sp', DBG['xT0'], xT, (), [('dbg', 'xT0')])

        def norm_phase(hT, gname):
            with ExitStack() as ph:
                ph.enter_context(nc.named_scope("norm"))
                xin = [sb("nx%d" % i, [128, KD, 512], st=ph) for i in range(2)]
                sq = [sb("nsq%d" % i, [128, 512], st=ph) for i in range(2)]
                rs = [sb("nrs%d" % i, [128, 512], st=ph) for i in range(2)]
                for tg in range(NTG):
                    b = tg % 2
                    tsl = slice(tg * 512, (tg + 1) * 512)
                    DMA('sp', xin[b][:], xTv[:, :, tsl], ['xT'], [('nx', b)])
                    for dk in range(KD):
                        ACT(sq[dk % 2][:], xin[b][:, dk, :], AF.Square, [('nx', b)], [('nsq', dk % 2)])
                        MM(PSb(b), onesF[:], sq[dk % 2][:], dk == 0, dk == KD - 1, [('nsq', dk % 2), 'const'], [('ps', b)])
                    ACT(rs[b][:], PSb(b), AF.Sqrt, [('ps', b)], [('nrs', b)], scale=1.0 / D, bias=RMS_EPS)
                    RECIP(rs[b][:], rs[b][:], [('nrs', b)], [('nrs', b)])
                    for dk in range(KD):
                        STT(hT[:, dk, tsl], xin[b][:, dk, :], pc(gname, dk), rs[b][:], ALU.mult, ALU.mult,
                            [('nx', b), 'pcol', ('nrs', b)], [('hT', dk, tg)])
                S.barrier()

        def linear_fm(Wv, KT, ctiles, rhs_fn, rkey_fn, epi, wp, wpname, full=False, banks=(0, 1, 2, 3, 4, 5, 6, 7)):
            panels = []
            cur = None
            for ci, (c0, m) in enumerate(ctiles):
                if cur is None or (c0 + m - cur[0]) > 512 or c0 != cur[1]:
                    cur = [c0, c0, []]
                    panels.append(cur)
                cur[2].append((ci, c0, m))
                cur[1] = c0 + m
            nb = 0
            for pi, (p0, p1, tl) in enumerate(panels):
                b = pi % 2
                DMA('pool', wp[b][:, 0:KT, 0:p1 - p0], Wv[:, :, p0:p1], (), [(wpname, b)])
                for (ci, c0, m) in tl:
                    lo = c0 - p0
                    if full:
                        a = nb % 2
                        nb += 1
                        for tg in range(NTG):
                            bank = a * 4 + tg
                            for kt in range(KT):
                                MM(PSb(bank)[0:m, :], wp[b][:, kt, lo:lo + m], rhs_fn(kt, tg), kt == 0, kt == KT - 1,
                                   [(wpname, b), rkey_fn(kt, tg)], [('ps', bank)])
                        epi(ci, c0, m, None, PSfull(a)[0:m, :], [('ps', a * 4 + j) for j in range(4)])
                    else:
                        for tg in range(NTG):
                            bank = banks[nb % len(banks)]
                            nb += 1
                            for kt in range(KT):
                                MM(PSb(bank)[0:m, :], wp[b][:, kt, lo:lo + m], rhs_fn(kt, tg), kt == 0, kt == KT - 1,
                                   [(wpname, b), rkey_fn(kt, tg)], [('ps', bank)])
                            epi(ci, c0, m, tg, PSb(bank)[0:m, :], [('ps', bank)])


        SW = 256
        NSG = T // SW
        INV_SCALE = 2.0 / (2 * T)

        def load_slabs(slC, slS, b, Cm, Sm, g):
            DMA('sp', slC[b][:], Cm.rearrange("(kt p) c -> p kt c", p=128)[:, :, g * SW:(g + 1) * SW], (), [('slC', b)])
            DMA('act', slS[b][:], Sm.rearrange("(kt p) c -> p kt c", p=128)[:, :, g * SW:(g + 1) * SW], (), [('slS', b)])

        def dft_fwd(slC, slS, srcC, srcS, keyC, keyS, epi):
            nb = 0
            for g in range(NSG):
                b = g % 2
                load_slabs(slC, slS, b, CD['CT'], CD['ST'], g)
                for fi in range(SW // 128):
                    ft = g * (SW // 128) + fi
                    for chh in range(2):
                        bC = (nb % 4) * 2
                        bS = bC + 1
                        nb += 1
                        for nt in range(NT):
                            MM(PSb(bC), slC[b][:, nt, fi * 128:(fi + 1) * 128], srcC[:, nt, chh * 512:(chh + 1) * 512],
                               nt == 0, nt == NT - 1, [('slC', b), (keyC, nt)], [('ps', bC)])
                        for nt in range(NT):
                            MM(PSb(bS), slS[b][:, nt, fi * 128:(fi + 1) * 128], srcS[:, nt, chh * 512:(chh + 1) * 512],
                               nt == 0, nt == NT - 1, [('slS', b), (keyS, nt)], [('ps', bS)])
                        epi(ft, chh, bC, bS)

        def hyena_phase(l):
            hy0 = ExitStack()
            asum = sb("hy_asum", [128, 4096], st=hy0)
            with ExitStack() as ph:
                ph.enter_context(nc.named_scope("hy_filt"))
                w1 = sb("hy_w1", [33, 64], st=ph)
                w2 = sb("hy_w2", [64, 64], st=ph)
                w3 = sb("hy_w3", [64, 64], st=ph)
                w4 = sb("hy_w4", [64, 4096], BF16, st=ph)
                z3b = sb("hy_z3b", [64, T], BF16, st=ph)
                fT = sb("hy_fT", [33, T], st=ph)
                zT = [sb("hy_zT%d" % i, [64, T], st=ph) for i in range(2)]
                tmpz = sb("hy_tmpz", [64, 512], st=ph)
                edb = sb("hy_edb", [128, 4096], st=ph)
                dec = [sb("hy_dec%d" % i, [128, 512], st=ph) for i in range(4)]
                hdt = [sb("hy_hd%d" % i, [128, 512], st=ph) for i in range(8)]
                hab2 = [sb("hy_hab2%d" % i, [128, 512], st=ph) for i in range(4)]
                habs = [sb("hy_ha%d" % i, [128, 512], st=ph) for i in range(2)]
                DMA('sp', w1[:], W['hy_w1'][l], (), ['hy_w'])
                DMA('sp', w2[:], W['hy_w2'][l], (), ['hy_w'])
                DMA('sp', w3[:], W['hy_w3'][l], (), ['hy_w'])
                DMA('pool', w4[:], W['hy_w4'][l], (), ['hy_w4'])
                DMA('sp', fT[:], CD['featsT'][:, :], (), ['hy_fT'])
                DMA('sp', edb[:], W['hy_log_decay'][l:l + 1].rearrange("o a b c -> o (a b c)").broadcast_to([128, 4096]), (), ['hy_edb'])
                ACT(edb[:], edb[:], AF.Exp, ['hy_edb'], ['hy_edb'])
                ob_, _ = PC['hyb']
                fq = pcol[0:64, ob_ + 3:ob_ + 4]
                srcs = [(w1, fT, 33), (w2, zT[0], 64), (w3, zT[1], 64)]
                for li, (wl, src, kk_) in enumerate(srcs):
                    dst = zT[li % 2]
                    bcol = pcol[0:64, ob_ + li:ob_ + li + 1]
                    for tg in range(NTG):
                        tsl = slice(tg * 512, (tg + 1) * 512)
                        bank = tg % 2
                        MM(PSb(bank)[0:64, :], wl[0:kk_, :], src[0:kk_, tsl], True, True, ['hy_w', 'hy_fT', ('hy_zT', 0), ('hy_zT', 1)], [('ps', bank)])
                        TS(dst[:, tsl], PSb(bank)[0:64, :], bcol, fq, ALU.add, ALU.mult, [('ps', bank), 'pcol'], [('hy_zT', li % 2)])
                        for rnd in range(3):
                            TS(tmpz[:], dst[:, tsl], math.pi, -2 * math.pi, ALU.is_gt, ALU.mult, [('hy_zT', li % 2)], ['hy_tmpz'])
                            TT(dst[:, tsl], dst[:, tsl], tmpz[:], ALU.add, [('hy_zT', li % 2), 'hy_tmpz'], [('hy_zT', li % 2)])
                            TS(tmpz[:], dst[:, tsl], -math.pi, 2 * math.pi, ALU.is_lt, ALU.mult, [('hy_zT', li % 2)], ['hy_tmpz'])
                            TT(dst[:, tsl], dst[:, tsl], tmpz[:], ALU.add, [('hy_zT', li % 2), 'hy_tmpz'], [('hy_zT', li % 2)])
                        ACT(dst[:, tsl], dst[:, tsl], AF.Sin, [('hy_zT', li % 2)], [('hy_zT', li % 2)])
                z3 = zT[0]
                CP(z3b[:], z3[:], [('hy_zT', 0)], ['hy_z3b'], eng='act')
                its = [(cg, i) for cg in range(8) for i in range(NT)]

                def emit_dec(n):
                    cg, i = its[n]
                    ACT(dec[n % 4][:], edb[:, cg * 512:(cg + 1) * 512], AF.Exp, ['hy_edb', 'const'], [('hy_dec', n % 4)], scale=tneg[:, i:i + 1])
                emit_dec(0)
                for n, (cg, i) in enumerate(its):
                    csl = slice(cg * 512, (cg + 1) * 512)
                    a_ = cg % 2
                    k = n % 4
                    k8 = n % 8
                    bank = 2 + k
                    if i == 0:
                        MSET(habs[a_][:], 0.0, [('hy_ha', a_)], eng='pool')
                    MM(PSb(bank), z3b[:, i * 128:(i + 1) * 128], w4[:, csl], True, True, ['hy_z3b', 'hy_w4'], [('ps', bank)])
                    if n + 1 < len(its):
                        emit_dec(n + 1)
                    TT(hdt[k8][:], PSb(bank), dec[k][:], ALU.mult, [('ps', bank), ('hy_dec', k)], [('hy_hd', k8)])
                    DMA('sp' if n % 2 else 'act', hfil[i * 128:(i + 1) * 128, csl], hdt[k8][:], [('hy_hd', k8)], ['hfil'])
                    ACT(hab2[k][:], hdt[k8][:], AF.Abs, [('hy_hd', k8)], [('hy_hab2', k)])
                    TT(habs[a_][:], habs[a_][:], hab2[k][:], ALU.add, [('hy_hab2', k), ('hy_ha', a_)], [('hy_ha', a_)], eng='pool')
                    if i == NT - 1:
                        MM(PSb(6 + a_), onesF[:], habs[a_][:], True, True, [('hy_ha', a_), 'const'], [('ps', 6 + a_)])
                        EVAC(asum[:, csl], PSb(6 + a_), [('ps', 6 + a_)], ['hy_asum'])
                S.barrier()
            with ExitStack() as ph:
                ph.enter_context(nc.named_scope("hy_spec"))
                rinv = sb("hy_rinv", [128, 2048], st=ph)
                bdb = sb("hy_bdb", [128, 2048], st=ph)
                hs = sb("hy_hs", [128, NT, 1024], BF16, st=ph)
                hdf = sb("hy_hdf", [128, NT, 1024], BF16, st=ph)
                hf = [sb("hy_hf%d" % i, [128, 1024], st=ph) for i in range(2)]
                hb = [sb("hy_hb%d" % i, [128, 1024], st=ph) for i in range(2)]
                t1 = [sb("hy_t1%d" % i, [128, 1024], st=ph) for i in range(2)]
                slC = [sb("hy_slC%d" % i, [128, NT, SW], BF16, st=ph) for i in range(2)]
                slS = [sb("hy_slS%d" % i, [128, NT, SW], BF16, st=ph) for i in range(2)]
                ko = [sb("hy_ko%d" % i, [128, 512], st=ph) for i in range(4)]
                TT(rinv[:], asum[:, 0:2048], asum[:, 2048:4096], ALU.add, (), ['hy_rinv'])
                ACT(rinv[:], rinv[:], AF.Ln, ['hy_rinv'], ['hy_rinv'])
                ACT(rinv[:], rinv[:], AF.Exp, ['hy_rinv'], ['hy_rinv'], scale=-1.0)
                DMA('sp', bdb[:], W['hy_bias_d'][l:l + 1].rearrange("o a c -> o (a c)").broadcast_to([128, 2048]), (), ['hy_bdb'])
                for o in range(2):
                    osl = slice(o * 1024, (o + 1) * 1024)
                    for i in range(NT):
                        k = i % 2
                        rows = slice(i * 128, (i + 1) * 128)
                        DMA('sp', hf[k][:], hfil[rows, o * 1024:(o + 1) * 1024], ['hfil'], [('hy_hf', k)])
                        DMA('act', hb[k][:], hfil[rows, 2048 + o * 1024:2048 + (o + 1) * 1024], ['hfil'], [('hy_hb', k)])
                        if i == 0:
                            MSET(hb[k][0:1, :], 0.0, [('hy_hb', k)], eng='dve')
                        TT(hs[:, i, :], hf[k][:], hb[k][:], ALU.add, [('hy_hf', k), ('hy_hb', k)], [('hy_hs', i)])
                        TT(hdf[:, i, :], hf[k][:], hb[k][:], ALU.subtract, [('hy_hf', k), ('hy_hb', k)], [('hy_hdf', i)], eng='pool')
                    kc = [0]

                    def epi_spec(ft, chh, bC, bS):
                        k0 = kc[0] % 2
                        kc[0] += 1
                        csl = slice(o * 1024 + chh * 512, o * 1024 + (chh + 1) * 512)
                        TT(ko[k0][:], PSb(bC), rinv[:, csl], ALU.mult, [('ps', bC), 'hy_rinv'], [('hy_ko', k0)])
                        TT(ko[k0][:], ko[k0][:], bdb[:, csl], ALU.add, [('hy_ko', k0), 'hy_bdb'], [('hy_ko', k0)], eng='pool')
                        DMA('sp', spec[o, 0, ft * 128:(ft + 1) * 128, chh * 512:(chh + 1) * 512], ko[k0][:], [('hy_ko', k0)], ['spec'])
                        TT(ko[2 + k0][:], PSb(bS), rinv[:, csl], ALU.mult, [('ps', bS), 'hy_rinv'], [('hy_ko', 2 + k0)])
                        DMA('sp', spec[o, 1, ft * 128:(ft + 1) * 128, chh * 512:(chh + 1) * 512], ko[2 + k0][:], [('hy_ko', 2 + k0)], ['spec'])
                    dft_fwd(slC, slS, hs, hdf, 'hy_hs', 'hy_hdf', epi_spec)
                S.barrier()
            hy0.close()
            with ExitStack() as ph:
                z = sb("hy_z", [128, NT, 1024], BF16, st=ph)
                Yr = sb("hy_Yr", [128, NT, 1024], BF16, st=ph)
                Ys = sb("hy_Ys", [128, NT, 1024], BF16, st=ph)
                slC = [sb("hy_slC%d" % i, [128, NT, SW], BF16, st=ph) for i in range(2)]
                slS = [sb("hy_slS%d" % i, [128, NT, SW], BF16, st=ph) for i in range(2)]
                Kr = [sb("hy_Kr%d" % i, [128, 512], st=ph) for i in range(2)]
                Ks = [sb("hy_Ks%d" % i, [128, 512], st=ph) for i in range(2)]
                Zc = [sb("hy_Zc%d" % i, [128, 512], st=ph) for i in range(2)]
                Zs = [sb("hy_Zs%d" % i, [128, 512], st=ph) for i in range(2)]
                ta = [sb("hy_ta%d" % i, [128, 512], st=ph) for i in range(2)]
                tb = [sb("hy_tb%d" % i, [128, 512], st=ph) for i in range(2)]
                ld = [sb("hy_ld%d" % i, [128, T], st=ph) for i in range(2)]
                stg = [sb("hy_stg%d" % i, [128, 4, 128], st=ph) for i in range(2)]
                xm = [sb("hy_xm%d" % i, [128, 512], st=ph) for i in range(2)]
                yo = [sb("hy_yo%d" % i, [128, 512], BF16, st=ph) for i in range(2)]
                x1v = x1tok.rearrange("(nt p) c -> p nt c", p=128)
                sc_ = nc.named_scope("hy_tr")
                sc_.__enter__()
                n = 0
                for which in range(2):
                    for ct in range(8):
                        k = n % 2
                        r0 = (2048 if which == 0 else 0) + ct * 128
                        DMA('sp', ld[k][:], bT[r0:r0 + 128, :], ['bcT'], [('hy_ld', k)])
                        for q in range(4):
                            bank = (n * 4 + q) % 8
                            for j in range(4):
                                nt = q * 4 + j
                                TR(PSb(bank)[:, j * 128:(j + 1) * 128], ld[k][:, nt * 128:(nt + 1) * 128], identF[:],
                                   [('hy_ld', k), 'const'], [('ps', bank)])
                            if which == 0:
                                EVAC(z[:, q * 4:(q + 1) * 4, ct * 128:(ct + 1) * 128], PSb(bank).rearrange("p (a b) -> p a b", a=4),
                                     [('ps', bank)], [('hy_z', q * 4 + j) for j in range(4)])
                            else:
                                s_ = q % 2
                                EVAC(stg[s_][:], PSb(bank).rearrange("p (a b) -> p a b", a=4), [('ps', bank)], [('hy_stg', s_)])
                                DMA('act', x1v[:, q * 4:(q + 1) * 4, ct * 128:(ct + 1) * 128], stg[s_][:], [('hy_stg', s_)], ['x1tok'])
                        n += 1
                sc_.__exit__(None, None, None)
                for o in range(2):
                    kc = [0]
                    sc_ = nc.named_scope("hy_fwd%d" % o)
                    sc_.__enter__()

                    def epi_mul(ft, chh, bC, bS):
                        k0 = kc[0] % 2
                        kc[0] += 1
                        rows = slice(ft * 128, (ft + 1) * 128)
                        cs = slice(chh * 512, (chh + 1) * 512)
                        DMA('sp', Kr[k0][:], spec[o, 0, rows, cs], ['spec'], [('hy_Kr', k0)])
                        DMA('sp', Ks[k0][:], spec[o, 1, rows, cs], ['spec'], [('hy_Ks', k0)])
                        CP(Zc[k0][:], PSb(bC), [('ps', bC)], [('hy_Zc', k0)], eng='act')
                        CP(Zs[k0][:], PSb(bS), [('ps', bS)], [('hy_Zs', k0)], eng='act')
                        TT(ta[k0][:], Zc[k0][:], Kr[k0][:], ALU.mult, [('hy_Zc', k0), ('hy_Kr', k0)], [('hy_ta', k0)])
                        TT(tb[k0][:], Zs[k0][:], Ks[k0][:], ALU.mult, [('hy_Zs', k0), ('hy_Ks', k0)], [('hy_tb', k0)], eng='pool')
                        TT(Yr[:, ft, cs], ta[k0][:], tb[k0][:], ALU.subtract, [('hy_ta', k0), ('hy_tb', k0)], [('hy_Y', ft)])
                        TT(ta[k0][:], Zc[k0][:], Ks[k0][:], ALU.mult, [('hy_Zc', k0), ('hy_Ks', k0)], [('hy_ta', k0)])
                        TT(tb[k0][:], Zs[k0][:], Kr[k0][:], ALU.mult, [('hy_Zs', k0), ('hy_Kr', k0)], [('hy_tb', k0)], eng='pool')
                        TT(Ys[:, ft, cs], ta[k0][:], tb[k0][:], ALU.add, [('hy_ta', k0), ('hy_tb', k0)], [('hy_Y', ft)], eng='pool')
                    dft_fwd(slC, slS, z, z, 'hy_z', 'hy_z', epi_mul)
                    sc_.__exit__(None, None, None)
                    sc_ = nc.named_scope("hy_inv%d" % o)
                    sc_.__enter__()
                    nb = 0
                    for g in range(NSG):
                        b = g % 2
                        load_slabs(slC, slS, b, CD['CF'], CD['SF'], g)
                        nsl = slice(g * SW, (g + 1) * SW)
                        if o == 0:
                            for ni in range(SW // 128):
                                nt = g * (SW // 128) + ni
                                for chh in range(2):
                                    bank = nb % 8
                                    k0 = nb % 2
                                    nb += 1
                                    cs = slice(chh * 512, (chh + 1) * 512)
                                    DMA('sp', xm[k0][:], x1tok[nt * 128:(nt + 1) * 128, cs], ['x1tok'], [('hy_xm', k0)])
                                    for ft in range(NT):
                                        MM(PSb(bank), slC[b][:, ft, ni * 128:(ni + 1) * 128], Yr[:, ft, cs], ft == 0, False,
                                           [('slC', b), ('hy_Y', ft)], [('ps', bank)])
                                    for ft in range(NT):
                                        MM(PSb(bank), slS[b][:, ft, ni * 128:(ni + 1) * 128], Ys[:, ft, cs], False, ft == NT - 1,
                                           [('slS', b), ('hy_Y', ft)], [('ps', bank)])
                                    STT(z[:, nt, cs], PSb(bank), INV_SCALE, xm[k0][:], ALU.mult, ALU.mult,
                                        [('ps', bank), ('hy_xm', k0)], [('hy_z', nt)])
                        else:
                            for ct in range(8):
                                bank = nb % 8
                                k0 = nb % 2
                                nb += 1
                                csl = slice(ct * 128, (ct + 1) * 128)
                                DMA('sp', xm[k0][:, 0:SW], bT[1024 + ct * 128:1024 + (ct + 1) * 128, nsl], ['bcT'], [('hy_xm', k0)])
                                for ft in range(NT):
                                    MM(PSb(bank)[:, 0:SW], Yr[:, ft, csl], slC[b][:, ft, :], ft == 0, False,
                                       [('slC', b), ('hy_Y', ft)], [('ps', bank)])
                                for ft in range(NT):
                                    MM(PSb(bank)[:, 0:SW], Ys[:, ft, csl], slS[b][:, ft, :], False, ft == NT - 1,
                                       [('slS', b), ('hy_Y', ft)], [('ps', bank)])
                                STT(yo[k0][:, 0:SW], PSb(bank)[:, 0:SW], INV_SCALE, xm[k0][:, 0:SW], ALU.mult, ALU.mult,
                                    [('ps', bank), ('hy_xm', k0)], [('hy_yo', k0)])
                                DMA('act', ybT[ct * 128:(ct + 1) * 128, nsl], yo[k0][:, 0:SW], [('hy_yo', k0)], ['ybT'])
                    sc_.__exit__(None, None, None)
                S.barrier()


        SDT = BF16
        NINV = 1
        NSET = NINV + 1
        import os as _os
        INTERLEAVE = _os.environ.get('RW_IL', '1') == '1'
        WSC = -math.exp(-0.5)

        def rwkv_phase(l):
            with ExitStack() as ph:
                ph.enter_context(nc.named_scope("rwkv"))
                twp = sb("rw_twp", [128, T], BF16, st=ph)
                adp = sb("rw_adp", [128, T], BF16, st=ph)
                sgd = sb("rw_sgd", [128, T], BF16, st=ph)
                w2p = sb("rw_w2p", [128, 1024], BF16, st=ph)
                a2p = sb("rw_a2p", [128, 1024], BF16, st=ph)
                g2 = sb("rw_g2", [128, 1024], BF16, st=ph)
                omka = sb("rw_omka", [128, 8], st=ph)
                rT = sb("rw_r", [128, T], st=ph)
                kT = sb("rw_k", [128, T], st=ph)
                vT = sb("rw_v", [128, T], st=ph)
                kkn = sb("rw_kkn", [128, T], st=ph)
                rkacc = sb("rw_rkacc", [128, T], st=ph)
                ysum = sb("rw_ysum", [128, T], st=ph)
                vR = sb("rw_vR", [128, T], st=ph)
                lw = sb("rw_lw", [128, T], st=ph)
                kd = sb("rw_kd", [128, T], st=ph)
                bd = sb("rw_bd", [128, T], st=ph)
                Lc = sb("rw_L", [128, T], st=ph)
                ex = sb("rw_ex", [128, T], st=ph)
                tmp2 = sb("rw_tmp2", [128, T], st=ph)
                yT_ = sb("rw_y", [128, T], st=ph)
                tot = sb("rw_tot", [128, 16], st=ph)
                Pc = sb("rw_Pc", [128, 16], st=ph)
                ARh = [sb("rw_AR%d" % i, [128, 16, 2, 128], SDT, st=ph) for i in range(2)]
                BK = sb("rw_BK", [128, 16, 2, 128], SDT, st=ph)
                BhT = sb("rw_BhT", [128, 16, 128], SDT, st=ph)
                KhT = sb("rw_KhT", [128, 16, 128], SDT, st=ph)
                VT = sb("rw_VT", [128, 16, 128], SDT, st=ph)
                NB = [sb("rw_NB%d" % i, [128, 4, 2, 128], SDT, st=ph) for i in range(NSET)]
                AK = [sb("rw_AK%d" % i, [128, 4, 2, 128], SDT, st=ph) for i in range(NSET)]
                Mm2 = [[sb("rw_M%d_%d" % (j, i), [128, 4, 128], SDT, st=ph) for i in range(2)] for j in range(NINV)]
                MT2 = [[sb("rw_MT%d_%d" % (j, i), [128, 4, 128], SDT, st=ph) for i in range(2)] for j in range(NINV)]
                Xx2 = [[sb("rw_X%d_%d" % (j, i), [128, 4, 128], SDT, st=ph) for i in range(2)] for j in range(NINV)]
                Xfin = [sb("rw_Xf%d" % i, [128, 4, 128], SDT, st=ph) for i in range(NSET)]
                tb16 = sb("rw_tb16", [128, T], BF16, st=ph)
                St = sb("rw_St", [128, 64], st=ph)
                Sb_ = sb("rw_Sb", [128, 64], SDT, st=ph)
                Wt = sb("rw_Wt", [128, 128], SDT, st=ph)
                Ut = sb("rw_Ut", [128, 128], SDT, st=ph)
                ob = [sb("rw_ob%d" % i, [128, 512], BF16, st=ph) for i in range(2)]

                DMA('pool', g2[:], W['rw_g2'][l], (), ['rw_c'])
                MSET(ARh[0][64:128].rearrange("p a b c -> p (a b c)"), 0.0, ['rw_AR'])
                MSET(ARh[1][0:64].rearrange("p a b c -> p (a b c)"), 0.0, ['rw_AR'])
                for d in range(2):
                    ps_ = slice(d * 64, d * 64 + 48)
                    DMA('pool', w2p[ps_, :], W['rw_w2'][l, d], (), ['rw_c'])
                    DMA('pool', a2p[ps_, :], W['rw_a2'][l, d], (), ['rw_c'])
                    DMA('sp', tmp2[ps_, :], cT[3200 + d * 48:3248 + d * 48, :], ['bcT'], ['rw_tmp2'])
                    DMA('sp', ex[ps_, :], cT[3296 + d * 48:3344 + d * 48, :], ['bcT'], ['rw_ex'])
                    if d == 0:
                        ACT(twp[ps_, :], tmp2[ps_, :], AF.Tanh, ['rw_tmp2'], ['rw_twp'])
                        CP(adp[ps_, :], ex[ps_, :], ['rw_ex'], ['rw_adp'], eng='act')
                    else:
                        ACT(twp[ps_, :], tmp2[ps_, ::-1], AF.Tanh, ['rw_tmp2'], ['rw_twp'])
                        CP(adp[ps_, :], ex[ps_, ::-1], ['rw_ex'], ['rw_adp'], eng='act')
                DMA('sp', Lc[:], cT[3072:3200, :], ['bcT'], ['rw_L'])
                ACT(sgd[:], Lc[:], AF.Sigmoid, ['rw_L'], ['rw_sgd'])
                oka, _ = PC['ka']
                TS(omka[:], pcol[:, oka:oka + 8], -1.0, 1.0, ALU.mult, ALU.add, ['pcol'], ['rw_omka'])
                if l > 0:
                    v1 = sb("rw_v1", [128, 8, 32], st=ph)
                    v2 = sb("rw_v2", [32, 1024], BF16, st=ph)
                    t1v = sb("rw_t1v", [32, T], BF16, st=ph)
                    DMA('sp', v1[:], W['rw_v1'][l - 1].rearrange("(ct p) r -> p ct r", p=128), (), ['rw_c'])
                    DMA('pool', v2[:], W['rw_v2'][l - 1], (), ['rw_c'])
                    for ct in range(8):
                        DMA('sp', kd[:], cT[2048 + ct * 128:2048 + (ct + 1) * 128, :], ['bcT'], ['rw_kd'])
                        for tg in range(NTG):
                            MM(PSb(tg)[0:32, :], v1[:, ct, :], kd[:, tg * 512:(tg + 1) * 512], ct == 0, ct == 7,
                               ['rw_c', 'rw_kd'], [('ps', tg)])
                    for tg in range(NTG):
                        EVAC(t1v[:, tg * 512:(tg + 1) * 512], PSb(tg)[0:32, :], [('ps', tg)], ['rw_t1v'])

                def f2(ap):
                    return ap.rearrange("p a b -> p (a b)")

                def c3(ap):
                    return ap.rearrange("p (c t) -> p c t", t=128)

                def scan_unit(r_ap, lw_, kd_, v_, kk_ap, bd_, yout, ykey):
                    for c in range(16):
                        csl = slice(c * 128, (c + 1) * 128)
                        S.op('dve', lambda: nc.vector.tensor_tensor_scan(out=Lc[:, csl], data0=onesF[:], data1=lw_[:, csl], initial=0.0,
                                                                         op0=ALU.mult, op1=ALU.add), ['const', 'rw_lw'], ['rw_L'])
                    L3 = c3(Lc[:])
                    CP(tot[:], L3[:, :, 127], ['rw_L'], ['rw_tot'])
                    ACT(Pc[:], tot[:], AF.Exp, ['rw_tot'], ['rw_Pc'])
                    ACT(ex[:], Lc[:], AF.Exp, ['rw_L'], ['rw_ex'])
                    for h in range(2):
                        hp_ = slice(h * 64, (h + 1) * 64)
                        TT(ARh[h][hp_, :, 1, :], c3(r_ap[hp_]), c3(ex[hp_, :]), ALU.mult, ['rw_ex', 'rw_in'], ['rw_AR'])
                    ACT(ex[:], Lc[:], AF.Exp, ['rw_L', 'rw_AR'], ['rw_ex'], scale=-1.0)
                    TT(BK[:, :, 0, :], c3(bd_[:]), c3(ex[:]), ALU.mult, ['rw_ex', 'rw_bd'], ['rw_BK'])
                    TT(BK[:, :, 1, :], c3(kd_[:]), c3(ex[:]), ALU.mult, ['rw_ex', 'rw_kd'], ['rw_BK'], eng='pool')
                    TT(tmp2[:], Lc[:], lw_[:], ALU.subtract, ['rw_L', 'rw_lw', 'rw_tmp2'], ['rw_tmp2'])
                    ACT(ex[:], tmp2[:], AF.Exp, ['rw_tmp2', 'rw_BK'], ['rw_ex'])
                    for h in range(2):
                        hp_ = slice(h * 64, (h + 1) * 64)
                        STT(ARh[h][hp_, :, 0, :], c3(kk_ap[hp_]), -1.0, c3(ex[hp_, :]), ALU.mult, ALU.mult, ['rw_ex', 'rw_kkn'], ['rw_AR'])
                    TT(c3(tmp2[:]), tot[:].unsqueeze(2).to_broadcast([128, 16, 128]), L3, ALU.subtract,
                       ['rw_tot', 'rw_L', 'rw_tmp2'], ['rw_tmp2'])
                    ACT(ex[:], tmp2[:], AF.Exp, ['rw_tmp2', 'rw_AR'], ['rw_ex'])
                    for wi, (src, skey, dstT) in enumerate(((bd_, 'rw_bd', BhT), (kd_, 'rw_kd', KhT), (v_, 'rw_in', VT))):
                        if wi < 2:
                            TT(tb16[:], src[:], ex[:], ALU.mult, ['rw_ex', skey, 'rw_tb16'], ['rw_tb16'], eng='pool' if wi else 'dve')
                        else:
                            CP(tb16[:], src[:], ['rw_in', 'rw_vR', 'rw_tb16'], ['rw_tb16'], eng='act')
                        for q in range(4):
                            bank = q
                            pb16 = PSb(bank).bitcast(BF16)
                            for j in range(4):
                                c = q * 4 + j
                                TR(pb16[:, j * 128:(j + 1) * 128], tb16[:, c * 128:(c + 1) * 128], identB[:],
                                   ['rw_tb16', 'const'], [('ps', bank)])
                            EVAC(f2(dstT[:, q * 4:(q + 1) * 4, :]), pb16[:, 0:512], [('ps', bank)], ['rw_T%d' % wi])
                    MSET(St[:], 0.0, ['rw_St'], eng='dve')
                    MSET(Sb_[:], 0.0, ['rw_Sb'], eng='dve')

                    def inv_chain(qd):
                        q3 = qd % NSET
                        st_ = qd % NINV
                        b0, b1, b2 = (2, 3, 4) if st_ == 0 else (5, 6, 7)
                        MTs, Mms, Xxs = MT2[st_], Mm2[st_], Xx2[st_]
                        kM, kMT, kX = 'rw_M%d' % st_, 'rw_MT%d' % st_, 'rw_X%d' % st_
                        gb_ = (b0, b1)
                        for cc in range(2):
                            c = qd * 2 + cc
                            for h in range(2):
                                u = cc * 2 + h
                                MM(PSb(gb_[u // 2])[:, (u % 2) * 256:(u % 2 + 1) * 256], BK[:, c, 0, :], f2(ARh[h][:, c, :, :]), True, True,
                                   ['rw_BK', 'rw_AR'], [('ps', gb_[u // 2])])
                                MM(PSb(b2)[:, u * 128:(u + 1) * 128], ARh[h][:, c, 0, :], BK[:, c, 0, :], True, True,
                                   ['rw_BK', 'rw_AR'], [('ps', b2)])
                        for hb in range(2):
                            TT(NB[q3][:, hb * 2:(hb + 1) * 2, :, :].rearrange("p a b c -> p (a b c)"), PSb(gb_[hb]), maskU4[:], ALU.mult,
                               [('ps', gb_[hb]), 'const'], [('rw_NB', q3)])
                        TT(f2(MTs[0][:]), PSb(b2), maskL4[:], ALU.mult, [('ps', b2), 'const'], [(kMT, 0)])
                        TT(Xxs[0][:], NB[q3][:, :, 0, :], ident4[:].rearrange("p (a b) -> p a b", a=4), ALU.add,
                           [('rw_NB', q3), 'const'], [(kX, 0)], eng='pool')
                        yield
                        cur = 0
                        for lev in range(1, 8):
                            nxt = 1 - cur
                            pA, pB, pC = b2, b1, b0
                            if lev == 1:
                                for cc in range(2):
                                    c = qd * 2 + cc
                                    for h in range(2):
                                        u = cc * 2 + h
                                        MM(PSb(gb_[u // 2])[:, (u % 2) * 256:(u % 2 + 1) * 256], BK[:, c, 1, :], f2(ARh[h][:, c, :, :]), True, True,
                                           ['rw_BK', 'rw_AR'], [('ps', gb_[u // 2])])
                            if lev >= 2:
                                xs, xk = Xxs[(lev - 2) % 2], (kX, (lev - 2) % 2)
                                for u in range(4):
                                    MM(PSb(pC)[:, u * 128:(u + 1) * 128], MTs[cur][:, u, :], xs[:, u, :], True, True,
                                       [(kMT, cur), xk], [('ps', pC)])
                            if lev == 1:
                                for hb in range(2):
                                    TT(AK[q3][:, hb * 2:(hb + 1) * 2, :, :].rearrange("p a b c -> p (a b c)"), PSb(gb_[hb]), maskU4[:], ALU.mult,
                                       [('ps', gb_[hb]), 'const'], [('rw_AK', q3)])
                            if lev <= 6:
                                for u in range(4):
                                    m_prev = NB[q3][:, u, 0, :] if lev == 1 else Mms[cur][:, u, :]
                                    mk = ('rw_NB', q3) if lev == 1 else (kM, cur)
                                    MM(PSb(pA)[:, u * 128:(u + 1) * 128], m_prev, MTs[cur][:, u, :], True, True, [mk, (kMT, cur)], [('ps', pA)])
                                    if lev < 6:
                                        MM(PSb(pB)[:, u * 128:(u + 1) * 128], MTs[cur][:, u, :], m_prev, True, True, [mk, (kMT, cur)], [('ps', pB)])
                            if lev <= 6:
                                CP(f2(MTs[nxt][:]), PSb(pA), [('ps', pA)], [(kMT, nxt)], eng='act')
                            if lev >= 2:
                                if lev == 7:
                                    xd, xdk = Xfin[q3], ('rw_Xfin', q3)
                                else:
                                    xd, xdk = Xxs[(lev - 1) % 2], (kX, (lev - 1) % 2)
                                TT(f2(xd[:]), PSb(pC), f2(xs[:]), ALU.add, [('ps', pC), xk], [xdk])
                            if lev < 6:
                                CP(f2(Mms[nxt][:]), PSb(pB), [('ps', pB)], [(kM, nxt)], eng='act')
                            cur = nxt
                            yield

                    def state_chain(qd):
                        q3 = qd % NSET
                        Xf = Xfin[q3]
                        xkey = ('rw_Xfin', q3)
                        for cc in range(2):
                            c = qd * 2 + cc
                            for h in range(2):
                                u = cc * 2 + h
                                hs_ = slice(h * 64, (h + 1) * 64)
                                MM(PSb(0)[:, hs_], ARh[h][:, c, 0, :], Sb_[:, :], True, False, ['rw_AR', 'rw_Sb'], [('ps', 0)])
                                MM(PSb(0)[:, hs_], AK[q3][:, u, 0, :], VT[:, c, hs_], False, True, [('rw_AK', q3), 'rw_T2'], [('ps', 0)])
                            CP(Wt[:], PSb(0)[:, 0:128], [('ps', 0)], ['rw_Wt'], eng='act')
                            yield
                            for h in range(2):
                                u = cc * 2 + h
                                hs_ = slice(h * 64, (h + 1) * 64)
                                MM(PSb(0)[:, 128 + h * 64:128 + (h + 1) * 64], Xf[:, u, :], Wt[:, hs_], True, True, [xkey, 'rw_Wt'], [('ps', 0)])
                            CP(Ut[:], PSb(0)[:, 128:256], [('ps', 0)], ['rw_Ut'], eng='dve')
                            yield
                            for h in range(2):
                                u = cc * 2 + h
                                hs_ = slice(h * 64, (h + 1) * 64)
                                so_ = PSb(0)[hs_, 256:320]
                                MM(so_, BhT[:, c, hs_], Ut[:, hs_], True, False, ['rw_T0', 'rw_Ut'], [('ps', 0)])
                                MM(so_, KhT[:, c, hs_], VT[:, c, hs_], False, True, ['rw_T1', 'rw_T2'], [('ps', 0)])
                                yo_ = PSb(1)[hs_, cc * 128:(cc + 1) * 128]
                                MM(yo_, Sb_[:, :], ARh[h][:, c, 1, :], True, False, ['rw_Sb', 'rw_AR'], [('ps', 1)])
                                MM(yo_, Ut[:, hs_], NB[q3][:, u, 1, :], False, False, ['rw_Ut', ('rw_NB', q3)], [('ps', 1)])
                                MM(yo_, VT[:, c, hs_], AK[q3][:, u, 1, :], False, True, ['rw_T2', ('rw_AK', q3)], [('ps', 1)])
                            STT(St[:], St[:], Pc[:, c:c + 1], PSb(0)[:, 256:320], ALU.mult, ALU.add, ['rw_St', 'rw_Pc', ('ps', 0)], ['rw_St'])
                            CP(Sb_[:], St[:], ['rw_St'], ['rw_Sb'], eng='act')
                            if cc == 1:
                                CP(yout[:, qd * 256:(qd + 1) * 256], PSb(1)[:, 0:256], [('ps', 1)], [ykey], eng='act')
                            yield

                    next_inv = 0
                    inv_done = [False] * 8
                    active = []
                    state_q = 0
                    state_gen = None
                    states_done = 0
                    while states_done < 8:
                        while len(active) < NINV and next_inv < 8 and next_inv < states_done + NSET:
                            active.append((next_inv, inv_chain(next_inv)))
                            next_inv += 1
                        if state_gen is None and state_q < 8 and inv_done[state_q]:
                            state_gen = state_chain(state_q)
                        if state_gen is not None:
                            try:
                                next(state_gen)
                            except StopIteration:
                                state_gen = None
                                states_done += 1
                                state_q += 1
                        for item in list(active):
                            try:
                                next(item[1])
                            except StopIteration:
                                inv_done[item[0]] = True
                                active.remove(item)

                for hp in range(8):
                    hsl = slice(hp * 128, (hp + 1) * 128)
                    DMA('sp', rT[:], cT[hp * 128:(hp + 1) * 128, :], ['bcT'], ['rw_in'])
                    DMA('sp', kT[:], cT[1024 + hp * 128:1024 + (hp + 1) * 128, :], ['bcT'], ['rw_in'])
                    DMA('sp', vT[:], cT[2048 + hp * 128:2048 + (hp + 1) * 128, :], ['bcT'], ['rw_in'])
                    if l > 0:
                        DMA('sp', tmp2[:], vfT[hsl, :], ['vfT'], ['rw_tmp2'])
                        for tg in range(NTG):
                            tsl = slice(tg * 512, (tg + 1) * 512)
                            MM(PSb(tg), v2[:, hsl], t1v[:, tsl], True, True, ['rw_c', 'rw_t1v'], [('ps', tg)])
                            ACT(ex[:, tsl], PSb(tg), AF.Sigmoid, [('ps', tg), 'pcol'], ['rw_ex'], bias=pc('v0', hp))
                        TT(tmp2[:], tmp2[:], vT[:], ALU.subtract, ['rw_tmp2', 'rw_in'], ['rw_tmp2'])
                        TT(tmp2[:], tmp2[:], ex[:], ALU.mult, ['rw_tmp2', 'rw_ex'], ['rw_tmp2'])
                        TT(vT[:], vT[:], tmp2[:], ALU.add, ['rw_tmp2', 'rw_in'], ['rw_in'])
                    else:
                        DMA('act', vfT[hsl, :], vT[:], ['rw_in'], ['vfT'])
                    TS(kkn[:], kT[:], pc('kk', hp), None, ALU.mult, None, ['rw_in', 'pcol'], ['rw_kkn'])
                    ACT(tmp2[:], kkn[:], AF.Square, ['rw_kkn', 'rw_tmp2'], ['rw_tmp2'])
                    for tg in range(NTG):
                        tsl = slice(tg * 512, (tg + 1) * 512)
                        MM(PSb(tg), blockones[:], tmp2[:, tsl], True, True, ['const', 'rw_tmp2'], [('ps', tg)])
                        ACT(ex[:, tsl], PSb(tg), AF.Ln, [('ps', tg)], ['rw_ex'], bias=1e-30)
                    ACT(ex[:], ex[:], AF.Exp, ['rw_ex'], ['rw_ex'], scale=-0.5)
                    TS(ex[:], ex[:], 1e12, None, ALU.min, None, ['rw_ex'], ['rw_ex'])
                    TT(kkn[:], kkn[:], ex[:], ALU.mult, ['rw_kkn', 'rw_ex'], ['rw_kkn'])
                    CP(vR[:], vT[:, ::-1], ['rw_in'], ['rw_vR'], eng='act')
                    for d in range(2):
                        ps_ = slice(d * 64, d * 64 + 48)
                        if d == 0:
                            r_ap, k_ap, kk_ap, v_ = rT[:], kT[:], kkn[:], vT
                        else:
                            r_ap, k_ap, kk_ap, v_ = rT[:, ::-1], kT[:, ::-1], kkn[:, ::-1], vR
                        for tg in range(NTG):
                            tsl = slice(tg * 512, (tg + 1) * 512)
                            MM(PSb(tg), w2p[ps_, hsl], twp[ps_, tsl], True, True, ['rw_c', 'rw_twp'], [('ps', tg)])
                            ACT(lw[:, tsl], PSb(tg), AF.Sigmoid, [('ps', tg), 'pcol'], ['rw_lw'], bias=pc('w0', d * 8 + hp))
                            MM(PSb(4 + tg), a2p[ps_, hsl], adp[ps_, tsl], True, True, ['rw_c', 'rw_adp'], [('ps', 4 + tg)])
                            ACT(yT_[:, tsl], PSb(4 + tg), AF.Sigmoid, [('ps', 4 + tg), 'pcol'], ['rw_y'], bias=pc('a0', d * 8 + hp))
                        TS(lw[:], lw[:], WSC, None, ALU.mult, None, ['rw_lw'], ['rw_lw'])
                        TS(tmp2[:], yT_[:], pc('ka', hp), omka[:, hp:hp + 1], ALU.mult, ALU.add, ['rw_y', 'pcol', 'rw_omka', 'rw_tmp2'], ['rw_tmp2'])
                        TT(kd[:], tmp2[:], k_ap, ALU.mult, ['rw_tmp2', 'rw_in'], ['rw_kd'])
                        TT(bd[:], yT_[:], kk_ap, ALU.mult, ['rw_kkn', 'rw_y'], ['rw_bd'])
                        if d == 0:
                            STT(rkacc[:], r_ap, pc('rk', hp), kd[:], ALU.mult, ALU.mult, ['rw_in', 'pcol', 'rw_kd'], ['rw_rkacc'])
                        else:
                            STT(tmp2[:], r_ap, pc('rk', hp), kd[:], ALU.mult, ALU.mult, ['rw_in', 'pcol', 'rw_kd', 'rw_tmp2'], ['rw_tmp2'])
                            TT(rkacc[:], rkacc[:], tmp2[:, ::-1], ALU.add, ['rw_rkacc', 'rw_tmp2'], ['rw_rkacc'])
                        if d == 0:
                            scan_unit(r_ap, lw, kd, v_, kk_ap, bd, ysum, 'rw_ysum')
                        else:
                            scan_unit(r_ap, lw, kd, v_, kk_ap, bd, yT_, 'rw_y')
                            TT(ysum[:], ysum[:], yT_[:, ::-1], ALU.add, ['rw_y', 'rw_ysum'], ['rw_ysum'])
                    for tg in range(NTG):
                        tsl = slice(tg * 512, (tg + 1) * 512)
                        k0 = tg % 2
                        MM(PSb(k0), blockones[:], ysum[:, tsl], True, True, ['const', 'rw_ysum'], [('ps', k0)])
                        STT(tmp2[:, tsl], PSb(k0), -1.0 / 64, ysum[:, tsl], ALU.mult, ALU.add, [('ps', k0), 'rw_ysum', 'rw_tmp2'], ['rw_tmp2'])
                        ACT(ex[:, tsl], tmp2[:, tsl], AF.Square, ['rw_tmp2', 'rw_ex'], ['rw_ex'])
                        MM(PSb(2 + k0), blockones[:], ex[:, tsl], True, True, ['const', 'rw_ex'], [('ps', 2 + k0)])
                        ACT(ex[:, tsl], PSb(2 + k0), AF.Ln, [('ps', 2 + k0)], ['rw_ex'], scale=1.0 / 64, bias=GN_EPS)
                        ACT(ex[:, tsl], ex[:, tsl], AF.Exp, ['rw_ex'], ['rw_ex'], scale=-0.5)
                        TT(tmp2[:, tsl], tmp2[:, tsl], ex[:, tsl], ALU.mult, ['rw_tmp2', 'rw_ex'], ['rw_tmp2'])
                        TS(tmp2[:, tsl], tmp2[:, tsl], pc('lng', hp), pc('lnb', hp), ALU.mult, ALU.add, ['rw_tmp2', 'pcol'], ['rw_tmp2'])
                        MM(PSb(4 + k0), blockones[:], rkacc[:, tsl], True, True, ['const', 'rw_rkacc'], [('ps', 4 + k0)])
                        TT(ex[:, tsl], PSb(4 + k0), vT[:, tsl], ALU.mult, [('ps', 4 + k0), 'rw_in', 'rw_ex'], ['rw_ex'])
                        TT(tmp2[:, tsl], tmp2[:, tsl], ex[:, tsl], ALU.add, ['rw_tmp2', 'rw_ex'], ['rw_tmp2'])
                        MM(PSb(6 + k0), g2[:, hsl], sgd[:, tsl], True, True, ['rw_c', 'rw_sgd'], [('ps', 6 + k0)])
                        TT(ob[k0][:], tmp2[:, tsl], PSb(6 + k0), ALU.mult, ['rw_tmp2', ('ps', 6 + k0)], [('rw_ob', k0)])
                        DMA('act', ycT[hsl, tsl], ob[k0][:], [('rw_ob', k0)], ['ycT'])
                S.barrier()


        def merge_phase(l):
            with ExitStack() as ph:
                ph.enter_context(nc.named_scope("merge"))
                PW = 256
                yT3 = [sb("mg_y%d" % i, [128, 8, T], BF16, st=ph) for i in range(3)]
                wpb = [[sb("mg_w%d_%d" % (i, j), [128, 8, PW], BF16, st=ph) for j in range(2)] for i in range(3)]
                gt = [[sb("mg_g%d_%d" % (i, j), [128, T], BF16, st=ph) for j in range(2)] for i in range(3)]
                ta = [sb("mg_ta%d" % i, [128, 512], st=ph) for i in range(2)]
                tb = [sb("mg_tb%d" % i, [128, 512], st=ph) for i in range(2)]
                mo = [sb("mg_mo%d" % i, [128, 512], BF16, st=ph) for i in range(2)]
                for i, src in enumerate((yaT, ybT, ycT)):
                    DMA('sp', yT3[i][:], src.rearrange("(kt p) t -> p kt t", p=128), ['yaT', 'ybT', 'ycT'], [('mg_y', i)])
                Wb = [W[n][l].rearrange("(kt p) c -> p kt c", p=128) for n in ('w_branch_a', 'w_branch_b', 'w_branch_c')]
                n = 0
                for pi in range(D // PW):
                    b = pi % 2
                    for i in range(3):
                        DMA('pool', wpb[i][b][:], Wb[i][:, :, pi * PW:(pi + 1) * PW], (), [('mg_w', i, b)])
                    for ci in range(PW // 128):
                        ct = pi * (PW // 128) + ci
                        gb_ = ct % 2
                        for i in range(3):
                            DMA('sp', gt[i][gb_][:], gT[i * D + ct * 128:i * D + (ct + 1) * 128, :], ['gT'], [('mg_g', i, gb_)])
                        for tg in range(NTG):
                            tsl = slice(tg * 512, (tg + 1) * 512)
                            k0 = n % 2
                            base = (n % 2) * 3
                            n += 1
                            for i in range(3):
                                for kt in range(8):
                                    MM(PSb(base + i), wpb[i][b][:, kt, ci * 128:(ci + 1) * 128], yT3[i][:, kt, tsl], kt == 0, kt == 7,
                                       [('mg_w', i, b), ('mg_y', i)], [('ps', base + i)])
                            TT(ta[k0][:], PSb(base + 0), gt[0][gb_][:, tsl], ALU.mult, [('ps', base + 0), ('mg_g', 0, gb_)], [('mg_ta', k0)])
                            TT(tb[k0][:], PSb(base + 1), gt[1][gb_][:, tsl], ALU.mult, [('ps', base + 1), ('mg_g', 1, gb_)], [('mg_tb', k0)])
                            TT(ta[k0][:], ta[k0][:], tb[k0][:], ALU.add, [('mg_ta', k0), ('mg_tb', k0)], [('mg_ta', k0)], eng='pool')
                            TT(tb[k0][:], PSb(base + 2), gt[2][gb_][:, tsl], ALU.mult, [('ps', base + 2), ('mg_g', 2, gb_), ('mg_ta', k0)], [('mg_tb', k0)])
                            TT(mo[k0][:], ta[k0][:], tb[k0][:], ALU.add, [('mg_ta', k0), ('mg_tb', k0)], [('mg_mo', k0)], eng='pool')
                            DMA('act', mgT[ct * 128:(ct + 1) * 128, tsl], mo[k0][:], [('mg_mo', k0)], ['mgT'])
                S.barrier()

        def resid_epi(xl, xo):
            cnt = [0]

            def epi(ci, c0, m, tg, ps, pk):
                k = cnt[0] % 3
                cnt[0] += 1
                tsl = slice(tg * 512, (tg + 1) * 512)
                DMA('sp', xl[k][:], xT[c0:c0 + 128, tsl], [('xT', ci, tg)], [('rs_xl', k)])
                TT(xo[k][:], ps, xl[k][:], ALU.add, pk + [('rs_xl', k)], [('rs_xo', k)])
                DMA('act', xT[c0:c0 + 128, tsl], xo[k][:], [('rs_xo', k)], [('xT', ci, tg)])
            return epi

        def outproj_phase(l):
            with ExitStack() as ph:
                ph.enter_context(nc.named_scope("outp"))
                mT = sb("op_mT", [128, KD, T], BF16, st=ph)
                wp = [sb("op_wp%d" % i, [128, KD, 512], BF16, st=ph) for i in range(2)]
                xl = [sb("op_xl%d" % i, [128, 512], st=ph) for i in range(3)]
                xo = [sb("op_xo%d" % i, [128, 512], st=ph) for i in range(3)]
                DMA('sp', mT[:], mgT.rearrange("(kt p) t -> p kt t", p=128), ['mgT'], ['op_mT'])
                linear_fm(W['w_out'][l].rearrange("(kt p) c -> p kt c", p=128), KD, [(i * 128, 128) for i in range(16)],
                          lambda kt, tg: mT[:, kt, tg * 512:(tg + 1) * 512], lambda kt, tg: 'op_mT', resid_epi(xl, xo), wp, 'op_wp')
                S.barrier()

        def ffn_phase(l):
            with ExitStack() as hs:
                hT = sb("hT2", [128, KD, T], BF16, st=hs)
                norm_phase(hT, 'nfg')
                with ExitStack() as ph:
                    ph.enter_context(nc.named_scope("ffn_up"))
                    PW = 256
                    wg = [sb("ff_wg%d" % i, [128, KD, PW], BF16, st=ph) for i in range(2)]
                    wu = [sb("ff_wu%d" % i, [128, KD, PW], BF16, st=ph) for i in range(2)]
                    sg = [sb("ff_sg%d" % i, [128, 512], st=ph) for i in range(2)]
                    ao = [sb("ff_ao%d" % i, [128, 512], BF16, st=ph) for i in range(2)]
                    Wg = W['w_ffn_gate'][l].rearrange("(kt p) c -> p kt c", p=128)
                    Wu = W['w_ffn_up'][l].rearrange("(kt p) c -> p kt c", p=128)
                    n = 0
                    for pi in range(D_FF // PW):
                        b = pi % 2
                        DMA('pool', wg[b][:], Wg[:, :, pi * PW:(pi + 1) * PW], (), [('ff_wg', b)])
                        DMA('pool', wu[b][:], Wu[:, :, pi * PW:(pi + 1) * PW], (), [('ff_wu', b)])
                        for ci in range(PW // 128):
                            ft = pi * (PW // 128) + ci
                            for tg in range(NTG):
                                tsl = slice(tg * 512, (tg + 1) * 512)
                                k0 = n % 2
                                bG = (n % 4) * 2
                                bU = bG + 1
                                n += 1
                                for kt in range(KD):
                                    MM(PSb(bG), wg[b][:, kt, ci * 128:(ci + 1) * 128], hT[:, kt, tsl], kt == 0, kt == KD - 1,
                                       [('ff_wg', b), ('hT', kt, tg)], [('ps', bG)])
                                for kt in range(KD):
                                    MM(PSb(bU), wu[b][:, kt, ci * 128:(ci + 1) * 128], hT[:, kt, tsl], kt == 0, kt == KD - 1,
                                       [('ff_wu', b), ('hT', kt, tg)], [('ps', bU)])
                                ACT(sg[k0][:], PSb(bG), AF.Silu, [('ps', bG)], [('ff_sg', k0)])
                                TT(ao[k0][:], sg[k0][:], PSb(bU), ALU.mult, [('ff_sg', k0), ('ps', bU)], [('ff_ao', k0)])
                                DMA('sp', actT[ft * 128:(ft + 1) * 128, tsl], ao[k0][:], [('ff_ao', k0)], ['actT'])
                    S.barrier()
            with ExitStack() as ph:
                ph.enter_context(nc.named_scope("ffn_down"))
                KF = D_FF // 128
                PW = 256
                TH = 1024
                asb = sb("ff_act", [128, KF, TH], BF16, st=ph)
                wd = [sb("ff_wd%d" % i, [128, KF, PW], BF16, st=ph) for i in range(2)]
                xl = [sb("ff_xl%d" % i, [128, 512], st=ph) for i in range(3)]
                xo = [sb("ff_xo%d" % i, [128, 512], st=ph) for i in range(3)]
                Wd = W['w_ffn_down'][l].rearrange("(kt p) c -> p kt c", p=128)
                aTv = actT.rearrange("(kt p) t -> p kt t", p=128)
                n = 0
                pn = 0
                for th in range(T // TH):
                    for kq in range(4):
                        ks = slice(kq * 11, (kq + 1) * 11)
                        DMA('sp', asb[:, ks, :], aTv[:, ks, th * TH:(th + 1) * TH], ['actT'], [('ff_act', kq)])
                    epi = resid_epi(xl, xo)
                    for pi in range(D // PW):
                        b = pn % 2
                        pn += 1
                        DMA('pool', wd[b][:], Wd[:, :, pi * PW:(pi + 1) * PW], (), [('ff_wd', b)])
                        for ci in range(PW // 128):
                            ct = pi * (PW // 128) + ci
                            for tgi in range(TH // 512):
                                tg = th * (TH // 512) + tgi
                                bank = n % 8
                                n += 1
                                for kt in range(KF):
                                    MM(PSb(bank), wd[b][:, kt, ci * 128:(ci + 1) * 128], asb[:, kt, tgi * 512:(tgi + 1) * 512], kt == 0, kt == KF - 1,
                                       [('ff_wd', b), ('ff_act', kt // 11)], [('ps', bank)])
                                epi(ct, ct * 128, 128, tg, PSb(bank), [('ps', bank)])
                S.barrier()

        def final_phase():
            with ExitStack() as ph:
                xin = [sb("fx%d" % i, [128, KD, 512], st=ph) for i in range(2)]
                sq = [sb("fsq%d" % i, [128, 512], st=ph) for i in range(2)]
                rs = [sb("frs%d" % i, [128, 512], st=ph) for i in range(2)]
                ot = [sb("fot%d" % i, [128, D], st=ph) for i in range(2)]
                n = 0
                no = 0
                for tg in range(NTG):
                    b = tg % 2
                    tsl = slice(tg * 512, (tg + 1) * 512)
                    DMA('sp', xin[b][:], xTv[:, :, tsl], ['xT'], [('fx', b)])
                    for dk in range(KD):
                        ACT(sq[dk % 2][:], xin[b][:, dk, :], AF.Square, [('fx', b)], [('fsq', dk % 2)])
                        MM(PSb(b), onesF[:], sq[dk % 2][:], dk == 0, dk == KD - 1, [('fsq', dk % 2), 'const'], [('ps', b)])
                    ACT(rs[b][:], PSb(b), AF.Sqrt, [('ps', b)], [('frs', b)], scale=1.0 / D, bias=RMS_EPS)
                    RECIP(rs[b][:], rs[b][:], [('frs', b)], [('frs', b)])
                    for dk in range(KD):
                        STT(xin[b][:, dk, :], xin[b][:, dk, :], pc('nfin', dk), rs[b][:], ALU.mult, ALU.mult,
                            [('fx', b), 'pcol', ('frs', b)], [('fx', b)])
                    for tt in range(4):
                        o_ = no % 2
                        no += 1
                        for q in range(4):
                            bank = 2 + n % 6
                            n += 1
                            for j in range(4):
                                dk = q * 4 + j
                                TR(PSb(bank)[:, j * 128:(j + 1) * 128], xin[b][:, dk, tt * 128:(tt + 1) * 128], identF[:],
                                   [('fx', b), 'const'], [('ps', bank)])
                            EVAC(ot[o_][:, q * 512:(q + 1) * 512], PSb(bank), [('ps', bank)], [('fot', o_)])
                        row = (tg * 4 + tt) * 128
                        DMA('act', out_d[row:row + 128, :], ot[o_][:], [('fot', o_)], [('out', row)])
                S.barrier()

        for l in range(NL):
            DMA('sp', pcol[:], pcol_d[l], (), ['pcol'])
            Winv = W['w_in'][l].rearrange("(kt p) c -> p kt c", p=128)
            with ExitStack() as hs:
                hT = sb("hT", [128, KD, T], BF16, st=hs)
                norm_phase(hT, 'nmg')
                if l == 0 and 'hTd' in DBG:
                    DMA('sp', DBG['hTd'].rearrange("(dk p) t -> p dk t", p=128), hT[:], [('hT', dk, tg) for dk in range(KD) for tg in range(NTG)], [('dbg', 'hTd')])

                def h_rhs(kt, tg):
                    return hT[:, kt, tg * 512:(tg + 1) * 512]

                def h_key(kt, tg):
                    return ('hT', kt, tg)

                with ExitStack() as ph:
                    ph.enter_context(nc.named_scope("proj"))
                    wp = [sb("wp%d" % i, [128, KD, 512], BF16, st=ph) for i in range(2)]
                    ob = [sb("pob%d" % i, [128, 512], F32, st=ph) for i in range(3)]
                    obh = [sb("pobh%d" % i, [128, 512], BF16, st=ph) for i in range(3)]
                    zc = [sb("pzc%d" % i, [128, T], F32, st=ph) for i in range(2)]
                    ccol = sb("ccol", [128, 29], F32, st=ph)
                    cnt = [0]
                    o_p, _ = PC['mup']
                    o_n, _ = PC['mun']
                    TT(ccol[:], pcol[:, o_p:o_p + 29], pcol[:, o_n:o_n + 29], ALU.add, ['pcol'], ['ccol'])
                    TS(ccol[:], ccol[:], -1.0, 1.0, ALU.mult, ALU.add, ['ccol'], ['ccol'])

                    def epi_u(ci, c0, m, tg, ps, pk):
                        k = cnt[0] % 3
                        cnt[0] += 1
                        ACT(ob[k][:], ps, AF.Gelu, pk, [('pob', k)])
                        DMA('sp', uT[c0:c0 + 128, tg * 512:(tg + 1) * 512], ob[k][:], [('pob', k)], ['uT'])

                    def epi_g(ci, c0, m, tg, ps, pk):
                        k = cnt[0] % 3
                        cnt[0] += 1
                        ACT(obh[k][:], ps, AF.Sigmoid, pk, [('pobh', k)])
                        r0 = c0 - OFF_G
                        DMA('sp', gT[r0:r0 + 128, tg * 512:(tg + 1) * 512], obh[k][:], [('pobh', k)], ['gT'])

                    def tap3(dst, r0, m, ps, pk, a_ap, b_ap, p_ap, n_ap, keys):
                        k = cnt[0] % 2
                        cnt[0] += 1
                        z = zc[k]
                        ACT(z[0:m, :], ps, AF.Identity, pk + keys, [('pzc', k)], scale=a_ap, bias=b_ap)
                        STT(z[0:m, 1:T], ps[:, 0:T - 1], p_ap, z[0:m, 1:T], ALU.mult, ALU.add, pk + keys + [('pzc', k)], [('pzc', k)])
                        STT(z[0:m, 0:T - 1], ps[:, 1:T], n_ap, z[0:m, 0:T - 1], ALU.mult, ALU.add, pk + keys + [('pzc', k)], [('pzc', k)])
                        DMA('sp', dst[r0:r0 + m, :], z[0:m, :], [('pzc', k)], ['bcT'])

                    def epi_b(ci, c0, m, tg, ps, pk):
                        tap3(bT, c0 - OFF_B, m, ps, pk, pc('cw1', ci), pc('cb', ci), pc('cw0', ci), pc('cw2', ci), ['pcol'])

                    def epi_c(ci, c0, m, tg, ps, pk):
                        tap3(cT, c0 - OFF_C, m, ps, pk, ccol[0:m, ci:ci + 1], 0.0, pc('mup', ci, m), pc('mun', ci, m), ['pcol', 'ccol'])

                    linear_fm(Winv, KD, [(i * 128, 128) for i in range(8)], h_rhs, h_key, epi_u, wp, 'wp')
                    linear_fm(Winv, KD, [(OFF_B + i * 128, 128) for i in range(24)], h_rhs, h_key, epi_b, wp, 'wp', full=True)
                    linear_fm(Winv, KD, [(OFF_C + c0, m) for (c0, m) in C_TILES], h_rhs, h_key, epi_c, wp, 'wp', full=True)
                    linear_fm(Winv, KD, [(OFF_G + i * 128, 128) for i in range(48)], h_rhs, h_key, epi_g, wp, 'wp')
                S.barrier()
                if l == 0:
                    dump('uT', uT)
                    dump('bT', bT)
                    dump('cT', cT)
                    dump('gT', gT)

                with ExitStack() as ph:
                    ph.enter_context(nc.named_scope("mixa"))
                    wv = sb("ma_wv", [128, KD, 1024], BF16, st=ph)
                    lng = sb("ma_lng", [128, 1024], F32, st=ph)
                    lnb = sb("ma_lnb", [128, 1024], F32, st=ph)
                    bsb = sb("ma_bsb", [128, 8, 128], F32, st=ph)
                    wsn = sb("ma_wsn", [128, 8, 128], F32, st=ph)
                    wsT = sb("ma_wsT", [128, 8, 128], BF16, st=ph)
                    vg = [sb("ma_vg%d" % i, [128, 1024], F32, st=ph) for i in range(2)]
                    vc = [sb("ma_vc%d" % i, [128, 1024], F32, st=ph) for i in range(2)]
                    vln = [sb("ma_vln%d" % i, [128, 1024], BF16, st=ph) for i in range(2)]
                    ut = [sb("ma_ut%d" % i, [128, 8, 128], F32, st=ph) for i in range(2)]
                    ya = [sb("ma_ya%d" % i, [128, 8, 128], BF16, st=ph) for i in range(2)]
                    tm = [sb("ma_tm%d" % i, [128, 512], F32, st=ph) for i in range(2)]
                    stt = [sb("ma_st%d" % i, [128, 8], F32, st=ph) for i in range(2)]
                    DMA('pool', wv[:], Winv[:, :, A_W:2 * A_W], (), ['ma_wv'])
                    DMA('sp', lng[:], W['gm_ln_g'][l:l + 1, :].broadcast_to([128, 1024]), (), ['ma_c'])
                    DMA('sp', lnb[:], W['gm_ln_b'][l:l + 1, :].broadcast_to([128, 1024]), (), ['ma_c'])
                    DMA('sp', bsb[:].rearrange("p g q -> p (g q)"),
                        W['gm_bs'][l:l + 1].rearrange("o g q -> o (g q)").broadcast_to([128, 1024]), (), ['ma_c'])
                    DMA('sp', wsn[:], W['gm_ws'][l].rearrange("g p q -> p g q"), (), ['ma_wsn'])
                    for hb in range(2):
                        for j in range(4):
                            g = hb * 4 + j
                            TR(PSb(hb)[:, j * 128:(j + 1) * 128], wsn[:, g, :], identF[:], ['ma_wsn', 'const'], [('ps', hb)])
                        EVAC(wsT[:, hb * 4:(hb + 1) * 4, :].rearrange("p g q -> p (g q)"), PSb(hb), [('ps', hb)], ['ma_wsT'])
                    uTv = uT.rearrange("(g d) t -> d g t", d=128)
                    yaTv = yaT.rearrange("(g d) t -> d g t", d=128)
                    for i in range(NT):
                        b = i % 2
                        tsl = slice(i * 128, (i + 1) * 128)
                        tgi = i // 4
                        DMA('sp', ut[b][:], uTv[:, :, tsl], ['uT'], [('ma_ut', b)])
                        for half in range(2):
                            bank = 2 + b * 2 + half
                            for kt in range(KD):
                                MM(PSb(bank), hT[:, kt, tsl], wv[:, kt, half * 512:(half + 1) * 512], kt == 0, kt == KD - 1,
                                   [('hT', kt, tgi), 'ma_wv'], [('ps', bank)])
                            ACT(vg[b][:, half * 512:(half + 1) * 512], PSb(bank), AF.Gelu, [('ps', bank)], [('ma_vg', b, half), ('ma_st', b)],
                                accum_out=stt[b][:, half:half + 1])
                        TT(stt[b][:, 2:3], stt[b][:, 0:1], stt[b][:, 1:2], ALU.add, [('ma_st', b)], [('ma_st', b)])
                        TS(stt[b][:, 3:4], stt[b][:, 2:3], -1.0 / A_W, None, ALU.mult, None, [('ma_st', b)], [('ma_st', b)])
                        TS(vc[b][:], vg[b][:], stt[b][:, 3:4], None, ALU.add, None,
                           [('ma_vg', b, 0), ('ma_vg', b, 1), ('ma_st', b)], [('ma_vc', b)])
                        ACT(vg[b][:], vc[b][:], AF.Square, [('ma_vc', b)], [('ma_vg', b, 0), ('ma_vg', b, 1), ('ma_st', b)],
                            accum_out=stt[b][:, 4:5])
                        ACT(stt[b][:, 5:6], stt[b][:, 4:5], AF.Sqrt, [('ma_st', b)], [('ma_st', b)], scale=1.0 / A_W, bias=LN_EPS)
                        RECIP(stt[b][:, 6:7], stt[b][:, 5:6], [('ma_st', b)], [('ma_st', b)])
                        STT(vc[b][:], vc[b][:], stt[b][:, 6:7], lng[:], ALU.mult, ALU.mult, [('ma_vc', b), ('ma_st', b), 'ma_c'], [('ma_vc', b)])
                        TT(vln[b][:], vc[b][:], lnb[:], ALU.add, [('ma_vc', b), 'ma_c'], [('ma_vln', b)])
                        for hb in range(2):
                            bank = 6 + hb
                            for j in range(4):
                                g = hb * 4 + j
                                MM(PSb(bank)[:, j * 128:(j + 1) * 128], vln[b][:, g * 128:(g + 1) * 128], wsT[:, g, :], True, True,
                                   [('ma_vln', b), 'ma_wsT'], [('ps', bank)])
                            TT(tm[hb][:], PSb(bank), bsb[:, hb * 4:(hb + 1) * 4, :].rearrange("p g q -> p (g q)"), ALU.add,
                               [('ps', bank), 'ma_c'], [('ma_tm', hb)])
                            TT(ya[b][:, hb * 4:(hb + 1) * 4, :].rearrange("p g q -> p (g q)"), tm[hb][:],
                               ut[b][:, hb * 4:(hb + 1) * 4, :].rearrange("p g q -> p (g q)"), ALU.mult,
                               [('ma_tm', hb), ('ma_ut', b)], [('ma_ya', b)])
                        DMA('act', yaTv[:, :, tsl], ya[b][:], [('ma_ya', b)], ['yaT'])
                S.barrier()
            if l == 0:
                dump('yaT', yaT)
            if stop == 'mixa':
                break
            hyena_phase(l)
            if stop == 'hy':
                break
            if l == 0:
                dump('spec', spec)
                dump('ybT', ybT)
            rwkv_phase(l)
            if l == 0:
                dump('ycT', ycT)
            if stop == 'rw':
                break
            merge_phase(l)
            if l == 0:
                dump('mergedT', mgT)
            outproj_phase(l)
            if stop == 'outp':
                break
            ffn_phase(l)
        dump('xT', xT)
        if stop is None:
            final_phase()

        S.barrier()
    return nc


_NC_CACHE = {}


def kernel(**inputs):
    if 'nc' not in _NC_CACHE:
        _NC_CACHE['nc'] = build()
        _NC_CACHE['consts'] = make_consts()
    nc = _NC_CACHE['nc']
    consts = _NC_CACHE['consts']
    inp = {k_: np.ascontiguousarray(np.asarray(v)) for k_, v in inputs.items()}
    pcol = make_pcol(inp)
    base = {n: inp[n] for n in WEIGHT_SHAPES}
    base.update(consts)
    base['pcol'] = pcol
    NB_ = inp['x'].shape[0]
    in_maps = []
    for c in range(NCORES_USED):
        m = dict(base)
        m['x'] = np.ascontiguousarray(inp['x'][c % NB_])
        in_maps.append(m)
    res = run_bass_kernel_spmd(nc, in_maps, core_ids=list(range(NCORES_USED)))
    out = np.stack([np.asarray(res.results[b]['out'], dtype=np.float32) for b in range(NB_)], axis=0)
    return out
```

```python
import math
import numpy as np
import ml_dtypes
from contextlib import ExitStack
import concourse.bass as bass
import concourse.mybir as mybir
from concourse.bass_utils import run_bass_kernel_spmd

F32 = mybir.dt.float32
BF16 = mybir.dt.bfloat16
AF = mybir.ActivationFunctionType
ALU = mybir.AluOpType
AX = mybir.AxisListType

NCORES = 8
NCORES_USED = 4
T = 2048
D = 2048
DEPTH = 4
A_W = 1024
B_W = 1024
C_W = 1024
C_IN = 3392
N_IN = 14656
D_FF = 5632
NT = T // 128
NTG = T // 512
KD = D // 128
RMS_EPS = 1e-6
LN_EPS = 1e-5
GN_EPS = 64e-5
HY_MIN_DECAY = -math.log(1e-2) / 1.5
HY_MAX_DECAY = -math.log(1e-2) / 0.3
OFF_B = 2 * A_W
OFF_C = OFF_B + 3 * B_W
OFF_G = OFF_C + C_IN

WEIGHT_SHAPES = {
    'w_in': (DEPTH, D, N_IN), 'gm_ln_g': (DEPTH, A_W), 'gm_ln_b': (DEPTH, A_W),
    'gm_ws': (DEPTH, 8, 128, 128), 'gm_bs': (DEPTH, 8, 128),
    'hy_w1': (DEPTH, 33, 64), 'hy_w2': (DEPTH, 64, 64), 'hy_w3': (DEPTH, 64, 64), 'hy_w4': (DEPTH, 64, 4096),
    'hy_log_decay': (DEPTH, 2, 2, 1024), 'hy_bias_d': (DEPTH, 2, 1024),
    'rw_w2': (DEPTH, 2, 48, 1024), 'rw_a2': (DEPTH, 2, 48, 1024),
    'rw_v1': (DEPTH - 1, 1024, 32), 'rw_v2': (DEPTH - 1, 32, 1024), 'rw_g2': (DEPTH, 128, 1024),
    'w_branch_a': (DEPTH, A_W, D), 'w_branch_b': (DEPTH, B_W, D), 'w_branch_c': (DEPTH, C_W, D),
    'w_out': (DEPTH, D, D), 'w_ffn_gate': (DEPTH, D, D_FF), 'w_ffn_up': (DEPTH, D, D_FF),
    'w_ffn_down': (DEPTH, D_FF, D),
}

PC = {}
_o = 0
for _n, _c in [('nmg', 16), ('nfg', 16), ('cw0', 24), ('cw1', 24), ('cw2', 24), ('cb', 24), ('mup', 29), ('mun', 29),
               ('w0', 16), ('a0', 16), ('v0', 8), ('kk', 8), ('ka', 8), ('rk', 8), ('lng', 8), ('lnb', 8),
               ('hyb', 4), ('nfin', 16)]:
    PC[_n] = (_o, _c)
    _o += _c
NPC = _o
C_TILES = [(i * 128, 128) for i in range(25)] + [(3200, 48), (3248, 48), (3296, 48), (3344, 48)]


def _cols(v):
    return np.ascontiguousarray(np.asarray(v, np.float32).reshape(-1, 128).T)


def make_pcol(inp):
    pc = np.zeros((DEPTH, 128, NPC), np.float32)

    def put(l, name, arr):
        o, c = PC[name]
        assert arr.shape == (128, c), (name, arr.shape)
        pc[l, :, o:o + c] = arr
    for l in range(DEPTH):
        put(l, 'nmg', _cols(inp['norm_mix_g'][l]))
        put(l, 'nfg', _cols(inp['norm_ffn_g'][l]))
        for j in range(3):
            put(l, 'cw%d' % j, _cols(inp['hy_conv_w'][l, j]))
        put(l, 'cb', _cols(inp['hy_conv_b'][l]))
        for nm, src in (('mup', 'rw_mu_prev'), ('mun', 'rw_mu_next')):
            a = np.zeros((128, 29), np.float32)
            for i, (c0, m) in enumerate(C_TILES):
                a[:m, i] = inp[src][l, c0:c0 + m]
            put(l, nm, a)
        put(l, 'w0', _cols(inp['rw_w0'][l]))
        put(l, 'a0', _cols(inp['rw_a0'][l]))
        if l > 0:
            put(l, 'v0', _cols(inp['rw_v0'][l - 1]))
        put(l, 'kk', _cols(inp['rw_k_k'][l]))
        put(l, 'ka', _cols(inp['rw_k_a'][l]))
        put(l, 'rk', _cols(inp['rw_r_k'][l]))
        put(l, 'lng', _cols(inp['rw_ln_g'][l]))
        put(l, 'lnb', _cols(inp['rw_ln_b'][l]))
        hb = np.zeros((128, 4), np.float32)
        hb[:64, 0] = inp['hy_b1'][l]
        hb[:64, 1] = inp['hy_b2'][l]
        hb[:64, 2] = inp['hy_b3'][l]
        hb[:64, 3] = inp['hy_freq'][l]
        put(l, 'hyb', hb)
        put(l, 'nfin', _cols(inp['norm_final_g']))
    return pc


def make_consts():
    c = {}
    c['identF'] = np.eye(128, dtype=np.float32)
    s = np.arange(128)[:, None]
    t = np.arange(128)[None, :]
    su = (s < t).astype(np.float32)
    iu = (s <= t).astype(np.float32)
    c['maskU4'] = np.concatenate([su, iu, su, iu], axis=1)
    c['maskL'] = (s > t).astype(np.float32)
    c['blockones'] = ((s // 64) == (t // 64)).astype(np.float32)
    c['onesF'] = np.ones((128, 128), np.float32)
    c['identB'] = np.eye(128).astype(ml_dtypes.bfloat16)
    c['maskL4'] = np.tile(c['maskL'], (1, 4))
    c['ident4'] = np.tile(c['identF'], (1, 4))
    rm = np.ones((128, T), np.float32)
    rm[:, 0::128] = 0.0
    c['rmask'] = rm
    n = np.arange(T, dtype=np.float64)
    f = np.arange(T, dtype=np.float64) + 0.5
    ang = 2.0 * np.pi * np.outer(n, f) / (2 * T)
    c['CT'] = np.cos(ang).astype(ml_dtypes.bfloat16)
    c['ST'] = np.sin(ang).astype(ml_dtypes.bfloat16)
    c['CF'] = np.ascontiguousarray(np.cos(ang).T).astype(ml_dtypes.bfloat16)
    c['SF'] = np.ascontiguousarray(np.sin(ang).T).astype(ml_dtypes.bfloat16)
    tt = np.linspace(0.0, 1.0, T, dtype=np.float32)[:, None]
    bands = 16
    fr = np.linspace(1e-4, bands - 1, bands, dtype=np.float32)
    a2 = (np.float32(2.0 * math.pi / T) * np.arange(T, dtype=np.float32)[:, None]) * fr[None, :]
    feats = np.concatenate([tt, np.cos(a2), -np.sin(a2)], axis=-1).astype(np.float32)
    c['featsT'] = np.ascontiguousarray(feats.T)
    c['tneg'] = np.ascontiguousarray((-tt[:, 0]).reshape(NT, 128).T)
    return c


CONST_SHAPES = {'identF': ((128, 128), F32), 'maskU4': ((128, 512), F32), 'maskL': ((128, 128), F32),
                'blockones': ((128, 128), F32), 'onesF': ((128, 128), F32), 'maskL4': ((128, 512), F32),
                'ident4': ((128, 512), F32), 'rmask': ((128, T), F32), 'identB': ((128, 128), BF16),
                'CT': ((T, T), BF16), 'ST': ((T, T), BF16), 'CF': ((T, T), BF16), 'SF': ((T, T), BF16),
                'featsT': ((33, T), F32), 'tneg': ((128, NT), F32)}


class Sched:
    NDMA = 12

    def __init__(self, nc, es):
        self.nc = nc
        self.engs = {'pe': nc.tensor, 'act': nc.scalar, 'dve': nc.vector, 'pool': nc.gpsimd, 'sp': nc.sync}
        self.sem = {k: es.enter_context(nc.semaphore("s_" + k)) for k in self.engs}
        self.cnt = {k: 0 for k in self.engs}
        self.waited = {}
        self.dsem = {}
        self.dcnt = {}
        self.dnext = {}
        for q in ('sp', 'act', 'pool'):
            self.dsem[q] = [es.enter_context(nc.semaphore("d_%s%d" % (q, i))) for i in range(self.NDMA)]
            self.dcnt[q] = [0] * self.NDMA
            self.dnext[q] = 0
        self.res = {}
        self.ninst = 0

    def _wait(self, e, tok):
        if tok[0] == 'e':
            _, f, v = tok
            if f == e and e == 'pe':
                return
            key = (e, f)
        else:
            _, q, slot, v = tok
            key = (e, 'd', q, slot)
        if self.waited.get(key, 0) >= v:
            return
        self.waited[key] = v
        sem = self.sem[tok[1]] if tok[0] == 'e' else self.dsem[tok[1]][tok[2]]
        self.engs[e].wait_ge(sem, v)

    def _deps(self, e, reads, writes):
        for r in reads:
            st = self.res.get(r)
            if st and st[0] is not None:
                self._wait(e, st[0])
        for w in writes:
            st = self.res.get(w)
            if st:
                if st[0] is not None:
                    self._wait(e, st[0])
                for t in st[1].values():
                    self._wait(e, t)

    def _record(self, tok, reads, writes):
        for r in reads:
            st = self.res.get(r)
            if st is None:
                st = self.res[r] = [None, {}]
            k = tok[1] if tok[0] == 'e' else (tok[1], tok[2])
            st[1][k] = tok
        for w in writes:
            self.res[w] = [tok, {}]

    def op(self, e, fn, reads=(), writes=()):
        self._deps(e, reads, writes)
        ins = fn()
        self.cnt[e] += 1
        ins.then_inc(self.sem[e], 1)
        self._record(('e', e, self.cnt[e]), reads, writes)
        self.ninst += 1
        return ins

    def dma(self, q, out, in_, reads=(), writes=(), **kw):
        slot = self.dnext[q]
        self.dnext[q] = (slot + 1) % self.NDMA
        if self.dcnt[q][slot] > 0:
            self._wait(q, ('d', q, slot, self.dcnt[q][slot]))
        self._deps(q, reads, writes)
        ins = self.engs[q].dma_start(out=out, in_=in_, **kw)
        self.dcnt[q][slot] += 16
        ins.then_inc(self.dsem[q][slot], 16)
        self._record(('d', q, slot, self.dcnt[q][slot]), reads, writes)
        self.ninst += 1
        return ins

    def barrier(self):
        for e in self.engs:
            for f in self.engs:
                if self.cnt[f] > 0:
                    key = (e, f)
                    if self.waited.get(key, 0) < self.cnt[f]:
                        self.waited[key] = self.cnt[f]
                        self.engs[e].wait_ge(self.sem[f], self.cnt[f])
            for q in self.dsem:
                for slot in range(self.NDMA):
                    if self.dcnt[q][slot] > 0:
                        self._wait(e, ('d', q, slot, self.dcnt[q][slot]))
        self.res = {}


def build(NL=DEPTH, dbg=(), stop=None):
    nc = bass.Bass("TRN2", target_bir_lowering=False)

    def din(name, shape, dt=F32):
        return nc.dram_tensor(name, list(shape), dt, kind="ExternalInput").ap()

    def dscr(name, shape, dt=F32):
        return nc.dram_tensor(name, list(shape), dt, kind="Internal").ap()

    def dout(name, shape, dt=F32):
        return nc.dram_tensor(name, list(shape), dt, kind="ExternalOutput").ap()

    x_in = din("x", [T, D])
    W = {n: din(n, s) for n, s in WEIGHT_SHAPES.items()}
    pcol_d = din("pcol", [DEPTH, 128, NPC])
    CD = {n: din(n, s, dt) for n, (s, dt) in CONST_SHAPES.items()}
    out_d = dout("out", [T, D])
    DBG = {}
    dbg_shapes = {'xT': ([D, T], F32), 'xT0': ([D, T], F32), 'uT': ([A_W, T], F32), 'bT': ([3 * B_W, T], F32), 'cT': ([C_IN, T], F32),
                  'gT': ([3 * D, T], BF16), 'yaT': ([A_W, T], BF16), 'ybT': ([B_W, T], BF16),
                  'ycT': ([C_W, T], BF16), 'hTd': ([D, T], BF16), 'spec': ([2, 2, T, B_W], F32),
                  'mergedT': ([D, T], BF16)}
    for n in dbg:
        DBG[n] = dout("dbg_" + n, *dbg_shapes[n])

    xT = dscr("xT_s", [D, T])
    uT = dscr("uT_s", [A_W, T])
    bT = dscr("bT_s", [3 * B_W, T])
    cT = dscr("cT_s", [C_IN, T])
    gT = dscr("gT_s", [3 * D, T], BF16)
    yaT = dscr("yaT_s", [A_W, T], BF16)
    ybT = dscr("ybT_s", [B_W, T], BF16)
    ycT = dscr("ycT_s", [C_W, T], BF16)
    mgT = dscr("mgT_s", [D, T], BF16)
    actT = dscr("actT_s", [D_FF, T], BF16)
    vfT = dscr("vfT_s", [C_W, T])
    x1tok = dscr("x1tok_s", [T, B_W])
    hfil = dscr("hfil_s", [T, 4096])
    spec = dscr("spec_s", [2, 2, T, B_W])
    xTv = xT.rearrange("(dk p) t -> p dk t", p=128)

    with ExitStack() as es:
        S = Sched(nc, es)

        uid = [0]

        def sb(name, shape, dt=F32, st=es):
            uid[0] += 1
            return st.enter_context(nc.sbuf_tensor("sb%d_%s" % (uid[0], name), list(shape), dt))

        PSA = es.enter_context(nc.psum_tensor("psA", [128, 2048], F32))
        PSB = es.enter_context(nc.psum_tensor("psB", [128, 2048], F32))

        def PSb(i):
            return (PSA if i < 4 else PSB)[:, (i % 4) * 512:(i % 4 + 1) * 512]

        def PSfull(a):
            return PSA if a == 0 else PSB

        def ACT(out, in_, func, r, w, **kw):
            return S.op('act', lambda: nc.scalar.activation(out=out, in_=in_, func=func, **kw), r, w)

        def TT(out, a, b, op, r, w, eng='dve'):
            e = nc.vector if eng == 'dve' else nc.gpsimd
            return S.op(eng, lambda: e.tensor_tensor(out=out, in0=a, in1=b, op=op), r, w)

        def TS(out, a, s1, s2, op0, op1, r, w, eng='dve'):
            e = nc.vector if eng == 'dve' else nc.gpsimd
            if s2 is None:
                return S.op(eng, lambda: e.tensor_scalar(out=out, in0=a, scalar1=s1, scalar2=None, op0=op0), r, w)
            return S.op(eng, lambda: e.tensor_scalar(out=out, in0=a, scalar1=s1, scalar2=s2, op0=op0, op1=op1), r, w)

        def STT(out, a, s, b, op0, op1, r, w, eng='dve'):
            e = nc.vector if eng == 'dve' else nc.gpsimd
            return S.op(eng, lambda: e.scalar_tensor_tensor(out=out, in0=a, scalar=s, in1=b, op0=op0, op1=op1), r, w)

        def RECIP(out, in_, r, w):
            return S.op('dve', lambda: nc.vector.reciprocal(out=out, in_=in_), r, w)

        def CP(out, in_, r, w, eng='dve'):
            if eng == 'act':
                return S.op('act', lambda: nc.scalar.copy(out=out, in_=in_), r, w)
            e = nc.vector if eng == 'dve' else nc.gpsimd
            return S.op(eng, lambda: e.tensor_copy(out=out, in_=in_), r, w)

        def MSET(ap, val, w, eng='pool'):
            e = nc.vector if eng == 'dve' else nc.gpsimd
            return S.op(eng, lambda: e.memset(ap, val), (), w)

        def MM(out, lhsT, rhs, start, stop, r, w):
            return S.op('pe', lambda: nc.tensor.matmul(out, lhsT, rhs, start=start, stop=stop), r, w)

        def TR(out, in_, ident, r, w):
            return S.op('pe', lambda: nc.tensor.transpose(out, in_, ident), r, w)

        def DMA(q, out, in_, r, w, **kw):
            return S.dma(q, out, in_, r, w, **kw)

        cp_flip = [0]

        def EVAC(out, in_, r, w):
            cp_flip[0] ^= 1
            return CP(out, in_, r, w, eng='act' if cp_flip[0] else 'dve')

        identF = sb("identF", [128, 128])
        maskU4 = sb("maskU4", [128, 512])
        maskL = sb("maskL", [128, 128])
        blockones = sb("blockones", [128, 128])
        onesF = sb("onesF", [128, 128])
        tneg = sb("tneg", [128, NT])
        maskL4 = sb("maskL4", [128, 512])
        ident4 = sb("ident4", [128, 512])
        identB = sb("identB", [128, 128], BF16)
        pcol = sb("pcol", [128, NPC])
        for n, t_ in (('identF', identF), ('maskU4', maskU4), ('maskL', maskL), ('blockones', blockones),
                      ('onesF', onesF), ('tneg', tneg), ('maskL4', maskL4), ('ident4', ident4), ('identB', identB)):
            DMA('sp', t_[:], CD[n][:, :], (), ['const'])

        def pc(name, i=0, m=128):
            o, c = PC[name]
            return pcol[0:m, o + i:o + i + 1]

        def dump(name, src_ap):
            if name in DBG:
                DMA('sp', DBG[name], src_ap, ['ALLDRAM'], [('dbg', name)])

        with ExitStack() as ph:
            xt = [sb("p0x%d" % i, [128, D], st=ph) for i in range(2)]
            stg = [sb("p0s%d" % i, [128, 4, 128], st=ph) for i in range(4)]
            n = 0
            for i in range(NT):
                b = i % 2
                DMA('sp', xt[b][:], x_in[i * 128:(i + 1) * 128, :], (), [('p0x', b)])
                for q in range(4):
                    bank = n % 8
                    s_ = n % 4
                    for j in range(4):
                        dk = q * 4 + j
                        TR(PSb(bank)[:, j * 128:(j + 1) * 128], xt[b][:, dk * 128:(dk + 1) * 128], identF[:],
                           [('p0x', b), 'const'], [('ps', bank)])
                    EVAC(stg[s_][:].rearrange("p a b -> p (a b)"), PSb(bank), [('ps', bank)], [('p0s', s_)])
                    DMA('act', xTv[:, q * 4:(q + 1) * 4, i * 128:(i + 1) * 128], stg[s_][:], [('p0s', s_)], ['xT'])
                    n += 1
        S.barrier()
        if 'xT0' in DBG:
            DMA('sp', DBG['xT0'], xT, (), [('dbg', 'xT0')])

        def norm_phase(hT, gname):
            with ExitStack() as ph:
                ph.enter_context(nc.named_scope("norm"))
                xin = [sb("nx%d" % i, [128, KD, 512], st=ph) for i in range(2)]
                sq = [sb("nsq%d" % i, [128, 512], st=ph) for i in range(2)]
                rs = [sb("nrs%d" % i, [128, 512], st=ph) for i in range(2)]
                for tg in range(NTG):
                    b = tg % 2
                    tsl = slice(tg * 512, (tg + 1) * 512)
                    DMA('sp', xin[b][:], xTv[:, :, tsl], ['xT'], [('nx', b)])
                    for dk in range(KD):
                        ACT(sq[dk % 2][:], xin[b][:, dk, :], AF.Square, [('nx', b)], [('nsq', dk % 2)])
                        MM(PSb(b), onesF[:], sq[dk % 2][:], dk == 0, dk == KD - 1, [('nsq', dk % 2), 'const'], [('ps', b)])
                    ACT(rs[b][:], PSb(b), AF.Sqrt, [('ps', b)], [('nrs', b)], scale=1.0 / D, bias=RMS_EPS)
                    RECIP(rs[b][:], rs[b][:], [('nrs', b)], [('nrs', b)])
                    for dk in range(KD):
                        STT(hT[:, dk, tsl], xin[b][:, dk, :], pc(gname, dk), rs[b][:], ALU.mult, ALU.mult,
                            [('nx', b), 'pcol', ('nrs', b)], [('hT', dk, tg)])
                S.barrier()

        def linear_fm(Wv, KT, ctiles, rhs_fn, rkey_fn, epi, wp, wpname, full=False, banks=(0, 1, 2, 3, 4, 5, 6, 7)):
            panels = []
            cur = None
            for ci, (c0, m) in enumerate(ctiles):
                if cur is None or (c0 + m - cur[0]) > 512 or c0 != cur[1]:
                    cur = [c0, c0, []]
                    panels.append(cur)
                cur[2].append((ci, c0, m))
                cur[1] = c0 + m
            nb = 0
            for pi, (p0, p1, tl) in enumerate(panels):
                b = pi % 2
                DMA('pool', wp[b][:, 0:KT, 0:p1 - p0], Wv[:, :, p0:p1], (), [(wpname, b)])
                for (ci, c0, m) in tl:
                    lo = c0 - p0
                    if full:
                        a = nb % 2
                        nb += 1
                        for tg in range(NTG):
                            bank = a * 4 + tg
                            for kt in range(KT):
                                MM(PSb(bank)[0:m, :], wp[b][:, kt, lo:lo + m], rhs_fn(kt, tg), kt == 0, kt == KT - 1,
                                   [(wpname, b), rkey_fn(kt, tg)], [('ps', bank)])
                        epi(ci, c0, m, None, PSfull(a)[0:m, :], [('ps', a * 4 + j) for j in range(4)])
                    else:
                        for tg in range(NTG):
                            bank = banks[nb % len(banks)]
                            nb += 1
                            for kt in range(KT):
                                MM(PSb(bank)[0:m, :], wp[b][:, kt, lo:lo + m], rhs_fn(kt, tg), kt == 0, kt == KT - 1,
                                   [(wpname, b), rkey_fn(kt, tg)], [('ps', bank)])
                            epi(ci, c0, m, tg, PSb(bank)[0:m, :], [('ps', bank)])


        SW = 256
        NSG = T // SW
        INV_SCALE = 2.0 / (2 * T)

        def load_slabs(slC, slS, b, Cm, Sm, g):
            DMA('sp', slC[b][:], Cm.rearrange("(kt p) c -> p kt c", p=128)[:, :, g * SW:(g + 1) * SW], (), [('slC', b)])
            DMA('act', slS[b][:], Sm.rearrange("(kt p) c -> p kt c", p=128)[:, :, g * SW:(g + 1) * SW], (), [('slS', b)])

        def dft_fwd(slC, slS, srcC, srcS, keyC, keyS, epi):
            nb = 0
            load_slabs(slC, slS, 0, CD['CT'], CD['ST'], 0)
            for g in range(NSG):
                b = g % 2
                if g + 1 < NSG:
                    load_slabs(slC, slS, (g + 1) % 2, CD['CT'], CD['ST'], g + 1)
                for fi in range(SW // 128):
                    ft = g * (SW // 128) + fi
                    for chh in range(2):
                        bC = (nb % 4) * 2
                        bS = bC + 1
                        nb += 1
                        for nt in range(NT):
                            MM(PSb(bC), slC[b][:, nt, fi * 128:(fi + 1) * 128], srcC[:, nt, chh * 512:(chh + 1) * 512],
                               nt == 0, nt == NT - 1, [('slC', b), (keyC, nt)], [('ps', bC)])
                        for nt in range(NT):
                            MM(PSb(bS), slS[b][:, nt, fi * 128:(fi + 1) * 128], srcS[:, nt, chh * 512:(chh + 1) * 512],
                               nt == 0, nt == NT - 1, [('slS', b), (keyS, nt)], [('ps', bS)])
                        epi(ft, chh, bC, bS)

        def hyena_phase(l):
            hy0 = ExitStack()
            asum = sb("hy_asum", [128, 4096], st=hy0)
            with ExitStack() as ph:
                ph.enter_context(nc.named_scope("hy_filt"))
                w1 = sb("hy_w1", [33, 64], st=ph)
                w2 = sb("hy_w2", [64, 64], st=ph)
                w3 = sb("hy_w3", [64, 64], st=ph)
                w4 = sb("hy_w4", [64, 4096], BF16, st=ph)
                z3b = sb("hy_z3b", [64, T], BF16, st=ph)
                fT = sb("hy_fT", [33, T], st=ph)
                zT = [sb("hy_zT%d" % i, [64, T], st=ph) for i in range(2)]
                tmpz = sb("hy_tmpz", [64, 512], st=ph)
                edb = sb("hy_edb", [128, 4096], st=ph)
                dec = [sb("hy_dec%d" % i, [128, 512], st=ph) for i in range(4)]
                hdt = [sb("hy_hd%d" % i, [128, 512], st=ph) for i in range(8)]
                hab2 = [sb("hy_hab2%d" % i, [128, 512], st=ph) for i in range(4)]
                habs = [sb("hy_ha%d" % i, [128, 512], st=ph) for i in range(2)]
                DMA('sp', w1[:], W['hy_w1'][l], (), ['hy_w'])
                DMA('sp', w2[:], W['hy_w2'][l], (), ['hy_w'])
                DMA('sp', w3[:], W['hy_w3'][l], (), ['hy_w'])
                DMA('pool', w4[:], W['hy_w4'][l], (), ['hy_w4'])
                DMA('sp', fT[:], CD['featsT'][:, :], (), ['hy_fT'])
                DMA('sp', edb[:], W['hy_log_decay'][l:l + 1].rearrange("o a b c -> o (a b c)").broadcast_to([128, 4096]), (), ['hy_edb'])
                ACT(edb[:], edb[:], AF.Exp, ['hy_edb'], ['hy_edb'])
                ob_, _ = PC['hyb']
                fq = pcol[0:64, ob_ + 3:ob_ + 4]
                srcs = [(w1, fT, 33), (w2, zT[0], 64), (w3, zT[1], 64)]
                for li, (wl, src, kk_) in enumerate(srcs):
                    dst = zT[li % 2]
                    bcol = pcol[0:64, ob_ + li:ob_ + li + 1]
                    for tg in range(NTG):
                        tsl = slice(tg * 512, (tg + 1) * 512)
                        bank = tg % 2
                        MM(PSb(bank)[0:64, :], wl[0:kk_, :], src[0:kk_, tsl], True, True, ['hy_w', 'hy_fT', ('hy_zT', 0), ('hy_zT', 1)], [('ps', bank)])
                        TS(dst[:, tsl], PSb(bank)[0:64, :], bcol, fq, ALU.add, ALU.mult, [('ps', bank), 'pcol'], [('hy_zT', li % 2)])
                        for rnd in range(3):
                            TS(tmpz[:], dst[:, tsl], math.pi, -2 * math.pi, ALU.is_gt, ALU.mult, [('hy_zT', li % 2)], ['hy_tmpz'])
                            TT(dst[:, tsl], dst[:, tsl], tmpz[:], ALU.add, [('hy_zT', li % 2), 'hy_tmpz'], [('hy_zT', li % 2)])
                            TS(tmpz[:], dst[:, tsl], -math.pi, 2 * math.pi, ALU.is_lt, ALU.mult, [('hy_zT', li % 2)], ['hy_tmpz'])
                            TT(dst[:, tsl], dst[:, tsl], tmpz[:], ALU.add, [('hy_zT', li % 2), 'hy_tmpz'], [('hy_zT', li % 2)])
                        ACT(dst[:, tsl], dst[:, tsl], AF.Sin, [('hy_zT', li % 2)], [('hy_zT', li % 2)])
                z3 = zT[0]
                CP(z3b[:], z3[:], [('hy_zT', 0)], ['hy_z3b'], eng='act')
                its = [(cg, i) for cg in range(8) for i in range(NT)]

                def emit_dec(n):
                    cg, i = its[n]
                    ACT(dec[n % 4][:], edb[:, cg * 512:(cg + 1) * 512], AF.Exp, ['hy_edb', 'const'], [('hy_dec', n % 4)], scale=tneg[:, i:i + 1])
                emit_dec(0)
                for n, (cg, i) in enumerate(its):
                    csl = slice(cg * 512, (cg + 1) * 512)
                    a_ = cg % 2
                    k = n % 4
                    k8 = n % 8
                    bank = 2 + k
                    if i == 0:
                        MSET(habs[a_][:], 0.0, [('hy_ha', a_)], eng='pool')
                    MM(PSb(bank), z3b[:, i * 128:(i + 1) * 128], w4[:, csl], True, True, ['hy_z3b', 'hy_w4'], [('ps', bank)])
                    if n + 1 < len(its):
                        emit_dec(n + 1)
                    TT(hdt[k8][:], PSb(bank), dec[k][:], ALU.mult, [('ps', bank), ('hy_dec', k)], [('hy_hd', k8)])
                    DMA('sp' if n % 2 else 'act', hfil[i * 128:(i + 1) * 128, csl], hdt[k8][:], [('hy_hd', k8)], ['hfil'])
                    ACT(hab2[k][:], hdt[k8][:], AF.Abs, [('hy_hd', k8)], [('hy_hab2', k)])
                    TT(habs[a_][:], habs[a_][:], hab2[k][:], ALU.add, [('hy_hab2', k), ('hy_ha', a_)], [('hy_ha', a_)], eng='pool')
                    if i == NT - 1:
                        MM(PSb(6 + a_), onesF[:], habs[a_][:], True, True, [('hy_ha', a_), 'const'], [('ps', 6 + a_)])
                        EVAC(asum[:, csl], PSb(6 + a_), [('ps', 6 + a_)], ['hy_asum'])
                S.barrier()
            with ExitStack() as ph:
                ph.enter_context(nc.named_scope("hy_spec"))
                rinv = sb("hy_rinv", [128, 2048], st=ph)
                bdb = sb("hy_bdb", [128, 2048], st=ph)
                hs = sb("hy_hs", [128, NT, 1024], BF16, st=ph)
                hdf = sb("hy_hdf", [128, NT, 1024], BF16, st=ph)
                hf = [sb("hy_hf%d" % i, [128, 1024], st=ph) for i in range(2)]
                hb = [sb("hy_hb%d" % i, [128, 1024], st=ph) for i in range(2)]
                t1 = [sb("hy_t1%d" % i, [128, 1024], st=ph) for i in range(2)]
                slC = [sb("hy_slC%d" % i, [128, NT, SW], BF16, st=ph) for i in range(2)]
                slS = [sb("hy_slS%d" % i, [128, NT, SW], BF16, st=ph) for i in range(2)]
                ko = [sb("hy_ko%d" % i, [128, 512], st=ph) for i in range(4)]
                TT(rinv[:], asum[:, 0:2048], asum[:, 2048:4096], ALU.add, (), ['hy_rinv'])
                ACT(rinv[:], rinv[:], AF.Ln, ['hy_rinv'], ['hy_rinv'])
                ACT(rinv[:], rinv[:], AF.Exp, ['hy_rinv'], ['hy_rinv'], scale=-1.0)
                DMA('sp', bdb[:], W['hy_bias_d'][l:l + 1].rearrange("o a c -> o (a c)").broadcast_to([128, 2048]), (), ['hy_bdb'])
                for o in range(2):
                    osl = slice(o * 1024, (o + 1) * 1024)
                    for i in range(NT):
                        k = i % 2
                        rows = slice(i * 128, (i + 1) * 128)
                        DMA('sp', hf[k][:], hfil[rows, o * 1024:(o + 1) * 1024], ['hfil'], [('hy_hf', k)])
                        DMA('act', hb[k][:], hfil[rows, 2048 + o * 1024:2048 + (o + 1) * 1024], ['hfil'], [('hy_hb', k)])
                        if i == 0:
                            MSET(hb[k][0:1, :], 0.0, [('hy_hb', k)], eng='dve')
                        TT(hs[:, i, :], hf[k][:], hb[k][:], ALU.add, [('hy_hf', k), ('hy_hb', k)], [('hy_hs', i)])
                        TT(hdf[:, i, :], hf[k][:], hb[k][:], ALU.subtract, [('hy_hf', k), ('hy_hb', k)], [('hy_hdf', i)], eng='pool')
                    kc = [0]

                    def epi_spec(ft, chh, bC, bS):
                        k0 = kc[0] % 2
                        kc[0] += 1
                        csl = slice(o * 1024 + chh * 512, o * 1024 + (chh + 1) * 512)
                        TT(ko[k0][:], PSb(bC), rinv[:, csl], ALU.mult, [('ps', bC), 'hy_rinv'], [('hy_ko', k0)])
                        TT(ko[k0][:], ko[k0][:], bdb[:, csl], ALU.add, [('hy_ko', k0), 'hy_bdb'], [('hy_ko', k0)], eng='pool')
                        DMA('sp', spec[o, 0, ft * 128:(ft + 1) * 128, chh * 512:(chh + 1) * 512], ko[k0][:], [('hy_ko', k0)], ['spec'])
                        TT(ko[2 + k0][:], PSb(bS), rinv[:, csl], ALU.mult, [('ps', bS), 'hy_rinv'], [('hy_ko', 2 + k0)])
                        DMA('sp', spec[o, 1, ft * 128:(ft + 1) * 128, chh * 512:(chh + 1) * 512], ko[2 + k0][:], [('hy_ko', 2 + k0)], ['spec'])
                    dft_fwd(slC, slS, hs, hdf, 'hy_hs', 'hy_hdf', epi_spec)
                S.barrier()
            hy0.close()
            with ExitStack() as ph:
                z = sb("hy_z", [128, NT, 1024], BF16, st=ph)
                Yr = sb("hy_Yr", [128, NT, 1024], BF16, st=ph)
                Ys = sb("hy_Ys", [128, NT, 1024], BF16, st=ph)
                slC = [sb("hy_slC%d" % i, [128, NT, SW], BF16, st=ph) for i in range(2)]
                slS = [sb("hy_slS%d" % i, [128, NT, SW], BF16, st=ph) for i in range(2)]
                Kr = [sb("hy_Kr%d" % i, [128, 512], st=ph) for i in range(2)]
                Ks = [sb("hy_Ks%d" % i, [128, 512], st=ph) for i in range(2)]
                Zc = [sb("hy_Zc%d" % i, [128, 512], st=ph) for i in range(2)]
                Zs = [sb("hy_Zs%d" % i, [128, 512], st=ph) for i in range(2)]
                ta = [sb("hy_ta%d" % i, [128, 512], st=ph) for i in range(2)]
                tb = [sb("hy_tb%d" % i, [128, 512], st=ph) for i in range(2)]
                ld = [sb("hy_ld%d" % i, [128, T], st=ph) for i in range(2)]
                stg = [sb("hy_stg%d" % i, [128, 4, 128], st=ph) for i in range(2)]
                xm = [sb("hy_xm%d" % i, [128, 512], st=ph) for i in range(2)]
                yo = [sb("hy_yo%d" % i, [128, 512], BF16, st=ph) for i in range(2)]
                x1v = x1tok.rearrange("(nt p) c -> p nt c", p=128)
                sc_ = nc.named_scope("hy_tr")
                sc_.__enter__()
                n = 0
                for which in range(2):
                    for ct in range(8):
                        k = n % 2
                        r0 = (2048 if which == 0 else 0) + ct * 128
                        DMA('sp', ld[k][:], bT[r0:r0 + 128, :], ['bcT'], [('hy_ld', k)])
                        for q in range(4):
                            bank = (n * 4 + q) % 8
                            for j in range(4):
                                nt = q * 4 + j
                                TR(PSb(bank)[:, j * 128:(j + 1) * 128], ld[k][:, nt * 128:(nt + 1) * 128], identF[:],
                                   [('hy_ld', k), 'const'], [('ps', bank)])
                            if which == 0:
                                EVAC(z[:, q * 4:(q + 1) * 4, ct * 128:(ct + 1) * 128], PSb(bank).rearrange("p (a b) -> p a b", a=4),
                                     [('ps', bank)], [('hy_z', q * 4 + j) for j in range(4)])
                            else:
                                s_ = q % 2
                                EVAC(stg[s_][:], PSb(bank).rearrange("p (a b) -> p a b", a=4), [('ps', bank)], [('hy_stg', s_)])
                                DMA('act', x1v[:, q * 4:(q + 1) * 4, ct * 128:(ct + 1) * 128], stg[s_][:], [('hy_stg', s_)], ['x1tok'])
                        n += 1
                sc_.__exit__(None, None, None)
                for o in range(2):
                    kc = [0]
                    sc_ = nc.named_scope("hy_fwd%d" % o)
                    sc_.__enter__()

                    def epi_mul(ft, chh, bC, bS):
                        k0 = kc[0] % 2
                        kc[0] += 1
                        rows = slice(ft * 128, (ft + 1) * 128)
                        cs = slice(chh * 512, (chh + 1) * 512)
                        DMA('sp', Kr[k0][:], spec[o, 0, rows, cs], ['spec'], [('hy_Kr', k0)])
                        DMA('sp', Ks[k0][:], spec[o, 1, rows, cs], ['spec'], [('hy_Ks', k0)])
                        CP(Zc[k0][:], PSb(bC), [('ps', bC)], [('hy_Zc', k0)], eng='act')
                        CP(Zs[k0][:], PSb(bS), [('ps', bS)], [('hy_Zs', k0)], eng='act')
                        TT(ta[k0][:], Zc[k0][:], Kr[k0][:], ALU.mult, [('hy_Zc', k0), ('hy_Kr', k0)], [('hy_ta', k0)])
                        TT(tb[k0][:], Zs[k0][:], Ks[k0][:], ALU.mult, [('hy_Zs', k0), ('hy_Ks', k0)], [('hy_tb', k0)], eng='pool')
                        TT(Yr[:, ft, cs], ta[k0][:], tb[k0][:], ALU.subtract, [('hy_ta', k0), ('hy_tb', k0)], [('hy_Y', ft)])
                        TT(ta[k0][:], Zc[k0][:], Ks[k0][:], ALU.mult, [('hy_Zc', k0), ('hy_Ks', k0)], [('hy_ta', k0)])
                        TT(tb[k0][:], Zs[k0][:], Kr[k0][:], ALU.mult, [('hy_Zs', k0), ('hy_Kr', k0)], [('hy_tb', k0)], eng='pool')
                        TT(Ys[:, ft, cs], ta[k0][:], tb[k0][:], ALU.add, [('hy_ta', k0), ('hy_tb', k0)], [('hy_Y', ft)], eng='pool')
                    dft_fwd(slC, slS, z, z, 'hy_z', 'hy_z', epi_mul)
                    sc_.__exit__(None, None, None)
                    sc_ = nc.named_scope("hy_inv%d" % o)
                    sc_.__enter__()
                    nb = 0
                    load_slabs(slC, slS, 0, CD['CF'], CD['SF'], 0)
                    for g in range(NSG):
                        b = g % 2
                        if g + 1 < NSG:
                            load_slabs(slC, slS, (g + 1) % 2, CD['CF'], CD['SF'], g + 1)
                        nsl = slice(g * SW, (g + 1) * SW)
                        if o == 0:
                            for ni in range(SW // 128):
                                nt = g * (SW // 128) + ni
                                for chh in range(2):
                                    bank = nb % 8
                                    k0 = nb % 2
                                    nb += 1
                                    cs = slice(chh * 512, (chh + 1) * 512)
                                    DMA('sp', xm[k0][:], x1tok[nt * 128:(nt + 1) * 128, cs], ['x1tok'], [('hy_xm', k0)])
                                    for ft in range(NT):
                                        MM(PSb(bank), slC[b][:, ft, ni * 128:(ni + 1) * 128], Yr[:, ft, cs], ft == 0, False,
                                           [('slC', b), ('hy_Y', ft)], [('ps', bank)])
                                    for ft in range(NT):
                                        MM(PSb(bank), slS[b][:, ft, ni * 128:(ni + 1) * 128], Ys[:, ft, cs], False, ft == NT - 1,
                                           [('slS', b), ('hy_Y', ft)], [('ps', bank)])
                                    STT(z[:, nt, cs], PSb(bank), INV_SCALE, xm[k0][:], ALU.mult, ALU.mult,
                                        [('ps', bank), ('hy_xm', k0)], [('hy_z', nt)])
                        else:
                            for ct in range(8):
                                bank = nb % 8
                                k0 = nb % 2
                                nb += 1
                                csl = slice(ct * 128, (ct + 1) * 128)
                                DMA('sp', xm[k0][:, 0:SW], bT[1024 + ct * 128:1024 + (ct + 1) * 128, nsl], ['bcT'], [('hy_xm', k0)])
                                for ft in range(NT):
                                    MM(PSb(bank)[:, 0:SW], Yr[:, ft, csl], slC[b][:, ft, :], ft == 0, False,
                                       [('slC', b), ('hy_Y', ft)], [('ps', bank)])
                                for ft in range(NT):
                                    MM(PSb(bank)[:, 0:SW], Ys[:, ft, csl], slS[b][:, ft, :], False, ft == NT - 1,
                                       [('slS', b), ('hy_Y', ft)], [('ps', bank)])
                                STT(yo[k0][:, 0:SW], PSb(bank)[:, 0:SW], INV_SCALE, xm[k0][:, 0:SW], ALU.mult, ALU.mult,
                                    [('ps', bank), ('hy_xm', k0)], [('hy_yo', k0)])
                                DMA('act', ybT[ct * 128:(ct + 1) * 128, nsl], yo[k0][:, 0:SW], [('hy_yo', k0)], ['ybT'])
                    sc_.__exit__(None, None, None)
                S.barrier()


        SDT = BF16
        NINV = 1
        NSET = NINV + 1
        WSC = -math.exp(-0.5)

        def rwkv_phase(l):
            with ExitStack() as ph:
                ph.enter_context(nc.named_scope("rwkv"))
                twp = sb("rw_twp", [128, T], BF16, st=ph)
                adp = sb("rw_adp", [128, T], BF16, st=ph)
                sgd = sb("rw_sgd", [128, T], BF16, st=ph)
                w2p = sb("rw_w2p", [128, 1024], BF16, st=ph)
                a2p = sb("rw_a2p", [128, 1024], BF16, st=ph)
                g2 = sb("rw_g2", [128, 1024], BF16, st=ph)
                omka = sb("rw_omka", [128, 8], st=ph)
                rT = sb("rw_r", [128, T], st=ph)
                kT = sb("rw_k", [128, T], st=ph)
                vT = sb("rw_v", [128, T], st=ph)
                kkn = sb("rw_kkn", [128, T], st=ph)
                rkacc = sb("rw_rkacc", [128, T], st=ph)
                ysum = sb("rw_ysum", [128, T], st=ph)
                vR = sb("rw_vR", [128, T], st=ph)
                lw = sb("rw_lw", [128, T], st=ph)
                kd = sb("rw_kd", [128, T], st=ph)
                bd = sb("rw_bd", [128, T], st=ph)
                Lc = sb("rw_L", [128, T], st=ph)
                ex = sb("rw_ex", [128, T], st=ph)
                tmp2 = sb("rw_tmp2", [128, T], st=ph)
                yT_ = sb("rw_y", [128, T], st=ph)
                tot = sb("rw_tot", [128, 16], st=ph)
                Pc = sb("rw_Pc", [128, 16], st=ph)
                ARh = [sb("rw_AR%d" % i, [128, 16, 2, 128], SDT, st=ph) for i in range(2)]
                BK = sb("rw_BK", [128, 16, 2, 128], SDT, st=ph)
                BhT = sb("rw_BhT", [128, 16, 128], SDT, st=ph)
                KhT = sb("rw_KhT", [128, 16, 128], SDT, st=ph)
                VT = sb("rw_VT", [128, 16, 128], SDT, st=ph)
                NB = [sb("rw_NB%d" % i, [128, 4, 2, 128], SDT, st=ph) for i in range(NSET)]
                AK = [sb("rw_AK%d" % i, [128, 4, 2, 128], SDT, st=ph) for i in range(NSET)]
                Mm2 = [[sb("rw_M%d_%d" % (j, i), [128, 4, 128], SDT, st=ph) for i in range(2)] for j in range(NINV)]
                MT2 = [[sb("rw_MT%d_%d" % (j, i), [128, 4, 128], SDT, st=ph) for i in range(2)] for j in range(NINV)]
                Xx2 = [[sb("rw_X%d_%d" % (j, i), [128, 4, 128], SDT, st=ph) for i in range(2)] for j in range(NINV)]
                Xfin = [sb("rw_Xf%d" % i, [128, 4, 128], SDT, st=ph) for i in range(NSET)]
                tb16 = sb("rw_tb16", [128, T], BF16, st=ph)
                St = sb("rw_St", [128, 64], st=ph)
                Sb_ = sb("rw_Sb", [128, 64], SDT, st=ph)
                Wt = sb("rw_Wt", [128, 128], SDT, st=ph)
                Ut = sb("rw_Ut", [128, 128], SDT, st=ph)
                ob = [sb("rw_ob%d" % i, [128, 512], BF16, st=ph) for i in range(2)]

                EXK = [('rw_ex', p_) for p_ in range(4)]
                T2K = [('rw_tmp2', p_) for p_ in range(4)]
                KDK = [('rw_kd', p_) for p_ in range(4)]
                LK = [('rw_L', p_) for p_ in range(4)]
                LWK = [('rw_lw', p_) for p_ in range(4)]
                BDK = [('rw_bd', p_) for p_ in range(4)]
                DMA('pool', g2[:], W['rw_g2'][l], (), ['rw_c'])
                MSET(ARh[0][64:128].rearrange("p a b c -> p (a b c)"), 0.0, ['rw_AR'])
                MSET(ARh[1][0:64].rearrange("p a b c -> p (a b c)"), 0.0, ['rw_AR'])
                for d in range(2):
                    ps_ = slice(d * 64, d * 64 + 48)
                    DMA('pool', w2p[ps_, :], W['rw_w2'][l, d], (), ['rw_c'])
                    DMA('pool', a2p[ps_, :], W['rw_a2'][l, d], (), ['rw_c'])
                    DMA('sp', tmp2[ps_, :], cT[3200 + d * 48:3248 + d * 48, :], ['bcT'], [*T2K])
                    DMA('sp', ex[ps_, :], cT[3296 + d * 48:3344 + d * 48, :], ['bcT'], [*EXK])
                    if d == 0:
                        ACT(twp[ps_, :], tmp2[ps_, :], AF.Tanh, [*T2K], ['rw_twp'])
                        CP(adp[ps_, :], ex[ps_, :], [*EXK], ['rw_adp'], eng='act')
                    else:
                        ACT(twp[ps_, :], tmp2[ps_, ::-1], AF.Tanh, [*T2K], ['rw_twp'])
                        CP(adp[ps_, :], ex[ps_, ::-1], [*EXK], ['rw_adp'], eng='act')
                DMA('sp', Lc[:], cT[3072:3200, :], ['bcT'], [*LK])
                ACT(sgd[:], Lc[:], AF.Sigmoid, [*LK], ['rw_sgd'])
                oka, _ = PC['ka']
                TS(omka[:], pcol[:, oka:oka + 8], -1.0, 1.0, ALU.mult, ALU.add, ['pcol'], ['rw_omka'])
                if l > 0:
                    v1 = sb("rw_v1", [128, 8, 32], st=ph)
                    v2 = sb("rw_v2", [32, 1024], BF16, st=ph)
                    t1v = sb("rw_t1v", [32, T], BF16, st=ph)
                    DMA('sp', v1[:], W['rw_v1'][l - 1].rearrange("(ct p) r -> p ct r", p=128), (), ['rw_c'])
                    DMA('pool', v2[:], W['rw_v2'][l - 1], (), ['rw_c'])
                    for ct in range(8):
                        DMA('sp', kd[:], cT[2048 + ct * 128:2048 + (ct + 1) * 128, :], ['bcT'], [*KDK])
                        for tg in range(NTG):
                            MM(PSb(tg)[0:32, :], v1[:, ct, :], kd[:, tg * 512:(tg + 1) * 512], ct == 0, ct == 7,
                               ['rw_c', *KDK], [('ps', tg)])
                    for tg in range(NTG):
                        EVAC(t1v[:, tg * 512:(tg + 1) * 512], PSb(tg)[0:32, :], [('ps', tg)], ['rw_t1v'])

                def f2(ap):
                    return ap.rearrange("p a b -> p (a b)")

                def c3(ap):
                    return ap.rearrange("p (c t) -> p c t", t=128)

                def scan_unit(hp, d, r_ap, k_ap, kk_ap, v_, yout, ykey):
                    hsl = slice(hp * 128, (hp + 1) * 128)
                    ps_ = slice(d * 64, d * 64 + 48)

                    def prep_part(p):
                        psl = slice(p * 512, (p + 1) * 512)
                        c4 = slice(p * 4, (p + 1) * 4)
                        kl, ke, kt, kkd, kbd, kL, kb16 = ('rw_lw', p), ('rw_ex', p), ('rw_tmp2', p), ('rw_kd', p), ('rw_bd', p), ('rw_L', p), ('rw_tb16', p)
                        MM(PSb(5), w2p[ps_, hsl], twp[ps_, psl], True, True, ['rw_c', 'rw_twp'], [('ps', 5)])
                        ACT(lw[:, psl], PSb(5), AF.Sigmoid, [('ps', 5), 'pcol'], [kl], bias=pc('w0', d * 8 + hp))
                        MM(PSb(6), a2p[ps_, hsl], adp[ps_, psl], True, True, ['rw_c', 'rw_adp'], [('ps', 6)])
                        ACT(ex[:, psl], PSb(6), AF.Sigmoid, [('ps', 6), 'pcol'], [ke], bias=pc('a0', d * 8 + hp))
                        yield
                        TS(tmp2[:, psl], ex[:, psl], pc('ka', hp), omka[:, hp:hp + 1], ALU.mult, ALU.add, [ke, 'pcol', 'rw_omka'], [kt])
                        TT(kd[:, psl], tmp2[:, psl], k_ap[:, psl], ALU.mult, [kt, 'rw_in'], [kkd])
                        TT(bd[:, psl], ex[:, psl], kk_ap[:, psl], ALU.mult, ['rw_kkn', ke], [kbd], eng='pool')
                        if d == 0:
                            STT(rkacc[:, psl], r_ap[:, psl], pc('rk', hp), kd[:, psl], ALU.mult, ALU.mult, ['rw_in', 'pcol', kkd], [('rw_rkacc', p)])
                        else:
                            STT(tmp2[:, psl], r_ap[:, psl], pc('rk', hp), kd[:, psl], ALU.mult, ALU.mult, ['rw_in', 'pcol', kkd, kt], [kt])
                            rsl = slice(T - (p + 1) * 512, T - p * 512)
                            TT(rkacc[:, rsl], rkacc[:, rsl], tmp2[:, psl][:, ::-1], ALU.add, [('rw_rkacc', 3 - p), kt], [('rw_rkacc', 3 - p)])
                        yield
                        for c in range(p * 4, (p + 1) * 4):
                            csl = slice(c * 128, (c + 1) * 128)
                            S.op('dve', lambda: nc.vector.tensor_tensor_scan(out=Lc[:, csl], data0=onesF[:], data1=lw[:, csl], initial=0.0,
                                                                             op0=ALU.mult, op1=ALU.add), ['const', kl], [kL])
                        CP(tot[:, c4], c3(Lc[:])[:, c4, 127], [kL], [('rw_tot', p)])
                        ACT(Pc[:, c4], tot[:, c4], AF.Exp, [('rw_tot', p)], [('rw_Pc', p)], scale=WSC)
                        yield
                        ACT(ex[:, psl], Lc[:, psl], AF.Exp, [kL, kbd, kt], [ke], scale=WSC)
                        for h in range(2):
                            hp_ = slice(h * 64, (h + 1) * 64)
                            TT(ARh[h][hp_, c4, 1, :], c3(r_ap[hp_, psl]), c3(ex[hp_, psl]), ALU.mult, [ke, 'rw_in'], [('rw_AR', p)])
                        yield
                        ACT(ex[:, psl], Lc[:, psl], AF.Exp, [kL, ('rw_AR', p)], [ke], scale=-WSC)
                        TT(BK[:, c4, 0, :], c3(bd[:, psl]), c3(ex[:, psl]), ALU.mult, [ke, kbd], [('rw_BK', p)])
                        TT(BK[:, c4, 1, :], c3(kd[:, psl]), c3(ex[:, psl]), ALU.mult, [ke, kkd], [('rw_BK', p)], eng='pool')
                        yield
                        TT(tmp2[:, psl], Lc[:, psl], lw[:, psl], ALU.subtract, [kL, kl, kt], [kt])
                        ACT(ex[:, psl], tmp2[:, psl], AF.Exp, [kt, ('rw_BK', p)], [ke], scale=WSC)
                        for h in range(2):
                            hp_ = slice(h * 64, (h + 1) * 64)
                            STT(ARh[h][hp_, c4, 0, :], c3(kk_ap[hp_, psl]), -1.0, c3(ex[hp_, psl]), ALU.mult, ALU.mult, [ke, 'rw_kkn'], [('rw_AR', p)])
                        yield
                        TT(c3(tmp2[:, psl]), tot[:, c4].unsqueeze(2).to_broadcast([128, 4, 128]), c3(Lc[:, psl]), ALU.subtract,
                           [('rw_tot', p), kL, kt], [kt])
                        ACT(ex[:, psl], tmp2[:, psl], AF.Exp, [kt, ('rw_AR', p)], [ke], scale=WSC)
                        yield
                        for wi, (src, skey, dstT) in enumerate(((bd, kbd, BhT), (kd, kkd, KhT), (v_, 'rw_in', VT))):
                            if wi < 2:
                                TT(tb16[:, psl], src[:, psl], ex[:, psl], ALU.mult, [ke, skey, kb16], [kb16], eng='pool' if wi else 'dve')
                            else:
                                CP(tb16[:, psl], src[:, psl], ['rw_in', 'rw_vR', kb16], [kb16], eng='act')
                            pb16 = PSb(7).bitcast(BF16)
                            for j in range(4):
                                TR(pb16[:, j * 128:(j + 1) * 128], tb16[:, (p * 4 + j) * 128:(p * 4 + j + 1) * 128], identB[:],
                                   [kb16, 'const'], [('ps', 7)])
                            EVAC(f2(dstT[:, c4, :]), pb16[:, 0:512], [('ps', 7)], [('rw_T%d' % wi, p)])
                            yield

                    MSET(St[:], 0.0, ['rw_St'], eng='dve')
                    MSET(Sb_[:], 0.0, ['rw_Sb'], eng='dve')

                    def inv_chain(qd):
                        q3 = qd % NSET
                        st_ = qd % NINV
                        b0, b1, b2 = (2, 3, 4) if st_ == 0 else (5, 6, 7)
                        MTs, Mms, Xxs = MT2[st_], Mm2[st_], Xx2[st_]
                        kM, kMT, kX = 'rw_M%d' % st_, 'rw_MT%d' % st_, 'rw_X%d' % st_
                        gb_ = (b0, b1)
                        for cc in range(2):
                            c = qd * 2 + cc
                            for h in range(2):
                                u = cc * 2 + h
                                MM(PSb(gb_[u // 2])[:, (u % 2) * 256:(u % 2 + 1) * 256], BK[:, c, 0, :], f2(ARh[h][:, c, :, :]), True, True,
                                   [('rw_BK', qd // 2), ('rw_AR', qd // 2)], [('ps', gb_[u // 2])])
                                MM(PSb(b2)[:, u * 128:(u + 1) * 128], ARh[h][:, c, 0, :], BK[:, c, 0, :], True, True,
                                   [('rw_BK', qd // 2), ('rw_AR', qd // 2)], [('ps', b2)])
                        for hb in range(2):
                            TT(NB[q3][:, hb * 2:(hb + 1) * 2, :, :].rearrange("p a b c -> p (a b c)"), PSb(gb_[hb]), maskU4[:], ALU.mult,
                               [('ps', gb_[hb]), 'const'], [('rw_NB', q3)])
                        TT(f2(MTs[0][:]), PSb(b2), maskL4[:], ALU.mult, [('ps', b2), 'const'], [(kMT, 0)])
                        TT(Xxs[0][:], NB[q3][:, :, 0, :], ident4[:].rearrange("p (a b) -> p a b", a=4), ALU.add,
                           [('rw_NB', q3), 'const'], [(kX, 0)], eng='pool')
                        yield
                        cur = 0
                        for lev in range(1, 8):
                            nxt = 1 - cur
                            pA, pB, pC = b2, b1, b0
                            if lev == 1:
                                for cc in range(2):
                                    c = qd * 2 + cc
                                    for h in range(2):
                                        u = cc * 2 + h
                                        MM(PSb(gb_[u // 2])[:, (u % 2) * 256:(u % 2 + 1) * 256], BK[:, c, 1, :], f2(ARh[h][:, c, :, :]), True, True,
                                           [('rw_BK', qd // 2), ('rw_AR', qd // 2)], [('ps', gb_[u // 2])])
                            if lev >= 2:
                                xs, xk = Xxs[(lev - 2) % 2], (kX, (lev - 2) % 2)
                                for u in range(4):
                                    MM(PSb(pC)[:, u * 128:(u + 1) * 128], MTs[cur][:, u, :], xs[:, u, :], True, True,
                                       [(kMT, cur), xk], [('ps', pC)])
                            if lev == 1:
                                for hb in range(2):
                                    TT(AK[q3][:, hb * 2:(hb + 1) * 2, :, :].rearrange("p a b c -> p (a b c)"), PSb(gb_[hb]), maskU4[:], ALU.mult,
                                       [('ps', gb_[hb]), 'const'], [('rw_AK', q3)])
                            if lev <= 6:
                                for u in range(4):
                                    m_prev = NB[q3][:, u, 0, :] if lev == 1 else Mms[cur][:, u, :]
                                    mk = ('rw_NB', q3) if lev == 1 else (kM, cur)
                                    MM(PSb(pA)[:, u * 128:(u + 1) * 128], m_prev, MTs[cur][:, u, :], True, True, [mk, (kMT, cur)], [('ps', pA)])
                                    if lev < 6:
                                        MM(PSb(pB)[:, u * 128:(u + 1) * 128], MTs[cur][:, u, :], m_prev, True, True, [mk, (kMT, cur)], [('ps', pB)])
                            if lev <= 6:
                                CP(f2(MTs[nxt][:]), PSb(pA), [('ps', pA)], [(kMT, nxt)], eng='act')
                            if lev >= 2:
                                if lev == 7:
                                    xd, xdk = Xfin[q3], ('rw_Xfin', q3)
                                else:
                                    xd, xdk = Xxs[(lev - 1) % 2], (kX, (lev - 1) % 2)
                                TT(f2(xd[:]), PSb(pC), f2(xs[:]), ALU.add, [('ps', pC), xk], [xdk])
                            if lev < 6:
                                CP(f2(Mms[nxt][:]), PSb(pB), [('ps', pB)], [(kM, nxt)], eng='act')
                            cur = nxt
                            yield

                    def state_chain(qd):
                        q3 = qd % NSET
                        Xf = Xfin[q3]
                        xkey = ('rw_Xfin', q3)
                        for cc in range(2):
                            c = qd * 2 + cc
                            for h in range(2):
                                u = cc * 2 + h
                                hs_ = slice(h * 64, (h + 1) * 64)
                                MM(PSb(0)[:, hs_], ARh[h][:, c, 0, :], Sb_[:, :], True, False, [('rw_AR', qd // 2), 'rw_Sb'], [('ps', 0)])
                                MM(PSb(0)[:, hs_], AK[q3][:, u, 0, :], VT[:, c, hs_], False, True, [('rw_AK', q3), ('rw_T2', qd // 2)], [('ps', 0)])
                            CP(Wt[:], PSb(0)[:, 0:128], [('ps', 0)], ['rw_Wt'], eng='act')
                            yield
                            for h in range(2):
                                u = cc * 2 + h
                                hs_ = slice(h * 64, (h + 1) * 64)
                                MM(PSb(0)[:, 128 + h * 64:128 + (h + 1) * 64], Xf[:, u, :], Wt[:, hs_], True, True, [xkey, 'rw_Wt'], [('ps', 0)])
                            CP(Ut[:], PSb(0)[:, 128:256], [('ps', 0)], ['rw_Ut'], eng='dve')
                            yield
                            for h in range(2):
                                u = cc * 2 + h
                                hs_ = slice(h * 64, (h + 1) * 64)
                                so_ = PSb(0)[hs_, 256:320]
                                MM(so_, BhT[:, c, hs_], Ut[:, hs_], True, False, [('rw_T0', qd // 2), 'rw_Ut'], [('ps', 0)])
                                MM(so_, KhT[:, c, hs_], VT[:, c, hs_], False, True, [('rw_T1', qd // 2), ('rw_T2', qd // 2)], [('ps', 0)])
                                yo_ = PSb(1)[hs_, cc * 128:(cc + 1) * 128]
                                MM(yo_, Sb_[:, :], ARh[h][:, c, 1, :], True, False, ['rw_Sb', ('rw_AR', qd // 2)], [('ps', 1)])
                                MM(yo_, Ut[:, hs_], NB[q3][:, u, 1, :], False, False, ['rw_Ut', ('rw_NB', q3)], [('ps', 1)])
                                MM(yo_, VT[:, c, hs_], AK[q3][:, u, 1, :], False, True, [('rw_T2', qd // 2), ('rw_AK', q3)], [('ps', 1)])
                            STT(St[:], St[:], Pc[:, c:c + 1], PSb(0)[:, 256:320], ALU.mult, ALU.add, ['rw_St', ('rw_Pc', qd // 2), ('ps', 0)], ['rw_St'])
                            CP(Sb_[:], St[:], ['rw_St'], ['rw_Sb'], eng='act')
                            if cc == 1:
                                CP(yout[:, qd * 256:(qd + 1) * 256], PSb(1)[:, 0:256], [('ps', 1)], [ykey], eng='act')
                            yield

                    for _ in prep_part(0):
                        pass
                    prep_done = [True, False, False, False]
                    prep_idx = 1
                    prep_gen = prep_part(1)
                    next_inv = 0
                    inv_done = [False] * 8
                    active = []
                    state_q = 0
                    state_gen = None
                    states_done = 0
                    while states_done < 8:
                        while (len(active) < NINV and next_inv < 8 and next_inv < states_done + NSET
                               and prep_done[next_inv // 2]):
                            active.append((next_inv, inv_chain(next_inv)))
                            next_inv += 1
                        if state_gen is None and state_q < 8 and inv_done[state_q]:
                            state_gen = state_chain(state_q)
                        if state_gen is not None:
                            try:
                                next(state_gen)
                            except StopIteration:
                                state_gen = None
                                states_done += 1
                                state_q += 1
                        for item in list(active):
                            try:
                                next(item[1])
                            except StopIteration:
                                inv_done[item[0]] = True
                                active.remove(item)
                        if prep_gen is not None:
                            try:
                                next(prep_gen)
                            except StopIteration:
                                prep_done[prep_idx] = True
                                prep_idx += 1
                                prep_gen = prep_part(prep_idx) if prep_idx < 4 else None

                for hp in range(8):
                    hsl = slice(hp * 128, (hp + 1) * 128)
                    DMA('sp', rT[:], cT[hp * 128:(hp + 1) * 128, :], ['bcT'], ['rw_in'])
                    DMA('sp', kT[:], cT[1024 + hp * 128:1024 + (hp + 1) * 128, :], ['bcT'], ['rw_in'])
                    DMA('sp', vT[:], cT[2048 + hp * 128:2048 + (hp + 1) * 128, :], ['bcT'], ['rw_in'])
                    if l > 0:
                        DMA('sp', tmp2[:], vfT[hsl, :], ['vfT'], [*T2K])
                        for tg in range(NTG):
                            tsl = slice(tg * 512, (tg + 1) * 512)
                            MM(PSb(tg), v2[:, hsl], t1v[:, tsl], True, True, ['rw_c', 'rw_t1v'], [('ps', tg)])
                            ACT(ex[:, tsl], PSb(tg), AF.Sigmoid, [('ps', tg), 'pcol'], [*EXK], bias=pc('v0', hp))
                        TT(tmp2[:], tmp2[:], vT[:], ALU.subtract, [*T2K, 'rw_in'], [*T2K])
                        TT(tmp2[:], tmp2[:], ex[:], ALU.mult, [*T2K, *EXK], [*T2K])
                        TT(vT[:], vT[:], tmp2[:], ALU.add, [*T2K, 'rw_in'], ['rw_in'])
                    else:
                        DMA('act', vfT[hsl, :], vT[:], ['rw_in'], ['vfT'])
                    TS(kkn[:], kT[:], pc('kk', hp), None, ALU.mult, None, ['rw_in', 'pcol'], ['rw_kkn'])
                    ACT(tmp2[:], kkn[:], AF.Square, ['rw_kkn', *T2K], [*T2K])
                    for tg in range(NTG):
                        tsl = slice(tg * 512, (tg + 1) * 512)
                        MM(PSb(tg), blockones[:], tmp2[:, tsl], True, True, ['const', *T2K], [('ps', tg)])
                        ACT(ex[:, tsl], PSb(tg), AF.Ln, [('ps', tg)], [*EXK], bias=1e-30)
                    ACT(ex[:], ex[:], AF.Exp, [*EXK], [*EXK], scale=-0.5)
                    TS(ex[:], ex[:], 1e12, None, ALU.min, None, [*EXK], [*EXK])
                    TT(kkn[:], kkn[:], ex[:], ALU.mult, ['rw_kkn', *EXK], ['rw_kkn'])
                    CP(vR[:], vT[:, ::-1], ['rw_in'], ['rw_vR'], eng='act')
                    for d in range(2):
                        ps_ = slice(d * 64, d * 64 + 48)
                        if d == 0:
                            r_ap, k_ap, kk_ap, v_ = rT[:], kT[:], kkn[:], vT
                        else:
                            r_ap, k_ap, kk_ap, v_ = rT[:, ::-1], kT[:, ::-1], kkn[:, ::-1], vR
                        if d == 0:
                            scan_unit(hp, d, r_ap, k_ap, kk_ap, v_, ysum, 'rw_ysum')
                        else:
                            scan_unit(hp, d, r_ap, k_ap, kk_ap, v_, yT_, 'rw_y')
                            TT(ysum[:], ysum[:], yT_[:, ::-1], ALU.add, ['rw_y', 'rw_ysum'], ['rw_ysum'])
                    for tg in range(NTG):
                        tsl = slice(tg * 512, (tg + 1) * 512)
                        k0 = tg % 2
                        MM(PSb(k0), blockones[:], ysum[:, tsl], True, True, ['const', 'rw_ysum'], [('ps', k0)])
                        STT(tmp2[:, tsl], PSb(k0), -1.0 / 64, ysum[:, tsl], ALU.mult, ALU.add, [('ps', k0), 'rw_ysum', *T2K], [*T2K])
                        ACT(ex[:, tsl], tmp2[:, tsl], AF.Square, [*T2K, *EXK], [*EXK])
                        MM(PSb(2 + k0), blockones[:], ex[:, tsl], True, True, ['const', *EXK], [('ps', 2 + k0)])
                        ACT(ex[:, tsl], PSb(2 + k0), AF.Ln, [('ps', 2 + k0)], [*EXK], scale=1.0 / 64, bias=GN_EPS)
                        ACT(ex[:, tsl], ex[:, tsl], AF.Exp, [*EXK], [*EXK], scale=-0.5)
                        TT(tmp2[:, tsl], tmp2[:, tsl], ex[:, tsl], ALU.mult, [*T2K, *EXK], [*T2K])
                        TS(tmp2[:, tsl], tmp2[:, tsl], pc('lng', hp), pc('lnb', hp), ALU.mult, ALU.add, [*T2K, 'pcol'], [*T2K])
                        MM(PSb(4 + k0), blockones[:], rkacc[:, tsl], True, True, ['const', ('rw_rkacc', tg)], [('ps', 4 + k0)])
                        TT(ex[:, tsl], PSb(4 + k0), vT[:, tsl], ALU.mult, [('ps', 4 + k0), 'rw_in', *EXK], [*EXK])
                        TT(tmp2[:, tsl], tmp2[:, tsl], ex[:, tsl], ALU.add, [*T2K, *EXK], [*T2K])
                        MM(PSb(6 + k0), g2[:, hsl], sgd[:, tsl], True, True, ['rw_c', 'rw_sgd'], [('ps', 6 + k0)])
                        TT(ob[k0][:], tmp2[:, tsl], PSb(6 + k0), ALU.mult, [*T2K, ('ps', 6 + k0)], [('rw_ob', k0)])
                        DMA('act', ycT[hsl, tsl], ob[k0][:], [('rw_ob', k0)], ['ycT'])
                S.barrier()


        def merge_phase(l):
            with ExitStack() as ph:
                ph.enter_context(nc.named_scope("merge"))
                PW = 256
                yT3 = [sb("mg_y%d" % i, [128, 8, T], BF16, st=ph) for i in range(3)]
                wpb = [[sb("mg_w%d_%d" % (i, j), [128, 8, PW], BF16, st=ph) for j in range(2)] for i in range(3)]
                gt = [[sb("mg_g%d_%d" % (i, j), [128, T], BF16, st=ph) for j in range(2)] for i in range(3)]
                ta = [sb("mg_ta%d" % i, [128, 512], st=ph) for i in range(2)]
                tb = [sb("mg_tb%d" % i, [128, 512], st=ph) for i in range(2)]
                mo = [sb("mg_mo%d" % i, [128, 512], BF16, st=ph) for i in range(2)]
                for i, src in enumerate((yaT, ybT, ycT)):
                    DMA('sp', yT3[i][:], src.rearrange("(kt p) t -> p kt t", p=128), ['yaT', 'ybT', 'ycT'], [('mg_y', i)])
                Wb = [W[n][l].rearrange("(kt p) c -> p kt c", p=128) for n in ('w_branch_a', 'w_branch_b', 'w_branch_c')]
                n = 0
                for pi in range(D // PW):
                    b = pi % 2
                    for i in range(3):
                        DMA('pool', wpb[i][b][:], Wb[i][:, :, pi * PW:(pi + 1) * PW], (), [('mg_w', i, b)])
                    for ci in range(PW // 128):
                        ct = pi * (PW // 128) + ci
                        gb_ = ct % 2
                        for i in range(3):
                            DMA('sp', gt[i][gb_][:], gT[i * D + ct * 128:i * D + (ct + 1) * 128, :], ['gT'], [('mg_g', i, gb_)])
                        for tg in range(NTG):
                            tsl = slice(tg * 512, (tg + 1) * 512)
                            k0 = n % 2
                            base = (n % 2) * 3
                            n += 1
                            for i in range(3):
                                for kt in range(8):
                                    MM(PSb(base + i), wpb[i][b][:, kt, ci * 128:(ci + 1) * 128], yT3[i][:, kt, tsl], kt == 0, kt == 7,
                                       [('mg_w', i, b), ('mg_y', i)], [('ps', base + i)])
                            TT(ta[k0][:], PSb(base + 0), gt[0][gb_][:, tsl], ALU.mult, [('ps', base + 0), ('mg_g', 0, gb_)], [('mg_ta', k0)])
                            TT(tb[k0][:], PSb(base + 1), gt[1][gb_][:, tsl], ALU.mult, [('ps', base + 1), ('mg_g', 1, gb_)], [('mg_tb', k0)])
                            TT(ta[k0][:], ta[k0][:], tb[k0][:], ALU.add, [('mg_ta', k0), ('mg_tb', k0)], [('mg_ta', k0)], eng='pool')
                            TT(tb[k0][:], PSb(base + 2), gt[2][gb_][:, tsl], ALU.mult, [('ps', base + 2), ('mg_g', 2, gb_), ('mg_ta', k0)], [('mg_tb', k0)])
                            TT(mo[k0][:], ta[k0][:], tb[k0][:], ALU.add, [('mg_ta', k0), ('mg_tb', k0)], [('mg_mo', k0)], eng='pool')
                            DMA('act', mgT[ct * 128:(ct + 1) * 128, tsl], mo[k0][:], [('mg_mo', k0)], ['mgT'])
                S.barrier()

        def resid_epi(xl, xo):
            cnt = [0]

            def epi(ci, c0, m, tg, ps, pk):
                k = cnt[0] % 3
                cnt[0] += 1
                tsl = slice(tg * 512, (tg + 1) * 512)
                DMA('sp', xl[k][:], xT[c0:c0 + 128, tsl], [('xT', ci, tg)], [('rs_xl', k)])
                TT(xo[k][:], ps, xl[k][:], ALU.add, pk + [('rs_xl', k)], [('rs_xo', k)])
                DMA('act', xT[c0:c0 + 128, tsl], xo[k][:], [('rs_xo', k)], [('xT', ci, tg)])
            return epi

        def outproj_phase(l):
            with ExitStack() as ph:
                ph.enter_context(nc.named_scope("outp"))
                mT = sb("op_mT", [128, KD, T], BF16, st=ph)
                wp = [sb("op_wp%d" % i, [128, KD, 512], BF16, st=ph) for i in range(2)]
                xl = [sb("op_xl%d" % i, [128, 512], st=ph) for i in range(3)]
                xo = [sb("op_xo%d" % i, [128, 512], st=ph) for i in range(3)]
                DMA('sp', mT[:], mgT.rearrange("(kt p) t -> p kt t", p=128), ['mgT'], ['op_mT'])
                linear_fm(W['w_out'][l].rearrange("(kt p) c -> p kt c", p=128), KD, [(i * 128, 128) for i in range(16)],
                          lambda kt, tg: mT[:, kt, tg * 512:(tg + 1) * 512], lambda kt, tg: 'op_mT', resid_epi(xl, xo), wp, 'op_wp')
                S.barrier()

        def ffn_phase(l):
            with ExitStack() as hs:
                hT = sb("hT2", [128, KD, T], BF16, st=hs)
                norm_phase(hT, 'nfg')
                with ExitStack() as ph:
                    ph.enter_context(nc.named_scope("ffn_up"))
                    PW = 256
                    wg = [sb("ff_wg%d" % i, [128, KD, PW], BF16, st=ph) for i in range(2)]
                    wu = [sb("ff_wu%d" % i, [128, KD, PW], BF16, st=ph) for i in range(2)]
                    sg = [sb("ff_sg%d" % i, [128, 512], st=ph) for i in range(2)]
                    ao = [sb("ff_ao%d" % i, [128, 512], BF16, st=ph) for i in range(2)]
                    Wg = W['w_ffn_gate'][l].rearrange("(kt p) c -> p kt c", p=128)
                    Wu = W['w_ffn_up'][l].rearrange("(kt p) c -> p kt c", p=128)
                    n = 0
                    for pi in range(D_FF // PW):
                        b = pi % 2
                        DMA('pool', wg[b][:], Wg[:, :, pi * PW:(pi + 1) * PW], (), [('ff_wg', b)])
                        DMA('pool', wu[b][:], Wu[:, :, pi * PW:(pi + 1) * PW], (), [('ff_wu', b)])
                        for ci in range(PW // 128):
                            ft = pi * (PW // 128) + ci
                            for tg in range(NTG):
                                tsl = slice(tg * 512, (tg + 1) * 512)
                                k0 = n % 2
                                bG = (n % 4) * 2
                                bU = bG + 1
                                n += 1
                                for kt in range(KD):
                                    MM(PSb(bG), wg[b][:, kt, ci * 128:(ci + 1) * 128], hT[:, kt, tsl], kt == 0, kt == KD - 1,
                                       [('ff_wg', b), ('hT', kt, tg)], [('ps', bG)])
                                for kt in range(KD):
                                    MM(PSb(bU), wu[b][:, kt, ci * 128:(ci + 1) * 128], hT[:, kt, tsl], kt == 0, kt == KD - 1,
                                       [('ff_wu', b), ('hT', kt, tg)], [('ps', bU)])
                                ACT(sg[k0][:], PSb(bG), AF.Silu, [('ps', bG)], [('ff_sg', k0)])
                                TT(ao[k0][:], sg[k0][:], PSb(bU), ALU.mult, [('ff_sg', k0), ('ps', bU)], [('ff_ao', k0)])
                                DMA('sp', actT[ft * 128:(ft + 1) * 128, tsl], ao[k0][:], [('ff_ao', k0)], ['actT'])
                    S.barrier()
            with ExitStack() as ph:
                ph.enter_context(nc.named_scope("ffn_down"))
                KF = D_FF // 128
                PW = 256
                TH = 1024
                asb = sb("ff_act", [128, KF, TH], BF16, st=ph)
                wd = [sb("ff_wd%d" % i, [128, KF, PW], BF16, st=ph) for i in range(2)]
                xl = [sb("ff_xl%d" % i, [128, 512], st=ph) for i in range(3)]
                xo = [sb("ff_xo%d" % i, [128, 512], st=ph) for i in range(3)]
                Wd = W['w_ffn_down'][l].rearrange("(kt p) c -> p kt c", p=128)
                aTv = actT.rearrange("(kt p) t -> p kt t", p=128)
                n = 0
                pn = 0
                for th in range(T // TH):
                    for kq in range(4):
                        ks = slice(kq * 11, (kq + 1) * 11)
                        DMA('sp', asb[:, ks, :], aTv[:, ks, th * TH:(th + 1) * TH], ['actT'], [('ff_act', kq)])
                    epi = resid_epi(xl, xo)
                    for pi in range(D // PW):
                        b = pn % 2
                        pn += 1
                        DMA('pool', wd[b][:], Wd[:, :, pi * PW:(pi + 1) * PW], (), [('ff_wd', b)])
                        for ci in range(PW // 128):
                            ct = pi * (PW // 128) + ci
                            for tgi in range(TH // 512):
                                tg = th * (TH // 512) + tgi
                                bank = n % 8
                                n += 1
                                for kt in range(KF):
                                    MM(PSb(bank), wd[b][:, kt, ci * 128:(ci + 1) * 128], asb[:, kt, tgi * 512:(tgi + 1) * 512], kt == 0, kt == KF - 1,
                                       [('ff_wd', b), ('ff_act', kt // 11)], [('ps', bank)])
                                epi(ct, ct * 128, 128, tg, PSb(bank), [('ps', bank)])
                S.barrier()

        def final_phase():
            with ExitStack() as ph:
                xin = [sb("fx%d" % i, [128, KD, 512], st=ph) for i in range(2)]
                sq = [sb("fsq%d" % i, [128, 512], st=ph) for i in range(2)]
                rs = [sb("frs%d" % i, [128, 512], st=ph) for i in range(2)]
                ot = [sb("fot%d" % i, [128, D], st=ph) for i in range(2)]
                n = 0
                no = 0
                for tg in range(NTG):
                    b = tg % 2
                    tsl = slice(tg * 512, (tg + 1) * 512)
                    DMA('sp', xin[b][:], xTv[:, :, tsl], ['xT'], [('fx', b)])
                    for dk in range(KD):
                        ACT(sq[dk % 2][:], xin[b][:, dk, :], AF.Square, [('fx', b)], [('fsq', dk % 2)])
                        MM(PSb(b), onesF[:], sq[dk % 2][:], dk == 0, dk == KD - 1, [('fsq', dk % 2), 'const'], [('ps', b)])
                    ACT(rs[b][:], PSb(b), AF.Sqrt, [('ps', b)], [('frs', b)], scale=1.0 / D, bias=RMS_EPS)
                    RECIP(rs[b][:], rs[b][:], [('frs', b)], [('frs', b)])
                    for dk in range(KD):
                        STT(xin[b][:, dk, :], xin[b][:, dk, :], pc('nfin', dk), rs[b][:], ALU.mult, ALU.mult,
                            [('fx', b), 'pcol', ('frs', b)], [('fx', b)])
                    for tt in range(4):
                        o_ = no % 2
                        no += 1
                        for q in range(4):
                            bank = 2 + n % 6
                            n += 1
                            for j in range(4):
                                dk = q * 4 + j
                                TR(PSb(bank)[:, j * 128:(j + 1) * 128], xin[b][:, dk, tt * 128:(tt + 1) * 128], identF[:],
                                   [('fx', b), 'const'], [('ps', bank)])
                            EVAC(ot[o_][:, q * 512:(q + 1) * 512], PSb(bank), [('ps', bank)], [('fot', o_)])
                        row = (tg * 4 + tt) * 128
                        DMA('act', out_d[row:row + 128, :], ot[o_][:], [('fot', o_)], [('out', row)])
                S.barrier()

        for l in range(NL):
            DMA('sp', pcol[:], pcol_d[l], (), ['pcol'])
            Winv = W['w_in'][l].rearrange("(kt p) c -> p kt c", p=128)
            with ExitStack() as hs:
                hT = sb("hT", [128, KD, T], BF16, st=hs)
                norm_phase(hT, 'nmg')
                if l == 0 and 'hTd' in DBG:
                    DMA('sp', DBG['hTd'].rearrange("(dk p) t -> p dk t", p=128), hT[:], [('hT', dk, tg) for dk in range(KD) for tg in range(NTG)], [('dbg', 'hTd')])

                def h_rhs(kt, tg):
                    return hT[:, kt, tg * 512:(tg + 1) * 512]

                def h_key(kt, tg):
                    return ('hT', kt, tg)

                with ExitStack() as ph:
                    ph.enter_context(nc.named_scope("proj"))
                    wp = [sb("wp%d" % i, [128, KD, 512], BF16, st=ph) for i in range(2)]
                    ob = [sb("pob%d" % i, [128, 512], F32, st=ph) for i in range(3)]
                    obh = [sb("pobh%d" % i, [128, 512], BF16, st=ph) for i in range(3)]
                    zc = [sb("pzc%d" % i, [128, T], F32, st=ph) for i in range(2)]
                    ccol = sb("ccol", [128, 29], F32, st=ph)
                    cnt = [0]
                    o_p, _ = PC['mup']
                    o_n, _ = PC['mun']
                    TT(ccol[:], pcol[:, o_p:o_p + 29], pcol[:, o_n:o_n + 29], ALU.add, ['pcol'], ['ccol'])
                    TS(ccol[:], ccol[:], -1.0, 1.0, ALU.mult, ALU.add, ['ccol'], ['ccol'])

                    def epi_u(ci, c0, m, tg, ps, pk):
                        k = cnt[0] % 3
                        cnt[0] += 1
                        ACT(ob[k][:], ps, AF.Gelu, pk, [('pob', k)])
                        DMA('sp', uT[c0:c0 + 128, tg * 512:(tg + 1) * 512], ob[k][:], [('pob', k)], ['uT'])

                    def epi_g(ci, c0, m, tg, ps, pk):
                        k = cnt[0] % 3
                        cnt[0] += 1
                        ACT(obh[k][:], ps, AF.Sigmoid, pk, [('pobh', k)])
                        r0 = c0 - OFF_G
                        DMA('sp', gT[r0:r0 + 128, tg * 512:(tg + 1) * 512], obh[k][:], [('pobh', k)], ['gT'])

                    def tap3(dst, r0, m, ps, pk, a_ap, b_ap, p_ap, n_ap, keys):
                        k = cnt[0] % 2
                        cnt[0] += 1
                        z = zc[k]
                        ACT(z[0:m, :], ps, AF.Identity, pk + keys, [('pzc', k)], scale=a_ap, bias=b_ap)
                        STT(z[0:m, 1:T], ps[:, 0:T - 1], p_ap, z[0:m, 1:T], ALU.mult, ALU.add, pk + keys + [('pzc', k)], [('pzc', k)])
                        STT(z[0:m, 0:T - 1], ps[:, 1:T], n_ap, z[0:m, 0:T - 1], ALU.mult, ALU.add, pk + keys + [('pzc', k)], [('pzc', k)])
                        DMA('sp', dst[r0:r0 + m, :], z[0:m, :], [('pzc', k)], ['bcT'])

                    def epi_b(ci, c0, m, tg, ps, pk):
                        tap3(bT, c0 - OFF_B, m, ps, pk, pc('cw1', ci), pc('cb', ci), pc('cw0', ci), pc('cw2', ci), ['pcol'])

                    def epi_c(ci, c0, m, tg, ps, pk):
                        tap3(cT, c0 - OFF_C, m, ps, pk, ccol[0:m, ci:ci + 1], 0.0, pc('mup', ci, m), pc('mun', ci, m), ['pcol', 'ccol'])

                    linear_fm(Winv, KD, [(i * 128, 128) for i in range(8)], h_rhs, h_key, epi_u, wp, 'wp')
                    linear_fm(Winv, KD, [(OFF_B + i * 128, 128) for i in range(24)], h_rhs, h_key, epi_b, wp, 'wp', full=True)
                    linear_fm(Winv, KD, [(OFF_C + c0, m) for (c0, m) in C_TILES], h_rhs, h_key, epi_c, wp, 'wp', full=True)
                    linear_fm(Winv, KD, [(OFF_G + i * 128, 128) for i in range(48)], h_rhs, h_key, epi_g, wp, 'wp')
                S.barrier()
                if l == 0:
                    dump('uT', uT)
                    dump('bT', bT)
                    dump('cT', cT)
                    dump('gT', gT)

                with ExitStack() as ph:
                    ph.enter_context(nc.named_scope("mixa"))
                    wv = sb("ma_wv", [128, KD, 1024], BF16, st=ph)
                    lng = sb("ma_lng", [128, 1024], F32, st=ph)
                    lnb = sb("ma_lnb", [128, 1024], F32, st=ph)
                    bsb = sb("ma_bsb", [128, 8, 128], F32, st=ph)
                    wsn = sb("ma_wsn", [128, 8, 128], F32, st=ph)
                    wsT = sb("ma_wsT", [128, 8, 128], BF16, st=ph)
                    vg = [sb("ma_vg%d" % i, [128, 1024], F32, st=ph) for i in range(2)]
                    vc = [sb("ma_vc%d" % i, [128, 1024], F32, st=ph) for i in range(2)]
                    vln = [sb("ma_vln%d" % i, [128, 1024], BF16, st=ph) for i in range(2)]
                    ut = [sb("ma_ut%d" % i, [128, 8, 128], F32, st=ph) for i in range(2)]
                    ya = [sb("ma_ya%d" % i, [128, 8, 128], BF16, st=ph) for i in range(2)]
                    tm = [sb("ma_tm%d" % i, [128, 512], F32, st=ph) for i in range(2)]
                    stt = [sb("ma_st%d" % i, [128, 8], F32, st=ph) for i in range(2)]
                    DMA('pool', wv[:], Winv[:, :, A_W:2 * A_W], (), ['ma_wv'])
                    DMA('sp', lng[:], W['gm_ln_g'][l:l + 1, :].broadcast_to([128, 1024]), (), ['ma_c'])
                    DMA('sp', lnb[:], W['gm_ln_b'][l:l + 1, :].broadcast_to([128, 1024]), (), ['ma_c'])
                    DMA('sp', bsb[:].rearrange("p g q -> p (g q)"),
                        W['gm_bs'][l:l + 1].rearrange("o g q -> o (g q)").broadcast_to([128, 1024]), (), ['ma_c'])
                    DMA('sp', wsn[:], W['gm_ws'][l].rearrange("g p q -> p g q"), (), ['ma_wsn'])
                    for hb in range(2):
                        for j in range(4):
                            g = hb * 4 + j
                            TR(PSb(hb)[:, j * 128:(j + 1) * 128], wsn[:, g, :], identF[:], ['ma_wsn', 'const'], [('ps', hb)])
                        EVAC(wsT[:, hb * 4:(hb + 1) * 4, :].rearrange("p g q -> p (g q)"), PSb(hb), [('ps', hb)], ['ma_wsT'])
                    uTv = uT.rearrange("(g d) t -> d g t", d=128)
                    yaTv = yaT.rearrange("(g d) t -> d g t", d=128)
                    for i in range(NT):
                        b = i % 2
                        tsl = slice(i * 128, (i + 1) * 128)
                        tgi = i // 4
                        DMA('sp', ut[b][:], uTv[:, :, tsl], ['uT'], [('ma_ut', b)])
                        for half in range(2):
                            bank = 2 + b * 2 + half
                            for kt in range(KD):
                                MM(PSb(bank), hT[:, kt, tsl], wv[:, kt, half * 512:(half + 1) * 512], kt == 0, kt == KD - 1,
                                   [('hT', kt, tgi), 'ma_wv'], [('ps', bank)])
                            ACT(vg[b][:, half * 512:(half + 1) * 512], PSb(bank), AF.Gelu, [('ps', bank)], [('ma_vg', b, half), ('ma_st', b)],
                                accum_out=stt[b][:, half:half + 1])
                        TT(stt[b][:, 2:3], stt[b][:, 0:1], stt[b][:, 1:2], ALU.add, [('ma_st', b)], [('ma_st', b)])
                        TS(stt[b][:, 3:4], stt[b][:, 2:3], -1.0 / A_W, None, ALU.mult, None, [('ma_st', b)], [('ma_st', b)])
                        TS(vc[b][:], vg[b][:], stt[b][:, 3:4], None, ALU.add, None,
                           [('ma_vg', b, 0), ('ma_vg', b, 1), ('ma_st', b)], [('ma_vc', b)])
                        ACT(vg[b][:], vc[b][:], AF.Square, [('ma_vc', b)], [('ma_vg', b, 0), ('ma_vg', b, 1), ('ma_st', b)],
                            accum_out=stt[b][:, 4:5])
                        ACT(stt[b][:, 5:6], stt[b][:, 4:5], AF.Sqrt, [('ma_st', b)], [('ma_st', b)], scale=1.0 / A_W, bias=LN_EPS)
                        RECIP(stt[b][:, 6:7], stt[b][:, 5:6], [('ma_st', b)], [('ma_st', b)])
                        STT(vc[b][:], vc[b][:], stt[b][:, 6:7], lng[:], ALU.mult, ALU.mult, [('ma_vc', b), ('ma_st', b), 'ma_c'], [('ma_vc', b)])
                        TT(vln[b][:], vc[b][:], lnb[:], ALU.add, [('ma_vc', b), 'ma_c'], [('ma_vln', b)])
                        for hb in range(2):
                            bank = 6 + hb
                            for j in range(4):
                                g = hb * 4 + j
                                MM(PSb(bank)[:, j * 128:(j + 1) * 128], vln[b][:, g * 128:(g + 1) * 128], wsT[:, g, :], True, True,
                                   [('ma_vln', b), 'ma_wsT'], [('ps', bank)])
                            TT(tm[hb][:], PSb(bank), bsb[:, hb * 4:(hb + 1) * 4, :].rearrange("p g q -> p (g q)"), ALU.add,
                               [('ps', bank), 'ma_c'], [('ma_tm', hb)])
                            TT(ya[b][:, hb * 4:(hb + 1) * 4, :].rearrange("p g q -> p (g q)"), tm[hb][:],
                               ut[b][:, hb * 4:(hb + 1) * 4, :].rearrange("p g q -> p (g q)"), ALU.mult,
                               [('ma_tm', hb), ('ma_ut', b)], [('ma_ya', b)])
                        DMA('act', yaTv[:, :, tsl], ya[b][:], [('ma_ya', b)], ['yaT'])
                S.barrier()
            if l == 0:
                dump('yaT', yaT)
            if stop == 'mixa':
                break
            hyena_phase(l)
            if stop == 'hy':
                break
            if l == 0:
                dump('spec', spec)
                dump('ybT', ybT)
            rwkv_phase(l)
            if l == 0:
                dump('ycT', ycT)
            if stop == 'rw':
                break
            merge_phase(l)
            if l == 0:
                dump('mergedT', mgT)
            outproj_phase(l)
            if stop == 'outp':
                break
            ffn_phase(l)
        dump('xT', xT)
        if stop is None:
            final_phase()

        S.barrier()
    return nc


_NC_CACHE = {}


def kernel(**inputs):
    if 'nc' not in _NC_CACHE:
        _NC_CACHE['nc'] = build()
        _NC_CACHE['consts'] = make_consts()
    nc = _NC_CACHE['nc']
    consts = _NC_CACHE['consts']
    inp = {k_: np.ascontiguousarray(np.asarray(v)) for k_, v in inputs.items()}
    pcol = make_pcol(inp)
    base = {n: inp[n] for n in WEIGHT_SHAPES}
    base.update(consts)
    base['pcol'] = pcol
    NB_ = inp['x'].shape[0]
    in_maps = []
    for c in range(NCORES_USED):
        m = dict(base)
        m['x'] = np.ascontiguousarray(inp['x'][c % NB_])
        in_maps.append(m)
    res = run_bass_kernel_spmd(nc, in_maps, core_ids=list(range(NCORES_USED)))
    out = np.stack([np.asarray(res.results[b]['out'], dtype=np.float32) for b in range(NB_)], axis=0)
    return out
```

```python
import math
import numpy as np
import ml_dtypes
from contextlib import ExitStack
import concourse.bass as bass
import concourse.mybir as mybir
from concourse.bass_utils import run_bass_kernel_spmd

F32 = mybir.dt.float32
BF16 = mybir.dt.bfloat16
AF = mybir.ActivationFunctionType
ALU = mybir.AluOpType
AX = mybir.AxisListType

NCORES = 8
NCORES_USED = 4
T = 2048
D = 2048
DEPTH = 4
A_W = 1024
B_W = 1024
C_W = 1024
C_IN = 3392
N_IN = 14656
D_FF = 5632
NT = T // 128
NTG = T // 512
KD = D // 128
RMS_EPS = 1e-6
LN_EPS = 1e-5
GN_EPS = 64e-5
HY_MIN_DECAY = -math.log(1e-2) / 1.5
HY_MAX_DECAY = -math.log(1e-2) / 0.3
OFF_B = 2 * A_W
OFF_C = OFF_B + 3 * B_W
OFF_G = OFF_C + C_IN

WEIGHT_SHAPES = {
    'w_in': (DEPTH, D, N_IN), 'gm_ln_g': (DEPTH, A_W), 'gm_ln_b': (DEPTH, A_W),
    'gm_ws': (DEPTH, 8, 128, 128), 'gm_bs': (DEPTH, 8, 128),
    'hy_w1': (DEPTH, 33, 64), 'hy_w2': (DEPTH, 64, 64), 'hy_w3': (DEPTH, 64, 64), 'hy_w4': (DEPTH, 64, 4096),
    'hy_log_decay': (DEPTH, 2, 2, 1024), 'hy_bias_d': (DEPTH, 2, 1024),
    'rw_w2': (DEPTH, 2, 48, 1024), 'rw_a2': (DEPTH, 2, 48, 1024),
    'rw_v1': (DEPTH - 1, 1024, 32), 'rw_v2': (DEPTH - 1, 32, 1024), 'rw_g2': (DEPTH, 128, 1024),
    'w_branch_a': (DEPTH, A_W, D), 'w_branch_b': (DEPTH, B_W, D), 'w_branch_c': (DEPTH, C_W, D),
    'w_out': (DEPTH, D, D), 'w_ffn_gate': (DEPTH, D, D_FF), 'w_ffn_up': (DEPTH, D, D_FF),
    'w_ffn_down': (DEPTH, D_FF, D),
}

PC = {}
_o = 0
for _n, _c in [('nmg', 16), ('nfg', 16), ('cw0', 24), ('cw1', 24), ('cw2', 24), ('cb', 24), ('mup', 29), ('mun', 29),
               ('w0', 16), ('a0', 16), ('v0', 8), ('kk', 8), ('ka', 8), ('rk', 8), ('lng', 8), ('lnb', 8),
               ('hyb', 4), ('nfin', 16)]:
    PC[_n] = (_o, _c)
    _o += _c
NPC = _o
C_TILES = [(i * 128, 128) for i in range(25)] + [(3200, 48), (3248, 48), (3296, 48), (3344, 48)]


def _cols(v):
    return np.ascontiguousarray(np.asarray(v, np.float32).reshape(-1, 128).T)


def make_pcol(inp):
    pc = np.zeros((DEPTH, 128, NPC), np.float32)

    def put(l, name, arr):
        o, c = PC[name]
        assert arr.shape == (128, c), (name, arr.shape)
        pc[l, :, o:o + c] = arr
    for l in range(DEPTH):
        put(l, 'nmg', _cols(inp['norm_mix_g'][l]))
        put(l, 'nfg', _cols(inp['norm_ffn_g'][l]))
        for j in range(3):
            put(l, 'cw%d' % j, _cols(inp['hy_conv_w'][l, j]))
        put(l, 'cb', _cols(inp['hy_conv_b'][l]))
        for nm, src in (('mup', 'rw_mu_prev'), ('mun', 'rw_mu_next')):
            a = np.zeros((128, 29), np.float32)
            for i, (c0, m) in enumerate(C_TILES):
                a[:m, i] = inp[src][l, c0:c0 + m]
            put(l, nm, a)
        put(l, 'w0', _cols(inp['rw_w0'][l]))
        put(l, 'a0', _cols(inp['rw_a0'][l]))
        if l > 0:
            put(l, 'v0', _cols(inp['rw_v0'][l - 1]))
        put(l, 'kk', _cols(inp['rw_k_k'][l]))
        put(l, 'ka', _cols(inp['rw_k_a'][l]))
        put(l, 'rk', _cols(inp['rw_r_k'][l]))
        put(l, 'lng', _cols(inp['rw_ln_g'][l]))
        put(l, 'lnb', _cols(inp['rw_ln_b'][l]))
        hb = np.zeros((128, 4), np.float32)
        hb[:64, 0] = inp['hy_b1'][l]
        hb[:64, 1] = inp['hy_b2'][l]
        hb[:64, 2] = inp['hy_b3'][l]
        hb[:64, 3] = inp['hy_freq'][l]
        put(l, 'hyb', hb)
        put(l, 'nfin', _cols(inp['norm_final_g']))
    return pc


def make_consts():
    c = {}
    c['identF'] = np.eye(128, dtype=np.float32)
    s = np.arange(128)[:, None]
    t = np.arange(128)[None, :]
    su = (s < t).astype(np.float32)
    iu = (s <= t).astype(np.float32)
    c['maskU4'] = np.concatenate([su, iu, su, iu], axis=1)
    c['maskL'] = (s > t).astype(np.float32)
    c['blockones'] = ((s // 64) == (t // 64)).astype(np.float32)
    c['onesF'] = np.ones((128, 128), np.float32)
    c['identB'] = np.eye(128).astype(ml_dtypes.bfloat16)
    c['maskL4'] = np.tile(c['maskL'], (1, 4))
    c['ident4'] = np.tile(c['identF'], (1, 4))
    rm = np.ones((128, T), np.float32)
    rm[:, 0::128] = 0.0
    c['rmask'] = rm
    n = np.arange(T, dtype=np.float64)
    f = np.arange(T, dtype=np.float64) + 0.5
    ang = 2.0 * np.pi * np.outer(n, f) / (2 * T)
    c['CT'] = np.cos(ang).astype(ml_dtypes.bfloat16)
    c['ST'] = np.sin(ang).astype(ml_dtypes.bfloat16)
    c['CF'] = np.ascontiguousarray(np.cos(ang).T).astype(ml_dtypes.bfloat16)
    c['SF'] = np.ascontiguousarray(np.sin(ang).T).astype(ml_dtypes.bfloat16)
    tt = np.linspace(0.0, 1.0, T, dtype=np.float32)[:, None]
    bands = 16
    fr = np.linspace(1e-4, bands - 1, bands, dtype=np.float32)
    a2 = (np.float32(2.0 * math.pi / T) * np.arange(T, dtype=np.float32)[:, None]) * fr[None, :]
    feats = np.concatenate([tt, np.cos(a2), -np.sin(a2)], axis=-1).astype(np.float32)
    c['featsT'] = np.ascontiguousarray(feats.T)
    c['tneg'] = np.ascontiguousarray((-tt[:, 0]).reshape(NT, 128).T)
    return c


CONST_SHAPES = {'identF': ((128, 128), F32), 'maskU4': ((128, 512), F32), 'maskL': ((128, 128), F32),
                'blockones': ((128, 128), F32), 'onesF': ((128, 128), F32), 'maskL4': ((128, 512), F32),
                'ident4': ((128, 512), F32), 'rmask': ((128, T), F32), 'identB': ((128, 128), BF16),
                'CT': ((T, T), BF16), 'ST': ((T, T), BF16), 'CF': ((T, T), BF16), 'SF': ((T, T), BF16),
                'featsT': ((33, T), F32), 'tneg': ((128, NT), F32)}


class Sched:
    NDMA = 12

    def __init__(self, nc, es):
        self.nc = nc
        self.engs = {'pe': nc.tensor, 'act': nc.scalar, 'dve': nc.vector, 'pool': nc.gpsimd, 'sp': nc.sync}
        self.sem = {k: es.enter_context(nc.semaphore("s_" + k)) for k in self.engs}
        self.cnt = {k: 0 for k in self.engs}
        self.waited = {}
        self.dsem = {}
        self.dcnt = {}
        self.dnext = {}
        for q in ('sp', 'act', 'pool'):
            self.dsem[q] = [es.enter_context(nc.semaphore("d_%s%d" % (q, i))) for i in range(self.NDMA)]
            self.dcnt[q] = [0] * self.NDMA
            self.dnext[q] = 0
        self.res = {}
        self.ninst = 0

    def _wait(self, e, tok):
        if tok[0] == 'e':
            _, f, v = tok
            if f == e and e == 'pe':
                return
            key = (e, f)
        else:
            _, q, slot, v = tok
            key = (e, 'd', q, slot)
        if self.waited.get(key, 0) >= v:
            return
        self.waited[key] = v
        sem = self.sem[tok[1]] if tok[0] == 'e' else self.dsem[tok[1]][tok[2]]
        self.engs[e].wait_ge(sem, v)

    def _deps(self, e, reads, writes):
        for r in reads:
            st = self.res.get(r)
            if st and st[0] is not None:
                self._wait(e, st[0])
        for w in writes:
            st = self.res.get(w)
            if st:
                if st[0] is not None:
                    self._wait(e, st[0])
                for t in st[1].values():
                    self._wait(e, t)

    def _record(self, tok, reads, writes):
        for r in reads:
            st = self.res.get(r)
            if st is None:
                st = self.res[r] = [None, {}]
            k = tok[1] if tok[0] == 'e' else (tok[1], tok[2])
            st[1][k] = tok
        for w in writes:
            self.res[w] = [tok, {}]

    def op(self, e, fn, reads=(), writes=()):
        self._deps(e, reads, writes)
        ins = fn()
        self.cnt[e] += 1
        ins.then_inc(self.sem[e], 1)
        self._record(('e', e, self.cnt[e]), reads, writes)
        self.ninst += 1
        return ins

    def dma(self, q, out, in_, reads=(), writes=(), **kw):
        slot = self.dnext[q]
        self.dnext[q] = (slot + 1) % self.NDMA
        if self.dcnt[q][slot] > 0:
            self._wait(q, ('d', q, slot, self.dcnt[q][slot]))
        self._deps(q, reads, writes)
        ins = self.engs[q].dma_start(out=out, in_=in_, **kw)
        self.dcnt[q][slot] += 16
        ins.then_inc(self.dsem[q][slot], 16)
        self._record(('d', q, slot, self.dcnt[q][slot]), reads, writes)
        self.ninst += 1
        return ins

    def barrier(self):
        for e in self.engs:
            for f in self.engs:
                if self.cnt[f] > 0:
                    key = (e, f)
                    if self.waited.get(key, 0) < self.cnt[f]:
                        self.waited[key] = self.cnt[f]
                        self.engs[e].wait_ge(self.sem[f], self.cnt[f])
            for q in self.dsem:
                for slot in range(self.NDMA):
                    if self.dcnt[q][slot] > 0:
                        self._wait(e, ('d', q, slot, self.dcnt[q][slot]))
        self.res = {}


def build(NL=DEPTH, dbg=(), stop=None):
    nc = bass.Bass("TRN2", target_bir_lowering=False)

    def din(name, shape, dt=F32):
        return nc.dram_tensor(name, list(shape), dt, kind="ExternalInput").ap()

    def dscr(name, shape, dt=F32):
        return nc.dram_tensor(name, list(shape), dt, kind="Internal").ap()

    def dout(name, shape, dt=F32):
        return nc.dram_tensor(name, list(shape), dt, kind="ExternalOutput").ap()

    x_in = din("x", [T, D])
    W = {n: din(n, s) for n, s in WEIGHT_SHAPES.items()}
    pcol_d = din("pcol", [DEPTH, 128, NPC])
    CD = {n: din(n, s, dt) for n, (s, dt) in CONST_SHAPES.items()}
    out_d = dout("out", [T, D])
    DBG = {}
    dbg_shapes = {'xT': ([D, T], F32), 'xT0': ([D, T], F32), 'uT': ([A_W, T], F32), 'bT': ([3 * B_W, T], F32), 'cT': ([C_IN, T], F32),
                  'gT': ([3 * D, T], BF16), 'yaT': ([A_W, T], BF16), 'ybT': ([B_W, T], BF16),
                  'ycT': ([C_W, T], BF16), 'hTd': ([D, T], BF16), 'spec': ([2, 2, T, B_W], F32),
                  'mergedT': ([D, T], BF16)}
    for n in dbg:
        DBG[n] = dout("dbg_" + n, *dbg_shapes[n])

    xT = dscr("xT_s", [D, T])
    uT = dscr("uT_s", [A_W, T])
    bT = dscr("bT_s", [3 * B_W, T])
    cT = dscr("cT_s", [C_IN, T])
    gT = dscr("gT_s", [3 * D, T], BF16)
    yaT = dscr("yaT_s", [A_W, T], BF16)
    ybT = dscr("ybT_s", [B_W, T], BF16)
    ycT = dscr("ycT_s", [C_W, T], BF16)
    mgT = dscr("mgT_s", [D, T], BF16)
    actT = dscr("actT_s", [D_FF, T], BF16)
    vfT = dscr("vfT_s", [C_W, T])
    x1tok = dscr("x1tok_s", [T, B_W])
    hfil = dscr("hfil_s", [T, 4096])
    spec = dscr("spec_s", [2, 2, T, B_W])
    xTv = xT.rearrange("(dk p) t -> p dk t", p=128)

    with ExitStack() as es:
        S = Sched(nc, es)

        uid = [0]

        def sb(name, shape, dt=F32, st=es):
            uid[0] += 1
            return st.enter_context(nc.sbuf_tensor("sb%d_%s" % (uid[0], name), list(shape), dt))

        PSA = es.enter_context(nc.psum_tensor("psA", [128, 2048], F32))
        PSB = es.enter_context(nc.psum_tensor("psB", [128, 2048], F32))

        def PSb(i):
            return (PSA if i < 4 else PSB)[:, (i % 4) * 512:(i % 4 + 1) * 512]

        def PSfull(a):
            return PSA if a == 0 else PSB

        def ACT(out, in_, func, r, w, **kw):
            return S.op('act', lambda: nc.scalar.activation(out=out, in_=in_, func=func, **kw), r, w)

        def TT(out, a, b, op, r, w, eng='dve'):
            e = nc.vector if eng == 'dve' else nc.gpsimd
            return S.op(eng, lambda: e.tensor_tensor(out=out, in0=a, in1=b, op=op), r, w)

        def TS(out, a, s1, s2, op0, op1, r, w, eng='dve'):
            e = nc.vector if eng == 'dve' else nc.gpsimd
            if s2 is None:
                return S.op(eng, lambda: e.tensor_scalar(out=out, in0=a, scalar1=s1, scalar2=None, op0=op0), r, w)
            return S.op(eng, lambda: e.tensor_scalar(out=out, in0=a, scalar1=s1, scalar2=s2, op0=op0, op1=op1), r, w)

        def STT(out, a, s, b, op0, op1, r, w, eng='dve'):
            e = nc.vector if eng == 'dve' else nc.gpsimd
            return S.op(eng, lambda: e.scalar_tensor_tensor(out=out, in0=a, scalar=s, in1=b, op0=op0, op1=op1), r, w)

        def RECIP(out, in_, r, w):
            return S.op('dve', lambda: nc.vector.reciprocal(out=out, in_=in_), r, w)

        def CP(out, in_, r, w, eng='dve'):
            if eng == 'act':
                return S.op('act', lambda: nc.scalar.copy(out=out, in_=in_), r, w)
            e = nc.vector if eng == 'dve' else nc.gpsimd
            return S.op(eng, lambda: e.tensor_copy(out=out, in_=in_), r, w)

        def MSET(ap, val, w, eng='pool'):
            e = nc.vector if eng == 'dve' else nc.gpsimd
            return S.op(eng, lambda: e.memset(ap, val), (), w)

        def MM(out, lhsT, rhs, start, stop, r, w):
            return S.op('pe', lambda: nc.tensor.matmul(out, lhsT, rhs, start=start, stop=stop), r, w)

        def TR(out, in_, ident, r, w):
            return S.op('pe', lambda: nc.tensor.transpose(out, in_, ident), r, w)

        def DMA(q, out, in_, r, w, **kw):
            return S.dma(q, out, in_, r, w, **kw)

        cp_flip = [0]

        def EVAC(out, in_, r, w):
            cp_flip[0] ^= 1
            return CP(out, in_, r, w, eng='act' if cp_flip[0] else 'dve')

        identF = sb("identF", [128, 128])
        maskU4 = sb("maskU4", [128, 512])
        maskL = sb("maskL", [128, 128])
        blockones = sb("blockones", [128, 128])
        onesF = sb("onesF", [128, 128])
        tneg = sb("tneg", [128, NT])
        maskL4 = sb("maskL4", [128, 512])
        ident4 = sb("ident4", [128, 512])
        identB = sb("identB", [128, 128], BF16)
        pcol = sb("pcol", [128, NPC])
        for n, t_ in (('identF', identF), ('maskU4', maskU4), ('maskL', maskL), ('blockones', blockones),
                      ('onesF', onesF), ('tneg', tneg), ('maskL4', maskL4), ('ident4', ident4), ('identB', identB)):
            DMA('sp', t_[:], CD[n][:, :], (), ['const'])

        def pc(name, i=0, m=128):
            o, c = PC[name]
            return pcol[0:m, o + i:o + i + 1]

        def dump(name, src_ap):
            if name in DBG:
                DMA('sp', DBG[name], src_ap, ['ALLDRAM'], [('dbg', name)])

        with ExitStack() as ph:
            xt = [sb("p0x%d" % i, [128, D], st=ph) for i in range(2)]
            stg = [sb("p0s%d" % i, [128, 4, 128], st=ph) for i in range(4)]
            n = 0
            for i in range(NT):
                b = i % 2
                DMA('sp', xt[b][:], x_in[i * 128:(i + 1) * 128, :], (), [('p0x', b)])
                for q in range(4):
                    bank = n % 8
                    s_ = n % 4
                    for j in range(4):
                        dk = q * 4 + j
                        TR(PSb(bank)[:, j * 128:(j + 1) * 128], xt[b][:, dk * 128:(dk + 1) * 128], identF[:],
                           [('p0x', b), 'const'], [('ps', bank)])
                    EVAC(stg[s_][:].rearrange("p a b -> p (a b)"), PSb(bank), [('ps', bank)], [('p0s', s_)])
                    DMA('act', xTv[:, q * 4:(q + 1) * 4, i * 128:(i + 1) * 128], stg[s_][:], [('p0s', s_)], ['xT'])
                    n += 1
        S.barrier()
        if 'xT0' in DBG:
            DMA('sp', DBG['xT0'], xT, (), [('dbg', 'xT0')])

        def norm_phase(hT, gname):
            with ExitStack() as ph:
                ph.enter_context(nc.named_scope("norm"))
                xin = [sb("nx%d" % i, [128, KD, 512], st=ph) for i in range(2)]
                sq = [sb("nsq%d" % i, [128, 512], st=ph) for i in range(2)]
                rs = [sb("nrs%d" % i, [128, 512], st=ph) for i in range(2)]
                for tg in range(NTG):
                    b = tg % 2
                    tsl = slice(tg * 512, (tg + 1) * 512)
                    DMA('sp', xin[b][:], xTv[:, :, tsl], ['xT'], [('nx', b)])
                    for dk in range(KD):
                        ACT(sq[dk % 2][:], xin[b][:, dk, :], AF.Square, [('nx', b)], [('nsq', dk % 2)])
                        MM(PSb(b), onesF[:], sq[dk % 2][:], dk == 0, dk == KD - 1, [('nsq', dk % 2), 'const'], [('ps', b)])
                    ACT(rs[b][:], PSb(b), AF.Sqrt, [('ps', b)], [('nrs', b)], scale=1.0 / D, bias=RMS_EPS)
                    RECIP(rs[b][:], rs[b][:], [('nrs', b)], [('nrs', b)])
                    for dk in range(KD):
                        STT(hT[:, dk, tsl], xin[b][:, dk, :], pc(gname, dk), rs[b][:], ALU.mult, ALU.mult,
                            [('nx', b), 'pcol', ('nrs', b)], [('hT', dk, tg)])
                S.barrier()

        def linear_fm(Wv, KT, ctiles, rhs_fn, rkey_fn, epi, wp, wpname, full=False, banks=(0, 1, 2, 3, 4, 5, 6, 7)):
            panels = []
            cur = None
            for ci, (c0, m) in enumerate(ctiles):
                if cur is None or (c0 + m - cur[0]) > 512 or c0 != cur[1]:
                    cur = [c0, c0, []]
                    panels.append(cur)
                cur[2].append((ci, c0, m))
                cur[1] = c0 + m
            nb = 0
            for pi, (p0, p1, tl) in enumerate(panels):
                b = pi % 2
                DMA('pool', wp[b][:, 0:KT, 0:p1 - p0], Wv[:, :, p0:p1], (), [(wpname, b)])
                for (ci, c0, m) in tl:
                    lo = c0 - p0
                    if full:
                        a = nb % 2
                        nb += 1
                        for tg in range(NTG):
                            bank = a * 4 + tg
                            for kt in range(KT):
                                MM(PSb(bank)[0:m, :], wp[b][:, kt, lo:lo + m], rhs_fn(kt, tg), kt == 0, kt == KT - 1,
                                   [(wpname, b), rkey_fn(kt, tg)], [('ps', bank)])
                        epi(ci, c0, m, None, PSfull(a)[0:m, :], [('ps', a * 4 + j) for j in range(4)])
                    else:
                        for tg in range(NTG):
                            bank = banks[nb % len(banks)]
                            nb += 1
                            for kt in range(KT):
                                MM(PSb(bank)[0:m, :], wp[b][:, kt, lo:lo + m], rhs_fn(kt, tg), kt == 0, kt == KT - 1,
                                   [(wpname, b), rkey_fn(kt, tg)], [('ps', bank)])
                            epi(ci, c0, m, tg, PSb(bank)[0:m, :], [('ps', bank)])


        SW = 256
        NSG = T // SW
        INV_SCALE = 2.0 / (2 * T)

        def load_slabs(slC, slS, b, Cm, Sm, g):
            DMA('sp', slC[b][:], Cm.rearrange("(kt p) c -> p kt c", p=128)[:, :, g * SW:(g + 1) * SW], (), [('slC', b)])
            DMA('act', slS[b][:], Sm.rearrange("(kt p) c -> p kt c", p=128)[:, :, g * SW:(g + 1) * SW], (), [('slS', b)])

        def dft_fwd(slC, slS, srcC, srcS, keyC, keyS, epi):
            nb = 0
            load_slabs(slC, slS, 0, CD['CT'], CD['ST'], 0)
            for g in range(NSG):
                b = g % 2
                if g + 1 < NSG:
                    load_slabs(slC, slS, (g + 1) % 2, CD['CT'], CD['ST'], g + 1)
                for fi in range(SW // 128):
                    ft = g * (SW // 128) + fi
                    for chh in range(2):
                        bC = (nb % 4) * 2
                        bS = bC + 1
                        nb += 1
                        for nt in range(NT):
                            MM(PSb(bC), slC[b][:, nt, fi * 128:(fi + 1) * 128], srcC[:, nt, chh * 512:(chh + 1) * 512],
                               nt == 0, nt == NT - 1, [('slC', b), (keyC, nt)], [('ps', bC)])
                        for nt in range(NT):
                            MM(PSb(bS), slS[b][:, nt, fi * 128:(fi + 1) * 128], srcS[:, nt, chh * 512:(chh + 1) * 512],
                               nt == 0, nt == NT - 1, [('slS', b), (keyS, nt)], [('ps', bS)])
                        epi(ft, chh, bC, bS)

        def hyena_phase(l):
            hy0 = ExitStack()
            asum = sb("hy_asum", [128, 4096], st=hy0)
            with ExitStack() as ph:
                ph.enter_context(nc.named_scope("hy_filt"))
                w1 = sb("hy_w1", [33, 64], st=ph)
                w2 = sb("hy_w2", [64, 64], st=ph)
                w3 = sb("hy_w3", [64, 64], st=ph)
                w4 = sb("hy_w4", [64, 4096], BF16, st=ph)
                z3b = sb("hy_z3b", [64, T], BF16, st=ph)
                fT = sb("hy_fT", [33, T], st=ph)
                zT = [sb("hy_zT%d" % i, [64, T], st=ph) for i in range(2)]
                tmpz = sb("hy_tmpz", [64, 512], st=ph)
                edb = sb("hy_edb", [128, 4096], st=ph)
                dec = [sb("hy_dec%d" % i, [128, 512], st=ph) for i in range(4)]
                hdt = [sb("hy_hd%d" % i, [128, 512], st=ph) for i in range(8)]
                hab2 = [sb("hy_hab2%d" % i, [128, 512], st=ph) for i in range(4)]
                habs = [sb("hy_ha%d" % i, [128, 512], st=ph) for i in range(2)]
                DMA('sp', w1[:], W['hy_w1'][l], (), ['hy_w'])
                DMA('sp', w2[:], W['hy_w2'][l], (), ['hy_w'])
                DMA('sp', w3[:], W['hy_w3'][l], (), ['hy_w'])
                DMA('pool', w4[:], W['hy_w4'][l], (), ['hy_w4'])
                DMA('sp', fT[:], CD['featsT'][:, :], (), ['hy_fT'])
                DMA('sp', edb[:], W['hy_log_decay'][l:l + 1].rearrange("o a b c -> o (a b c)").broadcast_to([128, 4096]), (), ['hy_edb'])
                ACT(edb[:], edb[:], AF.Exp, ['hy_edb'], ['hy_edb'])
                ob_, _ = PC['hyb']
                fq = pcol[0:64, ob_ + 3:ob_ + 4]
                srcs = [(w1, fT, 33), (w2, zT[0], 64), (w3, zT[1], 64)]
                for li, (wl, src, kk_) in enumerate(srcs):
                    dst = zT[li % 2]
                    bcol = pcol[0:64, ob_ + li:ob_ + li + 1]
                    for tg in range(NTG):
                        tsl = slice(tg * 512, (tg + 1) * 512)
                        bank = tg % 2
                        MM(PSb(bank)[0:64, :], wl[0:kk_, :], src[0:kk_, tsl], True, True, ['hy_w', 'hy_fT', ('hy_zT', 0), ('hy_zT', 1)], [('ps', bank)])
                        TS(dst[:, tsl], PSb(bank)[0:64, :], bcol, fq, ALU.add, ALU.mult, [('ps', bank), 'pcol'], [('hy_zT', li % 2)])
                        for rnd in range(3):
                            TS(tmpz[:], dst[:, tsl], math.pi, -2 * math.pi, ALU.is_gt, ALU.mult, [('hy_zT', li % 2)], ['hy_tmpz'])
                            TT(dst[:, tsl], dst[:, tsl], tmpz[:], ALU.add, [('hy_zT', li % 2), 'hy_tmpz'], [('hy_zT', li % 2)])
                            TS(tmpz[:], dst[:, tsl], -math.pi, 2 * math.pi, ALU.is_lt, ALU.mult, [('hy_zT', li % 2)], ['hy_tmpz'])
                            TT(dst[:, tsl], dst[:, tsl], tmpz[:], ALU.add, [('hy_zT', li % 2), 'hy_tmpz'], [('hy_zT', li % 2)])
                        ACT(dst[:, tsl], dst[:, tsl], AF.Sin, [('hy_zT', li % 2)], [('hy_zT', li % 2)])
                z3 = zT[0]
                CP(z3b[:], z3[:], [('hy_zT', 0)], ['hy_z3b'], eng='act')
                its = [(cg, i) for cg in range(8) for i in range(NT)]

                def emit_dec(n):
                    cg, i = its[n]
                    ACT(dec[n % 4][:], edb[:, cg * 512:(cg + 1) * 512], AF.Exp, ['hy_edb', 'const'], [('hy_dec', n % 4)], scale=tneg[:, i:i + 1])
                emit_dec(0)
                for n, (cg, i) in enumerate(its):
                    csl = slice(cg * 512, (cg + 1) * 512)
                    a_ = cg % 2
                    k = n % 4
                    k8 = n % 8
                    bank = 2 + k
                    if i == 0:
                        MSET(habs[a_][:], 0.0, [('hy_ha', a_)], eng='pool')
                    MM(PSb(bank), z3b[:, i * 128:(i + 1) * 128], w4[:, csl], True, True, ['hy_z3b', 'hy_w4'], [('ps', bank)])
                    if n + 1 < len(its):
                        emit_dec(n + 1)
                    TT(hdt[k8][:], PSb(bank), dec[k][:], ALU.mult, [('ps', bank), ('hy_dec', k)], [('hy_hd', k8)])
                    DMA('sp' if n % 2 else 'act', hfil[i * 128:(i + 1) * 128, csl], hdt[k8][:], [('hy_hd', k8)], ['hfil'])
                    ACT(hab2[k][:], hdt[k8][:], AF.Abs, [('hy_hd', k8)], [('hy_hab2', k)])
                    TT(habs[a_][:], habs[a_][:], hab2[k][:], ALU.add, [('hy_hab2', k), ('hy_ha', a_)], [('hy_ha', a_)], eng='pool')
                    if i == NT - 1:
                        MM(PSb(6 + a_), onesF[:], habs[a_][:], True, True, [('hy_ha', a_), 'const'], [('ps', 6 + a_)])
                        EVAC(asum[:, csl], PSb(6 + a_), [('ps', 6 + a_)], ['hy_asum'])
                S.barrier()
            with ExitStack() as ph:
                ph.enter_context(nc.named_scope("hy_spec"))
                rinv = sb("hy_rinv", [128, 2048], st=ph)
                bdb = sb("hy_bdb", [128, 2048], st=ph)
                hs = sb("hy_hs", [128, NT, 1024], BF16, st=ph)
                hdf = sb("hy_hdf", [128, NT, 1024], BF16, st=ph)
                hf = [sb("hy_hf%d" % i, [128, 1024], st=ph) for i in range(2)]
                hb = [sb("hy_hb%d" % i, [128, 1024], st=ph) for i in range(2)]
                t1 = [sb("hy_t1%d" % i, [128, 1024], st=ph) for i in range(2)]
                slC = [sb("hy_slC%d" % i, [128, NT, SW], BF16, st=ph) for i in range(2)]
                slS = [sb("hy_slS%d" % i, [128, NT, SW], BF16, st=ph) for i in range(2)]
                ko = [sb("hy_ko%d" % i, [128, 512], st=ph) for i in range(4)]
                TT(rinv[:], asum[:, 0:2048], asum[:, 2048:4096], ALU.add, (), ['hy_rinv'])
                ACT(rinv[:], rinv[:], AF.Ln, ['hy_rinv'], ['hy_rinv'])
                ACT(rinv[:], rinv[:], AF.Exp, ['hy_rinv'], ['hy_rinv'], scale=-1.0)
                DMA('sp', bdb[:], W['hy_bias_d'][l:l + 1].rearrange("o a c -> o (a c)").broadcast_to([128, 2048]), (), ['hy_bdb'])
                for o in range(2):
                    osl = slice(o * 1024, (o + 1) * 1024)
                    for i in range(NT):
                        k = i % 2
                        rows = slice(i * 128, (i + 1) * 128)
                        DMA('sp', hf[k][:], hfil[rows, o * 1024:(o + 1) * 1024], ['hfil'], [('hy_hf', k)])
                        DMA('act', hb[k][:], hfil[rows, 2048 + o * 1024:2048 + (o + 1) * 1024], ['hfil'], [('hy_hb', k)])
                        if i == 0:
                            MSET(hb[k][0:1, :], 0.0, [('hy_hb', k)], eng='dve')
                        TT(hs[:, i, :], hf[k][:], hb[k][:], ALU.add, [('hy_hf', k), ('hy_hb', k)], [('hy_hs', i)])
                        TT(hdf[:, i, :], hf[k][:], hb[k][:], ALU.subtract, [('hy_hf', k), ('hy_hb', k)], [('hy_hdf', i)], eng='pool')
                    kc = [0]

                    def epi_spec(ft, chh, bC, bS):
                        k0 = kc[0] % 2
                        kc[0] += 1
                        csl = slice(o * 1024 + chh * 512, o * 1024 + (chh + 1) * 512)
                        TT(ko[k0][:], PSb(bC), rinv[:, csl], ALU.mult, [('ps', bC), 'hy_rinv'], [('hy_ko', k0)])
                        TT(ko[k0][:], ko[k0][:], bdb[:, csl], ALU.add, [('hy_ko', k0), 'hy_bdb'], [('hy_ko', k0)], eng='pool')
                        DMA('sp', spec[o, 0, ft * 128:(ft + 1) * 128, chh * 512:(chh + 1) * 512], ko[k0][:], [('hy_ko', k0)], ['spec'])
                        TT(ko[2 + k0][:], PSb(bS), rinv[:, csl], ALU.mult, [('ps', bS), 'hy_rinv'], [('hy_ko', 2 + k0)])
                        DMA('sp', spec[o, 1, ft * 128:(ft + 1) * 128, chh * 512:(chh + 1) * 512], ko[2 + k0][:], [('hy_ko', 2 + k0)], ['spec'])
                    dft_fwd(slC, slS, hs, hdf, 'hy_hs', 'hy_hdf', epi_spec)
                S.barrier()
            hy0.close()
            with ExitStack() as ph:
                z = sb("hy_z", [128, NT, 1024], BF16, st=ph)
                Yr = sb("hy_Yr", [128, NT, 1024], BF16, st=ph)
                Ys = sb("hy_Ys", [128, NT, 1024], BF16, st=ph)
                slC = [sb("hy_slC%d" % i, [128, NT, SW], BF16, st=ph) for i in range(2)]
                slS = [sb("hy_slS%d" % i, [128, NT, SW], BF16, st=ph) for i in range(2)]
                Kr = [sb("hy_Kr%d" % i, [128, 512], st=ph) for i in range(2)]
                Ks = [sb("hy_Ks%d" % i, [128, 512], st=ph) for i in range(2)]
                Zc = [sb("hy_Zc%d" % i, [128, 512], st=ph) for i in range(2)]
                Zs = [sb("hy_Zs%d" % i, [128, 512], st=ph) for i in range(2)]
                ta = [sb("hy_ta%d" % i, [128, 512], st=ph) for i in range(2)]
                tb = [sb("hy_tb%d" % i, [128, 512], st=ph) for i in range(2)]
                ld = [sb("hy_ld%d" % i, [128, T], st=ph) for i in range(2)]
                stg = [sb("hy_stg%d" % i, [128, 4, 128], st=ph) for i in range(2)]
                xm = [sb("hy_xm%d" % i, [128, 512], st=ph) for i in range(2)]
                yo = [sb("hy_yo%d" % i, [128, 512], BF16, st=ph) for i in range(2)]
                x1v = x1tok.rearrange("(nt p) c -> p nt c", p=128)
                sc_ = nc.named_scope("hy_tr")
                sc_.__enter__()
                n = 0
                for which in range(2):
                    for ct in range(8):
                        k = n % 2
                        r0 = (2048 if which == 0 else 0) + ct * 128
                        DMA('sp', ld[k][:], bT[r0:r0 + 128, :], ['bcT'], [('hy_ld', k)])
                        for q in range(4):
                            bank = (n * 4 + q) % 8
                            for j in range(4):
                                nt = q * 4 + j
                                TR(PSb(bank)[:, j * 128:(j + 1) * 128], ld[k][:, nt * 128:(nt + 1) * 128], identF[:],
                                   [('hy_ld', k), 'const'], [('ps', bank)])
                            if which == 0:
                                EVAC(z[:, q * 4:(q + 1) * 4, ct * 128:(ct + 1) * 128], PSb(bank).rearrange("p (a b) -> p a b", a=4),
                                     [('ps', bank)], [('hy_z', q * 4 + j) for j in range(4)])
                            else:
                                s_ = q % 2
                                EVAC(stg[s_][:], PSb(bank).rearrange("p (a b) -> p a b", a=4), [('ps', bank)], [('hy_stg', s_)])
                                DMA('act', x1v[:, q * 4:(q + 1) * 4, ct * 128:(ct + 1) * 128], stg[s_][:], [('hy_stg', s_)], ['x1tok'])
                        n += 1
                sc_.__exit__(None, None, None)
                for o in range(2):
                    kc = [0]
                    sc_ = nc.named_scope("hy_fwd%d" % o)
                    sc_.__enter__()

                    def epi_mul(ft, chh, bC, bS):
                        k0 = kc[0] % 2
                        kc[0] += 1
                        rows = slice(ft * 128, (ft + 1) * 128)
                        cs = slice(chh * 512, (chh + 1) * 512)
                        DMA('sp', Kr[k0][:], spec[o, 0, rows, cs], ['spec'], [('hy_Kr', k0)])
                        DMA('sp', Ks[k0][:], spec[o, 1, rows, cs], ['spec'], [('hy_Ks', k0)])
                        CP(Zc[k0][:], PSb(bC), [('ps', bC)], [('hy_Zc', k0)], eng='act')
                        CP(Zs[k0][:], PSb(bS), [('ps', bS)], [('hy_Zs', k0)], eng='act')
                        TT(ta[k0][:], Zc[k0][:], Kr[k0][:], ALU.mult, [('hy_Zc', k0), ('hy_Kr', k0)], [('hy_ta', k0)])
                        TT(tb[k0][:], Zs[k0][:], Ks[k0][:], ALU.mult, [('hy_Zs', k0), ('hy_Ks', k0)], [('hy_tb', k0)], eng='pool')
                        TT(Yr[:, ft, cs], ta[k0][:], tb[k0][:], ALU.subtract, [('hy_ta', k0), ('hy_tb', k0)], [('hy_Y', ft)])
                        TT(ta[k0][:], Zc[k0][:], Ks[k0][:], ALU.mult, [('hy_Zc', k0), ('hy_Ks', k0)], [('hy_ta', k0)])
                        TT(tb[k0][:], Zs[k0][:], Kr[k0][:], ALU.mult, [('hy_Zs', k0), ('hy_Kr', k0)], [('hy_tb', k0)], eng='pool')
                        TT(Ys[:, ft, cs], ta[k0][:], tb[k0][:], ALU.add, [('hy_ta', k0), ('hy_tb', k0)], [('hy_Y', ft)], eng='pool')
                    dft_fwd(slC, slS, z, z, 'hy_z', 'hy_z', epi_mul)
                    sc_.__exit__(None, None, None)
                    sc_ = nc.named_scope("hy_inv%d" % o)
                    sc_.__enter__()
                    nb = 0
                    load_slabs(slC, slS, 0, CD['CF'], CD['SF'], 0)
                    for g in range(NSG):
                        b = g % 2
                        if g + 1 < NSG:
                            load_slabs(slC, slS, (g + 1) % 2, CD['CF'], CD['SF'], g + 1)
                        nsl = slice(g * SW, (g + 1) * SW)
                        if o == 0:
                            for ni in range(SW // 128):
                                nt = g * (SW // 128) + ni
                                for chh in range(2):
                                    bank = nb % 8
                                    k0 = nb % 2
                                    nb += 1
                                    cs = slice(chh * 512, (chh + 1) * 512)
                                    DMA('sp', xm[k0][:], x1tok[nt * 128:(nt + 1) * 128, cs], ['x1tok'], [('hy_xm', k0)])
                                    for ft in range(NT):
                                        MM(PSb(bank), slC[b][:, ft, ni * 128:(ni + 1) * 128], Yr[:, ft, cs], ft == 0, False,
                                           [('slC', b), ('hy_Y', ft)], [('ps', bank)])
                                    for ft in range(NT):
                                        MM(PSb(bank), slS[b][:, ft, ni * 128:(ni + 1) * 128], Ys[:, ft, cs], False, ft == NT - 1,
                                           [('slS', b), ('hy_Y', ft)], [('ps', bank)])
                                    STT(z[:, nt, cs], PSb(bank), INV_SCALE, xm[k0][:], ALU.mult, ALU.mult,
                                        [('ps', bank), ('hy_xm', k0)], [('hy_z', nt)])
                        else:
                            for ct in range(8):
                                bank = nb % 8
                                k0 = nb % 2
                                nb += 1
                                csl = slice(ct * 128, (ct + 1) * 128)
                                DMA('sp', xm[k0][:, 0:SW], bT[1024 + ct * 128:1024 + (ct + 1) * 128, nsl], ['bcT'], [('hy_xm', k0)])
                                for ft in range(NT):
                                    MM(PSb(bank)[:, 0:SW], Yr[:, ft, csl], slC[b][:, ft, :], ft == 0, False,
                                       [('slC', b), ('hy_Y', ft)], [('ps', bank)])
                                for ft in range(NT):
                                    MM(PSb(bank)[:, 0:SW], Ys[:, ft, csl], slS[b][:, ft, :], False, ft == NT - 1,
                                       [('slS', b), ('hy_Y', ft)], [('ps', bank)])
                                STT(yo[k0][:, 0:SW], PSb(bank)[:, 0:SW], INV_SCALE, xm[k0][:, 0:SW], ALU.mult, ALU.mult,
                                    [('ps', bank), ('hy_xm', k0)], [('hy_yo', k0)])
                                DMA('act', ybT[ct * 128:(ct + 1) * 128, nsl], yo[k0][:, 0:SW], [('hy_yo', k0)], ['ybT'])
                    sc_.__exit__(None, None, None)
                S.barrier()


        SDT = BF16
        NINV = 1
        NSET = NINV + 1
        WSC = -math.exp(-0.5)

        def rwkv_phase(l):
            with ExitStack() as ph:
                ph.enter_context(nc.named_scope("rwkv"))
                twp = sb("rw_twp", [128, T], BF16, st=ph)
                adp = sb("rw_adp", [128, T], BF16, st=ph)
                sgd = sb("rw_sgd", [128, T], BF16, st=ph)
                w2p = sb("rw_w2p", [128, 1024], BF16, st=ph)
                a2p = sb("rw_a2p", [128, 1024], BF16, st=ph)
                g2 = sb("rw_g2", [128, 1024], BF16, st=ph)
                omka = sb("rw_omka", [128, 8], st=ph)
                rT = sb("rw_r", [128, T], st=ph)
                kT = sb("rw_k", [128, T], st=ph)
                vT = sb("rw_v", [128, T], st=ph)
                kkn = sb("rw_kkn", [128, T], st=ph)
                rkacc = sb("rw_rkacc", [128, T], st=ph)
                ysum = sb("rw_ysum", [128, T], st=ph)
                vR = sb("rw_vR", [128, T], st=ph)
                lw = sb("rw_lw", [128, T], st=ph)
                kd = sb("rw_kd", [128, T], st=ph)
                bd = sb("rw_bd", [128, T], st=ph)
                Lc = sb("rw_L", [128, T], st=ph)
                ex = sb("rw_ex", [128, T], st=ph)
                tmp2 = sb("rw_tmp2", [128, T], st=ph)
                yT_ = sb("rw_y", [128, T], st=ph)
                tot = sb("rw_tot", [128, 16], st=ph)
                Pc = sb("rw_Pc", [128, 16], st=ph)
                ARh = [sb("rw_AR%d" % i, [128, 16, 2, 128], SDT, st=ph) for i in range(2)]
                BK = sb("rw_BK", [128, 16, 2, 128], SDT, st=ph)
                BhT = sb("rw_BhT", [128, 16, 128], SDT, st=ph)
                KhT = sb("rw_KhT", [128, 16, 128], SDT, st=ph)
                VT = sb("rw_VT", [128, 16, 128], SDT, st=ph)
                NB = [sb("rw_NB%d" % i, [128, 4, 2, 128], SDT, st=ph) for i in range(NSET)]
                AK = [sb("rw_AK%d" % i, [128, 4, 2, 128], SDT, st=ph) for i in range(NSET)]
                Mm2 = [[sb("rw_M%d_%d" % (j, i), [128, 4, 128], SDT, st=ph) for i in range(2)] for j in range(NINV)]
                MT2 = [[sb("rw_MT%d_%d" % (j, i), [128, 4, 128], SDT, st=ph) for i in range(2)] for j in range(NINV)]
                Xx2 = [[sb("rw_X%d_%d" % (j, i), [128, 4, 128], SDT, st=ph) for i in range(2)] for j in range(NINV)]
                Xfin = [sb("rw_Xf%d" % i, [128, 4, 128], SDT, st=ph) for i in range(NSET)]
                tb16 = sb("rw_tb16", [128, T], BF16, st=ph)
                St = sb("rw_St", [128, 64], st=ph)
                Sb_ = sb("rw_Sb", [128, 64], SDT, st=ph)
                Wt = sb("rw_Wt", [128, 128], SDT, st=ph)
                Ut = sb("rw_Ut", [128, 128], SDT, st=ph)
                ob = [sb("rw_ob%d" % i, [128, 512], BF16, st=ph) for i in range(2)]

                EXK = [('rw_ex', p_) for p_ in range(4)]
                T2K = [('rw_tmp2', p_) for p_ in range(4)]
                KDK = [('rw_kd', p_) for p_ in range(4)]
                LK = [('rw_L', p_) for p_ in range(4)]
                LWK = [('rw_lw', p_) for p_ in range(4)]
                BDK = [('rw_bd', p_) for p_ in range(4)]
                DMA('pool', g2[:], W['rw_g2'][l], (), ['rw_c'])
                MSET(ARh[0][64:128].rearrange("p a b c -> p (a b c)"), 0.0, ['rw_AR'])
                MSET(ARh[1][0:64].rearrange("p a b c -> p (a b c)"), 0.0, ['rw_AR'])
                for d in range(2):
                    ps_ = slice(d * 64, d * 64 + 48)
                    DMA('pool', w2p[ps_, :], W['rw_w2'][l, d], (), ['rw_c'])
                    DMA('pool', a2p[ps_, :], W['rw_a2'][l, d], (), ['rw_c'])
                    DMA('sp', tmp2[ps_, :], cT[3200 + d * 48:3248 + d * 48, :], ['bcT'], [*T2K])
                    DMA('sp', ex[ps_, :], cT[3296 + d * 48:3344 + d * 48, :], ['bcT'], [*EXK])
                    if d == 0:
                        ACT(twp[ps_, :], tmp2[ps_, :], AF.Tanh, [*T2K], ['rw_twp'])
                        CP(adp[ps_, :], ex[ps_, :], [*EXK], ['rw_adp'], eng='act')
                    else:
                        ACT(twp[ps_, :], tmp2[ps_, ::-1], AF.Tanh, [*T2K], ['rw_twp'])
                        CP(adp[ps_, :], ex[ps_, ::-1], [*EXK], ['rw_adp'], eng='act')
                DMA('sp', Lc[:], cT[3072:3200, :], ['bcT'], [*LK])
                ACT(sgd[:], Lc[:], AF.Sigmoid, [*LK], ['rw_sgd'])
                oka, _ = PC['ka']
                TS(omka[:], pcol[:, oka:oka + 8], -1.0, 1.0, ALU.mult, ALU.add, ['pcol'], ['rw_omka'])
                if l > 0:
                    v1 = sb("rw_v1", [128, 8, 32], st=ph)
                    v2 = sb("rw_v2", [32, 1024], BF16, st=ph)
                    t1v = sb("rw_t1v", [32, T], BF16, st=ph)
                    DMA('sp', v1[:], W['rw_v1'][l - 1].rearrange("(ct p) r -> p ct r", p=128), (), ['rw_c'])
                    DMA('pool', v2[:], W['rw_v2'][l - 1], (), ['rw_c'])
                    for ct in range(8):
                        DMA('sp', kd[:], cT[2048 + ct * 128:2048 + (ct + 1) * 128, :], ['bcT'], [*KDK])
                        for tg in range(NTG):
                            MM(PSb(tg)[0:32, :], v1[:, ct, :], kd[:, tg * 512:(tg + 1) * 512], ct == 0, ct == 7,
                               ['rw_c', *KDK], [('ps', tg)])
                    for tg in range(NTG):
                        EVAC(t1v[:, tg * 512:(tg + 1) * 512], PSb(tg)[0:32, :], [('ps', tg)], ['rw_t1v'])

                def f2(ap):
                    return ap.rearrange("p a b -> p (a b)")

                def c3(ap):
                    return ap.rearrange("p (c t) -> p c t", t=128)

                def scan_unit(hp, d, r_ap, k_ap, kk_ap, v_, yout, ykey):
                    hsl = slice(hp * 128, (hp + 1) * 128)
                    ps_ = slice(d * 64, d * 64 + 48)

                    def prep_part(p):
                        psl = slice(p * 512, (p + 1) * 512)
                        c4 = slice(p * 4, (p + 1) * 4)
                        kl, ke, kt, kkd, kbd, kL, kb16 = ('rw_lw', p), ('rw_ex', p), ('rw_tmp2', p), ('rw_kd', p), ('rw_bd', p), ('rw_L', p), ('rw_tb16', p)
                        MM(PSb(5), w2p[ps_, hsl], twp[ps_, psl], True, True, ['rw_c', 'rw_twp'], [('ps', 5)])
                        ACT(lw[:, psl], PSb(5), AF.Sigmoid, [('ps', 5), 'pcol'], [kl], bias=pc('w0', d * 8 + hp))
                        MM(PSb(6), a2p[ps_, hsl], adp[ps_, psl], True, True, ['rw_c', 'rw_adp'], [('ps', 6)])
                        ACT(ex[:, psl], PSb(6), AF.Sigmoid, [('ps', 6), 'pcol'], [ke], bias=pc('a0', d * 8 + hp))
                        yield
                        TS(tmp2[:, psl], ex[:, psl], pc('ka', hp), omka[:, hp:hp + 1], ALU.mult, ALU.add, [ke, 'pcol', 'rw_omka'], [kt])
                        TT(kd[:, psl], tmp2[:, psl], k_ap[:, psl], ALU.mult, [kt, 'rw_in'], [kkd])
                        TT(bd[:, psl], ex[:, psl], kk_ap[:, psl], ALU.mult, ['rw_kkn', ke], [kbd], eng='pool')
                        if d == 0:
                            STT(rkacc[:, psl], r_ap[:, psl], pc('rk', hp), kd[:, psl], ALU.mult, ALU.mult, ['rw_in', 'pcol', kkd], [('rw_rkacc', p)])
                        else:
                            STT(tmp2[:, psl], r_ap[:, psl], pc('rk', hp), kd[:, psl], ALU.mult, ALU.mult, ['rw_in', 'pcol', kkd, kt], [kt])
                            rsl = slice(T - (p + 1) * 512, T - p * 512)
                            TT(rkacc[:, rsl], rkacc[:, rsl], tmp2[:, psl][:, ::-1], ALU.add, [('rw_rkacc', 3 - p), kt], [('rw_rkacc', 3 - p)])
                        yield
                        for c in range(p * 4, (p + 1) * 4):
                            csl = slice(c * 128, (c + 1) * 128)
                            S.op('dve', lambda: nc.vector.tensor_tensor_scan(out=Lc[:, csl], data0=onesF[:], data1=lw[:, csl], initial=0.0,
                                                                             op0=ALU.mult, op1=ALU.add), ['const', kl], [kL])
                        CP(tot[:, c4], c3(Lc[:])[:, c4, 127], [kL], [('rw_tot', p)])
                        ACT(Pc[:, c4], tot[:, c4], AF.Exp, [('rw_tot', p)], [('rw_Pc', p)], scale=WSC)
                        yield
                        ACT(ex[:, psl], Lc[:, psl], AF.Exp, [kL, kbd, kt], [ke], scale=WSC)
                        for h in range(2):
                            hp_ = slice(h * 64, (h + 1) * 64)
                            TT(ARh[h][hp_, c4, 1, :], c3(r_ap[hp_, psl]), c3(ex[hp_, psl]), ALU.mult, [ke, 'rw_in'], [('rw_AR', p)])
                        yield
                        ACT(ex[:, psl], Lc[:, psl], AF.Exp, [kL, ('rw_AR', p)], [ke], scale=-WSC)
                        TT(BK[:, c4, 0, :], c3(bd[:, psl]), c3(ex[:, psl]), ALU.mult, [ke, kbd], [('rw_BK', p)])
                        TT(BK[:, c4, 1, :], c3(kd[:, psl]), c3(ex[:, psl]), ALU.mult, [ke, kkd], [('rw_BK', p)], eng='pool')
                        yield
                        TT(tmp2[:, psl], Lc[:, psl], lw[:, psl], ALU.subtract, [kL, kl, kt], [kt])
                        ACT(ex[:, psl], tmp2[:, psl], AF.Exp, [kt, ('rw_BK', p)], [ke], scale=WSC)
                        for h in range(2):
                            hp_ = slice(h * 64, (h + 1) * 64)
                            STT(ARh[h][hp_, c4, 0, :], c3(kk_ap[hp_, psl]), -1.0, c3(ex[hp_, psl]), ALU.mult, ALU.mult, [ke, 'rw_kkn'], [('rw_AR', p)])
                        yield
                        TT(c3(tmp2[:, psl]), tot[:, c4].unsqueeze(2).to_broadcast([128, 4, 128]), c3(Lc[:, psl]), ALU.subtract,
                           [('rw_tot', p), kL, kt], [kt])
                        ACT(ex[:, psl], tmp2[:, psl], AF.Exp, [kt, ('rw_AR', p)], [ke], scale=WSC)
                        yield
                        for wi, (src, skey, dstT) in enumerate(((bd, kbd, BhT), (kd, kkd, KhT), (v_, 'rw_in', VT))):
                            if wi < 2:
                                TT(tb16[:, psl], src[:, psl], ex[:, psl], ALU.mult, [ke, skey, kb16], [kb16], eng='pool' if wi else 'dve')
                            else:
                                CP(tb16[:, psl], src[:, psl], ['rw_in', 'rw_vR', kb16], [kb16], eng='act')
                            pb16 = PSb(7).bitcast(BF16)
                            for j in range(4):
                                TR(pb16[:, j * 128:(j + 1) * 128], tb16[:, (p * 4 + j) * 128:(p * 4 + j + 1) * 128], identB[:],
                                   [kb16, 'const'], [('ps', 7)])
                            EVAC(f2(dstT[:, c4, :]), pb16[:, 0:512], [('ps', 7)], [('rw_T%d' % wi, p)])
                            yield

                    MSET(St[:], 0.0, ['rw_St'], eng='dve')
                    MSET(Sb_[:], 0.0, ['rw_Sb'], eng='dve')

                    def inv_chain(qd):
                        q3 = qd % NSET
                        st_ = qd % NINV
                        b0, b1, b2 = (2, 3, 4) if st_ == 0 else (5, 6, 7)
                        MTs, Mms, Xxs = MT2[st_], Mm2[st_], Xx2[st_]
                        kM, kMT, kX = 'rw_M%d' % st_, 'rw_MT%d' % st_, 'rw_X%d' % st_
                        gb_ = (b0, b1)
                        for cc in range(2):
                            c = qd * 2 + cc
                            for h in range(2):
                                u = cc * 2 + h
                                MM(PSb(gb_[u // 2])[:, (u % 2) * 256:(u % 2 + 1) * 256], BK[:, c, 0, :], f2(ARh[h][:, c, :, :]), True, True,
                                   [('rw_BK', qd // 2), ('rw_AR', qd // 2)], [('ps', gb_[u // 2])])
                                MM(PSb(b2)[:, u * 128:(u + 1) * 128], ARh[h][:, c, 0, :], BK[:, c, 0, :], True, True,
                                   [('rw_BK', qd // 2), ('rw_AR', qd // 2)], [('ps', b2)])
                        for hb in range(2):
                            TT(NB[q3][:, hb * 2:(hb + 1) * 2, :, :].rearrange("p a b c -> p (a b c)"), PSb(gb_[hb]), maskU4[:], ALU.mult,
                               [('ps', gb_[hb]), 'const'], [('rw_NB', q3)])
                        TT(f2(MTs[0][:]), PSb(b2), maskL4[:], ALU.mult, [('ps', b2), 'const'], [(kMT, 0)])
                        TT(Xxs[0][:], NB[q3][:, :, 0, :], ident4[:].rearrange("p (a b) -> p a b", a=4), ALU.add,
                           [('rw_NB', q3), 'const'], [(kX, 0)], eng='pool')
                        yield
                        cur = 0
                        for lev in range(1, 8):
                            nxt = 1 - cur
                            pA, pB, pC = b2, b1, b0
                            if lev == 1:
                                for cc in range(2):
                                    c = qd * 2 + cc
                                    for h in range(2):
                                        u = cc * 2 + h
                                        MM(PSb(gb_[u // 2])[:, (u % 2) * 256:(u % 2 + 1) * 256], BK[:, c, 1, :], f2(ARh[h][:, c, :, :]), True, True,
                                           [('rw_BK', qd // 2), ('rw_AR', qd // 2)], [('ps', gb_[u // 2])])
                            if lev >= 2:
                                xs, xk = Xxs[(lev - 2) % 2], (kX, (lev - 2) % 2)
                                for u in range(4):
                                    MM(PSb(pC)[:, u * 128:(u + 1) * 128], MTs[cur][:, u, :], xs[:, u, :], True, True,
                                       [(kMT, cur), xk], [('ps', pC)])
                            if lev == 1:
                                for hb in range(2):
                                    TT(AK[q3][:, hb * 2:(hb + 1) * 2, :, :].rearrange("p a b c -> p (a b c)"), PSb(gb_[hb]), maskU4[:], ALU.mult,
                                       [('ps', gb_[hb]), 'const'], [('rw_AK', q3)])
                            if lev <= 6:
                                for u in range(4):
                                    m_prev = NB[q3][:, u, 0, :] if lev == 1 else Mms[cur][:, u, :]
                                    mk = ('rw_NB', q3) if lev == 1 else (kM, cur)
                                    MM(PSb(pA)[:, u * 128:(u + 1) * 128], m_prev, MTs[cur][:, u, :], True, True, [mk, (kMT, cur)], [('ps', pA)])
                                    if lev < 6:
                                        MM(PSb(pB)[:, u * 128:(u + 1) * 128], MTs[cur][:, u, :], m_prev, True, True, [mk, (kMT, cur)], [('ps', pB)])
                            if lev <= 6:
                                CP(f2(MTs[nxt][:]), PSb(pA), [('ps', pA)], [(kMT, nxt)], eng='act')
                            if lev >= 2:
                                if lev == 7:
                                    xd, xdk = Xfin[q3], ('rw_Xfin', q3)
                                else:
                                    xd, xdk = Xxs[(lev - 1) % 2], (kX, (lev - 1) % 2)
                                TT(f2(xd[:]), PSb(pC), f2(xs[:]), ALU.add, [('ps', pC), xk], [xdk])
                            if lev < 6:
                                CP(f2(Mms[nxt][:]), PSb(pB), [('ps', pB)], [(kM, nxt)], eng='act')
                            cur = nxt
                            yield

                    def state_chain(qd):
                        q3 = qd % NSET
                        Xf = Xfin[q3]
                        xkey = ('rw_Xfin', q3)
                        for cc in range(2):
                            c = qd * 2 + cc
                            for h in range(2):
                                u = cc * 2 + h
                                hs_ = slice(h * 64, (h + 1) * 64)
                                MM(PSb(0)[:, hs_], ARh[h][:, c, 0, :], Sb_[:, :], True, False, [('rw_AR', qd // 2), 'rw_Sb'], [('ps', 0)])
                                MM(PSb(0)[:, hs_], AK[q3][:, u, 0, :], VT[:, c, hs_], False, True, [('rw_AK', q3), ('rw_T2', qd // 2)], [('ps', 0)])
                            CP(Wt[:], PSb(0)[:, 0:128], [('ps', 0)], ['rw_Wt'], eng='act')
                            yield
                            for h in range(2):
                                u = cc * 2 + h
                                hs_ = slice(h * 64, (h + 1) * 64)
                                MM(PSb(0)[:, 128 + h * 64:128 + (h + 1) * 64], Xf[:, u, :], Wt[:, hs_], True, True, [xkey, 'rw_Wt'], [('ps', 0)])
                            CP(Ut[:], PSb(0)[:, 128:256], [('ps', 0)], ['rw_Ut'], eng='dve')
                            yield
                            for h in range(2):
                                u = cc * 2 + h
                                hs_ = slice(h * 64, (h + 1) * 64)
                                so_ = PSb(0)[hs_, 256:320]
                                MM(so_, BhT[:, c, hs_], Ut[:, hs_], True, False, [('rw_T0', qd // 2), 'rw_Ut'], [('ps', 0)])
                                MM(so_, KhT[:, c, hs_], VT[:, c, hs_], False, True, [('rw_T1', qd // 2), ('rw_T2', qd // 2)], [('ps', 0)])
                                yo_ = PSb(1)[hs_, cc * 128:(cc + 1) * 128]
                                MM(yo_, Sb_[:, :], ARh[h][:, c, 1, :], True, False, ['rw_Sb', ('rw_AR', qd // 2)], [('ps', 1)])
                                MM(yo_, Ut[:, hs_], NB[q3][:, u, 1, :], False, False, ['rw_Ut', ('rw_NB', q3)], [('ps', 1)])
                                MM(yo_, VT[:, c, hs_], AK[q3][:, u, 1, :], False, True, [('rw_T2', qd // 2), ('rw_AK', q3)], [('ps', 1)])
                            STT(St[:], St[:], Pc[:, c:c + 1], PSb(0)[:, 256:320], ALU.mult, ALU.add, ['rw_St', ('rw_Pc', qd // 2), ('ps', 0)], ['rw_St'])
                            CP(Sb_[:], St[:], ['rw_St'], ['rw_Sb'], eng='act')
                            if cc == 1:
                                CP(yout[:, qd * 256:(qd + 1) * 256], PSb(1)[:, 0:256], [('ps', 1)], [ykey], eng='act')
                            yield

                    for _ in prep_part(0):
                        pass
                    prep_done = [True, False, False, False]
                    prep_idx = 1
                    prep_gen = prep_part(1)
                    next_inv = 0
                    inv_done = [False] * 8
                    active = []
                    state_q = 0
                    state_gen = None
                    states_done = 0
                    while states_done < 8:
                        while (len(active) < NINV and next_inv < 8 and next_inv < states_done + NSET
                               and prep_done[next_inv // 2]):
                            active.append((next_inv, inv_chain(next_inv)))
                            next_inv += 1
                        if state_gen is None and state_q < 8 and inv_done[state_q]:
                            state_gen = state_chain(state_q)
                        if state_gen is not None:
                            try:
                                next(state_gen)
                            except StopIteration:
                                state_gen = None
                                states_done += 1
                                state_q += 1
                        for item in list(active):
                            try:
                                next(item[1])
                            except StopIteration:
                                inv_done[item[0]] = True
                                active.remove(item)
                        if prep_gen is not None:
                            try:
                                next(prep_gen)
                            except StopIteration:
                                prep_done[prep_idx] = True
                                prep_idx += 1
                                prep_gen = prep_part(prep_idx) if prep_idx < 4 else None

                for hp in range(8):
                    hsl = slice(hp * 128, (hp + 1) * 128)
                    DMA('sp', rT[:], cT[hp * 128:(hp + 1) * 128, :], ['bcT'], ['rw_in'])
                    DMA('sp', kT[:], cT[1024 + hp * 128:1024 + (hp + 1) * 128, :], ['bcT'], ['rw_in'])
                    DMA('sp', vT[:], cT[2048 + hp * 128:2048 + (hp + 1) * 128, :], ['bcT'], ['rw_in'])
                    if l > 0:
                        DMA('sp', tmp2[:], vfT[hsl, :], ['vfT'], [*T2K])
                        for tg in range(NTG):
                            tsl = slice(tg * 512, (tg + 1) * 512)
                            MM(PSb(tg), v2[:, hsl], t1v[:, tsl], True, True, ['rw_c', 'rw_t1v'], [('ps', tg)])
                            ACT(ex[:, tsl], PSb(tg), AF.Sigmoid, [('ps', tg), 'pcol'], [*EXK], bias=pc('v0', hp))
                        TT(tmp2[:], tmp2[:], vT[:], ALU.subtract, [*T2K, 'rw_in'], [*T2K])
                        TT(tmp2[:], tmp2[:], ex[:], ALU.mult, [*T2K, *EXK], [*T2K])
                        TT(vT[:], vT[:], tmp2[:], ALU.add, [*T2K, 'rw_in'], ['rw_in'])
                    else:
                        DMA('act', vfT[hsl, :], vT[:], ['rw_in'], ['vfT'])
                    TS(kkn[:], kT[:], pc('kk', hp), None, ALU.mult, None, ['rw_in', 'pcol'], ['rw_kkn'])
                    ACT(tmp2[:], kkn[:], AF.Square, ['rw_kkn', *T2K], [*T2K])
                    for tg in range(NTG):
                        tsl = slice(tg * 512, (tg + 1) * 512)
                        MM(PSb(tg), blockones[:], tmp2[:, tsl], True, True, ['const', *T2K], [('ps', tg)])
                        ACT(ex[:, tsl], PSb(tg), AF.Ln, [('ps', tg)], [*EXK], bias=1e-30)
                    ACT(ex[:], ex[:], AF.Exp, [*EXK], [*EXK], scale=-0.5)
                    TS(ex[:], ex[:], 1e12, None, ALU.min, None, [*EXK], [*EXK])
                    TT(kkn[:], kkn[:], ex[:], ALU.mult, ['rw_kkn', *EXK], ['rw_kkn'])
                    CP(vR[:], vT[:, ::-1], ['rw_in'], ['rw_vR'], eng='act')
                    for d in range(2):
                        ps_ = slice(d * 64, d * 64 + 48)
                        if d == 0:
                            r_ap, k_ap, kk_ap, v_ = rT[:], kT[:], kkn[:], vT
                        else:
                            r_ap, k_ap, kk_ap, v_ = rT[:, ::-1], kT[:, ::-1], kkn[:, ::-1], vR
                        if d == 0:
                            scan_unit(hp, d, r_ap, k_ap, kk_ap, v_, ysum, 'rw_ysum')
                        else:
                            scan_unit(hp, d, r_ap, k_ap, kk_ap, v_, yT_, 'rw_y')
                            TT(ysum[:], ysum[:], yT_[:, ::-1], ALU.add, ['rw_y', 'rw_ysum'], ['rw_ysum'])
                    for tg in range(NTG):
                        tsl = slice(tg * 512, (tg + 1) * 512)
                        k0 = tg % 2
                        MM(PSb(k0), blockones[:], ysum[:, tsl], True, True, ['const', 'rw_ysum'], [('ps', k0)])
                        STT(tmp2[:, tsl], PSb(k0), -1.0 / 64, ysum[:, tsl], ALU.mult, ALU.add, [('ps', k0), 'rw_ysum', *T2K], [*T2K])
                        ACT(ex[:, tsl], tmp2[:, tsl], AF.Square, [*T2K, *EXK], [*EXK])
                        MM(PSb(2 + k0), blockones[:], ex[:, tsl], True, True, ['const', *EXK], [('ps', 2 + k0)])
                        ACT(ex[:, tsl], PSb(2 + k0), AF.Ln, [('ps', 2 + k0)], [*EXK], scale=1.0 / 64, bias=GN_EPS)
                        ACT(ex[:, tsl], ex[:, tsl], AF.Exp, [*EXK], [*EXK], scale=-0.5)
                        TT(tmp2[:, tsl], tmp2[:, tsl], ex[:, tsl], ALU.mult, [*T2K, *EXK], [*T2K])
                        TS(tmp2[:, tsl], tmp2[:, tsl], pc('lng', hp), pc('lnb', hp), ALU.mult, ALU.add, [*T2K, 'pcol'], [*T2K])
                        MM(PSb(4 + k0), blockones[:], rkacc[:, tsl], True, True, ['const', ('rw_rkacc', tg)], [('ps', 4 + k0)])
                        TT(ex[:, tsl], PSb(4 + k0), vT[:, tsl], ALU.mult, [('ps', 4 + k0), 'rw_in', *EXK], [*EXK])
                        TT(tmp2[:, tsl], tmp2[:, tsl], ex[:, tsl], ALU.add, [*T2K, *EXK], [*T2K])
                        MM(PSb(6 + k0), g2[:, hsl], sgd[:, tsl], True, True, ['rw_c', 'rw_sgd'], [('ps', 6 + k0)])
                        TT(ob[k0][:], tmp2[:, tsl], PSb(6 + k0), ALU.mult, [*T2K, ('ps', 6 + k0)], [('rw_ob', k0)])
                        DMA('act', ycT[hsl, tsl], ob[k0][:], [('rw_ob', k0)], ['ycT'])
                S.barrier()


        def merge_phase(l):
            with ExitStack() as ph:
                ph.enter_context(nc.named_scope("merge"))
                PW = 256
                yT3 = [sb("mg_y%d" % i, [128, 8, T], BF16, st=ph) for i in range(3)]
                wpb = [[sb("mg_w%d_%d" % (i, j), [128, 8, PW], BF16, st=ph) for j in range(2)] for i in range(3)]
                gt = [[sb("mg_g%d_%d" % (i, j), [128, T], BF16, st=ph) for j in range(2)] for i in range(3)]
                ta = [sb("mg_ta%d" % i, [128, 512], st=ph) for i in range(2)]
                tb = [sb("mg_tb%d" % i, [128, 512], st=ph) for i in range(2)]
                mo = [sb("mg_mo%d" % i, [128, 512], BF16, st=ph) for i in range(2)]
                for i, src in enumerate((yaT, ybT, ycT)):
                    DMA('sp', yT3[i][:], src.rearrange("(kt p) t -> p kt t", p=128), ['yaT', 'ybT', 'ycT'], [('mg_y', i)])
                Wb = [W[n][l].rearrange("(kt p) c -> p kt c", p=128) for n in ('w_branch_a', 'w_branch_b', 'w_branch_c')]
                n = 0
                for pi in range(D // PW):
                    b = pi % 2
                    for i in range(3):
                        DMA('pool', wpb[i][b][:], Wb[i][:, :, pi * PW:(pi + 1) * PW], (), [('mg_w', i, b)])
                    for ci in range(PW // 128):
                        ct = pi * (PW // 128) + ci
                        gb_ = ct % 2
                        for i in range(3):
                            DMA('sp', gt[i][gb_][:], gT[i * D + ct * 128:i * D + (ct + 1) * 128, :], ['gT'], [('mg_g', i, gb_)])
                        for tg in range(NTG):
                            tsl = slice(tg * 512, (tg + 1) * 512)
                            k0 = n % 2
                            base = (n % 2) * 3
                            n += 1
                            for i in range(3):
                                for kt in range(8):
                                    MM(PSb(base + i), wpb[i][b][:, kt, ci * 128:(ci + 1) * 128], yT3[i][:, kt, tsl], kt == 0, kt == 7,
                                       [('mg_w', i, b), ('mg_y', i)], [('ps', base + i)])
                            TT(ta[k0][:], PSb(base + 0), gt[0][gb_][:, tsl], ALU.mult, [('ps', base + 0), ('mg_g', 0, gb_)], [('mg_ta', k0)])
                            TT(tb[k0][:], PSb(base + 1), gt[1][gb_][:, tsl], ALU.mult, [('ps', base + 1), ('mg_g', 1, gb_)], [('mg_tb', k0)])
                            TT(ta[k0][:], ta[k0][:], tb[k0][:], ALU.add, [('mg_ta', k0), ('mg_tb', k0)], [('mg_ta', k0)], eng='pool')
                            TT(tb[k0][:], PSb(base + 2), gt[2][gb_][:, tsl], ALU.mult, [('ps', base + 2), ('mg_g', 2, gb_), ('mg_ta', k0)], [('mg_tb', k0)])
                            TT(mo[k0][:], ta[k0][:], tb[k0][:], ALU.add, [('mg_ta', k0), ('mg_tb', k0)], [('mg_mo', k0)], eng='pool')
                            DMA('act', mgT[ct * 128:(ct + 1) * 128, tsl], mo[k0][:], [('mg_mo', k0)], ['mgT'])
                S.barrier()

        def resid_epi(xl, xo):
            cnt = [0]

            def epi(ci, c0, m, tg, ps, pk):
                k = cnt[0] % 3
                cnt[0] += 1
                tsl = slice(tg * 512, (tg + 1) * 512)
                DMA('sp', xl[k][:], xT[c0:c0 + 128, tsl], [('xT', ci, tg)], [('rs_xl', k)])
                TT(xo[k][:], ps, xl[k][:], ALU.add, pk + [('rs_xl', k)], [('rs_xo', k)])
                DMA('act', xT[c0:c0 + 128, tsl], xo[k][:], [('rs_xo', k)], [('xT', ci, tg)])
            return epi

        def outproj_phase(l):
            with ExitStack() as ph:
                ph.enter_context(nc.named_scope("outp"))
                mT = sb("op_mT", [128, KD, T], BF16, st=ph)
                wp = [sb("op_wp%d" % i, [128, KD, 512], BF16, st=ph) for i in range(2)]
                xl = [sb("op_xl%d" % i, [128, 512], st=ph) for i in range(3)]
                xo = [sb("op_xo%d" % i, [128, 512], st=ph) for i in range(3)]
                DMA('sp', mT[:], mgT.rearrange("(kt p) t -> p kt t", p=128), ['mgT'], ['op_mT'])
                linear_fm(W['w_out'][l].rearrange("(kt p) c -> p kt c", p=128), KD, [(i * 128, 128) for i in range(16)],
                          lambda kt, tg: mT[:, kt, tg * 512:(tg + 1) * 512], lambda kt, tg: 'op_mT', resid_epi(xl, xo), wp, 'op_wp')
                S.barrier()

        def ffn_phase(l):
            with ExitStack() as hs:
                hT = sb("hT2", [128, KD, T], BF16, st=hs)
                norm_phase(hT, 'nfg')
                with ExitStack() as ph:
                    ph.enter_context(nc.named_scope("ffn_up"))
                    PW = 256
                    wg = [sb("ff_wg%d" % i, [128, KD, PW], BF16, st=ph) for i in range(2)]
                    wu = [sb("ff_wu%d" % i, [128, KD, PW], BF16, st=ph) for i in range(2)]
                    sg = [sb("ff_sg%d" % i, [128, 512], st=ph) for i in range(2)]
                    ao = [sb("ff_ao%d" % i, [128, 512], BF16, st=ph) for i in range(2)]
                    Wg = W['w_ffn_gate'][l].rearrange("(kt p) c -> p kt c", p=128)
                    Wu = W['w_ffn_up'][l].rearrange("(kt p) c -> p kt c", p=128)
                    n = 0
                    for pi in range(D_FF // PW):
                        b = pi % 2
                        DMA('pool', wg[b][:], Wg[:, :, pi * PW:(pi + 1) * PW], (), [('ff_wg', b)])
                        DMA('pool', wu[b][:], Wu[:, :, pi * PW:(pi + 1) * PW], (), [('ff_wu', b)])
                        for ci in range(PW // 128):
                            ft = pi * (PW // 128) + ci
                            for tg in range(NTG):
                                tsl = slice(tg * 512, (tg + 1) * 512)
                                k0 = n % 2
                                bG = (n % 4) * 2
                                bU = bG + 1
                                n += 1
                                for kt in range(KD):
                                    MM(PSb(bG), wg[b][:, kt, ci * 128:(ci + 1) * 128], hT[:, kt, tsl], kt == 0, kt == KD - 1,
                                       [('ff_wg', b), ('hT', kt, tg)], [('ps', bG)])
                                for kt in range(KD):
                                    MM(PSb(bU), wu[b][:, kt, ci * 128:(ci + 1) * 128], hT[:, kt, tsl], kt == 0, kt == KD - 1,
                                       [('ff_wu', b), ('hT', kt, tg)], [('ps', bU)])
                                ACT(sg[k0][:], PSb(bG), AF.Silu, [('ps', bG)], [('ff_sg', k0)])
                                TT(ao[k0][:], sg[k0][:], PSb(bU), ALU.mult, [('ff_sg', k0), ('ps', bU)], [('ff_ao', k0)])
                                DMA('sp', actT[ft * 128:(ft + 1) * 128, tsl], ao[k0][:], [('ff_ao', k0)], ['actT'])
                    S.barrier()
            with ExitStack() as ph:
                ph.enter_context(nc.named_scope("ffn_down"))
                KF = D_FF // 128
                PW = 256
                TH = 1024
                asb = sb("ff_act", [128, KF, TH], BF16, st=ph)
                wd = [sb("ff_wd%d" % i, [128, KF, PW], BF16, st=ph) for i in range(2)]
                xl = [sb("ff_xl%d" % i, [128, 512], st=ph) for i in range(3)]
                xo = [sb("ff_xo%d" % i, [128, 512], st=ph) for i in range(3)]
                Wd = W['w_ffn_down'][l].rearrange("(kt p) c -> p kt c", p=128)
                aTv = actT.rearrange("(kt p) t -> p kt t", p=128)
                n = 0
                pn = 0
                for th in range(T // TH):
                    for kq in range(4):
                        ks = slice(kq * 11, (kq + 1) * 11)
                        DMA('sp', asb[:, ks, :], aTv[:, ks, th * TH:(th + 1) * TH], ['actT'], [('ff_act', kq)])
                    epi = resid_epi(xl, xo)
                    for pi in range(D // PW):
                        b = pn % 2
                        pn += 1
                        DMA('pool', wd[b][:], Wd[:, :, pi * PW:(pi + 1) * PW], (), [('ff_wd', b)])
                        for ci in range(PW // 128):
                            ct = pi * (PW // 128) + ci
                            for tgi in range(TH // 512):
                                tg = th * (TH // 512) + tgi
                                bank = n % 8
                                n += 1
                                for kt in range(KF):
                                    MM(PSb(bank), wd[b][:, kt, ci * 128:(ci + 1) * 128], asb[:, kt, tgi * 512:(tgi + 1) * 512], kt == 0, kt == KF - 1,
                                       [('ff_wd', b), ('ff_act', kt // 11)], [('ps', bank)])
                                epi(ct, ct * 128, 128, tg, PSb(bank), [('ps', bank)])
                S.barrier()

        def final_phase():
            with ExitStack() as ph:
                xin = [sb("fx%d" % i, [128, KD, 512], st=ph) for i in range(2)]
                sq = [sb("fsq%d" % i, [128, 512], st=ph) for i in range(2)]
                rs = [sb("frs%d" % i, [128, 512], st=ph) for i in range(2)]
                ot = [sb("fot%d" % i, [128, D], st=ph) for i in range(2)]
                n = 0
                no = 0
                for tg in range(NTG):
                    b = tg % 2
                    tsl = slice(tg * 512, (tg + 1) * 512)
                    DMA('sp', xin[b][:], xTv[:, :, tsl], ['xT'], [('fx', b)])
                    for dk in range(KD):
                        ACT(sq[dk % 2][:], xin[b][:, dk, :], AF.Square, [('fx', b)], [('fsq', dk % 2)])
                        MM(PSb(b), onesF[:], sq[dk % 2][:], dk == 0, dk == KD - 1, [('fsq', dk % 2), 'const'], [('ps', b)])
                    ACT(rs[b][:], PSb(b), AF.Sqrt, [('ps', b)], [('frs', b)], scale=1.0 / D, bias=RMS_EPS)
                    RECIP(rs[b][:], rs[b][:], [('frs', b)], [('frs', b)])
                    for dk in range(KD):
                        STT(xin[b][:, dk, :], xin[b][:, dk, :], pc('nfin', dk), rs[b][:], ALU.mult, ALU.mult,
                            [('fx', b), 'pcol', ('frs', b)], [('fx', b)])
                    for tt in range(4):
                        o_ = no % 2
                        no += 1
                        for q in range(4):
                            bank = 2 + n % 6
                            n += 1
                            for j in range(4):
                                dk = q * 4 + j
                                TR(PSb(bank)[:, j * 128:(j + 1) * 128], xin[b][:, dk, tt * 128:(tt + 1) * 128], identF[:],
                                   [('fx', b), 'const'], [('ps', bank)])
                            EVAC(ot[o_][:, q * 512:(q + 1) * 512], PSb(bank), [('ps', bank)], [('fot', o_)])
                        row = (tg * 4 + tt) * 128
                        DMA('act', out_d[row:row + 128, :], ot[o_][:], [('fot', o_)], [('out', row)])
                S.barrier()

        for l in range(NL):
            DMA('sp', pcol[:], pcol_d[l], (), ['pcol'])
            Winv = W['w_in'][l].rearrange("(kt p) c -> p kt c", p=128)
            with ExitStack() as hs:
                hT = sb("hT", [128, KD, T], BF16, st=hs)
                norm_phase(hT, 'nmg')
                if l == 0 and 'hTd' in DBG:
                    DMA('sp', DBG['hTd'].rearrange("(dk p) t -> p dk t", p=128), hT[:], [('hT', dk, tg) for dk in range(KD) for tg in range(NTG)], [('dbg', 'hTd')])

                def h_rhs(kt, tg):
                    return hT[:, kt, tg * 512:(tg + 1) * 512]

                def h_key(kt, tg):
                    return ('hT', kt, tg)

                with ExitStack() as ph:
                    ph.enter_context(nc.named_scope("proj"))
                    wp = [sb("wp%d" % i, [128, KD, 512], BF16, st=ph) for i in range(2)]
                    ob = [sb("pob%d" % i, [128, 512], F32, st=ph) for i in range(3)]
                    obh = [sb("pobh%d" % i, [128, 512], BF16, st=ph) for i in range(3)]
                    zc = [sb("pzc%d" % i, [128, T], F32, st=ph) for i in range(2)]
                    ccol = sb("ccol", [128, 29], F32, st=ph)
                    cnt = [0]
                    o_p, _ = PC['mup']
                    o_n, _ = PC['mun']
                    TT(ccol[:], pcol[:, o_p:o_p + 29], pcol[:, o_n:o_n + 29], ALU.add, ['pcol'], ['ccol'])
                    TS(ccol[:], ccol[:], -1.0, 1.0, ALU.mult, ALU.add, ['ccol'], ['ccol'])

                    def epi_u(ci, c0, m, tg, ps, pk):
                        k = cnt[0] % 3
                        cnt[0] += 1
                        ACT(ob[k][:], ps, AF.Gelu, pk, [('pob', k)])
                        DMA('sp', uT[c0:c0 + 128, tg * 512:(tg + 1) * 512], ob[k][:], [('pob', k)], ['uT'])

                    def epi_g(ci, c0, m, tg, ps, pk):
                        k = cnt[0] % 3
                        cnt[0] += 1
                        ACT(obh[k][:], ps, AF.Sigmoid, pk, [('pobh', k)])
                        r0 = c0 - OFF_G
                        DMA('sp', gT[r0:r0 + 128, tg * 512:(tg + 1) * 512], obh[k][:], [('pobh', k)], ['gT'])

                    def tap3(dst, r0, m, ps, pk, a_ap, b_ap, p_ap, n_ap, keys):
                        k = cnt[0] % 2
                        cnt[0] += 1
                        z = zc[k]
                        ACT(z[0:m, :], ps, AF.Identity, pk + keys, [('pzc', k)], scale=a_ap, bias=b_ap)
                        STT(z[0:m, 1:T], ps[:, 0:T - 1], p_ap, z[0:m, 1:T], ALU.mult, ALU.add, pk + keys + [('pzc', k)], [('pzc', k)])
                        STT(z[0:m, 0:T - 1], ps[:, 1:T], n_ap, z[0:m, 0:T - 1], ALU.mult, ALU.add, pk + keys + [('pzc', k)], [('pzc', k)])
                        DMA('sp', dst[r0:r0 + m, :], z[0:m, :], [('pzc', k)], ['bcT'])

                    def epi_b(ci, c0, m, tg, ps, pk):
                        tap3(bT, c0 - OFF_B, m, ps, pk, pc('cw1', ci), pc('cb', ci), pc('cw0', ci), pc('cw2', ci), ['pcol'])

                    def epi_c(ci, c0, m, tg, ps, pk):
                        tap3(cT, c0 - OFF_C, m, ps, pk, ccol[0:m, ci:ci + 1], 0.0, pc('mup', ci, m), pc('mun', ci, m), ['pcol', 'ccol'])

                    linear_fm(Winv, KD, [(i * 128, 128) for i in range(8)], h_rhs, h_key, epi_u, wp, 'wp')
                    linear_fm(Winv, KD, [(OFF_B + i * 128, 128) for i in range(24)], h_rhs, h_key, epi_b, wp, 'wp', full=True)
                    linear_fm(Winv, KD, [(OFF_C + c0, m) for (c0, m) in C_TILES], h_rhs, h_key, epi_c, wp, 'wp', full=True)
                    linear_fm(Winv, KD, [(OFF_G + i * 128, 128) for i in range(48)], h_rhs, h_key, epi_g, wp, 'wp')
                S.barrier()
                if l == 0:
                    dump('uT', uT)
                    dump('bT', bT)
                    dump('cT', cT)
                    dump('gT', gT)

                with ExitStack() as ph:
                    ph.enter_context(nc.named_scope("mixa"))
                    wv = sb("ma_wv", [128, KD, 1024], BF16, st=ph)
                    lng = sb("ma_lng", [128, 1024], F32, st=ph)
                    lnb = sb("ma_lnb", [128, 1024], F32, st=ph)
                    bsb = sb("ma_bsb", [128, 8, 128], F32, st=ph)
                    wsn = sb("ma_wsn", [128, 8, 128], F32, st=ph)
                    wsT = sb("ma_wsT", [128, 8, 128], BF16, st=ph)
                    vg = [sb("ma_vg%d" % i, [128, 1024], F32, st=ph) for i in range(3)]
                    vc = [sb("ma_vc%d" % i, [128, 1024], F32, st=ph) for i in range(3)]
                    vln = [sb("ma_vln%d" % i, [128, 1024], BF16, st=ph) for i in range(3)]
                    ut = [sb("ma_ut%d" % i, [128, 8, 128], F32, st=ph) for i in range(3)]
                    ya = [sb("ma_ya%d" % i, [128, 8, 128], BF16, st=ph) for i in range(3)]
                    tm = [sb("ma_tm%d" % i, [128, 512], F32, st=ph) for i in range(2)]
                    stt = [sb("ma_st%d" % i, [128, 8], F32, st=ph) for i in range(3)]
                    DMA('pool', wv[:], Winv[:, :, A_W:2 * A_W], (), ['ma_wv'])
                    DMA('sp', lng[:], W['gm_ln_g'][l:l + 1, :].broadcast_to([128, 1024]), (), ['ma_c'])
                    DMA('sp', lnb[:], W['gm_ln_b'][l:l + 1, :].broadcast_to([128, 1024]), (), ['ma_c'])
                    DMA('sp', bsb[:].rearrange("p g q -> p (g q)"),
                        W['gm_bs'][l:l + 1].rearrange("o g q -> o (g q)").broadcast_to([128, 1024]), (), ['ma_c'])
                    DMA('sp', wsn[:], W['gm_ws'][l].rearrange("g p q -> p g q"), (), ['ma_wsn'])
                    for hb in range(2):
                        for j in range(4):
                            g = hb * 4 + j
                            TR(PSb(hb)[:, j * 128:(j + 1) * 128], wsn[:, g, :], identF[:], ['ma_wsn', 'const'], [('ps', hb)])
                        EVAC(wsT[:, hb * 4:(hb + 1) * 4, :].rearrange("p g q -> p (g q)"), PSb(hb), [('ps', hb)], ['ma_wsT'])
                    uTv = uT.rearrange("(g d) t -> d g t", d=128)
                    yaTv = yaT.rearrange("(g d) t -> d g t", d=128)
                    def ma_vproj(i):
                        b = i % 3
                        tsl = slice(i * 128, (i + 1) * 128)
                        DMA('sp', ut[b][:], uTv[:, :, tsl], ['uT'], [('ma_ut', b)])
                        for half in range(2):
                            bank = b * 2 + half
                            for kt in range(KD):
                                MM(PSb(bank), hT[:, kt, tsl], wv[:, kt, half * 512:(half + 1) * 512], kt == 0, kt == KD - 1,
                                   [('hT', kt, i // 4), 'ma_wv'], [('ps', bank)])

                    def ma_gelu(i):
                        b = i % 3
                        for half in range(2):
                            bank = b * 2 + half
                            ACT(vg[b][:, half * 512:(half + 1) * 512], PSb(bank), AF.Gelu, [('ps', bank)], [('ma_vg', b, half), ('ma_st', b)],
                                accum_out=stt[b][:, half:half + 1])

                    def ma_elem(i):
                        b = i % 3
                        TT(stt[b][:, 2:3], stt[b][:, 0:1], stt[b][:, 1:2], ALU.add, [('ma_st', b)], [('ma_st', b)])
                        TS(stt[b][:, 3:4], stt[b][:, 2:3], -1.0 / A_W, None, ALU.mult, None, [('ma_st', b)], [('ma_st', b)])
                        TS(vc[b][:], vg[b][:], stt[b][:, 3:4], None, ALU.add, None,
                           [('ma_vg', b, 0), ('ma_vg', b, 1), ('ma_st', b)], [('ma_vc', b)])
                        ACT(vg[b][:], vc[b][:], AF.Square, [('ma_vc', b)], [('ma_vg', b, 0), ('ma_vg', b, 1), ('ma_st', b)],
                            accum_out=stt[b][:, 4:5])
                        ACT(stt[b][:, 5:6], stt[b][:, 4:5], AF.Sqrt, [('ma_st', b)], [('ma_st', b)], scale=1.0 / A_W, bias=LN_EPS)
                        RECIP(stt[b][:, 6:7], stt[b][:, 5:6], [('ma_st', b)], [('ma_st', b)])
                        STT(vc[b][:], vc[b][:], stt[b][:, 6:7], lng[:], ALU.mult, ALU.mult, [('ma_vc', b), ('ma_st', b), 'ma_c'], [('ma_vc', b)])
                        TT(vln[b][:], vc[b][:], lnb[:], ALU.add, [('ma_vc', b), 'ma_c'], [('ma_vln', b)])

                    def ma_spatial(i):
                        b = i % 3
                        tsl = slice(i * 128, (i + 1) * 128)
                        for hb in range(2):
                            bank = 6 + hb
                            for j in range(4):
                                g = hb * 4 + j
                                MM(PSb(bank)[:, j * 128:(j + 1) * 128], vln[b][:, g * 128:(g + 1) * 128], wsT[:, g, :], True, True,
                                   [('ma_vln', b), 'ma_wsT'], [('ps', bank)])
                            TT(tm[hb][:], PSb(bank), bsb[:, hb * 4:(hb + 1) * 4, :].rearrange("p g q -> p (g q)"), ALU.add,
                               [('ps', bank), 'ma_c'], [('ma_tm', hb)])
                            TT(ya[b][:, hb * 4:(hb + 1) * 4, :].rearrange("p g q -> p (g q)"), tm[hb][:],
                               ut[b][:, hb * 4:(hb + 1) * 4, :].rearrange("p g q -> p (g q)"), ALU.mult,
                               [('ma_tm', hb), ('ma_ut', b)], [('ma_ya', b)])
                        DMA('act', yaTv[:, :, tsl], ya[b][:], [('ma_ya', b)], ['yaT'])

                    ma_vproj(0)
                    ma_gelu(0)
                    for i in range(NT):
                        if i + 1 < NT:
                            ma_vproj(i + 1)
                        ma_elem(i)
                        if i + 1 < NT:
                            ma_gelu(i + 1)
                        ma_spatial(i)
                S.barrier()
            if l == 0:
                dump('yaT', yaT)
            if stop == 'mixa':
                break
            hyena_phase(l)
            if stop == 'hy':
                break
            if l == 0:
                dump('spec', spec)
                dump('ybT', ybT)
            rwkv_phase(l)
            if l == 0:
                dump('ycT', ycT)
            if stop == 'rw':
                break
            merge_phase(l)
            if l == 0:
                dump('mergedT', mgT)
            outproj_phase(l)
            if stop == 'outp':
                break
            ffn_phase(l)
        dump('xT', xT)
        if stop is None:
            final_phase()

        S.barrier()
    return nc


_NC_CACHE = {}


def kernel(**inputs):
    if 'nc' not in _NC_CACHE:
        _NC_CACHE['nc'] = build()
        _NC_CACHE['consts'] = make_consts()
    nc = _NC_CACHE['nc']
    consts = _NC_CACHE['consts']
    inp = {k_: np.ascontiguousarray(np.asarray(v)) for k_, v in inputs.items()}
    pcol = make_pcol(inp)
    base = {n: inp[n] for n in WEIGHT_SHAPES}
    base.update(consts)
    base['pcol'] = pcol
    NB_ = inp['x'].shape[0]
    in_maps = []
    for c in range(NCORES_USED):
        m = dict(base)
        m['x'] = np.ascontiguousarray(inp['x'][c % NB_])
        in_maps.append(m)
    res = run_bass_kernel_spmd(nc, in_maps, core_ids=list(range(NCORES_USED)))
    out = np.stack([np.asarray(res.results[b]['out'], dtype=np.float32) for b in range(NB_)], axis=0)
    return out
```

```python
import math
import numpy as np
import ml_dtypes
from contextlib import ExitStack
import concourse.bass as bass
import concourse.mybir as mybir
from concourse.bass_utils import run_bass_kernel_spmd

F32 = mybir.dt.float32
BF16 = mybir.dt.bfloat16
AF = mybir.ActivationFunctionType
ALU = mybir.AluOpType
AX = mybir.AxisListType

NCORES = 8
NCORES_USED = 4
T = 2048
D = 2048
DEPTH = 4
A_W = 1024
B_W = 1024
C_W = 1024
C_IN = 3392
N_IN = 14656
D_FF = 5632
NT = T // 128
NTG = T // 512
KD = D // 128
RMS_EPS = 1e-6
LN_EPS = 1e-5
GN_EPS = 64e-5
HY_MIN_DECAY = -math.log(1e-2) / 1.5
HY_MAX_DECAY = -math.log(1e-2) / 0.3
OFF_B = 2 * A_W
OFF_C = OFF_B + 3 * B_W
OFF_G = OFF_C + C_IN

WEIGHT_SHAPES = {
    'w_in': (DEPTH, D, N_IN), 'gm_ln_g': (DEPTH, A_W), 'gm_ln_b': (DEPTH, A_W),
    'gm_ws': (DEPTH, 8, 128, 128), 'gm_bs': (DEPTH, 8, 128),
    'hy_w1': (DEPTH, 33, 64), 'hy_w2': (DEPTH, 64, 64), 'hy_w3': (DEPTH, 64, 64), 'hy_w4': (DEPTH, 64, 4096),
    'hy_log_decay': (DEPTH, 2, 2, 1024), 'hy_bias_d': (DEPTH, 2, 1024),
    'rw_w2': (DEPTH, 2, 48, 1024), 'rw_a2': (DEPTH, 2, 48, 1024),
    'rw_v1': (DEPTH - 1, 1024, 32), 'rw_v2': (DEPTH - 1, 32, 1024), 'rw_g2': (DEPTH, 128, 1024),
    'w_branch_a': (DEPTH, A_W, D), 'w_branch_b': (DEPTH, B_W, D), 'w_branch_c': (DEPTH, C_W, D),
    'w_out': (DEPTH, D, D), 'w_ffn_gate': (DEPTH, D, D_FF), 'w_ffn_up': (DEPTH, D, D_FF),
    'w_ffn_down': (DEPTH, D_FF, D),
}

PC = {}
_o = 0
for _n, _c in [('nmg', 16), ('nfg', 16), ('cw0', 24), ('cw1', 24), ('cw2', 24), ('cb', 24), ('mup', 29), ('mun', 29),
               ('w0', 16), ('a0', 16), ('v0', 8), ('kk', 8), ('ka', 8), ('rk', 8), ('lng', 8), ('lnb', 8),
               ('hyb', 4), ('nfin', 16)]:
    PC[_n] = (_o, _c)
    _o += _c
NPC = _o
C_TILES = [(i * 128, 128) for i in range(25)] + [(3200, 48), (3248, 48), (3296, 48), (3344, 48)]


def _cols(v):
    return np.ascontiguousarray(np.asarray(v, np.float32).reshape(-1, 128).T)


def make_pcol(inp):
    pc = np.zeros((DEPTH, 128, NPC), np.float32)

    def put(l, name, arr):
        o, c = PC[name]
        assert arr.shape == (128, c), (name, arr.shape)
        pc[l, :, o:o + c] = arr
    for l in range(DEPTH):
        put(l, 'nmg', _cols(inp['norm_mix_g'][l]))
        put(l, 'nfg', _cols(inp['norm_ffn_g'][l]))
        for j in range(3):
            put(l, 'cw%d' % j, _cols(inp['hy_conv_w'][l, j]))
        put(l, 'cb', _cols(inp['hy_conv_b'][l]))
        for nm, src in (('mup', 'rw_mu_prev'), ('mun', 'rw_mu_next')):
            a = np.zeros((128, 29), np.float32)
            for i, (c0, m) in enumerate(C_TILES):
                a[:m, i] = inp[src][l, c0:c0 + m]
            put(l, nm, a)
        put(l, 'w0', _cols(inp['rw_w0'][l]))
        put(l, 'a0', _cols(inp['rw_a0'][l]))
        if l > 0:
            put(l, 'v0', _cols(inp['rw_v0'][l - 1]))
        put(l, 'kk', _cols(inp['rw_k_k'][l]))
        put(l, 'ka', _cols(inp['rw_k_a'][l]))
        put(l, 'rk', _cols(inp['rw_r_k'][l]))
        put(l, 'lng', _cols(inp['rw_ln_g'][l]))
        put(l, 'lnb', _cols(inp['rw_ln_b'][l]))
        hb = np.zeros((128, 4), np.float32)
        hb[:64, 0] = inp['hy_b1'][l]
        hb[:64, 1] = inp['hy_b2'][l]
        hb[:64, 2] = inp['hy_b3'][l]
        hb[:64, 3] = inp['hy_freq'][l]
        put(l, 'hyb', hb)
        put(l, 'nfin', _cols(inp['norm_final_g']))
    return pc


def make_consts():
    c = {}
    c['identF'] = np.eye(128, dtype=np.float32)
    s = np.arange(128)[:, None]
    t = np.arange(128)[None, :]
    su = (s < t).astype(np.float32)
    iu = (s <= t).astype(np.float32)
    c['maskU4'] = np.concatenate([su, iu, su, iu], axis=1)
    c['maskL'] = (s > t).astype(np.float32)
    c['blockones'] = ((s // 64) == (t // 64)).astype(np.float32)
    c['onesF'] = np.ones((128, 128), np.float32)
    c['identB'] = np.eye(128).astype(ml_dtypes.bfloat16)
    c['maskL4'] = np.tile(c['maskL'], (1, 4))
    c['ident4'] = np.tile(c['identF'], (1, 4))
    rm = np.ones((128, T), np.float32)
    rm[:, 0::128] = 0.0
    c['rmask'] = rm
    n = np.arange(T, dtype=np.float64)
    f = np.arange(T, dtype=np.float64) + 0.5
    ang = 2.0 * np.pi * np.outer(n, f) / (2 * T)
    c['CT'] = np.cos(ang).astype(ml_dtypes.bfloat16)
    c['ST'] = np.sin(ang).astype(ml_dtypes.bfloat16)
    c['CF'] = np.ascontiguousarray(np.cos(ang).T).astype(ml_dtypes.bfloat16)
    c['SF'] = np.ascontiguousarray(np.sin(ang).T).astype(ml_dtypes.bfloat16)
    tt = np.linspace(0.0, 1.0, T, dtype=np.float32)[:, None]
    bands = 16
    fr = np.linspace(1e-4, bands - 1, bands, dtype=np.float32)
    a2 = (np.float32(2.0 * math.pi / T) * np.arange(T, dtype=np.float32)[:, None]) * fr[None, :]
    feats = np.concatenate([tt, np.cos(a2), -np.sin(a2)], axis=-1).astype(np.float32)
    c['featsT'] = np.ascontiguousarray(feats.T)
    c['tneg'] = np.ascontiguousarray((-tt[:, 0]).reshape(NT, 128).T)
    return c


CONST_SHAPES = {'identF': ((128, 128), F32), 'maskU4': ((128, 512), F32), 'maskL': ((128, 128), F32),
                'blockones': ((128, 128), F32), 'onesF': ((128, 128), F32), 'maskL4': ((128, 512), F32),
                'ident4': ((128, 512), F32), 'rmask': ((128, T), F32), 'identB': ((128, 128), BF16),
                'CT': ((T, T), BF16), 'ST': ((T, T), BF16), 'CF': ((T, T), BF16), 'SF': ((T, T), BF16),
                'featsT': ((33, T), F32), 'tneg': ((128, NT), F32)}


class Sched:
    NDMA = 12

    def __init__(self, nc, es):
        self.nc = nc
        self.engs = {'pe': nc.tensor, 'act': nc.scalar, 'dve': nc.vector, 'pool': nc.gpsimd, 'sp': nc.sync}
        self.sem = {k: es.enter_context(nc.semaphore("s_" + k)) for k in self.engs}
        self.cnt = {k: 0 for k in self.engs}
        self.waited = {}
        self.dsem = {}
        self.dcnt = {}
        self.dnext = {}
        for q in ('sp', 'act', 'pool'):
            self.dsem[q] = [es.enter_context(nc.semaphore("d_%s%d" % (q, i))) for i in range(self.NDMA)]
            self.dcnt[q] = [0] * self.NDMA
            self.dnext[q] = 0
        self.res = {}
        self.ninst = 0

    def _wait(self, e, tok):
        if tok[0] == 'e':
            _, f, v = tok
            if f == e and e == 'pe':
                return
            key = (e, f)
        else:
            _, q, slot, v = tok
            key = (e, 'd', q, slot)
        if self.waited.get(key, 0) >= v:
            return
        self.waited[key] = v
        sem = self.sem[tok[1]] if tok[0] == 'e' else self.dsem[tok[1]][tok[2]]
        self.engs[e].wait_ge(sem, v)

    def _deps(self, e, reads, writes):
        for r in reads:
            st = self.res.get(r)
            if st and st[0] is not None:
                self._wait(e, st[0])
        for w in writes:
            st = self.res.get(w)
            if st:
                if st[0] is not None:
                    self._wait(e, st[0])
                for t in st[1].values():
                    self._wait(e, t)

    def _record(self, tok, reads, writes):
        for r in reads:
            st = self.res.get(r)
            if st is None:
                st = self.res[r] = [None, {}]
            k = tok[1] if tok[0] == 'e' else (tok[1], tok[2])
            st[1][k] = tok
        for w in writes:
            self.res[w] = [tok, {}]

    def op(self, e, fn, reads=(), writes=()):
        self._deps(e, reads, writes)
        ins = fn()
        self.cnt[e] += 1
        ins.then_inc(self.sem[e], 1)
        self._record(('e', e, self.cnt[e]), reads, writes)
        self.ninst += 1
        return ins

    def dma(self, q, out, in_, reads=(), writes=(), **kw):
        slot = self.dnext[q]
        self.dnext[q] = (slot + 1) % self.NDMA
        if self.dcnt[q][slot] > 0:
            self._wait(q, ('d', q, slot, self.dcnt[q][slot]))
        self._deps(q, reads, writes)
        ins = self.engs[q].dma_start(out=out, in_=in_, **kw)
        self.dcnt[q][slot] += 16
        ins.then_inc(self.dsem[q][slot], 16)
        self._record(('d', q, slot, self.dcnt[q][slot]), reads, writes)
        self.ninst += 1
        return ins

    def barrier(self):
        for e in self.engs:
            for f in self.engs:
                if self.cnt[f] > 0:
                    key = (e, f)
                    if self.waited.get(key, 0) < self.cnt[f]:
                        self.waited[key] = self.cnt[f]
                        self.engs[e].wait_ge(self.sem[f], self.cnt[f])
            for q in self.dsem:
                for slot in range(self.NDMA):
                    if self.dcnt[q][slot] > 0:
                        self._wait(e, ('d', q, slot, self.dcnt[q][slot]))
        self.res = {}


def build(NL=DEPTH, dbg=(), stop=None):
    nc = bass.Bass("TRN2", target_bir_lowering=False)

    def din(name, shape, dt=F32):
        return nc.dram_tensor(name, list(shape), dt, kind="ExternalInput").ap()

    def dscr(name, shape, dt=F32):
        return nc.dram_tensor(name, list(shape), dt, kind="Internal").ap()

    def dout(name, shape, dt=F32):
        return nc.dram_tensor(name, list(shape), dt, kind="ExternalOutput").ap()

    x_in = din("x", [T, D])
    W = {n: din(n, s) for n, s in WEIGHT_SHAPES.items()}
    pcol_d = din("pcol", [DEPTH, 128, NPC])
    CD = {n: din(n, s, dt) for n, (s, dt) in CONST_SHAPES.items()}
    out_d = dout("out", [T, D])
    DBG = {}
    dbg_shapes = {'xT': ([D, T], F32), 'xT0': ([D, T], F32), 'uT': ([A_W, T], F32), 'bT': ([3 * B_W, T], F32), 'cT': ([C_IN, T], F32),
                  'gT': ([3 * D, T], BF16), 'yaT': ([A_W, T], BF16), 'ybT': ([B_W, T], BF16),
                  'ycT': ([C_W, T], BF16), 'hTd': ([D, T], BF16), 'spec': ([2, 2, T, B_W], F32),
                  'mergedT': ([D, T], BF16)}
    for n in dbg:
        DBG[n] = dout("dbg_" + n, *dbg_shapes[n])

    xT = dscr("xT_s", [D, T])
    uT = dscr("uT_s", [A_W, T])
    bT = dscr("bT_s", [3 * B_W, T])
    cT = dscr("cT_s", [C_IN, T])
    gT = dscr("gT_s", [3 * D, T], BF16)
    yaT = dscr("yaT_s", [A_W, T], BF16)
    ybT = dscr("ybT_s", [B_W, T], BF16)
    ycT = dscr("ycT_s", [C_W, T], BF16)
    mgT = dscr("mgT_s", [D, T], BF16)
    actT = dscr("actT_s", [D_FF, T], BF16)
    vfT = dscr("vfT_s", [C_W, T])
    x1tok = dscr("x1tok_s", [T, B_W])
    hfil = dscr("hfil_s", [T, 4096])
    spec = dscr("spec_s", [2, 2, T, B_W])
    xTv = xT.rearrange("(dk p) t -> p dk t", p=128)

    with ExitStack() as es:
        S = Sched(nc, es)

        uid = [0]

        def sb(name, shape, dt=F32, st=es):
            uid[0] += 1
            return st.enter_context(nc.sbuf_tensor("sb%d_%s" % (uid[0], name), list(shape), dt))

        PSA = es.enter_context(nc.psum_tensor("psA", [128, 2048], F32))
        PSB = es.enter_context(nc.psum_tensor("psB", [128, 2048], F32))

        def PSb(i):
            return (PSA if i < 4 else PSB)[:, (i % 4) * 512:(i % 4 + 1) * 512]

        def PSfull(a):
            return PSA if a == 0 else PSB

        def ACT(out, in_, func, r, w, **kw):
            return S.op('act', lambda: nc.scalar.activation(out=out, in_=in_, func=func, **kw), r, w)

        def TT(out, a, b, op, r, w, eng='dve'):
            e = nc.vector if eng == 'dve' else nc.gpsimd
            return S.op(eng, lambda: e.tensor_tensor(out=out, in0=a, in1=b, op=op), r, w)

        def TS(out, a, s1, s2, op0, op1, r, w, eng='dve'):
            e = nc.vector if eng == 'dve' else nc.gpsimd
            if s2 is None:
                return S.op(eng, lambda: e.tensor_scalar(out=out, in0=a, scalar1=s1, scalar2=None, op0=op0), r, w)
            return S.op(eng, lambda: e.tensor_scalar(out=out, in0=a, scalar1=s1, scalar2=s2, op0=op0, op1=op1), r, w)

        def STT(out, a, s, b, op0, op1, r, w, eng='dve'):
            e = nc.vector if eng == 'dve' else nc.gpsimd
            return S.op(eng, lambda: e.scalar_tensor_tensor(out=out, in0=a, scalar=s, in1=b, op0=op0, op1=op1), r, w)

        def RECIP(out, in_, r, w):
            return S.op('dve', lambda: nc.vector.reciprocal(out=out, in_=in_), r, w)

        def CP(out, in_, r, w, eng='dve'):
            if eng == 'act':
                return S.op('act', lambda: nc.scalar.copy(out=out, in_=in_), r, w)
            e = nc.vector if eng == 'dve' else nc.gpsimd
            return S.op(eng, lambda: e.tensor_copy(out=out, in_=in_), r, w)

        def MSET(ap, val, w, eng='pool'):
            e = nc.vector if eng == 'dve' else nc.gpsimd
            return S.op(eng, lambda: e.memset(ap, val), (), w)

        def MM(out, lhsT, rhs, start, stop, r, w):
            return S.op('pe', lambda: nc.tensor.matmul(out, lhsT, rhs, start=start, stop=stop), r, w)

        def TR(out, in_, ident, r, w):
            return S.op('pe', lambda: nc.tensor.transpose(out, in_, ident), r, w)

        def DMA(q, out, in_, r, w, **kw):
            return S.dma(q, out, in_, r, w, **kw)

        cp_flip = [0]

        def EVAC(out, in_, r, w):
            cp_flip[0] ^= 1
            return CP(out, in_, r, w, eng='act' if cp_flip[0] else 'dve')

        identF = sb("identF", [128, 128])
        maskU4 = sb("maskU4", [128, 512])
        maskL = sb("maskL", [128, 128])
        blockones = sb("blockones", [128, 128])
        onesF = sb("onesF", [128, 128])
        tneg = sb("tneg", [128, NT])
        maskL4 = sb("maskL4", [128, 512])
        ident4 = sb("ident4", [128, 512])
        identB = sb("identB", [128, 128], BF16)
        pcol = sb("pcol", [128, NPC])
        for n, t_ in (('identF', identF), ('maskU4', maskU4), ('maskL', maskL), ('blockones', blockones),
                      ('onesF', onesF), ('tneg', tneg), ('maskL4', maskL4), ('ident4', ident4), ('identB', identB)):
            DMA('sp', t_[:], CD[n][:, :], (), ['const'])

        def pc(name, i=0, m=128):
            o, c = PC[name]
            return pcol[0:m, o + i:o + i + 1]

        def dump(name, src_ap):
            if name in DBG:
                DMA('sp', DBG[name], src_ap, ['ALLDRAM'], [('dbg', name)])

        with ExitStack() as ph:
            xt = [sb("p0x%d" % i, [128, D], st=ph) for i in range(2)]
            stg = [sb("p0s%d" % i, [128, 4, 128], st=ph) for i in range(4)]
            n = 0
            for i in range(NT):
                b = i % 2
                DMA('sp', xt[b][:], x_in[i * 128:(i + 1) * 128, :], (), [('p0x', b)])
                for q in range(4):
                    bank = n % 8
                    s_ = n % 4
                    for j in range(4):
                        dk = q * 4 + j
                        TR(PSb(bank)[:, j * 128:(j + 1) * 128], xt[b][:, dk * 128:(dk + 1) * 128], identF[:],
                           [('p0x', b), 'const'], [('ps', bank)])
                    EVAC(stg[s_][:].rearrange("p a b -> p (a b)"), PSb(bank), [('ps', bank)], [('p0s', s_)])
                    DMA('act', xTv[:, q * 4:(q + 1) * 4, i * 128:(i + 1) * 128], stg[s_][:], [('p0s', s_)], ['xT'])
                    n += 1
        S.barrier()
        if 'xT0' in DBG:
            DMA('sp', DBG['xT0'], xT, (), [('dbg', 'xT0')])

        def norm_phase(hT, gname):
            with ExitStack() as ph:
                ph.enter_context(nc.named_scope("norm"))
                xin = [sb("nx%d" % i, [128, KD, 512], st=ph) for i in range(2)]
                sq = [sb("nsq%d" % i, [128, 512], st=ph) for i in range(2)]
                rs = [sb("nrs%d" % i, [128, 512], st=ph) for i in range(2)]
                for tg in range(NTG):
                    b = tg % 2
                    tsl = slice(tg * 512, (tg + 1) * 512)
                    DMA('sp', xin[b][:], xTv[:, :, tsl], ['xT'], [('nx', b)])
                    for dk in range(KD):
                        ACT(sq[dk % 2][:], xin[b][:, dk, :], AF.Square, [('nx', b)], [('nsq', dk % 2)])
                        MM(PSb(b), onesF[:], sq[dk % 2][:], dk == 0, dk == KD - 1, [('nsq', dk % 2), 'const'], [('ps', b)])
                    ACT(rs[b][:], PSb(b), AF.Sqrt, [('ps', b)], [('nrs', b)], scale=1.0 / D, bias=RMS_EPS)
                    RECIP(rs[b][:], rs[b][:], [('nrs', b)], [('nrs', b)])
                    for dk in range(KD):
                        STT(hT[:, dk, tsl], xin[b][:, dk, :], pc(gname, dk), rs[b][:], ALU.mult, ALU.mult,
                            [('nx', b), 'pcol', ('nrs', b)], [('hT', dk, tg)])
                S.barrier()

        def linear_fm(Wv, KT, ctiles, rhs_fn, rkey_fn, epi, wp, wpname, full=False, banks=(0, 1, 2, 3, 4, 5, 6, 7)):
            panels = []
            cur = None
            for ci, (c0, m) in enumerate(ctiles):
                if cur is None or (c0 + m - cur[0]) > 512 or c0 != cur[1]:
                    cur = [c0, c0, []]
                    panels.append(cur)
                cur[2].append((ci, c0, m))
                cur[1] = c0 + m
            nb = 0
            for pi, (p0, p1, tl) in enumerate(panels):
                b = pi % 2
                DMA('pool', wp[b][:, 0:KT, 0:p1 - p0], Wv[:, :, p0:p1], (), [(wpname, b)])
                for (ci, c0, m) in tl:
                    lo = c0 - p0
                    if full:
                        a = nb % 2
                        nb += 1
                        for tg in range(NTG):
                            bank = a * 4 + tg
                            for kt in range(KT):
                                MM(PSb(bank)[0:m, :], wp[b][:, kt, lo:lo + m], rhs_fn(kt, tg), kt == 0, kt == KT - 1,
                                   [(wpname, b), rkey_fn(kt, tg)], [('ps', bank)])
                        epi(ci, c0, m, None, PSfull(a)[0:m, :], [('ps', a * 4 + j) for j in range(4)])
                    else:
                        for tg in range(NTG):
                            bank = banks[nb % len(banks)]
                            nb += 1
                            for kt in range(KT):
                                MM(PSb(bank)[0:m, :], wp[b][:, kt, lo:lo + m], rhs_fn(kt, tg), kt == 0, kt == KT - 1,
                                   [(wpname, b), rkey_fn(kt, tg)], [('ps', bank)])
                            epi(ci, c0, m, tg, PSb(bank)[0:m, :], [('ps', bank)])


        SW = 256
        NSG = T // SW
        INV_SCALE = 2.0 / (2 * T)

        def load_slabs(slC, slS, b, Cm, Sm, g):
            DMA('sp', slC[b][:], Cm.rearrange("(kt p) c -> p kt c", p=128)[:, :, g * SW:(g + 1) * SW], (), [('slC', b)])
            DMA('act', slS[b][:], Sm.rearrange("(kt p) c -> p kt c", p=128)[:, :, g * SW:(g + 1) * SW], (), [('slS', b)])

        def dft_fwd(slC, slS, srcC, srcS, keyC, keyS, epi):
            nb = 0
            load_slabs(slC, slS, 0, CD['CT'], CD['ST'], 0)
            for g in range(NSG):
                b = g % 2
                if g + 1 < NSG:
                    load_slabs(slC, slS, (g + 1) % 2, CD['CT'], CD['ST'], g + 1)
                for fi in range(SW // 128):
                    ft = g * (SW // 128) + fi
                    for chh in range(2):
                        bC = (nb % 4) * 2
                        bS = bC + 1
                        nb += 1
                        for nt in range(NT):
                            MM(PSb(bC), slC[b][:, nt, fi * 128:(fi + 1) * 128], srcC[:, nt, chh * 512:(chh + 1) * 512],
                               nt == 0, nt == NT - 1, [('slC', b), (keyC, nt)], [('ps', bC)])
                        for nt in range(NT):
                            MM(PSb(bS), slS[b][:, nt, fi * 128:(fi + 1) * 128], srcS[:, nt, chh * 512:(chh + 1) * 512],
                               nt == 0, nt == NT - 1, [('slS', b), (keyS, nt)], [('ps', bS)])
                        epi(ft, chh, bC, bS)

        def hyena_phase(l):
            hy0 = ExitStack()
            asum = sb("hy_asum", [128, 4096], st=hy0)
            with ExitStack() as ph:
                ph.enter_context(nc.named_scope("hy_filt"))
                w1 = sb("hy_w1", [33, 64], st=ph)
                w2 = sb("hy_w2", [64, 64], st=ph)
                w3 = sb("hy_w3", [64, 64], st=ph)
                w4 = sb("hy_w4", [64, 4096], BF16, st=ph)
                z3b = sb("hy_z3b", [64, T], BF16, st=ph)
                fT = sb("hy_fT", [33, T], st=ph)
                zT = [sb("hy_zT%d" % i, [64, T], st=ph) for i in range(2)]
                tmpz = sb("hy_tmpz", [64, 512], st=ph)
                edb = sb("hy_edb", [128, 4096], st=ph)
                dec = [sb("hy_dec%d" % i, [128, 512], st=ph) for i in range(4)]
                hdt = [sb("hy_hd%d" % i, [128, 512], st=ph) for i in range(8)]
                hab2 = [sb("hy_hab2%d" % i, [128, 512], st=ph) for i in range(4)]
                habs = [sb("hy_ha%d" % i, [128, 512], st=ph) for i in range(2)]
                DMA('sp', w1[:], W['hy_w1'][l], (), ['hy_w'])
                DMA('sp', w2[:], W['hy_w2'][l], (), ['hy_w'])
                DMA('sp', w3[:], W['hy_w3'][l], (), ['hy_w'])
                DMA('pool', w4[:], W['hy_w4'][l], (), ['hy_w4'])
                DMA('sp', fT[:], CD['featsT'][:, :], (), ['hy_fT'])
                DMA('sp', edb[:], W['hy_log_decay'][l:l + 1].rearrange("o a b c -> o (a b c)").broadcast_to([128, 4096]), (), ['hy_edb'])
                ACT(edb[:], edb[:], AF.Exp, ['hy_edb'], ['hy_edb'])
                ob_, _ = PC['hyb']
                fq = pcol[0:64, ob_ + 3:ob_ + 4]
                srcs = [(w1, fT, 33), (w2, zT[0], 64), (w3, zT[1], 64)]
                for li, (wl, src, kk_) in enumerate(srcs):
                    dst = zT[li % 2]
                    bcol = pcol[0:64, ob_ + li:ob_ + li + 1]
                    for tg in range(NTG):
                        tsl = slice(tg * 512, (tg + 1) * 512)
                        bank = tg % 2
                        MM(PSb(bank)[0:64, :], wl[0:kk_, :], src[0:kk_, tsl], True, True, ['hy_w', 'hy_fT', ('hy_zT', 0), ('hy_zT', 1)], [('ps', bank)])
                        TS(dst[:, tsl], PSb(bank)[0:64, :], bcol, fq, ALU.add, ALU.mult, [('ps', bank), 'pcol'], [('hy_zT', li % 2)])
                        for rnd in range(3):
                            TS(tmpz[:], dst[:, tsl], math.pi, -2 * math.pi, ALU.is_gt, ALU.mult, [('hy_zT', li % 2)], ['hy_tmpz'])
                            TT(dst[:, tsl], dst[:, tsl], tmpz[:], ALU.add, [('hy_zT', li % 2), 'hy_tmpz'], [('hy_zT', li % 2)])
                            TS(tmpz[:], dst[:, tsl], -math.pi, 2 * math.pi, ALU.is_lt, ALU.mult, [('hy_zT', li % 2)], ['hy_tmpz'])
                            TT(dst[:, tsl], dst[:, tsl], tmpz[:], ALU.add, [('hy_zT', li % 2), 'hy_tmpz'], [('hy_zT', li % 2)])
                        ACT(dst[:, tsl], dst[:, tsl], AF.Sin, [('hy_zT', li % 2)], [('hy_zT', li % 2)])
                z3 = zT[0]
                CP(z3b[:], z3[:], [('hy_zT', 0)], ['hy_z3b'], eng='act')
                its = [(cg, i) for cg in range(8) for i in range(NT)]

                def emit_dec(n):
                    cg, i = its[n]
                    ACT(dec[n % 4][:], edb[:, cg * 512:(cg + 1) * 512], AF.Exp, ['hy_edb', 'const'], [('hy_dec', n % 4)], scale=tneg[:, i:i + 1])
                emit_dec(0)
                for n, (cg, i) in enumerate(its):
                    csl = slice(cg * 512, (cg + 1) * 512)
                    a_ = cg % 2
                    k = n % 4
                    k8 = n % 8
                    bank = 2 + k
                    if i == 0:
                        MSET(habs[a_][:], 0.0, [('hy_ha', a_)], eng='pool')
                    MM(PSb(bank), z3b[:, i * 128:(i + 1) * 128], w4[:, csl], True, True, ['hy_z3b', 'hy_w4'], [('ps', bank)])
                    if n + 1 < len(its):
                        emit_dec(n + 1)
                    TT(hdt[k8][:], PSb(bank), dec[k][:], ALU.mult, [('ps', bank), ('hy_dec', k)], [('hy_hd', k8)])
                    DMA('sp' if n % 2 else 'act', hfil[i * 128:(i + 1) * 128, csl], hdt[k8][:], [('hy_hd', k8)], ['hfil'])
                    ACT(hab2[k][:], hdt[k8][:], AF.Abs, [('hy_hd', k8)], [('hy_hab2', k)])
                    TT(habs[a_][:], habs[a_][:], hab2[k][:], ALU.add, [('hy_hab2', k), ('hy_ha', a_)], [('hy_ha', a_)], eng='pool')
                    if i == NT - 1:
                        MM(PSb(6 + a_), onesF[:], habs[a_][:], True, True, [('hy_ha', a_), 'const'], [('ps', 6 + a_)])
                        EVAC(asum[:, csl], PSb(6 + a_), [('ps', 6 + a_)], ['hy_asum'])
                S.barrier()
            with ExitStack() as ph:
                ph.enter_context(nc.named_scope("hy_spec"))
                rinv = sb("hy_rinv", [128, 2048], st=ph)
                bdb = sb("hy_bdb", [128, 2048], st=ph)
                hs = sb("hy_hs", [128, NT, 1024], BF16, st=ph)
                hdf = sb("hy_hdf", [128, NT, 1024], BF16, st=ph)
                hf = [sb("hy_hf%d" % i, [128, 1024], st=ph) for i in range(2)]
                hb = [sb("hy_hb%d" % i, [128, 1024], st=ph) for i in range(2)]
                t1 = [sb("hy_t1%d" % i, [128, 1024], st=ph) for i in range(2)]
                slC = [sb("hy_slC%d" % i, [128, NT, SW], BF16, st=ph) for i in range(2)]
                slS = [sb("hy_slS%d" % i, [128, NT, SW], BF16, st=ph) for i in range(2)]
                ko = [sb("hy_ko%d" % i, [128, 512], st=ph) for i in range(4)]
                TT(rinv[:], asum[:, 0:2048], asum[:, 2048:4096], ALU.add, (), ['hy_rinv'])
                ACT(rinv[:], rinv[:], AF.Ln, ['hy_rinv'], ['hy_rinv'])
                ACT(rinv[:], rinv[:], AF.Exp, ['hy_rinv'], ['hy_rinv'], scale=-1.0)
                DMA('sp', bdb[:], W['hy_bias_d'][l:l + 1].rearrange("o a c -> o (a c)").broadcast_to([128, 2048]), (), ['hy_bdb'])
                for o in range(2):
                    osl = slice(o * 1024, (o + 1) * 1024)
                    for i in range(NT):
                        k = i % 2
                        rows = slice(i * 128, (i + 1) * 128)
                        DMA('sp', hf[k][:], hfil[rows, o * 1024:(o + 1) * 1024], ['hfil'], [('hy_hf', k)])
                        DMA('act', hb[k][:], hfil[rows, 2048 + o * 1024:2048 + (o + 1) * 1024], ['hfil'], [('hy_hb', k)])
                        if i == 0:
                            MSET(hb[k][0:1, :], 0.0, [('hy_hb', k)], eng='dve')
                        TT(hs[:, i, :], hf[k][:], hb[k][:], ALU.add, [('hy_hf', k), ('hy_hb', k)], [('hy_hs', i)])
                        TT(hdf[:, i, :], hf[k][:], hb[k][:], ALU.subtract, [('hy_hf', k), ('hy_hb', k)], [('hy_hdf', i)], eng='pool')
                    kc = [0]

                    def epi_spec(ft, chh, bC, bS):
                        k0 = kc[0] % 2
                        kc[0] += 1
                        csl = slice(o * 1024 + chh * 512, o * 1024 + (chh + 1) * 512)
                        TT(ko[k0][:], PSb(bC), rinv[:, csl], ALU.mult, [('ps', bC), 'hy_rinv'], [('hy_ko', k0)])
                        TT(ko[k0][:], ko[k0][:], bdb[:, csl], ALU.add, [('hy_ko', k0), 'hy_bdb'], [('hy_ko', k0)], eng='pool')
                        DMA('sp', spec[o, 0, ft * 128:(ft + 1) * 128, chh * 512:(chh + 1) * 512], ko[k0][:], [('hy_ko', k0)], ['spec'])
                        TT(ko[2 + k0][:], PSb(bS), rinv[:, csl], ALU.mult, [('ps', bS), 'hy_rinv'], [('hy_ko', 2 + k0)])
                        DMA('sp', spec[o, 1, ft * 128:(ft + 1) * 128, chh * 512:(chh + 1) * 512], ko[2 + k0][:], [('hy_ko', 2 + k0)], ['spec'])
                    dft_fwd(slC, slS, hs, hdf, 'hy_hs', 'hy_hdf', epi_spec)
                S.barrier()
            hy0.close()
            with ExitStack() as ph:
                z = sb("hy_z", [128, NT, 1024], BF16, st=ph)
                Yr = sb("hy_Yr", [128, NT, 1024], BF16, st=ph)
                Ys = sb("hy_Ys", [128, NT, 1024], BF16, st=ph)
                slC = [sb("hy_slC%d" % i, [128, NT, SW], BF16, st=ph) for i in range(2)]
                slS = [sb("hy_slS%d" % i, [128, NT, SW], BF16, st=ph) for i in range(2)]
                Kr = [sb("hy_Kr%d" % i, [128, 512], st=ph) for i in range(2)]
                Ks = [sb("hy_Ks%d" % i, [128, 512], st=ph) for i in range(2)]
                Zc = [sb("hy_Zc%d" % i, [128, 512], st=ph) for i in range(2)]
                Zs = [sb("hy_Zs%d" % i, [128, 512], st=ph) for i in range(2)]
                ta = [sb("hy_ta%d" % i, [128, 512], st=ph) for i in range(2)]
                tb = [sb("hy_tb%d" % i, [128, 512], st=ph) for i in range(2)]
                ld = [sb("hy_ld%d" % i, [128, T], st=ph) for i in range(2)]
                stg = [sb("hy_stg%d" % i, [128, 4, 128], st=ph) for i in range(2)]
                xm = [sb("hy_xm%d" % i, [128, 512], st=ph) for i in range(2)]
                yo = [sb("hy_yo%d" % i, [128, 512], BF16, st=ph) for i in range(2)]
                x1v = x1tok.rearrange("(nt p) c -> p nt c", p=128)
                sc_ = nc.named_scope("hy_tr")
                sc_.__enter__()
                n = 0
                for which in range(2):
                    for ct in range(8):
                        k = n % 2
                        r0 = (2048 if which == 0 else 0) + ct * 128
                        DMA('sp', ld[k][:], bT[r0:r0 + 128, :], ['bcT'], [('hy_ld', k)])
                        for q in range(4):
                            bank = (n * 4 + q) % 8
                            for j in range(4):
                                nt = q * 4 + j
                                TR(PSb(bank)[:, j * 128:(j + 1) * 128], ld[k][:, nt * 128:(nt + 1) * 128], identF[:],
                                   [('hy_ld', k), 'const'], [('ps', bank)])
                            if which == 0:
                                EVAC(z[:, q * 4:(q + 1) * 4, ct * 128:(ct + 1) * 128], PSb(bank).rearrange("p (a b) -> p a b", a=4),
                                     [('ps', bank)], [('hy_z', q * 4 + j) for j in range(4)])
                            else:
                                s_ = q % 2
                                EVAC(stg[s_][:], PSb(bank).rearrange("p (a b) -> p a b", a=4), [('ps', bank)], [('hy_stg', s_)])
                                DMA('act', x1v[:, q * 4:(q + 1) * 4, ct * 128:(ct + 1) * 128], stg[s_][:], [('hy_stg', s_)], ['x1tok'])
                        n += 1
                sc_.__exit__(None, None, None)
                for o in range(2):
                    kc = [0]
                    sc_ = nc.named_scope("hy_fwd%d" % o)
                    sc_.__enter__()

                    def epi_mul(ft, chh, bC, bS):
                        k0 = kc[0] % 2
                        kc[0] += 1
                        rows = slice(ft * 128, (ft + 1) * 128)
                        cs = slice(chh * 512, (chh + 1) * 512)
                        DMA('sp', Kr[k0][:], spec[o, 0, rows, cs], ['spec'], [('hy_Kr', k0)])
                        DMA('sp', Ks[k0][:], spec[o, 1, rows, cs], ['spec'], [('hy_Ks', k0)])
                        CP(Zc[k0][:], PSb(bC), [('ps', bC)], [('hy_Zc', k0)], eng='act')
                        CP(Zs[k0][:], PSb(bS), [('ps', bS)], [('hy_Zs', k0)], eng='act')
                        TT(ta[k0][:], Zc[k0][:], Kr[k0][:], ALU.mult, [('hy_Zc', k0), ('hy_Kr', k0)], [('hy_ta', k0)])
                        TT(tb[k0][:], Zs[k0][:], Ks[k0][:], ALU.mult, [('hy_Zs', k0), ('hy_Ks', k0)], [('hy_tb', k0)], eng='pool')
                        TT(Yr[:, ft, cs], ta[k0][:], tb[k0][:], ALU.subtract, [('hy_ta', k0), ('hy_tb', k0)], [('hy_Y', ft)])
                        TT(ta[k0][:], Zc[k0][:], Ks[k0][:], ALU.mult, [('hy_Zc', k0), ('hy_Ks', k0)], [('hy_ta', k0)])
                        TT(tb[k0][:], Zs[k0][:], Kr[k0][:], ALU.mult, [('hy_Zs', k0), ('hy_Kr', k0)], [('hy_tb', k0)], eng='pool')
                        TT(Ys[:, ft, cs], ta[k0][:], tb[k0][:], ALU.add, [('hy_ta', k0), ('hy_tb', k0)], [('hy_Y', ft)], eng='pool')
                    dft_fwd(slC, slS, z, z, 'hy_z', 'hy_z', epi_mul)
                    sc_.__exit__(None, None, None)
                    sc_ = nc.named_scope("hy_inv%d" % o)
                    sc_.__enter__()
                    nb = 0
                    load_slabs(slC, slS, 0, CD['CF'], CD['SF'], 0)
                    for g in range(NSG):
                        b = g % 2
                        if g + 1 < NSG:
                            load_slabs(slC, slS, (g + 1) % 2, CD['CF'], CD['SF'], g + 1)
                        nsl = slice(g * SW, (g + 1) * SW)
                        if o == 0:
                            for ni in range(SW // 128):
                                nt = g * (SW // 128) + ni
                                for chh in range(2):
                                    bank = nb % 8
                                    k0 = nb % 2
                                    nb += 1
                                    cs = slice(chh * 512, (chh + 1) * 512)
                                    DMA('sp', xm[k0][:], x1tok[nt * 128:(nt + 1) * 128, cs], ['x1tok'], [('hy_xm', k0)])
                                    for ft in range(NT):
                                        MM(PSb(bank), slC[b][:, ft, ni * 128:(ni + 1) * 128], Yr[:, ft, cs], ft == 0, False,
                                           [('slC', b), ('hy_Y', ft)], [('ps', bank)])
                                    for ft in range(NT):
                                        MM(PSb(bank), slS[b][:, ft, ni * 128:(ni + 1) * 128], Ys[:, ft, cs], False, ft == NT - 1,
                                           [('slS', b), ('hy_Y', ft)], [('ps', bank)])
                                    STT(z[:, nt, cs], PSb(bank), INV_SCALE, xm[k0][:], ALU.mult, ALU.mult,
                                        [('ps', bank), ('hy_xm', k0)], [('hy_z', nt)])
                        else:
                            for ct in range(8):
                                bank = nb % 8
                                k0 = nb % 2
                                nb += 1
                                csl = slice(ct * 128, (ct + 1) * 128)
                                DMA('sp', xm[k0][:, 0:SW], bT[1024 + ct * 128:1024 + (ct + 1) * 128, nsl], ['bcT'], [('hy_xm', k0)])
                                for ft in range(NT):
                                    MM(PSb(bank)[:, 0:SW], Yr[:, ft, csl], slC[b][:, ft, :], ft == 0, False,
                                       [('slC', b), ('hy_Y', ft)], [('ps', bank)])
                                for ft in range(NT):
                                    MM(PSb(bank)[:, 0:SW], Ys[:, ft, csl], slS[b][:, ft, :], False, ft == NT - 1,
                                       [('slS', b), ('hy_Y', ft)], [('ps', bank)])
                                STT(yo[k0][:, 0:SW], PSb(bank)[:, 0:SW], INV_SCALE, xm[k0][:, 0:SW], ALU.mult, ALU.mult,
                                    [('ps', bank), ('hy_xm', k0)], [('hy_yo', k0)])
                                DMA('act', ybT[ct * 128:(ct + 1) * 128, nsl], yo[k0][:, 0:SW], [('hy_yo', k0)], ['ybT'])
                    sc_.__exit__(None, None, None)
                S.barrier()


        SDT = BF16
        NINV = 1
        NSET = NINV + 1
        WSC = -math.exp(-0.5)

        def rwkv_phase(l):
            with ExitStack() as ph:
                ph.enter_context(nc.named_scope("rwkv"))
                twp = sb("rw_twp", [128, T], BF16, st=ph)
                adp = sb("rw_adp", [128, T], BF16, st=ph)
                sgd = sb("rw_sgd", [128, T], BF16, st=ph)
                w2p = sb("rw_w2p", [128, 1024], BF16, st=ph)
                a2p = sb("rw_a2p", [128, 1024], BF16, st=ph)
                g2 = sb("rw_g2", [128, 1024], BF16, st=ph)
                omka = sb("rw_omka", [128, 8], st=ph)
                rT = sb("rw_r", [128, T], st=ph)
                kT = sb("rw_k", [128, T], st=ph)
                vT = sb("rw_v", [128, T], st=ph)
                kkn = sb("rw_kkn", [128, T], st=ph)
                rkacc = sb("rw_rkacc", [128, T], st=ph)
                ysum = sb("rw_ysum", [128, T], st=ph)
                vR = sb("rw_vR", [128, T], st=ph)
                lw = sb("rw_lw", [128, T], st=ph)
                kd = sb("rw_kd", [128, T], st=ph)
                bd = sb("rw_bd", [128, T], st=ph)
                Lc = sb("rw_L", [128, T], st=ph)
                ex = sb("rw_ex", [128, T], st=ph)
                tmp2 = sb("rw_tmp2", [128, T], st=ph)
                yT_ = sb("rw_y", [128, T], st=ph)
                tot = sb("rw_tot", [128, 16], st=ph)
                Pc = sb("rw_Pc", [128, 16], st=ph)
                ARh = [sb("rw_AR%d" % i, [128, 16, 2, 128], SDT, st=ph) for i in range(2)]
                BK = sb("rw_BK", [128, 16, 2, 128], SDT, st=ph)
                BhT = sb("rw_BhT", [128, 16, 128], SDT, st=ph)
                KhT = sb("rw_KhT", [128, 16, 128], SDT, st=ph)
                VT = sb("rw_VT", [128, 16, 128], SDT, st=ph)
                NB = [sb("rw_NB%d" % i, [128, 4, 2, 128], SDT, st=ph) for i in range(NSET)]
                AK = [sb("rw_AK%d" % i, [128, 4, 2, 128], SDT, st=ph) for i in range(NSET)]
                Mm2 = [[sb("rw_M%d_%d" % (j, i), [128, 4, 128], SDT, st=ph) for i in range(2)] for j in range(NINV)]
                MT2 = [[sb("rw_MT%d_%d" % (j, i), [128, 4, 128], SDT, st=ph) for i in range(2)] for j in range(NINV)]
                Xx2 = [[sb("rw_X%d_%d" % (j, i), [128, 4, 128], SDT, st=ph) for i in range(2)] for j in range(NINV)]
                Xfin = [sb("rw_Xf%d" % i, [128, 4, 128], SDT, st=ph) for i in range(NSET)]
                tb16 = sb("rw_tb16", [128, T], BF16, st=ph)
                St = sb("rw_St", [128, 64], st=ph)
                Sb_ = sb("rw_Sb", [128, 64], SDT, st=ph)
                Wt = sb("rw_Wt", [128, 128], SDT, st=ph)
                Ut = sb("rw_Ut", [128, 128], SDT, st=ph)
                ob = [sb("rw_ob%d" % i, [128, 512], BF16, st=ph) for i in range(2)]

                EXK = [('rw_ex', p_) for p_ in range(4)]
                T2K = [('rw_tmp2', p_) for p_ in range(4)]
                KDK = [('rw_kd', p_) for p_ in range(4)]
                LK = [('rw_L', p_) for p_ in range(4)]
                LWK = [('rw_lw', p_) for p_ in range(4)]
                BDK = [('rw_bd', p_) for p_ in range(4)]
                DMA('pool', g2[:], W['rw_g2'][l], (), ['rw_c'])
                MSET(ARh[0][64:128].rearrange("p a b c -> p (a b c)"), 0.0, ['rw_AR'])
                MSET(ARh[1][0:64].rearrange("p a b c -> p (a b c)"), 0.0, ['rw_AR'])
                for d in range(2):
                    ps_ = slice(d * 64, d * 64 + 48)
                    DMA('pool', w2p[ps_, :], W['rw_w2'][l, d], (), ['rw_c'])
                    DMA('pool', a2p[ps_, :], W['rw_a2'][l, d], (), ['rw_c'])
                    DMA('sp', tmp2[ps_, :], cT[3200 + d * 48:3248 + d * 48, :], ['bcT'], [*T2K])
                    DMA('sp', ex[ps_, :], cT[3296 + d * 48:3344 + d * 48, :], ['bcT'], [*EXK])
                    if d == 0:
                        ACT(twp[ps_, :], tmp2[ps_, :], AF.Tanh, [*T2K], ['rw_twp'])
                        CP(adp[ps_, :], ex[ps_, :], [*EXK], ['rw_adp'], eng='act')
                    else:
                        ACT(twp[ps_, :], tmp2[ps_, ::-1], AF.Tanh, [*T2K], ['rw_twp'])
                        CP(adp[ps_, :], ex[ps_, ::-1], [*EXK], ['rw_adp'], eng='act')
                DMA('sp', Lc[:], cT[3072:3200, :], ['bcT'], [*LK])
                ACT(sgd[:], Lc[:], AF.Sigmoid, [*LK], ['rw_sgd'])
                oka, _ = PC['ka']
                TS(omka[:], pcol[:, oka:oka + 8], -1.0, 1.0, ALU.mult, ALU.add, ['pcol'], ['rw_omka'])
                if l > 0:
                    v1 = sb("rw_v1", [128, 8, 32], st=ph)
                    v2 = sb("rw_v2", [32, 1024], BF16, st=ph)
                    t1v = sb("rw_t1v", [32, T], BF16, st=ph)
                    DMA('sp', v1[:], W['rw_v1'][l - 1].rearrange("(ct p) r -> p ct r", p=128), (), ['rw_c'])
                    DMA('pool', v2[:], W['rw_v2'][l - 1], (), ['rw_c'])
                    for ct in range(8):
                        DMA('sp', kd[:], cT[2048 + ct * 128:2048 + (ct + 1) * 128, :], ['bcT'], [*KDK])
                        for tg in range(NTG):
                            MM(PSb(tg)[0:32, :], v1[:, ct, :], kd[:, tg * 512:(tg + 1) * 512], ct == 0, ct == 7,
                               ['rw_c', *KDK], [('ps', tg)])
                    for tg in range(NTG):
                        EVAC(t1v[:, tg * 512:(tg + 1) * 512], PSb(tg)[0:32, :], [('ps', tg)], ['rw_t1v'])

                def f2(ap):
                    return ap.rearrange("p a b -> p (a b)")

                def c3(ap):
                    return ap.rearrange("p (c t) -> p c t", t=128)

                def scan_unit(hp, d, r_ap, k_ap, kk_ap, v_, yout, ykey):
                    hsl = slice(hp * 128, (hp + 1) * 128)
                    ps_ = slice(d * 64, d * 64 + 48)

                    def prep_part(p):
                        psl = slice(p * 512, (p + 1) * 512)
                        c4 = slice(p * 4, (p + 1) * 4)
                        kl, ke, kt, kkd, kbd, kL, kb16 = ('rw_lw', p), ('rw_ex', p), ('rw_tmp2', p), ('rw_kd', p), ('rw_bd', p), ('rw_L', p), ('rw_tb16', p)
                        MM(PSb(5), w2p[ps_, hsl], twp[ps_, psl], True, True, ['rw_c', 'rw_twp'], [('ps', 5)])
                        ACT(lw[:, psl], PSb(5), AF.Sigmoid, [('ps', 5), 'pcol'], [kl], bias=pc('w0', d * 8 + hp))
                        MM(PSb(6), a2p[ps_, hsl], adp[ps_, psl], True, True, ['rw_c', 'rw_adp'], [('ps', 6)])
                        ACT(ex[:, psl], PSb(6), AF.Sigmoid, [('ps', 6), 'pcol'], [ke], bias=pc('a0', d * 8 + hp))
                        yield
                        TS(tmp2[:, psl], ex[:, psl], pc('ka', hp), omka[:, hp:hp + 1], ALU.mult, ALU.add, [ke, 'pcol', 'rw_omka'], [kt])
                        TT(kd[:, psl], tmp2[:, psl], k_ap[:, psl], ALU.mult, [kt, 'rw_in'], [kkd])
                        TT(bd[:, psl], ex[:, psl], kk_ap[:, psl], ALU.mult, ['rw_kkn', ke], [kbd], eng='pool')
                        if d == 0:
                            STT(rkacc[:, psl], r_ap[:, psl], pc('rk', hp), kd[:, psl], ALU.mult, ALU.mult, ['rw_in', 'pcol', kkd], [('rw_rkacc', p)])
                        else:
                            STT(tmp2[:, psl], r_ap[:, psl], pc('rk', hp), kd[:, psl], ALU.mult, ALU.mult, ['rw_in', 'pcol', kkd, kt], [kt])
                            rsl = slice(T - (p + 1) * 512, T - p * 512)
                            TT(rkacc[:, rsl], rkacc[:, rsl], tmp2[:, psl][:, ::-1], ALU.add, [('rw_rkacc', 3 - p), kt], [('rw_rkacc', 3 - p)])
                        yield
                        for c in range(p * 4, (p + 1) * 4):
                            csl = slice(c * 128, (c + 1) * 128)
                            S.op('dve', lambda: nc.vector.tensor_tensor_scan(out=Lc[:, csl], data0=onesF[:], data1=lw[:, csl], initial=0.0,
                                                                             op0=ALU.mult, op1=ALU.add), ['const', kl], [kL])
                        CP(tot[:, c4], c3(Lc[:])[:, c4, 127], [kL], [('rw_tot', p)])
                        ACT(Pc[:, c4], tot[:, c4], AF.Exp, [('rw_tot', p)], [('rw_Pc', p)], scale=WSC)
                        yield
                        ACT(ex[:, psl], Lc[:, psl], AF.Exp, [kL, kbd, kt], [ke], scale=WSC)
                        for h in range(2):
                            hp_ = slice(h * 64, (h + 1) * 64)
                            TT(ARh[h][hp_, c4, 1, :], c3(r_ap[hp_, psl]), c3(ex[hp_, psl]), ALU.mult, [ke, 'rw_in'], [('rw_AR', p)])
                        yield
                        ACT(ex[:, psl], Lc[:, psl], AF.Exp, [kL, ('rw_AR', p)], [ke], scale=-WSC)
                        TT(BK[:, c4, 0, :], c3(bd[:, psl]), c3(ex[:, psl]), ALU.mult, [ke, kbd], [('rw_BK', p)])
                        TT(BK[:, c4, 1, :], c3(kd[:, psl]), c3(ex[:, psl]), ALU.mult, [ke, kkd], [('rw_BK', p)], eng='pool')
                        yield
                        TT(tmp2[:, psl], Lc[:, psl], lw[:, psl], ALU.subtract, [kL, kl, kt], [kt])
                        ACT(ex[:, psl], tmp2[:, psl], AF.Exp, [kt, ('rw_BK', p)], [ke], scale=WSC)
                        for h in range(2):
                            hp_ = slice(h * 64, (h + 1) * 64)
                            STT(ARh[h][hp_, c4, 0, :], c3(kk_ap[hp_, psl]), -1.0, c3(ex[hp_, psl]), ALU.mult, ALU.mult, [ke, 'rw_kkn'], [('rw_AR', p)])
                        yield
                        TT(c3(tmp2[:, psl]), tot[:, c4].unsqueeze(2).to_broadcast([128, 4, 128]), c3(Lc[:, psl]), ALU.subtract,
                           [('rw_tot', p), kL, kt], [kt])
                        ACT(ex[:, psl], tmp2[:, psl], AF.Exp, [kt, ('rw_AR', p)], [ke], scale=WSC)
                        yield
                        for wi, (src, skey, dstT) in enumerate(((bd, kbd, BhT), (kd, kkd, KhT), (v_, 'rw_in', VT))):
                            if wi < 2:
                                TT(tb16[:, psl], src[:, psl], ex[:, psl], ALU.mult, [ke, skey, kb16], [kb16], eng='pool' if wi else 'dve')
                            else:
                                CP(tb16[:, psl], src[:, psl], ['rw_in', 'rw_vR', kb16], [kb16], eng='act')
                            pb16 = PSb(7).bitcast(BF16)
                            for j in range(4):
                                TR(pb16[:, j * 128:(j + 1) * 128], tb16[:, (p * 4 + j) * 128:(p * 4 + j + 1) * 128], identB[:],
                                   [kb16, 'const'], [('ps', 7)])
                            EVAC(f2(dstT[:, c4, :]), pb16[:, 0:512], [('ps', 7)], [('rw_T%d' % wi, p)])
                            yield

                    MSET(St[:], 0.0, ['rw_St'], eng='dve')
                    MSET(Sb_[:], 0.0, ['rw_Sb'], eng='dve')

                    def inv_chain(qd):
                        q3 = qd % NSET
                        st_ = qd % NINV
                        b0, b1, b2 = (2, 3, 4) if st_ == 0 else (5, 6, 7)
                        MTs, Mms, Xxs = MT2[st_], Mm2[st_], Xx2[st_]
                        kM, kMT, kX = 'rw_M%d' % st_, 'rw_MT%d' % st_, 'rw_X%d' % st_
                        gb_ = (b0, b1)
                        for cc in range(2):
                            c = qd * 2 + cc
                            for h in range(2):
                                u = cc * 2 + h
                                MM(PSb(gb_[u // 2])[:, (u % 2) * 256:(u % 2 + 1) * 256], BK[:, c, 0, :], f2(ARh[h][:, c, :, :]), True, True,
                                   [('rw_BK', qd // 2), ('rw_AR', qd // 2)], [('ps', gb_[u // 2])])
                                MM(PSb(b2)[:, u * 128:(u + 1) * 128], ARh[h][:, c, 0, :], BK[:, c, 0, :], True, True,
                                   [('rw_BK', qd // 2), ('rw_AR', qd // 2)], [('ps', b2)])
                        for hb in range(2):
                            TT(NB[q3][:, hb * 2:(hb + 1) * 2, :, :].rearrange("p a b c -> p (a b c)"), PSb(gb_[hb]), maskU4[:], ALU.mult,
                               [('ps', gb_[hb]), 'const'], [('rw_NB', q3)])
                        TT(f2(MTs[0][:]), PSb(b2), maskL4[:], ALU.mult, [('ps', b2), 'const'], [(kMT, 0)])
                        TT(Xxs[0][:], NB[q3][:, :, 0, :], ident4[:].rearrange("p (a b) -> p a b", a=4), ALU.add,
                           [('rw_NB', q3), 'const'], [(kX, 0)], eng='pool')
                        yield
                        cur = 0
                        for lev in range(1, 8):
                            nxt = 1 - cur
                            pA, pB, pC = b2, b1, b0
                            if lev == 1:
                                for cc in range(2):
                                    c = qd * 2 + cc
                                    for h in range(2):
                                        u = cc * 2 + h
                                        MM(PSb(gb_[u // 2])[:, (u % 2) * 256:(u % 2 + 1) * 256], BK[:, c, 1, :], f2(ARh[h][:, c, :, :]), True, True,
                                           [('rw_BK', qd // 2), ('rw_AR', qd // 2)], [('ps', gb_[u // 2])])
                            if lev >= 2:
                                xs, xk = Xxs[(lev - 2) % 2], (kX, (lev - 2) % 2)
                                for u in range(4):
                                    MM(PSb(pC)[:, u * 128:(u + 1) * 128], MTs[cur][:, u, :], xs[:, u, :], True, True,
                                       [(kMT, cur), xk], [('ps', pC)])
                            if lev == 1:
                                for hb in range(2):
                                    TT(AK[q3][:, hb * 2:(hb + 1) * 2, :, :].rearrange("p a b c -> p (a b c)"), PSb(gb_[hb]), maskU4[:], ALU.mult,
                                       [('ps', gb_[hb]), 'const'], [('rw_AK', q3)])
                            if lev <= 6:
                                for u in range(4):
                                    m_prev = NB[q3][:, u, 0, :] if lev == 1 else Mms[cur][:, u, :]
                                    mk = ('rw_NB', q3) if lev == 1 else (kM, cur)
                                    MM(PSb(pA)[:, u * 128:(u + 1) * 128], m_prev, MTs[cur][:, u, :], True, True, [mk, (kMT, cur)], [('ps', pA)])
                                    if lev < 6:
                                        MM(PSb(pB)[:, u * 128:(u + 1) * 128], MTs[cur][:, u, :], m_prev, True, True, [mk, (kMT, cur)], [('ps', pB)])
                            if lev <= 6:
                                CP(f2(MTs[nxt][:]), PSb(pA), [('ps', pA)], [(kMT, nxt)], eng='act')
                            if lev >= 2:
                                if lev == 7:
                                    xd, xdk = Xfin[q3], ('rw_Xfin', q3)
                                else:
                                    xd, xdk = Xxs[(lev - 1) % 2], (kX, (lev - 1) % 2)
                                TT(f2(xd[:]), PSb(pC), f2(xs[:]), ALU.add, [('ps', pC), xk], [xdk])
                            if lev < 6:
                                CP(f2(Mms[nxt][:]), PSb(pB), [('ps', pB)], [(kM, nxt)], eng='act')
                            cur = nxt
                            yield

                    def state_chain(qd):
                        q3 = qd % NSET
                        Xf = Xfin[q3]
                        xkey = ('rw_Xfin', q3)
                        for cc in range(2):
                            c = qd * 2 + cc
                            for h in range(2):
                                u = cc * 2 + h
                                hs_ = slice(h * 64, (h + 1) * 64)
                                MM(PSb(0)[:, hs_], ARh[h][:, c, 0, :], Sb_[:, :], True, False, [('rw_AR', qd // 2), 'rw_Sb'], [('ps', 0)])
                                MM(PSb(0)[:, hs_], AK[q3][:, u, 0, :], VT[:, c, hs_], False, True, [('rw_AK', q3), ('rw_T2', qd // 2)], [('ps', 0)])
                            CP(Wt[:], PSb(0)[:, 0:128], [('ps', 0)], ['rw_Wt'], eng='act')
                            yield
                            for h in range(2):
                                u = cc * 2 + h
                                hs_ = slice(h * 64, (h + 1) * 64)
                                MM(PSb(0)[:, 128 + h * 64:128 + (h + 1) * 64], Xf[:, u, :], Wt[:, hs_], True, True, [xkey, 'rw_Wt'], [('ps', 0)])
                            CP(Ut[:], PSb(0)[:, 128:256], [('ps', 0)], ['rw_Ut'], eng='dve')
                            yield
                            for h in range(2):
                                u = cc * 2 + h
                                hs_ = slice(h * 64, (h + 1) * 64)
                                so_ = PSb(0)[hs_, 256:320]
                                MM(so_, BhT[:, c, hs_], Ut[:, hs_], True, False, [('rw_T0', qd // 2), 'rw_Ut'], [('ps', 0)])
                                MM(so_, KhT[:, c, hs_], VT[:, c, hs_], False, True, [('rw_T1', qd // 2), ('rw_T2', qd // 2)], [('ps', 0)])
                                yo_ = PSb(1)[hs_, cc * 128:(cc + 1) * 128]
                                MM(yo_, Sb_[:, :], ARh[h][:, c, 1, :], True, False, ['rw_Sb', ('rw_AR', qd // 2)], [('ps', 1)])
                                MM(yo_, Ut[:, hs_], NB[q3][:, u, 1, :], False, False, ['rw_Ut', ('rw_NB', q3)], [('ps', 1)])
                                MM(yo_, VT[:, c, hs_], AK[q3][:, u, 1, :], False, True, [('rw_T2', qd // 2), ('rw_AK', q3)], [('ps', 1)])
                            STT(St[:], St[:], Pc[:, c:c + 1], PSb(0)[:, 256:320], ALU.mult, ALU.add, ['rw_St', ('rw_Pc', qd // 2), ('ps', 0)], ['rw_St'])
                            CP(Sb_[:], St[:], ['rw_St'], ['rw_Sb'], eng='act')
                            if cc == 1:
                                CP(yout[:, qd * 256:(qd + 1) * 256], PSb(1)[:, 0:256], [('ps', 1)], [ykey], eng='act')
                            yield

                    for _ in prep_part(0):
                        pass
                    prep_done = [True, False, False, False]
                    prep_idx = 1
                    prep_gen = prep_part(1)
                    next_inv = 0
                    inv_done = [False] * 8
                    active = []
                    state_q = 0
                    state_gen = None
                    states_done = 0
                    while states_done < 8:
                        while (len(active) < NINV and next_inv < 8 and next_inv < states_done + NSET
                               and prep_done[next_inv // 2]):
                            active.append((next_inv, inv_chain(next_inv)))
                            next_inv += 1
                        if state_gen is None and state_q < 8 and inv_done[state_q]:
                            state_gen = state_chain(state_q)
                        if state_gen is not None:
                            try:
                                next(state_gen)
                            except StopIteration:
                                state_gen = None
                                states_done += 1
                                state_q += 1
                        for item in list(active):
                            try:
                                next(item[1])
                            except StopIteration:
                                inv_done[item[0]] = True
                                active.remove(item)
                        if prep_gen is not None:
                            try:
                                next(prep_gen)
                            except StopIteration:
                                prep_done[prep_idx] = True
                                prep_idx += 1
                                prep_gen = prep_part(prep_idx) if prep_idx < 4 else None

                for hp in range(8):
                    hsl = slice(hp * 128, (hp + 1) * 128)
                    DMA('sp', rT[:], cT[hp * 128:(hp + 1) * 128, :], ['bcT'], ['rw_in'])
                    DMA('sp', kT[:], cT[1024 + hp * 128:1024 + (hp + 1) * 128, :], ['bcT'], ['rw_in'])
                    DMA('sp', vT[:], cT[2048 + hp * 128:2048 + (hp + 1) * 128, :], ['bcT'], ['rw_in'])
                    if l > 0:
                        DMA('sp', tmp2[:], vfT[hsl, :], ['vfT'], [*T2K])
                        for tg in range(NTG):
                            tsl = slice(tg * 512, (tg + 1) * 512)
                            MM(PSb(tg), v2[:, hsl], t1v[:, tsl], True, True, ['rw_c', 'rw_t1v'], [('ps', tg)])
                            ACT(ex[:, tsl], PSb(tg), AF.Sigmoid, [('ps', tg), 'pcol'], [*EXK], bias=pc('v0', hp))
                        TT(tmp2[:], tmp2[:], vT[:], ALU.subtract, [*T2K, 'rw_in'], [*T2K])
                        TT(tmp2[:], tmp2[:], ex[:], ALU.mult, [*T2K, *EXK], [*T2K])
                        TT(vT[:], vT[:], tmp2[:], ALU.add, [*T2K, 'rw_in'], ['rw_in'])
                    else:
                        DMA('act', vfT[hsl, :], vT[:], ['rw_in'], ['vfT'])
                    TS(kkn[:], kT[:], pc('kk', hp), None, ALU.mult, None, ['rw_in', 'pcol'], ['rw_kkn'])
                    ACT(tmp2[:], kkn[:], AF.Square, ['rw_kkn', *T2K], [*T2K])
                    for tg in range(NTG):
                        tsl = slice(tg * 512, (tg + 1) * 512)
                        MM(PSb(tg), blockones[:], tmp2[:, tsl], True, True, ['const', *T2K], [('ps', tg)])
                        ACT(ex[:, tsl], PSb(tg), AF.Ln, [('ps', tg)], [*EXK], bias=1e-30)
                    ACT(ex[:], ex[:], AF.Exp, [*EXK], [*EXK], scale=-0.5)
                    TS(ex[:], ex[:], 1e12, None, ALU.min, None, [*EXK], [*EXK])
                    TT(kkn[:], kkn[:], ex[:], ALU.mult, ['rw_kkn', *EXK], ['rw_kkn'])
                    CP(vR[:], vT[:, ::-1], ['rw_in'], ['rw_vR'], eng='act')
                    for d in range(2):
                        ps_ = slice(d * 64, d * 64 + 48)
                        if d == 0:
                            r_ap, k_ap, kk_ap, v_ = rT[:], kT[:], kkn[:], vT
                        else:
                            r_ap, k_ap, kk_ap, v_ = rT[:, ::-1], kT[:, ::-1], kkn[:, ::-1], vR
                        if d == 0:
                            scan_unit(hp, d, r_ap, k_ap, kk_ap, v_, ysum, 'rw_ysum')
                        else:
                            scan_unit(hp, d, r_ap, k_ap, kk_ap, v_, yT_, 'rw_y')
                            TT(ysum[:], ysum[:], yT_[:, ::-1], ALU.add, ['rw_y', 'rw_ysum'], ['rw_ysum'])
                    def post_indep(tg_):
                        ts_ = slice(tg_ * 512, (tg_ + 1) * 512)
                        k_ = tg_ % 2
                        MM(PSb(k_), blockones[:], ysum[:, ts_], True, True, ['const', 'rw_ysum'], [('ps', k_)])
                        MM(PSb(4 + k_), blockones[:], rkacc[:, ts_], True, True, ['const', ('rw_rkacc', tg_)], [('ps', 4 + k_)])
                        MM(PSb(6 + k_), g2[:, hsl], sgd[:, ts_], True, True, ['rw_c', 'rw_sgd'], [('ps', 6 + k_)])
                    post_indep(0)
                    for tg in range(NTG):
                        tsl = slice(tg * 512, (tg + 1) * 512)
                        k0 = tg % 2
                        if tg + 1 < NTG:
                            post_indep(tg + 1)
                        STT(tmp2[:, tsl], PSb(k0), -1.0 / 64, ysum[:, tsl], ALU.mult, ALU.add, [('ps', k0), 'rw_ysum', *T2K], [*T2K])
                        ACT(ex[:, tsl], tmp2[:, tsl], AF.Square, [*T2K, *EXK], [*EXK])
                        MM(PSb(2 + k0), blockones[:], ex[:, tsl], True, True, ['const', *EXK], [('ps', 2 + k0)])
                        ACT(ex[:, tsl], PSb(2 + k0), AF.Ln, [('ps', 2 + k0)], [*EXK], scale=1.0 / 64, bias=GN_EPS)
                        ACT(ex[:, tsl], ex[:, tsl], AF.Exp, [*EXK], [*EXK], scale=-0.5)
                        TT(tmp2[:, tsl], tmp2[:, tsl], ex[:, tsl], ALU.mult, [*T2K, *EXK], [*T2K])
                        TS(tmp2[:, tsl], tmp2[:, tsl], pc('lng', hp), pc('lnb', hp), ALU.mult, ALU.add, [*T2K, 'pcol'], [*T2K])
                        TT(ex[:, tsl], PSb(4 + k0), vT[:, tsl], ALU.mult, [('ps', 4 + k0), 'rw_in', *EXK], [*EXK])
                        TT(tmp2[:, tsl], tmp2[:, tsl], ex[:, tsl], ALU.add, [*T2K, *EXK], [*T2K])
                        TT(ob[k0][:], tmp2[:, tsl], PSb(6 + k0), ALU.mult, [*T2K, ('ps', 6 + k0)], [('rw_ob', k0)])
                        DMA('act', ycT[hsl, tsl], ob[k0][:], [('rw_ob', k0)], ['ycT'])
                S.barrier()


        def merge_phase(l):
            with ExitStack() as ph:
                ph.enter_context(nc.named_scope("merge"))
                PW = 256
                yT3 = [sb("mg_y%d" % i, [128, 8, T], BF16, st=ph) for i in range(3)]
                wpb = [[sb("mg_w%d_%d" % (i, j), [128, 8, PW], BF16, st=ph) for j in range(2)] for i in range(3)]
                gt = [[sb("mg_g%d_%d" % (i, j), [128, T], BF16, st=ph) for j in range(2)] for i in range(3)]
                ta = [sb("mg_ta%d" % i, [128, 512], st=ph) for i in range(2)]
                tb = [sb("mg_tb%d" % i, [128, 512], st=ph) for i in range(2)]
                mo = [sb("mg_mo%d" % i, [128, 512], BF16, st=ph) for i in range(2)]
                for i, src in enumerate((yaT, ybT, ycT)):
                    DMA('sp', yT3[i][:], src.rearrange("(kt p) t -> p kt t", p=128), ['yaT', 'ybT', 'ycT'], [('mg_y', i)])
                Wb = [W[n][l].rearrange("(kt p) c -> p kt c", p=128) for n in ('w_branch_a', 'w_branch_b', 'w_branch_c')]
                n = 0
                for pi in range(D // PW):
                    b = pi % 2
                    for i in range(3):
                        DMA('pool', wpb[i][b][:], Wb[i][:, :, pi * PW:(pi + 1) * PW], (), [('mg_w', i, b)])
                    for ci in range(PW // 128):
                        ct = pi * (PW // 128) + ci
                        gb_ = ct % 2
                        for i in range(3):
                            DMA('sp', gt[i][gb_][:], gT[i * D + ct * 128:i * D + (ct + 1) * 128, :], ['gT'], [('mg_g', i, gb_)])
                        for tg in range(NTG):
                            tsl = slice(tg * 512, (tg + 1) * 512)
                            k0 = n % 2
                            base = (n % 2) * 3
                            n += 1
                            for i in range(3):
                                for kt in range(8):
                                    MM(PSb(base + i), wpb[i][b][:, kt, ci * 128:(ci + 1) * 128], yT3[i][:, kt, tsl], kt == 0, kt == 7,
                                       [('mg_w', i, b), ('mg_y', i)], [('ps', base + i)])
                            TT(ta[k0][:], PSb(base + 0), gt[0][gb_][:, tsl], ALU.mult, [('ps', base + 0), ('mg_g', 0, gb_)], [('mg_ta', k0)])
                            TT(tb[k0][:], PSb(base + 1), gt[1][gb_][:, tsl], ALU.mult, [('ps', base + 1), ('mg_g', 1, gb_)], [('mg_tb', k0)])
                            TT(ta[k0][:], ta[k0][:], tb[k0][:], ALU.add, [('mg_ta', k0), ('mg_tb', k0)], [('mg_ta', k0)], eng='pool')
                            TT(tb[k0][:], PSb(base + 2), gt[2][gb_][:, tsl], ALU.mult, [('ps', base + 2), ('mg_g', 2, gb_), ('mg_ta', k0)], [('mg_tb', k0)])
                            TT(mo[k0][:], ta[k0][:], tb[k0][:], ALU.add, [('mg_ta', k0), ('mg_tb', k0)], [('mg_mo', k0)], eng='pool')
                            DMA('act', mgT[ct * 128:(ct + 1) * 128, tsl], mo[k0][:], [('mg_mo', k0)], ['mgT'])
                S.barrier()

        def resid_epi(xl, xo):
            cnt = [0]

            def epi(ci, c0, m, tg, ps, pk):
                k = cnt[0] % 3
                cnt[0] += 1
                tsl = slice(tg * 512, (tg + 1) * 512)
                DMA('sp', xl[k][:], xT[c0:c0 + 128, tsl], [('xT', ci, tg)], [('rs_xl', k)])
                TT(xo[k][:], ps, xl[k][:], ALU.add, pk + [('rs_xl', k)], [('rs_xo', k)])
                DMA('act', xT[c0:c0 + 128, tsl], xo[k][:], [('rs_xo', k)], [('xT', ci, tg)])
            return epi

        def outproj_phase(l):
            with ExitStack() as ph:
                ph.enter_context(nc.named_scope("outp"))
                mT = sb("op_mT", [128, KD, T], BF16, st=ph)
                wp = [sb("op_wp%d" % i, [128, KD, 512], BF16, st=ph) for i in range(2)]
                xl = [sb("op_xl%d" % i, [128, 512], st=ph) for i in range(3)]
                xo = [sb("op_xo%d" % i, [128, 512], st=ph) for i in range(3)]
                DMA('sp', mT[:], mgT.rearrange("(kt p) t -> p kt t", p=128), ['mgT'], ['op_mT'])
                linear_fm(W['w_out'][l].rearrange("(kt p) c -> p kt c", p=128), KD, [(i * 128, 128) for i in range(16)],
                          lambda kt, tg: mT[:, kt, tg * 512:(tg + 1) * 512], lambda kt, tg: 'op_mT', resid_epi(xl, xo), wp, 'op_wp')
                S.barrier()

        def ffn_phase(l):
            with ExitStack() as hs:
                hT = sb("hT2", [128, KD, T], BF16, st=hs)
                norm_phase(hT, 'nfg')
                with ExitStack() as ph:
                    ph.enter_context(nc.named_scope("ffn_up"))
                    PW = 256
                    wg = [sb("ff_wg%d" % i, [128, KD, PW], BF16, st=ph) for i in range(2)]
                    wu = [sb("ff_wu%d" % i, [128, KD, PW], BF16, st=ph) for i in range(2)]
                    sg = [sb("ff_sg%d" % i, [128, 512], st=ph) for i in range(2)]
                    ao = [sb("ff_ao%d" % i, [128, 512], BF16, st=ph) for i in range(2)]
                    Wg = W['w_ffn_gate'][l].rearrange("(kt p) c -> p kt c", p=128)
                    Wu = W['w_ffn_up'][l].rearrange("(kt p) c -> p kt c", p=128)
                    n = 0
                    for pi in range(D_FF // PW):
                        b = pi % 2
                        DMA('pool', wg[b][:], Wg[:, :, pi * PW:(pi + 1) * PW], (), [('ff_wg', b)])
                        DMA('pool', wu[b][:], Wu[:, :, pi * PW:(pi + 1) * PW], (), [('ff_wu', b)])
                        for ci in range(PW // 128):
                            ft = pi * (PW // 128) + ci
                            for tg in range(NTG):
                                tsl = slice(tg * 512, (tg + 1) * 512)
                                k0 = n % 2
                                bG = (n % 4) * 2
                                bU = bG + 1
                                n += 1
                                for kt in range(KD):
                                    MM(PSb(bG), wg[b][:, kt, ci * 128:(ci + 1) * 128], hT[:, kt, tsl], kt == 0, kt == KD - 1,
                                       [('ff_wg', b), ('hT', kt, tg)], [('ps', bG)])
                                for kt in range(KD):
                                    MM(PSb(bU), wu[b][:, kt, ci * 128:(ci + 1) * 128], hT[:, kt, tsl], kt == 0, kt == KD - 1,
                                       [('ff_wu', b), ('hT', kt, tg)], [('ps', bU)])
                                ACT(sg[k0][:], PSb(bG), AF.Silu, [('ps', bG)], [('ff_sg', k0)])
                                TT(ao[k0][:], sg[k0][:], PSb(bU), ALU.mult, [('ff_sg', k0), ('ps', bU)], [('ff_ao', k0)])
                                DMA('sp', actT[ft * 128:(ft + 1) * 128, tsl], ao[k0][:], [('ff_ao', k0)], ['actT'])
                    S.barrier()
            with ExitStack() as ph:
                ph.enter_context(nc.named_scope("ffn_down"))
                KF = D_FF // 128
                PW = 256
                TH = 1024
                asb = sb("ff_act", [128, KF, TH], BF16, st=ph)
                wd = [sb("ff_wd%d" % i, [128, KF, PW], BF16, st=ph) for i in range(2)]
                xl = [sb("ff_xl%d" % i, [128, 512], st=ph) for i in range(3)]
                xo = [sb("ff_xo%d" % i, [128, 512], st=ph) for i in range(3)]
                Wd = W['w_ffn_down'][l].rearrange("(kt p) c -> p kt c", p=128)
                aTv = actT.rearrange("(kt p) t -> p kt t", p=128)
                n = 0
                pn = 0
                for th in range(T // TH):
                    for kq in range(4):
                        ks = slice(kq * 11, (kq + 1) * 11)
                        DMA('sp', asb[:, ks, :], aTv[:, ks, th * TH:(th + 1) * TH], ['actT'], [('ff_act', kq)])
                    epi = resid_epi(xl, xo)
                    for pi in range(D // PW):
                        b = pn % 2
                        pn += 1
                        DMA('pool', wd[b][:], Wd[:, :, pi * PW:(pi + 1) * PW], (), [('ff_wd', b)])
                        for ci in range(PW // 128):
                            ct = pi * (PW // 128) + ci
                            for tgi in range(TH // 512):
                                tg = th * (TH // 512) + tgi
                                bank = n % 8
                                n += 1
                                for kt in range(KF):
                                    MM(PSb(bank), wd[b][:, kt, ci * 128:(ci + 1) * 128], asb[:, kt, tgi * 512:(tgi + 1) * 512], kt == 0, kt == KF - 1,
                                       [('ff_wd', b), ('ff_act', kt // 11)], [('ps', bank)])
                                epi(ct, ct * 128, 128, tg, PSb(bank), [('ps', bank)])
                S.barrier()

        def final_phase():
            with ExitStack() as ph:
                xin = [sb("fx%d" % i, [128, KD, 512], st=ph) for i in range(2)]
                sq = [sb("fsq%d" % i, [128, 512], st=ph) for i in range(2)]
                rs = [sb("frs%d" % i, [128, 512], st=ph) for i in range(2)]
                ot = [sb("fot%d" % i, [128, D], st=ph) for i in range(2)]
                n = 0
                no = 0
                for tg in range(NTG):
                    b = tg % 2
                    tsl = slice(tg * 512, (tg + 1) * 512)
                    DMA('sp', xin[b][:], xTv[:, :, tsl], ['xT'], [('fx', b)])
                    for dk in range(KD):
                        ACT(sq[dk % 2][:], xin[b][:, dk, :], AF.Square, [('fx', b)], [('fsq', dk % 2)])
                        MM(PSb(b), onesF[:], sq[dk % 2][:], dk == 0, dk == KD - 1, [('fsq', dk % 2), 'const'], [('ps', b)])
                    ACT(rs[b][:], PSb(b), AF.Sqrt, [('ps', b)], [('frs', b)], scale=1.0 / D, bias=RMS_EPS)
                    RECIP(rs[b][:], rs[b][:], [('frs', b)], [('frs', b)])
                    for dk in range(KD):
                        STT(xin[b][:, dk, :], xin[b][:, dk, :], pc('nfin', dk), rs[b][:], ALU.mult, ALU.mult,
                            [('fx', b), 'pcol', ('frs', b)], [('fx', b)])
                    for tt in range(4):
                        o_ = no % 2
                        no += 1
                        for q in range(4):
                            bank = 2 + n % 6
                            n += 1
                            for j in range(4):
                                dk = q * 4 + j
                                TR(PSb(bank)[:, j * 128:(j + 1) * 128], xin[b][:, dk, tt * 128:(tt + 1) * 128], identF[:],
                                   [('fx', b), 'const'], [('ps', bank)])
                            EVAC(ot[o_][:, q * 512:(q + 1) * 512], PSb(bank), [('ps', bank)], [('fot', o_)])
                        row = (tg * 4 + tt) * 128
                        DMA('act', out_d[row:row + 128, :], ot[o_][:], [('fot', o_)], [('out', row)])
                S.barrier()

        for l in range(NL):
            DMA('sp', pcol[:], pcol_d[l], (), ['pcol'])
            Winv = W['w_in'][l].rearrange("(kt p) c -> p kt c", p=128)
            with ExitStack() as hs:
                hT = sb("hT", [128, KD, T], BF16, st=hs)
                norm_phase(hT, 'nmg')
                if l == 0 and 'hTd' in DBG:
                    DMA('sp', DBG['hTd'].rearrange("(dk p) t -> p dk t", p=128), hT[:], [('hT', dk, tg) for dk in range(KD) for tg in range(NTG)], [('dbg', 'hTd')])

                def h_rhs(kt, tg):
                    return hT[:, kt, tg * 512:(tg + 1) * 512]

                def h_key(kt, tg):
                    return ('hT', kt, tg)

                with ExitStack() as ph:
                    ph.enter_context(nc.named_scope("proj"))
                    wp = [sb("wp%d" % i, [128, KD, 512], BF16, st=ph) for i in range(2)]
                    ob = [sb("pob%d" % i, [128, 512], F32, st=ph) for i in range(3)]
                    obh = [sb("pobh%d" % i, [128, 512], BF16, st=ph) for i in range(3)]
                    zc = [sb("pzc%d" % i, [128, T], F32, st=ph) for i in range(2)]
                    ccol = sb("ccol", [128, 29], F32, st=ph)
                    cnt = [0]
                    o_p, _ = PC['mup']
                    o_n, _ = PC['mun']
                    TT(ccol[:], pcol[:, o_p:o_p + 29], pcol[:, o_n:o_n + 29], ALU.add, ['pcol'], ['ccol'])
                    TS(ccol[:], ccol[:], -1.0, 1.0, ALU.mult, ALU.add, ['ccol'], ['ccol'])

                    def epi_u(ci, c0, m, tg, ps, pk):
                        k = cnt[0] % 3
                        cnt[0] += 1
                        ACT(ob[k][:], ps, AF.Gelu, pk, [('pob', k)])
                        DMA('sp', uT[c0:c0 + 128, tg * 512:(tg + 1) * 512], ob[k][:], [('pob', k)], ['uT'])

                    def epi_g(ci, c0, m, tg, ps, pk):
                        k = cnt[0] % 3
                        cnt[0] += 1
                        ACT(obh[k][:], ps, AF.Sigmoid, pk, [('pobh', k)])
                        r0 = c0 - OFF_G
                        DMA('sp', gT[r0:r0 + 128, tg * 512:(tg + 1) * 512], obh[k][:], [('pobh', k)], ['gT'])

                    def tap3(dst, r0, m, ps, pk, a_ap, b_ap, p_ap, n_ap, keys):
                        k = cnt[0] % 2
                        cnt[0] += 1
                        z = zc[k]
                        ACT(z[0:m, :], ps, AF.Identity, pk + keys, [('pzc', k)], scale=a_ap, bias=b_ap)
                        STT(z[0:m, 1:T], ps[:, 0:T - 1], p_ap, z[0:m, 1:T], ALU.mult, ALU.add, pk + keys + [('pzc', k)], [('pzc', k)])
                        STT(z[0:m, 0:T - 1], ps[:, 1:T], n_ap, z[0:m, 0:T - 1], ALU.mult, ALU.add, pk + keys + [('pzc', k)], [('pzc', k)])
                        DMA('sp', dst[r0:r0 + m, :], z[0:m, :], [('pzc', k)], ['bcT'])

                    def epi_b(ci, c0, m, tg, ps, pk):
                        tap3(bT, c0 - OFF_B, m, ps, pk, pc('cw1', ci), pc('cb', ci), pc('cw0', ci), pc('cw2', ci), ['pcol'])

                    def epi_c(ci, c0, m, tg, ps, pk):
                        tap3(cT, c0 - OFF_C, m, ps, pk, ccol[0:m, ci:ci + 1], 0.0, pc('mup', ci, m), pc('mun', ci, m), ['pcol', 'ccol'])

                    linear_fm(Winv, KD, [(i * 128, 128) for i in range(8)], h_rhs, h_key, epi_u, wp, 'wp')
                    linear_fm(Winv, KD, [(OFF_B + i * 128, 128) for i in range(24)], h_rhs, h_key, epi_b, wp, 'wp', full=True)
                    linear_fm(Winv, KD, [(OFF_C + c0, m) for (c0, m) in C_TILES], h_rhs, h_key, epi_c, wp, 'wp', full=True)
                    linear_fm(Winv, KD, [(OFF_G + i * 128, 128) for i in range(48)], h_rhs, h_key, epi_g, wp, 'wp')
                S.barrier()
                if l == 0:
                    dump('uT', uT)
                    dump('bT', bT)
                    dump('cT', cT)
                    dump('gT', gT)

                with ExitStack() as ph:
                    ph.enter_context(nc.named_scope("mixa"))
                    wv = sb("ma_wv", [128, KD, 1024], BF16, st=ph)
                    lng = sb("ma_lng", [128, 1024], F32, st=ph)
                    lnb = sb("ma_lnb", [128, 1024], F32, st=ph)
                    bsb = sb("ma_bsb", [128, 8, 128], F32, st=ph)
                    wsn = sb("ma_wsn", [128, 8, 128], F32, st=ph)
                    wsT = sb("ma_wsT", [128, 8, 128], BF16, st=ph)
                    vg = [sb("ma_vg%d" % i, [128, 1024], F32, st=ph) for i in range(3)]
                    vc = [sb("ma_vc%d" % i, [128, 1024], F32, st=ph) for i in range(3)]
                    vln = [sb("ma_vln%d" % i, [128, 1024], BF16, st=ph) for i in range(3)]
                    ut = [sb("ma_ut%d" % i, [128, 8, 128], F32, st=ph) for i in range(3)]
                    ya = [sb("ma_ya%d" % i, [128, 8, 128], BF16, st=ph) for i in range(3)]
                    tm = [sb("ma_tm%d" % i, [128, 512], F32, st=ph) for i in range(2)]
                    stt = [sb("ma_st%d" % i, [128, 8], F32, st=ph) for i in range(3)]
                    DMA('pool', wv[:], Winv[:, :, A_W:2 * A_W], (), ['ma_wv'])
                    DMA('sp', lng[:], W['gm_ln_g'][l:l + 1, :].broadcast_to([128, 1024]), (), ['ma_c'])
                    DMA('sp', lnb[:], W['gm_ln_b'][l:l + 1, :].broadcast_to([128, 1024]), (), ['ma_c'])
                    DMA('sp', bsb[:].rearrange("p g q -> p (g q)"),
                        W['gm_bs'][l:l + 1].rearrange("o g q -> o (g q)").broadcast_to([128, 1024]), (), ['ma_c'])
                    DMA('sp', wsn[:], W['gm_ws'][l].rearrange("g p q -> p g q"), (), ['ma_wsn'])
                    for hb in range(2):
                        for j in range(4):
                            g = hb * 4 + j
                            TR(PSb(hb)[:, j * 128:(j + 1) * 128], wsn[:, g, :], identF[:], ['ma_wsn', 'const'], [('ps', hb)])
                        EVAC(wsT[:, hb * 4:(hb + 1) * 4, :].rearrange("p g q -> p (g q)"), PSb(hb), [('ps', hb)], ['ma_wsT'])
                    uTv = uT.rearrange("(g d) t -> d g t", d=128)
                    yaTv = yaT.rearrange("(g d) t -> d g t", d=128)
                    def ma_vproj(i):
                        b = i % 3
                        tsl = slice(i * 128, (i + 1) * 128)
                        DMA('sp', ut[b][:], uTv[:, :, tsl], ['uT'], [('ma_ut', b)])
                        for half in range(2):
                            bank = b * 2 + half
                            for kt in range(KD):
                                MM(PSb(bank), hT[:, kt, tsl], wv[:, kt, half * 512:(half + 1) * 512], kt == 0, kt == KD - 1,
                                   [('hT', kt, i // 4), 'ma_wv'], [('ps', bank)])

                    def ma_gelu(i):
                        b = i % 3
                        for half in range(2):
                            bank = b * 2 + half
                            ACT(vg[b][:, half * 512:(half + 1) * 512], PSb(bank), AF.Gelu, [('ps', bank)], [('ma_vg', b, half), ('ma_st', b)],
                                accum_out=stt[b][:, half:half + 1])

                    def ma_elem(i):
                        b = i % 3
                        TT(stt[b][:, 2:3], stt[b][:, 0:1], stt[b][:, 1:2], ALU.add, [('ma_st', b)], [('ma_st', b)])
                        TS(stt[b][:, 3:4], stt[b][:, 2:3], -1.0 / A_W, None, ALU.mult, None, [('ma_st', b)], [('ma_st', b)])
                        TS(vc[b][:], vg[b][:], stt[b][:, 3:4], None, ALU.add, None,
                           [('ma_vg', b, 0), ('ma_vg', b, 1), ('ma_st', b)], [('ma_vc', b)])
                        ACT(vg[b][:], vc[b][:], AF.Square, [('ma_vc', b)], [('ma_vg', b, 0), ('ma_vg', b, 1), ('ma_st', b)],
                            accum_out=stt[b][:, 4:5])
                        ACT(stt[b][:, 5:6], stt[b][:, 4:5], AF.Sqrt, [('ma_st', b)], [('ma_st', b)], scale=1.0 / A_W, bias=LN_EPS)
                        RECIP(stt[b][:, 6:7], stt[b][:, 5:6], [('ma_st', b)], [('ma_st', b)])
                        STT(vc[b][:], vc[b][:], stt[b][:, 6:7], lng[:], ALU.mult, ALU.mult, [('ma_vc', b), ('ma_st', b), 'ma_c'], [('ma_vc', b)])
                        TT(vln[b][:], vc[b][:], lnb[:], ALU.add, [('ma_vc', b), 'ma_c'], [('ma_vln', b)])

                    def ma_spatial(i):
                        b = i % 3
                        tsl = slice(i * 128, (i + 1) * 128)
                        for hb in range(2):
                            bank = 6 + hb
                            for j in range(4):
                                g = hb * 4 + j
                                MM(PSb(bank)[:, j * 128:(j + 1) * 128], vln[b][:, g * 128:(g + 1) * 128], wsT[:, g, :], True, True,
                                   [('ma_vln', b), 'ma_wsT'], [('ps', bank)])
                            TT(tm[hb][:], PSb(bank), bsb[:, hb * 4:(hb + 1) * 4, :].rearrange("p g q -> p (g q)"), ALU.add,
                               [('ps', bank), 'ma_c'], [('ma_tm', hb)])
                            TT(ya[b][:, hb * 4:(hb + 1) * 4, :].rearrange("p g q -> p (g q)"), tm[hb][:],
                               ut[b][:, hb * 4:(hb + 1) * 4, :].rearrange("p g q -> p (g q)"), ALU.mult,
                               [('ma_tm', hb), ('ma_ut', b)], [('ma_ya', b)])
                        DMA('act', yaTv[:, :, tsl], ya[b][:], [('ma_ya', b)], ['yaT'])

                    ma_vproj(0)
                    ma_gelu(0)
                    for i in range(NT):
                        if i + 1 < NT:
                            ma_vproj(i + 1)
                        ma_elem(i)
                        if i + 1 < NT:
                            ma_gelu(i + 1)
                        ma_spatial(i)
                S.barrier()
            if l == 0:
                dump('yaT', yaT)
            if stop == 'mixa':
                break
            hyena_phase(l)
            if stop == 'hy':
                break
            if l == 0:
                dump('spec', spec)
                dump('ybT', ybT)
            rwkv_phase(l)
            if l == 0:
                dump('ycT', ycT)
            if stop == 'rw':
                break
            merge_phase(l)
            if l == 0:
                dump('mergedT', mgT)
            outproj_phase(l)
            if stop == 'outp':
                break
            ffn_phase(l)
        dump('xT', xT)
        if stop is None:
            final_phase()

        S.barrier()
    return nc


_NC_CACHE = {}


def kernel(**inputs):
    if 'nc' not in _NC_CACHE:
        _NC_CACHE['nc'] = build()
        _NC_CACHE['consts'] = make_consts()
    nc = _NC_CACHE['nc']
    consts = _NC_CACHE['consts']
    inp = {k_: np.ascontiguousarray(np.asarray(v)) for k_, v in inputs.items()}
    pcol = make_pcol(inp)
    base = {n: inp[n] for n in WEIGHT_SHAPES}
    base.update(consts)
    base['pcol'] = pcol
    NB_ = inp['x'].shape[0]
    in_maps = []
    for c in range(NCORES_USED):
        m = dict(base)
        m['x'] = np.ascontiguousarray(inp['x'][c % NB_])
        in_maps.append(m)
    res = run_bass_kernel_spmd(nc, in_maps, core_ids=list(range(NCORES_USED)))
    out = np.stack([np.asarray(res.results[b]['out'], dtype=np.float32) for b in range(NB_)], axis=0)
    return out
```
